# Optimizing a Trainium2 kernel written in Bass

```python
import math
import jax, jax.numpy as jnp
from jax import lax
import numpy as np

D_MODEL = 1024
BATCH = 4
SEQ = 8192
DEPTH = 4
DEC_BATCH = 8
DEC_SEQ = 32
PAST_LEN = 4096

CHUNK = 64
N_MIXERS = 3
N_SB_LAYERS = (DEPTH + 2) // 3
N_ML_LAYERS = (DEPTH + 1) // 3
N_CV_LAYERS = DEPTH // 3
D_MIX = D_MODEL
SB_HEADS = 8
SB_HEAD_DIM = D_MIX // SB_HEADS
SB_BLOCK = 128
ML_HEADS = 4
ML_HEAD_DIM = D_MIX // ML_HEADS
ML_CONV = 4
CV_CONV = 3
MEM_TOKENS = 256
MEM_HEADS = 4
MEM_HEAD_DIM = 128
D_MEM = MEM_HEADS * MEM_HEAD_DIM
D_BRANCH = D_MIX + D_MEM
IN_SB = 3 * D_MIX + D_MEM + D_BRANCH
IN_ML = 3 * D_MIX + 2 * ML_HEADS + D_MEM + D_BRANCH
IN_CV = 3 * D_MIX + D_MEM + D_BRANCH
EPS = 1e-6

kernel_name = "hybrid_sb_mlstm_shortconv_stream_step"


def rmsnorm(x, g):
    xf = x.astype(jnp.float32)
    y = xf * lax.rsqrt(jnp.mean(xf * xf, axis=-1, keepdims=True) + EPS)
    return (y * g.astype(jnp.float32)).astype(x.dtype)


def causal_dwconv(x, buf, w):
    width = w.shape[0]
    t = x.shape[1]
    xp = jnp.concatenate([buf.astype(x.dtype), x], axis=1)
    y = xp[:, 0:t] * w[0]
    for j in range(1, width):
        y = y + xp[:, j:j + t] * w[j]
    return y, xp[:, xp.shape[1] - (width - 1):]


def sb_attend(q, k, v, q_pos, k_pos):
    z = jnp.einsum('bqhd,bkhd->bhqk', q, k).astype(jnp.float32) / math.sqrt(SB_HEAD_DIM)
    mask = k_pos[None, :] < q_pos[:, None]
    c = jnp.where(mask, jax.nn.softplus(z), 0.0)
    later = lax.cumsum(c, axis=3, reverse=True) - c
    a = jnp.where(mask, jnp.exp(jax.nn.log_sigmoid(z) - later), 0.0)
    return jnp.einsum('bhqk,bkhd->bqhd', a.astype(v.dtype), v)


def sb_prompt_sweep(q, k, v):
    b, t = q.shape[:2]
    nb = t // SB_BLOCK
    qb = q.reshape(b, nb, SB_BLOCK, SB_HEADS, SB_HEAD_DIM).swapaxes(0, 1)
    k_pos = jnp.arange(t)

    def one_block(args):
        q_blk, start = args
        return sb_attend(q_blk, k, v, start + jnp.arange(SB_BLOCK), k_pos)

    out = lax.map(one_block, (qb, jnp.arange(nb) * SB_BLOCK))
    return out.swapaxes(0, 1).reshape(b, t, SB_HEADS, SB_HEAD_DIM)


def sb_mix(u, cache_k, cache_v):
    b, t = u.shape[:2]
    q, k, v = (p.reshape(b, t, SB_HEADS, SB_HEAD_DIM) for p in jnp.split(u, 3, axis=-1))
    if cache_k is None:
        o = sb_prompt_sweep(q, k, v)
    else:
        past = cache_k.shape[1]
        kk = jnp.concatenate([cache_k.astype(k.dtype), k], axis=1)
        vv = jnp.concatenate([cache_v.astype(v.dtype), v], axis=1)
        o = sb_attend(q, kk, vv, past + jnp.arange(t), jnp.arange(past + t))
    return o.reshape(b, t, D_MIX), (k, v)


def ml_chunk(carry, inp):
    c0, n0, m0 = carry
    q, k, v, ig, lf = inp
    L = q.shape[1]
    bcum = jnp.cumsum(lf, axis=1).swapaxes(1, 2)
    igt = ig.swapaxes(1, 2)
    causal = jnp.tril(jnp.ones((L, L), dtype=bool))
    dmat = jnp.where(causal, bcum[..., :, None] - bcum[..., None, :] + igt[..., None, :], -jnp.inf)
    m_t = jnp.maximum(bcum + m0[..., None], jnp.max(dmat, axis=-1))
    w = jnp.exp(dmat - m_t[..., None])
    inter = jnp.exp(bcum + m0[..., None] - m_t)
    s = jnp.einsum('blhd,bshd->bhls', q, k) * w
    num = jnp.einsum('bhls,bshe->blhe', s, v) + inter.swapaxes(1, 2)[..., None] * jnp.einsum('blhd,bhde->blhe', q, c0)
    den = jnp.sum(s, axis=-1) + inter * jnp.einsum('blhd,bhd->bhl', q, n0)
    h = num / jnp.maximum(jnp.abs(den), jnp.exp(-m_t)).swapaxes(1, 2)[..., None]
    m1 = m_t[..., -1]
    g = jnp.exp(bcum[..., -1:] - bcum + igt - m1[..., None])
    decay = jnp.exp(bcum[..., -1] + m0 - m1)
    c1 = decay[..., None, None] * c0 + jnp.einsum('bhs,bshd,bshe->bhde', g, k, v)
    n1 = decay[..., None] * n0 + jnp.einsum('bhs,bshd->bhd', g, k)
    return (c1, n1, m1), h


def ml_mix(u, params, state):
    conv_w, conv_b, wq, wk, b_ig, b_fg, g_head, skip = params
    c0, n0, m0, buf = state
    f32 = jnp.float32
    b, t = u.shape[:2]
    xm = u[..., :D_MIX]
    v = u[..., D_MIX:2 * D_MIX]
    o_pre = u[..., 2 * D_MIX:3 * D_MIX]
    ig = (u[..., 3 * D_MIX:3 * D_MIX + ML_HEADS] + b_ig).astype(f32)
    lf = jax.nn.log_sigmoid((u[..., 3 * D_MIX + ML_HEADS:] + b_fg).astype(f32))
    xc_lin, buf_new = causal_dwconv(xm, buf, conv_w)
    xc = jax.nn.silu(xc_lin + conv_b)
    xch = xc.reshape(b, t, ML_HEADS, ML_HEAD_DIM)
    q = jnp.einsum('bthd,hde->bthe', xch, wq).astype(f32)
    k = (jnp.einsum('bthd,hde->bthe', xch, wk) / math.sqrt(ML_HEAD_DIM)).astype(f32)
    vh = v.reshape(b, t, ML_HEADS, ML_HEAD_DIM).astype(f32)
    L = min(t, CHUNK)
    nc = t // L

    def to_chunks(a):
        return a.reshape((b, nc, L) + a.shape[2:]).swapaxes(0, 1)

    carry0 = (c0.astype(f32), n0.astype(f32), m0.astype(f32))
    (c1, n1, m1), hs = lax.scan(ml_chunk, carry0, (to_chunks(q), to_chunks(k), to_chunks(vh), to_chunks(ig), to_chunks(lf)))
    h_tilde = hs.swapaxes(0, 1).reshape(b, t, ML_HEADS, ML_HEAD_DIM)
    h_norm = rmsnorm(h_tilde, g_head.reshape(ML_HEADS, ML_HEAD_DIM)).reshape(b, t, D_MIX).astype(u.dtype)
    y = jax.nn.sigmoid(o_pre) * h_norm + skip * xc
    return y, (c1, n1, m1, buf_new)


def cv_mix(u, conv_w, buf):
    bg, cg, hx = jnp.split(u, 3, axis=-1)
    w, buf_new = causal_dwconv(cg * hx, buf, conv_w)
    return bg * w, buf_new


def mem_kv(mem, g_mem_l, w_mem_kv_l):
    b = mem.shape[0]
    kv = rmsnorm(mem, g_mem_l) @ w_mem_kv_l
    mk = kv[..., :D_MEM].reshape(b, MEM_TOKENS, MEM_HEADS, MEM_HEAD_DIM)
    mv = kv[..., D_MEM:].reshape(b, MEM_TOKENS, MEM_HEADS, MEM_HEAD_DIM)
    return mk, mv


def mem_attend(qm, mk, mv):
    b, t = qm.shape[:2]
    q = qm.reshape(b, t, MEM_HEADS, MEM_HEAD_DIM)
    s = jnp.einsum('bthd,bmhd->bhtm', q, mk.astype(q.dtype)).astype(jnp.float32) / math.sqrt(MEM_HEAD_DIM)
    p = jax.nn.softmax(s, axis=-1).astype(q.dtype)
    return jnp.einsum('bhtm,bmhd->bthd', p, mv.astype(q.dtype)).reshape(b, t, D_MEM)


def sandwich_layer(x, w_in, mixer, mk, mv, g_pre_l, g_post_l, w_out_l):
    h = rmsnorm(x, g_pre_l)
    u = h @ w_in
    n_mix = w_in.shape[1] - D_MEM - D_BRANCH
    y_mix, new_state = mixer(u[..., :n_mix])
    y_mem = mem_attend(u[..., n_mix:n_mix + D_MEM], mk, mv)
    z = u[..., n_mix + D_MEM:]
    y = jnp.concatenate([y_mix.astype(u.dtype), y_mem], axis=-1) * jax.nn.silu(z)
    return x + rmsnorm(y @ w_out_l, g_post_l), new_state


def setup_inputs(seed: int = 0) -> dict:
    key = jax.random.key(seed)
    ks = jax.random.split(key, 32)

    def nrm(k, shape, scale=1.0):
        return scale * jax.random.normal(k, shape, jnp.float32)

    return {
        "x_prompt": nrm(ks[0], (BATCH, SEQ, D_MODEL)),
        "x_sample": nrm(ks[1], (DEC_BATCH, DEC_SEQ, D_MODEL)),
        "cache_sb_k": nrm(ks[2], (N_SB_LAYERS, DEC_BATCH, PAST_LEN, SB_HEADS, SB_HEAD_DIM)),
        "cache_sb_v": nrm(ks[3], (N_SB_LAYERS, DEC_BATCH, PAST_LEN, SB_HEADS, SB_HEAD_DIM)),
        "state_ml_c": nrm(ks[4], (N_ML_LAYERS, DEC_BATCH, ML_HEADS, ML_HEAD_DIM, ML_HEAD_DIM), 0.05),
        "state_ml_n": nrm(ks[5], (N_ML_LAYERS, DEC_BATCH, ML_HEADS, ML_HEAD_DIM), 0.05),
        "state_ml_m": nrm(ks[6], (N_ML_LAYERS, DEC_BATCH, ML_HEADS), 0.5),
        "state_ml_conv": nrm(ks[7], (N_ML_LAYERS, DEC_BATCH, ML_CONV - 1, D_MIX)),
        "state_cv_conv": nrm(ks[8], (N_CV_LAYERS, DEC_BATCH, CV_CONV - 1, D_MIX)),
        "cache_mem_k": nrm(ks[9], (DEPTH, DEC_BATCH, MEM_TOKENS, MEM_HEADS, MEM_HEAD_DIM)),
        "cache_mem_v": nrm(ks[10], (DEPTH, DEC_BATCH, MEM_TOKENS, MEM_HEADS, MEM_HEAD_DIM)),
        "mem_prompt": nrm(ks[11], (BATCH, MEM_TOKENS, D_MODEL)),
        "g_pre": 1.0 + nrm(ks[12], (DEPTH, D_MODEL), 0.02),
        "g_post": 1.0 + nrm(ks[13], (DEPTH, D_MODEL), 0.02),
        "g_mem": 1.0 + nrm(ks[14], (DEPTH, D_MODEL), 0.02),
        "w_mem_kv": nrm(ks[15], (DEPTH, D_MODEL, 2 * D_MEM), D_MODEL ** -0.5),
        "w_out": nrm(ks[16], (DEPTH, D_BRANCH, D_MODEL), D_BRANCH ** -0.5),
        "w_in_sb": nrm(ks[17], (N_SB_LAYERS, D_MODEL, IN_SB), D_MODEL ** -0.5),
        "w_in_ml": nrm(ks[18], (N_ML_LAYERS, D_MODEL, IN_ML), D_MODEL ** -0.5),
        "conv_ml_w": nrm(ks[19], (N_ML_LAYERS, ML_CONV, D_MIX), ML_CONV ** -0.5),
        "conv_ml_b": nrm(ks[20], (N_ML_LAYERS, D_MIX), 0.02),
        "wq_ml": nrm(ks[21], (N_ML_LAYERS, ML_HEADS, ML_HEAD_DIM, ML_HEAD_DIM), ML_HEAD_DIM ** -0.5),
        "wk_ml": nrm(ks[22], (N_ML_LAYERS, ML_HEADS, ML_HEAD_DIM, ML_HEAD_DIM), ML_HEAD_DIM ** -0.5),
        "b_ig_ml": nrm(ks[23], (N_ML_LAYERS, ML_HEADS), 0.1),
        "b_fg_ml": jnp.linspace(3.0, 6.0, ML_HEADS, dtype=jnp.float32)[None, :] + nrm(ks[24], (N_ML_LAYERS, ML_HEADS), 0.1),
        "g_head_ml": 1.0 + nrm(ks[25], (N_ML_LAYERS, D_MIX), 0.02),
        "skip_ml": 1.0 + nrm(ks[26], (N_ML_LAYERS, D_MIX), 0.02),
        "w_in_cv": nrm(ks[27], (N_CV_LAYERS, D_MODEL, IN_CV), D_MODEL ** -0.5),
        "conv_cv_w": nrm(ks[28], (N_CV_LAYERS, CV_CONV, D_MIX), CV_CONV ** -0.5),
    }


def reference(x_prompt, x_sample, cache_sb_k, cache_sb_v, state_ml_c, state_ml_n, state_ml_m,
              state_ml_conv, state_cv_conv, cache_mem_k, cache_mem_v, mem_prompt,
              g_pre, g_post, g_mem, w_mem_kv, w_out, w_in_sb, w_in_ml, conv_ml_w, conv_ml_b,
              wq_ml, wk_ml, b_ig_ml, b_fg_ml, g_head_ml, skip_ml, w_in_cv, conv_cv_w):
    f32 = jnp.float32
    bp = x_prompt.shape[0]
    y_p, y_s = x_prompt, x_sample
    sb_k_p, sb_v_p, sb_k_s, sb_v_s = [], [], [], []
    ml_p, ml_s = [], []
    cv_p, cv_s = [], []
    mem_k_p, mem_v_p = [], []
    for i in range(DEPTH):
        kind, j = i % N_MIXERS, i // N_MIXERS
        mk_p, mv_p = mem_kv(mem_prompt, g_mem[i], w_mem_kv[i])
        mem_k_p.append(mk_p)
        mem_v_p.append(mv_p)
        mk_s, mv_s = cache_mem_k[i], cache_mem_v[i]
        norms = (g_pre[i], g_post[i], w_out[i])
        if kind == 0:
            y_p, (k_new, v_new) = sandwich_layer(y_p, w_in_sb[j], lambda u: sb_mix(u, None, None), mk_p, mv_p, *norms)
            sb_k_p.append(k_new)
            sb_v_p.append(v_new)
            y_s, (k_new, v_new) = sandwich_layer(y_s, w_in_sb[j], lambda u: sb_mix(u, cache_sb_k[j], cache_sb_v[j]), mk_s, mv_s, *norms)
            sb_k_s.append(k_new)
            sb_v_s.append(v_new)
        elif kind == 1:
            params = (conv_ml_w[j], conv_ml_b[j], wq_ml[j], wk_ml[j], b_ig_ml[j], b_fg_ml[j], g_head_ml[j], skip_ml[j])
            zero_state = (jnp.zeros((bp, ML_HEADS, ML_HEAD_DIM, ML_HEAD_DIM), f32),
                          jnp.zeros((bp, ML_HEADS, ML_HEAD_DIM), f32),
                          jnp.zeros((bp, ML_HEADS), f32),
                          jnp.zeros((bp, ML_CONV - 1, D_MIX), x_prompt.dtype))
            cached_state = (state_ml_c[j], state_ml_n[j], state_ml_m[j], state_ml_conv[j])
            y_p, st = sandwich_layer(y_p, w_in_ml[j], lambda u: ml_mix(u, params, zero_state), mk_p, mv_p, *norms)
            ml_p.append(st)
            y_s, st = sandwich_layer(y_s, w_in_ml[j], lambda u: ml_mix(u, params, cached_state), mk_s, mv_s, *norms)
            ml_s.append(st)
        else:
            zero_buf = jnp.zeros((bp, CV_CONV - 1, D_MIX), x_prompt.dtype)
            y_p, st = sandwich_layer(y_p, w_in_cv[j], lambda u: cv_mix(u, conv_cv_w[j], zero_buf), mk_p, mv_p, *norms)
            cv_p.append(st)
            y_s, st = sandwich_layer(y_s, w_in_cv[j], lambda u: cv_mix(u, conv_cv_w[j], state_cv_conv[j]), mk_s, mv_s, *norms)
            cv_s.append(st)
    return (y_p, y_s,
            jnp.stack(sb_k_p), jnp.stack(sb_v_p),
            jnp.stack([s[0] for s in ml_p]), jnp.stack([s[1] for s in ml_p]),
            jnp.stack([s[2] for s in ml_p]), jnp.stack([s[3] for s in ml_p]),
            jnp.stack(cv_p),
            jnp.stack(mem_k_p), jnp.stack(mem_v_p),
            jnp.stack(sb_k_s), jnp.stack(sb_v_s),
            jnp.stack([s[0] for s in ml_s]), jnp.stack([s[1] for s in ml_s]),
            jnp.stack([s[2] for s in ml_s]), jnp.stack([s[3] for s in ml_s]),
            jnp.stack(cv_s))
```

```python
import contextlib
import math
import numpy as np
import concourse.bass as bass
import concourse.mybir as mybir
from concourse.bass_utils import run_bass_kernel_spmd

F32 = mybir.dt.float32
BF16 = mybir.dt.bfloat16
AF = mybir.ActivationFunctionType
ALU = mybir.AluOpType
AX = mybir.AxisListType

NDS = 48
D = 1024
EPS = 1e-6


class Tile:
    __slots__ = ("ap", "w", "r", "name")

    def __init__(self, ap, name=""):
        self.ap = ap
        self.w = None
        self.r = {}
        self.name = name

    def __getitem__(self, idx):
        return self.ap[idx]


class KB:
    def __init__(self, nc, stack):
        self.nc = nc
        self.stack = stack
        self.E = {"pe": nc.tensor, "act": nc.scalar, "dve": nc.vector, "pool": nc.gpsimd, "sp": nc.sync}
        self.csem = {e: stack.enter_context(nc.semaphore("c_" + e)) for e in ("pe", "act", "dve", "pool")}
        self.ccnt = {e: 0 for e in self.csem}
        self.dsem = [stack.enter_context(nc.semaphore("d%d" % i)) for i in range(NDS)]
        self.dcnt = [0] * NDS
        self.dnext = {"sp": 0, "pool": NDS // 2}
        self.seen = {e: {} for e in self.E}
        self.ninst = 0
        self.rr = {}

    def _wait(self, eng, tok):
        sem, val, key, kind = tok
        if self.seen[eng].get(key, 0) >= val:
            return
        self.E[eng].wait_ge(sem, val)
        self.seen[eng][key] = val
        self.ninst += 1

    def _deps(self, eng, reads, writes):
        toks = []
        for t in reads:
            if t.w is not None:
                toks.append(t.w)
        for t in writes:
            if t.w is not None:
                toks.append(t.w)
            toks.extend(t.r.values())
        for tok in toks:
            if eng == "pe" and tok[3] == "pe":
                continue
            self._wait(eng, tok)

    def _mark(self, tok, reads, writes):
        for t in reads:
            old = t.r.get(tok[2])
            if old is None or old[1] < tok[1]:
                t.r[tok[2]] = tok
        for t in writes:
            t.w = tok
            t.r = {}

    def op(self, eng, fn, reads=(), writes=()):
        self._deps(eng, reads, writes)
        ins = fn(self.E[eng])
        self.ccnt[eng] += 1
        ins.then_inc(self.csem[eng], 1)
        tok = (self.csem[eng], self.ccnt[eng], eng, eng)
        self._mark(tok, reads, writes)
        self.ninst += 1
        return ins

    def mm(self, out_tile, out_ap, pairs, reads, start=True, stop=True):
        writes = [out_tile]
        self._deps("pe", reads, writes)
        n = len(pairs)
        ins = None
        for i, (l, r) in enumerate(pairs):
            ins = self.nc.tensor.matmul(out_ap, l, r, start=(start and i == 0), stop=(stop and i == n - 1))
            self.ninst += 1
        self.ccnt["pe"] += 1
        ins.then_inc(self.csem["pe"], 1)
        tok = (self.csem["pe"], self.ccnt["pe"], "pe", "pe")
        self._mark(tok, reads, writes)
        return ins

    def tr(self, out_tile, out_ap, in_ap, ident_ap, reads):
        self._deps("pe", reads, [out_tile])
        ins = self.nc.tensor.transpose(out_ap, in_ap, ident_ap)
        self.ccnt["pe"] += 1
        ins.then_inc(self.csem["pe"], 1)
        tok = (self.csem["pe"], self.ccnt["pe"], "pe", "pe")
        self._mark(tok, reads, [out_tile])
        self.ninst += 1
        return ins

    def dma(self, q, out_ap, in_ap, reads=(), writes=(), **kw):
        self._deps(q, reads, writes)
        i = self.dnext[q]
        base = 0 if q == "sp" else NDS // 2
        self.dnext[q] = base + (i - base + 1) % (NDS // 2)
        if self.dcnt[i] > 0:
            self._wait(q, (self.dsem[i], 16 * self.dcnt[i], ("d", i), "dma"))
        self.E[q].dma_start(out=out_ap, in_=in_ap, **kw).then_inc(self.dsem[i], 16)
        self.dcnt[i] += 1
        tok = (self.dsem[i], 16 * self.dcnt[i], ("d", i), "dma")
        self._mark(tok, reads, writes)
        self.ninst += 1

    def barrier(self, engines=("pe", "act", "dve", "pool", "sp")):
        for e in engines:
            for c in self.csem:
                if self.ccnt[c] > 0:
                    self._wait(e, (self.csem[c], self.ccnt[c], c, c))
            for i in range(NDS):
                if self.dcnt[i] > 0:
                    self._wait(e, (self.dsem[i], 16 * self.dcnt[i], ("d", i), "dma"))

    def sb(self, name, shape, dtype, stack=None):
        self.uid = getattr(self, "uid", 0) + 1
        name = "%s_u%d" % (name, self.uid)
        t = (stack or self.stack).enter_context(self.nc.sbuf_tensor(name, list(shape), dtype))
        return Tile(t, name)

    def ps(self, name, shape, dtype=F32, stack=None):
        self.uid = getattr(self, "uid", 0) + 1
        name = "%s_u%d" % (name, self.uid)
        t = (stack or self.stack).enter_context(self.nc.psum_tensor(name, list(shape), dtype))
        return Tile(t, name)

    def ring(self, key, tiles):
        i = self.rr.get(key, 0)
        self.rr[key] = i + 1
        return tiles[i % len(tiles)]


class Prog:
    def __init__(self, T, TS, PAST, DEPTH):
        self.T, self.TS, self.PAST, self.DEPTH = T, TS, PAST, DEPTH
        self.NSB, self.NML, self.NCV = (DEPTH + 2) // 3, (DEPTH + 1) // 3, DEPTH // 3
        self.nc = bass.Bass("TRN2", target_bir_lowering=False)
        self.in_shapes = {}
        self.out_shapes = {}

    def din(self, name, shape, dt=F32):
        self.in_shapes[name] = tuple(shape)
        return self.nc.dram_tensor(name, list(shape), dt, kind="ExternalInput").ap()

    def dout(self, name, shape, dt=F32):
        self.out_shapes[name] = tuple(shape)
        return self.nc.dram_tensor(name, list(shape), dt, kind="ExternalOutput").ap()

    def dscr(self, name, shape, dt=F32):
        return self.nc.dram_tensor(name, list(shape), dt, kind="Internal").ap()

    def build(self):
        T, TS, PAST, DEPTH = self.T, self.TS, self.PAST, self.DEPTH
        NSB, NML, NCV = self.NSB, self.NML, self.NCV
        I = {}
        I["xp"] = self.din("xp", [T, D])
        I["xs"] = self.din("xs", [TS, D])
        I["csk"] = self.din("csk", [NSB, PAST, 8, 128])
        I["csv"] = self.din("csv", [NSB, PAST, 8, 128])
        I["smc"] = self.din("smc", [max(NML, 1), 4, 256, 256])
        I["smn"] = self.din("smn", [max(NML, 1), 4, 256])
        I["smm"] = self.din("smm", [max(NML, 1), 4])
        I["smconvT"] = self.din("smconvT", [max(NML, 1), 128, 8, 3])
        I["scvT"] = self.din("scvT", [max(NCV, 1), 128, 8, 2])
        I["cmk"] = self.din("cmk", [DEPTH, 256, 4, 128])
        I["cmv"] = self.din("cmv", [DEPTH, 256, 4, 128])
        I["memp"] = self.din("memp", [256, D])
        I["gpre_r"] = self.din("gpre_r", [DEPTH, 128, D])
        I["gpost_r"] = self.din("gpost_r", [DEPTH, 128, D])
        I["gmem_r"] = self.din("gmem_r", [DEPTH, 128, D])
        I["wmemkv"] = self.din("wmemkv", [DEPTH, D, D])
        I["wout"] = self.din("wout", [DEPTH, 1536, D])
        I["winsb"] = self.din("winsb", [NSB, D, 5120])
        I["winml"] = self.din("winml", [max(NML, 1), D, 5128])
        I["wincv"] = self.din("wincv", [max(NCV, 1), D, 5120])
        I["convmlT"] = self.din("convmlT", [max(NML, 1), 128, 8, 5])
        I["wq"] = self.din("wq", [max(NML, 1), 4, 256, 256])
        I["wk"] = self.din("wk", [max(NML, 1), 4, 256, 256])
        I["gcol"] = self.din("gcol", [max(NML, 1), 4, 2])
        I["ghead_r"] = self.din("ghead_r", [max(NML, 1), 128, D])
        I["negmask"] = self.din("negmask", [128, 128])
        I["sel4"] = self.din("sel4", [4, 512])
        I["skipT"] = self.din("skipT", [max(NML, 1), 128, 8])
        I["convcvT"] = self.din("convcvT", [max(NCV, 1), 128, 8, 3])
        I["consts"] = self.din("consts", [128, 128 * 5 + 4 * 512 * 1])
        I["smask"] = self.din("smask", [128, 3 * 32])
        O = {}
        O["yp"] = self.dout("yp", [T, D])
        O["ys"] = self.dout("ys", [TS, D])
        O["sbk_p"] = self.dout("sbk_p", [NSB, T, 8, 128])
        O["sbv_p"] = self.dout("sbv_p", [NSB, T, 8, 128])
        O["mlc_p"] = self.dout("mlc_p", [max(NML, 1), 4, 256, 256])
        O["mln_p"] = self.dout("mln_p", [max(NML, 1), 4, 256])
        O["mlm_p"] = self.dout("mlm_p", [max(NML, 1), 4])
        O["mlconv_p"] = self.dout("mlconv_p", [max(NML, 1), 3, D])
        O["cv_p"] = self.dout("cv_p", [max(NCV, 1), 2, D])
        O["memk_p"] = self.dout("memk_p", [DEPTH, 256, 4, 128])
        O["memv_p"] = self.dout("memv_p", [DEPTH, 256, 4, 128])
        O["sbk_s"] = self.dout("sbk_s", [NSB, TS, 8, 128])
        O["sbv_s"] = self.dout("sbv_s", [NSB, TS, 8, 128])
        O["mlc_s"] = self.dout("mlc_s", [max(NML, 1), 4, 256, 256])
        O["mln_s"] = self.dout("mln_s", [max(NML, 1), 4, 256])
        O["mlm_s"] = self.dout("mlm_s", [max(NML, 1), 4])
        O["mlconv_s"] = self.dout("mlconv_s", [max(NML, 1), 3, D])
        O["cv_s"] = self.dout("cv_s", [max(NCV, 1), 2, D])
        self.I, self.O = I, O
        self.SP = dict(name="p", T=T, xin=I["xp"], yout=O["yp"], xres=self.dscr("xres_p", [T, D]),
                       FM=self.dscr("fm_p", [40, 128, T], BF16), YT=self.dscr("yt_p", [12, 128, T], BF16),
                       TMs=self.dscr("tm_p", [2, T, D], BF16), GT=self.dscr("gt_p", [8, T], F32))
        self.SS = dict(name="s", T=TS, xin=I["xs"], yout=O["ys"], xres=self.dscr("xres_s", [TS, D]),
                       FM=self.dscr("fm_s", [40, 128, TS], BF16), YT=self.dscr("yt_s", [12, 128, TS], BF16),
                       TMs=self.dscr("tm_s", [2, TS, D], BF16), GT=self.dscr("gt_s", [8, TS], F32))
        with contextlib.ExitStack() as st:
            k = KB(self.nc, st)
            self.k = k
            self.cst_f = k.sb("cst_f", [128, 128], F32)
            self.cst_b = k.sb("cst_b", [128, 4, 128], BF16)
            self.msk = k.sb("msk", [128, 4, 512], BF16)
            self.smsk = k.sb("smsk", [128, 3, 32], BF16)
            with contextlib.ExitStack() as s2:
                tmp = k.sb("cst_tmp", [128, 128 * 5 + 2048], F32, s2)
                tmp2 = k.sb("cst_tmp2", [128, 96], F32, s2)
                k.dma("sp", tmp[:], I["consts"][:, :], writes=[tmp])
                k.dma("sp", tmp2[:], I["smask"][:, :], writes=[tmp2])
                k.op("dve", lambda e: e.tensor_copy(out=self.cst_f[:], in_=tmp[:, 0:128]), reads=[tmp], writes=[self.cst_f])
                k.op("dve", lambda e: e.tensor_copy(out=self.cst_b[:].rearrange("p a b -> p (a b)"), in_=tmp[:, 0:512]), reads=[tmp], writes=[self.cst_b])
                k.op("dve", lambda e: e.tensor_copy(out=self.msk[:].rearrange("p a b -> p (a b)"), in_=tmp[:, 640:640 + 2048]), reads=[tmp], writes=[self.msk])
                k.op("dve", lambda e: e.tensor_copy(out=self.smsk[:].rearrange("p a b -> p (a b)"), in_=tmp2[:]), reads=[tmp2], writes=[self.smsk])
                k.barrier()
            for li in range(DEPTH):
                kind, j = li % 3, li // 3
                self.layer(li, kind, j)
            k.barrier()
        return self.nc

    def load_w_gen(self, k, st, dst, src, ncols, nkc, name, CP=256):
        stg = [k.sb("%s_stg%d" % (name, i), [128, nkc, CP], F32, st) for i in range(2)]
        srcv = src.rearrange("(c p) n -> p c n", p=128)
        c0 = 0
        i = 0
        while c0 < ncols:
            cw = min(CP, ncols - c0)
            s = stg[i % 2]
            k.dma("sp", s[:, :, 0:cw], srcv[:, :, c0:c0 + cw], writes=[s])
            eng = "pool" if i % 2 == 0 else "act"
            if eng == "pool":
                k.op("pool", lambda e, s=s, c0=c0, cw=cw: e.tensor_copy(out=dst[:, :, c0:c0 + cw], in_=s[:, :, 0:cw]), reads=[s], writes=[dst])
            else:
                k.op("act", lambda e, s=s, c0=c0, cw=cw: e.activation(out=dst[:, :, c0:c0 + cw], in_=s[:, :, 0:cw], func=AF.Copy), reads=[s], writes=[dst])
            c0 += cw
            i += 1
            yield

    def load_w(self, k, st, dst, src, ncols, nkc, name, CP=256):
        for _ in self.load_w_gen(k, st, dst, src, ncols, nkc, name, CP):
            pass

    def layer(self, li, kind, j):
        k = self.k
        I, O = self.I, self.O
        last = (li == self.DEPTH - 1)
        with contextlib.ExitStack() as st:
            mk = {"p": k.sb("mkT_p", [128, 4, 256], BF16, st), "s": k.sb("mkT_s", [128, 4, 256], BF16, st)}
            mv = {"p": k.sb("mv_p", [128, 2, 512], BF16, st), "s": k.sb("mv_s", [128, 2, 512], BF16, st)}
            self.memkv_phase(li, mk, mv)
            k.barrier()
            with contextlib.ExitStack() as s1:
                ncols = 5128 if kind == 1 else 5120
                if getattr(self, "wb_pref", None) is not None:
                    Wb = self.wb_pref
                else:
                    Wb = k.sb("Wb", [128, 8, ncols], BF16, s1)
                    wsrc = (I["winsb"], I["winml"], I["wincv"])[kind][j]
                    with contextlib.ExitStack() as sw:
                        self.load_w(k, sw, Wb, wsrc, ncols, 8, "win")
                        k.barrier()
                for stm in (self.SS, self.SP):
                    with contextlib.ExitStack() as s3:
                        self.p1(stm, li, kind, j, Wb, mk[stm["name"]], mv[stm["name"]], s3)
                        k.barrier()
        if getattr(self, "wb_pref", None) is not None:
            self.wb_stack.close()
            self.wb_pref = None
        if kind == 0:
            for stm in (self.SS, self.SP):
                with contextlib.ExitStack() as s3:
                    self.p2_sb(stm, li, j, s3)
                    k.barrier()
        elif kind == 1:
            for stm in (self.SS, self.SP):
                with contextlib.ExitStack() as s3:
                    self.p2_ml(stm, li, j, s3)
                    k.barrier()
        bg = None
        if not last:
            nkind, nj = (li + 1) % 3, (li + 1) // 3
            nncols = 5128 if nkind == 1 else 5120
            self.wb_stack = contextlib.ExitStack()
            self.wb_pref = k.sb("Wbn", [128, 8, nncols], BF16, self.wb_stack)
            self.wb_stg_stack = contextlib.ExitStack()
            nsrc = (I["winsb"], I["winml"], I["wincv"])[nkind][nj]
            bg = self.load_w_gen(k, self.wb_stg_stack, self.wb_pref, nsrc, nncols, 8, "winn")
            next(bg, None)
        with contextlib.ExitStack() as s1:
            Wo = k.sb("Wo", [128, 12, D], BF16, s1)
            gpo = k.sb("gpo", [128, D], F32, s1)
            with contextlib.ExitStack() as sw:
                self.load_w(k, sw, Wo, I["wout"][li], D, 12, "wout")
                k.dma("sp", gpo[:], I["gpost_r"][li], writes=[gpo])
                k.barrier()
            for stm in (self.SS, self.SP):
                with contextlib.ExitStack() as s3:
                    self.p3(stm, li, Wo, gpo, last, s3, bg if stm is self.SP else None)
                    if stm is self.SP and bg is not None:
                        for _ in bg:
                            pass
                    k.barrier()
        if bg is not None:
            self.wb_stg_stack.close()

    def rms_front(self, k, xsrc_ap, np_, xt, junk, ss, hb, grep):
        k.dma("sp", xt[0:np_, :], xsrc_ap, writes=[xt])
        k.op("pool", lambda e: e.memset(ss[:], 0.0), writes=[ss])
        k.op("act", lambda e: e.activation(out=junk[0:np_, :], in_=xt[0:np_, :], func=AF.Square, accum_out=ss[0:np_, 0:1]), reads=[xt, ss], writes=[junk, ss])
        k.op("act", lambda e: e.activation(out=ss[0:np_, 1:2], in_=ss[0:np_, 0:1], func=AF.Ln, scale=1.0 / D, bias=EPS), reads=[ss], writes=[ss])
        k.op("act", lambda e: e.activation(out=ss[0:np_, 1:2], in_=ss[0:np_, 1:2], func=AF.Exp, scale=-0.5), reads=[ss], writes=[ss])
        k.op("dve", lambda e: e.scalar_tensor_tensor(out=hb[0:np_, :], in0=xt[0:np_, :], scalar=ss[0:np_, 1:2], in1=grep[0:np_, :], op0=ALU.mult, op1=ALU.mult), reads=[xt, ss, grep], writes=[hb])

    def to_fm(self, k, hb, np_, ptr, hT, col0):
        for kc in range(8):
            k.tr(ptr, ptr[:, kc, 0:np_], hb[0:np_, kc * 128:(kc + 1) * 128], self.cst_b[0:np_, 0, 0:np_], reads=[hb, self.cst_b])
        k.op("dve", lambda e: e.tensor_copy(out=hT[:, :, col0:col0 + np_], in_=ptr[:, :, 0:np_]), reads=[ptr], writes=[hT])

    def memkv_phase(self, li, mk, mv):
        k = self.k
        I, O = self.I, self.O
        with contextlib.ExitStack() as st:
            Wm = k.sb("Wm", [128, 8, D], BF16, st)
            gm = k.sb("gm", [128, D], F32, st)
            with contextlib.ExitStack() as sw:
                self.load_w(k, sw, Wm, I["wmemkv"][li], D, 8, "wmem")
                k.barrier()
            k.dma("sp", gm[:], I["gmem_r"][li], writes=[gm])
            xt = [k.sb("mxt%d" % i, [128, D], F32, st) for i in range(2)]
            junk = k.sb("mjunk", [128, D], BF16, st)
            ss = [k.sb("mss%d" % i, [128, 2], F32, st) for i in range(2)]
            hb = [k.sb("mhb%d" % i, [128, D], BF16, st) for i in range(2)]
            hT = k.sb("mhT", [128, 8, 256], BF16, st)
            ptr = k.ps("mptr", [128, 8, 128], BF16, st)
            acc = [k.ps("macc%d" % i, [128, 512], F32, st) for i in range(3)]
            og = [k.sb("mog%d" % i, [128, 512], F32, st) for i in range(2)]
            for s in range(2):
                self.rms_front(k, I["memp"][s * 128:(s + 1) * 128, :], 128, xt[s], junk, ss[s], hb[s], gm)
                self.to_fm(k, hb[s], 128, ptr, hT, s * 128)
            for s in range(2):
                for nb in range(2):
                    a = k.ring("macc", acc)
                    k.mm(a, a[:, :], [(hT[:, kc, s * 128:(s + 1) * 128], Wm[:, kc, nb * 512:(nb + 1) * 512]) for kc in range(8)], reads=[hT, Wm])
                    o = k.ring("mog", og)
                    k.op("act", lambda e, o=o, a=a: e.activation(out=o[:], in_=a[:], func=AF.Copy), reads=[a], writes=[o])
                    dst = (O["memk_p"], O["memv_p"])[nb][li, s * 128:(s + 1) * 128].rearrange("m h d -> m (h d)")
                    k.dma("pool", dst, o[:], reads=[o])
                    if nb == 1:
                        k.op("dve", lambda e, o=o, s=s: e.tensor_copy(out=mv["p"][:, s, :], in_=o[:]), reads=[o], writes=[mv["p"]])
            for h in range(4):
                a = k.ring("macc", acc)
                k.mm(a, a[:, 0:256], [(Wm[:, kc, h * 128:(h + 1) * 128], hT[:, kc, :]) for kc in range(8)], reads=[hT, Wm])
                k.op("dve", lambda e, a=a, h=h: e.tensor_copy(out=mk["p"][:, h, :], in_=a[:, 0:256]), reads=[a], writes=[mk["p"]])
            ck = k.sb("mck", [128, 2, 512], F32, st)
            cv = k.sb("mcv", [128, 2, 512], F32, st)
            k.dma("sp", ck[:], I["cmk"][li].rearrange("(s p) h d -> p s (h d)", p=128), writes=[ck])
            k.dma("sp", cv[:], I["cmv"][li].rearrange("(s p) h d -> p s (h d)", p=128), writes=[cv])
            k.op("dve", lambda e: e.tensor_copy(out=mv["s"][:], in_=cv[:]), reads=[cv], writes=[mv["s"]])
            for h in range(4):
                a = k.ring("macc", acc)
                for s in range(2):
                    k.tr(a, a[:, s * 128:(s + 1) * 128], ck[:, s, h * 128:(h + 1) * 128], self.cst_f[:], reads=[ck, self.cst_f])
                k.op("dve", lambda e, a=a, h=h: e.tensor_copy(out=mk["s"][:, h, :], in_=a[:, 0:256]), reads=[a], writes=[mk["s"]])

    def p1(self, stm, li, kind, j, Wb, mk, mv, st):
        k = self.k
        I, O = self.I, self.O
        T = stm["T"]
        isp = stm["name"] == "p"
        TT = min(256 if kind == 1 else 512, T)
        np_ = min(128, T)
        nsub = TT // np_
        ntt = T // TT
        xsrc = stm["xin"] if li == 0 else stm["xres"]
        FM, YT = stm["FM"], stm["YT"]
        if kind == 1:
            self.tmob = [k.sb("tmob%d" % i, [128, 512], BF16, st) for i in range(2)]
        gpre = k.sb("gpre", [128, D], F32, st)
        k.dma("sp", gpre[:], I["gpre_r"][li], writes=[gpre])
        xt = [k.sb("xt%d" % i, [128, D], F32, st) for i in range(3)]
        junk = k.sb("junk", [128, D], BF16, st)
        ss = [k.sb("ss%d" % i, [128, 2], F32, st) for i in range(3)]
        hb = [k.sb("hb%d" % i, [128, D], BF16, st) for i in range(2)]
        hT = [k.sb("hT%d" % i, [128, 8, TT], BF16, st) for i in range(2)]
        ptr = k.ps("ptr", [128, 8, 128], BF16, st)
        acc = [k.ps("acc%d" % i, [128, 512], F32, st) for i in range(4)]
        memS = k.ps("memS", [128, 2, 512], F32, st)
        nring = 2 if kind == 2 else 3
        stg = [k.sb("stg%d" % i, [128, 4, TT], BF16, st) for i in range(nring)]
        tmo = [k.sb("tmo%d" % i, [128, 512], F32, st) for i in range(nring)]
        mq = k.sb("mq", [128, 4, TT], BF16, st)
        zm = k.sb("zm", [128, 4, TT], BF16, st)
        eT = k.sb("eT", [128, 2, TT], BF16, st)
        rden = k.sb("rden", [128, TT], F32, st)
        ymem = k.sb("ymem", [128, 4, TT], BF16, st)
        if kind == 0:
            mqc, zc, zmc = 3072, 3584, 4608
        elif kind == 1:
            mqc, zc, zmc = 3080, 3592, 4616
        else:
            mqc, zc, zmc = 3072, 3584, 4608

        def fm_chunk(hTt, col):
            a = k.ring("acc", acc)
            k.mm(a, a[:, 0:TT], [(Wb[:, kc, col:col + 128], hTt[:, kc, :]) for kc in range(8)], reads=[hTt, Wb])
            return a

        def fm_group(hTt, col0, nch, func, tok0, dst_fm0=None, dst_tile=None, scale=1.0):
            t = dst_tile if dst_tile is not None else k.ring("stg", stg)
            for c in range(nch):
                a = fm_chunk(hTt, col0 + c * 128)
                if func is None:
                    eng = k.ring("evac", ["dve", "act"])
                    if eng == "dve":
                        k.op("dve", lambda e, a=a, c=c: e.tensor_copy(out=t[:, c, :], in_=a[:, 0:TT]), reads=[a], writes=[t])
                    else:
                        k.op("act", lambda e, a=a, c=c: e.activation(out=t[:, c, :], in_=a[:, 0:TT], func=AF.Copy), reads=[a], writes=[t])
                else:
                    k.op("act", lambda e, a=a, c=c: e.activation(out=t[:, c, :], in_=a[:, 0:TT], func=func, scale=scale), reads=[a], writes=[t])
            if dst_fm0 is not None:
                k.dma("pool", FM[dst_fm0:dst_fm0 + nch, :, tok0:tok0 + TT].rearrange("c p t -> p c t"), t[:, 0:nch, :], reads=[t])
            return t

        def tm_block(hTt, s, col0, ncols):
            a = k.ring("acc", acc)
            k.mm(a, a[0:np_, 0:ncols], [(hTt[:, kc, s * np_:(s + 1) * np_], Wb[:, kc, col0:col0 + ncols]) for kc in range(8)], reads=[hTt, Wb])
            return a

        def mem_attn(tok0):
            sc = 1.0 / math.sqrt(128.0)
            for h in range(4):
                for mb in range(2):
                    k.mm(memS, memS[:, mb, 0:TT], [(mk[:, h, mb * 128:(mb + 1) * 128], mq[:, h, :])], reads=[mk, mq])
                k.op("act", lambda e: e.activation(out=eT[:], in_=memS[:, :, 0:TT], func=AF.Exp, scale=sc), reads=[memS], writes=[eT])
                den = k.ring("acc", acc)
                k.mm(den, den[:, 0:TT], [(self.cst_b[:, 2, :], eT[:, mb, :]) for mb in range(2)], reads=[eT, self.cst_b])
                oT = k.ring("acc", acc)
                k.mm(oT, oT[:, 0:TT], [(mv[:, mb, h * 128:(h + 1) * 128], eT[:, mb, :]) for mb in range(2)], reads=[eT, mv])
                k.op("dve", lambda e, den=den: e.reciprocal(out=rden[:], in_=den[:, 0:TT]), reads=[den], writes=[rden])
                k.op("dve", lambda e: e.tensor_tensor(out=rden[:], in0=rden[:], in1=zm[:, h, :], op=ALU.mult), reads=[rden, zm], writes=[rden])
                k.op("dve", lambda e, oT=oT, h=h: e.tensor_tensor(out=ymem[:, h, :], in0=oT[:, 0:TT], in1=rden[:], op=ALU.mult), reads=[oT, rden], writes=[ymem])
            k.dma("pool", YT[8:12, :, tok0:tok0 + TT].rearrange("c p t -> p c t"), ymem[:], reads=[ymem])

        if kind == 1:
            cw = k.sb("cw", [128, 8, 5], F32, st)
            k.dma("sp", cw[:], I["convmlT"][j], writes=[cw])
            xm = k.sb("xm", [128, 8, 3 + TT], F32, st)
            if isp:
                k.op("pool", lambda e: e.memset(xm[:, :, 0:3], 0.0), writes=[xm])
            else:
                k.dma("sp", xm[:, :, 0:3], I["smconvT"][j], writes=[xm])
            cacc = [k.sb("cacc%d" % i, [128, TT], F32, st) for i in range(2)]
            xc = k.sb("xc", [128, 8, TT], BF16, st)
            wqb = k.sb("wqb", [128, 8, 256], BF16, st)
            wkb = k.sb("wkb", [128, 8, 256], BF16, st)
            with contextlib.ExitStack() as sw:
                self.load_w(k, sw, wqb, I["wq"][j].rearrange("h d e -> (h d) e"), 256, 8, "wq")
                self.load_w(k, sw, wkb, I["wk"][j].rearrange("h d e -> (h d) e"), 256, 8, "wk")
                k.barrier()
            gtl = [k.sb("gtl%d" % i, [8, TT], F32, st) for i in range(2)]
            self.xcs = [k.sb("xcs%d" % i, [128, 8, TT], BF16, st) for i in range(2)]
            self.skp1 = k.sb("skp1", [128, 8], F32, st)
            k.dma("sp", self.skp1[:], I["skipT"][j], writes=[self.skp1])
        if kind == 2:
            cw = k.sb("cw", [128, 8, 3], F32, st)
            k.dma("sp", cw[:], I["convcvT"][j], writes=[cw])
            ch = k.sb("ch", [128, 8, 2 + TT], F32, st)
            if isp:
                k.op("pool", lambda e: e.memset(ch[:, :, 0:2], 0.0), writes=[ch])
            else:
                k.dma("sp", ch[:, :, 0:2], I["scvT"][j], writes=[ch])
            bT = [k.sb("bT%d" % i, [128, 4, TT], BF16, st) for i in range(2)]
            cT = [k.sb("cT%d" % i, [128, 4, TT], BF16, st) for i in range(2)]
            cacc = [k.sb("cacc%d" % i, [128, TT], F32, st) for i in range(2)]
            yst = [k.sb("yst%d" % i, [128, 4, TT], BF16, st) for i in range(2)]

        for tt in range(ntt):
            tok0 = tt * TT
            hTt = hT[tt % 2]
            for s in range(nsub):
                x_ = k.ring("xt", xt)
                s_ = k.ring("ss", ss)
                h_ = k.ring("hb", hb)
                self.rms_front(k, xsrc[tok0 + s * np_: tok0 + (s + 1) * np_, :], np_, x_, junk, s_, h_, gpre)
                self.to_fm(k, h_, np_, ptr, hTt, s * np_)
            fm_group(hTt, mqc, 4, None, tok0, dst_tile=mq)
            fm_group(hTt, zmc, 4, AF.Silu, tok0, dst_tile=zm)
            mem_attn(tok0)
            if kind == 0:
                fm_group(hTt, 0, 4, None, tok0, dst_fm0=0)
                fm_group(hTt, 512, 4, None, tok0, dst_fm0=4)
                fm_group(hTt, 1024, 4, None, tok0, dst_fm0=8)
                fm_group(hTt, 1536, 4, None, tok0, dst_fm0=12)
                fm_group(hTt, zc, 4, AF.Silu, tok0, dst_fm0=16)
                fm_group(hTt, zc + 512, 4, AF.Silu, tok0, dst_fm0=20)
                ko = (O["sbk_p"] if isp else O["sbk_s"])[j].rearrange("t h d -> t (h d)")
                vo = (O["sbv_p"] if isp else O["sbv_s"])[j].rearrange("t h d -> t (h d)")
                for s in range(nsub):
                    for (dst, c0) in ((ko, 1024), (vo, 2048)):
                        for nb in range(2):
                            a = tm_block(hTt, s, c0 + nb * 512, 512)
                            o = k.ring("tmo", tmo)
                            eng = k.ring("evac", ["dve", "act"])
                            if eng == "dve":
                                k.op("dve", lambda e, a=a, o=o: e.tensor_copy(out=o[0:np_, :], in_=a[0:np_, :]), reads=[a], writes=[o])
                            else:
                                k.op("act", lambda e, a=a, o=o: e.activation(out=o[0:np_, :], in_=a[0:np_, :], func=AF.Copy), reads=[a], writes=[o])
                            k.dma("pool", dst[tok0 + s * np_: tok0 + (s + 1) * np_, nb * 512:(nb + 1) * 512], o[0:np_, :], reads=[o])
            elif kind == 2:
                for g in range(2):
                    bt = fm_group(hTt, 0 + g * 512, 4, None, tok0, dst_tile=bT[g])
                    ct = fm_group(hTt, 1024 + g * 512, 4, None, tok0, dst_tile=cT[g])
                    for c in range(4):
                        a = fm_chunk(hTt, 2048 + (g * 4 + c) * 128)
                        k.op("dve", lambda e, a=a, c=c, g=g, ct=ct: e.tensor_tensor(out=ch[:, g * 4 + c, 2:2 + TT], in0=a[:, 0:TT], in1=ct[:, c, :], op=ALU.mult), reads=[a, ct, ch], writes=[ch])
                    zt = fm_group(hTt, zc + g * 512, 4, AF.Silu, tok0, dst_tile=k.ring("stg", stg))
                    yt = yst[g]
                    for c in range(4):
                        cc = g * 4 + c
                        ca = k.ring("cacc", cacc)
                        k.op("dve", lambda e, ca=ca, cc=cc: e.tensor_scalar(out=ca[:], in0=ch[:, cc, 0:TT], scalar1=cw[:, cc, 0:1], scalar2=None, op0=ALU.mult), reads=[ch, cw], writes=[ca])
                        k.op("dve", lambda e, ca=ca, cc=cc: e.scalar_tensor_tensor(out=ca[:], in0=ch[:, cc, 1:1 + TT], scalar=cw[:, cc, 1:2], in1=ca[:], op0=ALU.mult, op1=ALU.add), reads=[ch, cw, ca], writes=[ca])
                        k.op("dve", lambda e, ca=ca, cc=cc: e.scalar_tensor_tensor(out=ca[:], in0=ch[:, cc, 2:2 + TT], scalar=cw[:, cc, 2:3], in1=ca[:], op0=ALU.mult, op1=ALU.add), reads=[ch, cw, ca], writes=[ca])
                        k.op("pool", lambda e, ca=ca, c=c, bt=bt: e.tensor_tensor(out=ca[:], in0=ca[:], in1=bt[:, c, :], op=ALU.mult), reads=[ca, bt], writes=[ca])
                        k.op("pool", lambda e, ca=ca, c=c, zt=zt, yt=yt: e.tensor_tensor(out=yt[:, c, :], in0=ca[:], in1=zt[:, c, :], op=ALU.mult), reads=[ca, zt], writes=[yt])
                    k.dma("pool", YT[g * 4:g * 4 + 4, :, tok0:tok0 + TT].rearrange("c p t -> p c t"), yt[:], reads=[yt])
                if tt == ntt - 1:
                    s = nsub - 1
                    cvo = (O["cv_p"] if isp else O["cv_s"])[j]
                    for nb in range(2):
                        a1 = tm_block(hTt, s, 1024 + nb * 512, 512)
                        o1 = k.ring("tmo", tmo)
                        k.op("act", lambda e, a1=a1, o1=o1: e.activation(out=o1[0:np_, :], in_=a1[0:np_, :], func=AF.Copy), reads=[a1], writes=[o1])
                        a2 = tm_block(hTt, s, 2048 + nb * 512, 512)
                        k.op("dve", lambda e, a2=a2, o1=o1: e.tensor_tensor(out=o1[0:np_, :], in0=a2[0:np_, :], in1=o1[0:np_, :], op=ALU.mult), reads=[a2, o1], writes=[o1])
                        k.dma("pool", cvo[:, nb * 512:(nb + 1) * 512], o1[np_ - 2:np_, :], reads=[o1])
                if tt < ntt - 1:
                    k.op("dve", lambda e: e.tensor_copy(out=ch[:, :, 0:2], in_=ch[:, :, TT:TT + 2]), reads=[ch], writes=[ch])
            else:
                self.p1_ml(stm, li, j, tt, ntt, tok0, TT, np_, nsub, hTt, fm_chunk, fm_group, tm_block, tmo, stg, acc, xm, cw, cacc, xc, wqb, wkb, gtl, zc, Wb)

    def p1_ml(self, stm, li, j, tt, ntt, tok0, TT, np_, nsub, hTt, fm_chunk, fm_group, tm_block, tmo, stg, acc, xm, cw, cacc, xc, wqb, wkb, gtl, zc, Wb):
        k = self.k
        I, O = self.I, self.O
        isp = stm["name"] == "p"
        FM, TMs, GT = stm["FM"], stm["TMs"], stm["GT"]
        for c in range(8):
            a = fm_chunk(hTt, c * 128)
            k.op("act", lambda e, a=a, c=c: e.activation(out=xm[:, c, 3:3 + TT], in_=a[:, 0:TT], func=AF.Copy), reads=[a, xm], writes=[xm])
        fm_group(hTt, 2048, 4, AF.Sigmoid, tok0, dst_fm0=24)
        fm_group(hTt, 2048 + 512, 4, AF.Sigmoid, tok0, dst_fm0=28)
        fm_group(hTt, zc, 4, AF.Silu, tok0, dst_fm0=32)
        fm_group(hTt, zc + 512, 4, AF.Silu, tok0, dst_fm0=36)
        a = k.ring("acc", acc)
        k.mm(a, a[0:8, 0:TT], [(Wb[:, kc, 3072:3080], hTt[:, kc, :]) for kc in range(8)], reads=[hTt, Wb])
        g = k.ring("gtl", gtl)
        k.op("dve", lambda e, a=a, g=g: e.tensor_copy(out=g[:, :], in_=a[0:8, 0:TT]), reads=[a], writes=[g])
        k.dma("pool", GT[:, tok0:tok0 + TT], g[:, :], reads=[g])

        for s in range(nsub):
            for nb in range(2):
                a = tm_block(hTt, s, 1024 + nb * 512, 512)
                o = k.ring("tmob", self.tmob)
                k.op("dve", lambda e, a=a, o=o: e.tensor_copy(out=o[0:np_, :], in_=a[0:np_, :]), reads=[a], writes=[o])
                k.dma("pool", TMs[1, tok0 + s * np_: tok0 + (s + 1) * np_, nb * 512:(nb + 1) * 512], o[0:np_, :], reads=[o])
        for c in range(8):
            ca = k.ring("cacc", cacc)
            k.op("dve", lambda e, ca=ca, c=c: e.tensor_scalar(out=ca[:], in0=xm[:, c, 0:TT], scalar1=cw[:, c, 0:1], scalar2=None, op0=ALU.mult), reads=[xm, cw], writes=[ca])
            for jj in range(1, 4):
                k.op("dve", lambda e, ca=ca, c=c, jj=jj: e.scalar_tensor_tensor(out=ca[:], in0=xm[:, c, jj:jj + TT], scalar=cw[:, c, jj:jj + 1], in1=ca[:], op0=ALU.mult, op1=ALU.add), reads=[xm, cw, ca], writes=[ca])
            k.op("act", lambda e, ca=ca, c=c: e.activation(out=xc[:, c, :], in_=ca[:], func=AF.Silu, bias=cw[:, c, 4:5]), reads=[ca, cw], writes=[xc])
        if tt == ntt - 1:
            s = nsub - 1
            mco = (O["mlconv_p"] if isp else O["mlconv_s"])[j]
            for nb in range(2):
                a1 = tm_block(hTt, s, nb * 512, 512)
                o1 = k.ring("tmo", tmo)
                k.op("act", lambda e, a1=a1, o1=o1: e.activation(out=o1[0:np_, :], in_=a1[0:np_, :], func=AF.Copy), reads=[a1], writes=[o1])
                k.dma("pool", mco[:, nb * 512:(nb + 1) * 512], o1[np_ - 3:np_, :], reads=[o1])
        if tt < ntt - 1:
            k.op("dve", lambda e: e.tensor_copy(out=xm[:, :, 0:3], in_=xm[:, :, TT:TT + 3]), reads=[xm], writes=[xm])
        xs_ = k.ring("stgx", self.xcs)
        for c in range(8):
            k.op("act", lambda e, c=c: e.activation(out=xs_[:, c, :], in_=xc[:, c, :], func=AF.Copy, scale=self.skp1[:, c:c + 1]), reads=[xc, self.skp1], writes=[xs_])
        k.dma("pool", FM[16:24, :, tok0:tok0 + TT].rearrange("c p t -> p c t"), xs_[:], reads=[xs_])
        for (wb, f0, scl) in ((wqb, 0, 1.0), (wkb, 8, 1.0 / 16.0)):
            for g in range(2):
                t = k.ring("stg", stg)
                for c in range(4):
                    hc = g * 4 + c
                    h, ec = hc // 2, hc % 2
                    a = k.ring("acc", acc)
                    k.mm(a, a[:, 0:TT], [(wb[:, 2 * h + dc, ec * 128:(ec + 1) * 128], xc[:, 2 * h + dc, :]) for dc in range(2)], reads=[xc, wb])
                    k.op("act", lambda e, a=a, c=c, t=t, scl=scl: e.activation(out=t[:, c, :], in_=a[:, 0:TT], func=AF.Copy, scale=scl), reads=[a], writes=[t])
                k.dma("pool", FM[f0 + g * 4:f0 + g * 4 + 4, :, tok0:tok0 + TT].rearrange("c p t -> p c t"), t[:], reads=[t])
        for s in range(nsub):
            for nb in range(2):
                a = k.ring("acc", acc)
                for hh in range(2):
                    h = nb * 2 + hh
                    k.mm(a, a[0:np_, hh * 256:(hh + 1) * 256], [(xc[:, 2 * h + dc, s * np_:(s + 1) * np_], wkb[:, 2 * h + dc, :]) for dc in range(2)], reads=[xc, wkb])
                o = k.ring("tmob", self.tmob)
                k.op("act", lambda e, a=a, o=o: e.activation(out=o[0:np_, :], in_=a[0:np_, :], func=AF.Copy, scale=1.0 / 16.0), reads=[a], writes=[o])
                k.dma("pool", TMs[0, tok0 + s * np_: tok0 + (s + 1) * np_, nb * 512:(nb + 1) * 512], o[0:np_, :], reads=[o])
    def p3(self, stm, li, Wo, gpo, last, st, bg=None):
        k = self.k
        T = stm["T"]
        TT = min(512, T)
        np_ = min(128, T)
        nsub = TT // np_
        ntt = T // TT
        xsrc = stm["xin"] if li == 0 else stm["xres"]
        xdst = stm["yout"] if last else stm["xres"]
        YT = stm["YT"]
        yt = [k.sb("yt%d" % i, [128, 12, TT], BF16, st) for i in range(2)]
        xt = [k.sb("p3x%d" % i, [128, D], F32, st) for i in range(3)]
        ot = [k.sb("p3o%d" % i, [128, D], F32, st) for i in range(2)]
        junk = k.sb("p3junk", [128, D], BF16, st)
        ss = [k.sb("p3ss%d" % i, [128, 2], F32, st) for i in range(3)]
        acc = [k.ps("p3acc%d" % i, [128, D], F32, st) for i in range(3)]
        for tt in range(ntt):
            tok0 = tt * TT
            y_ = yt[tt % 2]
            k.dma("sp", y_[:], YT[:, :, tok0:tok0 + TT].rearrange("c p t -> p c t"), writes=[y_])
            for s in range(nsub):
                if bg is not None and s % 3 != 2:
                    next(bg, None)
                x_ = k.ring("p3x", xt)
                k.dma("sp", x_[0:np_, :], xsrc[tok0 + s * np_: tok0 + (s + 1) * np_, :], writes=[x_])
                a = k.ring("p3acc", acc)
                for nb in range(2):
                    k.mm(a, a[0:np_, nb * 512:(nb + 1) * 512], [(y_[:, fc, s * np_:(s + 1) * np_], Wo[:, fc, nb * 512:(nb + 1) * 512]) for fc in range(12)], reads=[y_, Wo])
                s_ = k.ring("p3ss", ss)
                k.op("pool", lambda e, s_=s_: e.memset(s_[:], 0.0), writes=[s_])
                k.op("act", lambda e, a=a, s_=s_: e.activation(out=junk[0:np_, :], in_=a[0:np_, :], func=AF.Square, accum_out=s_[0:np_, 0:1]), reads=[a, s_], writes=[junk, s_])
                k.op("act", lambda e, s_=s_: e.activation(out=s_[0:np_, 1:2], in_=s_[0:np_, 0:1], func=AF.Ln, scale=1.0 / D, bias=EPS), reads=[s_], writes=[s_])
                k.op("act", lambda e, s_=s_: e.activation(out=s_[0:np_, 1:2], in_=s_[0:np_, 1:2], func=AF.Exp, scale=-0.5), reads=[s_], writes=[s_])
                o_ = k.ring("p3o", ot)
                k.op("dve", lambda e, a=a, s_=s_, o_=o_: e.scalar_tensor_tensor(out=o_[0:np_, :], in0=a[0:np_, :], scalar=s_[0:np_, 1:2], in1=gpo[0:np_, :], op0=ALU.mult, op1=ALU.mult), reads=[a, s_, gpo], writes=[o_])
                k.op("pool", lambda e, o_=o_, x_=x_: e.tensor_tensor(out=o_[0:np_, :], in0=o_[0:np_, :], in1=x_[0:np_, :], op=ALU.add), reads=[o_, x_], writes=[o_])
                k.dma("pool", xdst[tok0 + s * np_: tok0 + (s + 1) * np_, :], o_[0:np_, :], reads=[o_])

    def p2_sb(self, stm, li, j, st):
        k = self.k
        I, O = self.I, self.O
        T = stm["T"]
        isp = stm["name"] == "p"
        FM, YT = stm["FM"], stm["YT"]
        N = min(512, T)
        sc = 1.0 / math.sqrt(128.0)
        PAST = self.PAST
        if isp:
            NB = T // 128
            koff = 0
        else:
            NB = (PAST + T + 127) // 128
            if NB % 2:
                NB += 1
            koff = NB * 128 - (PAST + T)
        nqs = T // N
        NSET = 2 if isp else 4
        early = not isp
        qT = [k.sb("qT%d" % i, [128, T], BF16, st) for i in range(NSET)]
        kT = [k.sb("kT%d" % i, [128, NB * 128], BF16, st) for i in range(NSET)]
        szT = [k.sb("szT%d" % i, [128, T], BF16, st) for i in range(NSET)]
        Vb = [k.sb("Vb%d" % i, [128, NB, 128], BF16, st) for i in range(NSET)]
        VST = 16
        vstg = [k.sb("vstg%d" % i, [128, VST, 128], F32, st) for i in range(2)] if isp else None
        vfull = [k.sb("vfull%d" % i, [128, NB, 128], F32, st) for i in range(2)] if not isp else None
        S = [k.ps("S%d" % i, [128, 2, N], F32, st) for i in range(2)]
        L = k.ps("L", [128, 2, N], F32, st)
        Racc = k.ps("Racc", [128, N], F32, st)
        oacc = k.ps("oacc", [128, N], F32, st)
        trs = k.ps("trs", [128, 512], F32, st) if not isp else None
        e_ = [k.sb("e%d" % i, [128, 2, N], BF16, st) for i in range(5)]
        c_ = [k.sb("c%d" % i, [128, 2, N], BF16, st) for i in range(3)]
        g_ = [k.sb("g%d" % i, [128, 2, N], BF16, st) for i in range(2)]
        a_ = [k.sb("a%d" % i, [128, 2, N], BF16, st) for i in range(3)]
        Lr = [k.sb("Lr%d" % i, [128, 2, N], F32, st) for i in range(2)]
        R = [k.sb("R%d" % i, [128, N], F32, st) for i in range(3)]
        yst = [k.sb("ysb%d" % i, [128, N], BF16, st) for i in range(2)]
        tri, ones = self.cst_b[:, 1, :], self.cst_b[:, 2, :]
        vout = (O["sbv_p"] if isp else O["sbv_s"])[j]
        kout = (O["sbk_p"] if isp else O["sbk_s"])[j]

        def load_head(h):
            si = h % NSET
            k.dma("sp", qT[si][:], FM[h], writes=[qT[si]])
            k.dma("sp", szT[si][:], FM[16 + h], writes=[szT[si]])
            if isp:
                k.dma("sp", kT[si][:], FM[8 + h], writes=[kT[si]])
                for b0 in range(0, NB, VST):
                    nb_ = min(VST, NB - b0)
                    vs = k.ring("vstg", vstg)
                    k.dma("sp", vs[:, 0:nb_, :], vout[b0 * 128:(b0 + nb_) * 128, h, :].rearrange("(b p) d -> p b d", p=128), writes=[vs])
                    k.op("pool", lambda e, vs=vs, b0=b0, nb_=nb_: e.tensor_copy(out=Vb[si][:, b0:b0 + nb_, :], in_=vs[:, 0:nb_, :]), reads=[vs], writes=[Vb[si]])
            else:
                for (src_c, src_n, isk) in ((I["csk"][j], kout, True), (I["csv"][j], vout, False)):
                    vs = k.ring("vfull", vfull)
                    k.op("pool", lambda e, vs=vs: e.memset(vs[:], 0.0), writes=[vs])
                    p0 = koff % 128
                    bq = koff // 128
                    t0 = (128 - p0) % 128
                    if t0:
                        k.dma("sp", vs[p0:128, bq, :], src_c[0:t0, h, :], writes=[vs])
                        bq += 1
                    nfull = (PAST - t0) // 128
                    if nfull:
                        k.dma("sp", vs[:, bq:bq + nfull, :], src_c[t0:t0 + nfull * 128, h, :].rearrange("(b p) d -> p b d", p=128), writes=[vs])
                    rem = PAST - t0 - nfull * 128
                    if rem:
                        k.dma("sp", vs[0:rem, bq + nfull, :], src_c[t0 + nfull * 128:PAST, h, :], writes=[vs])
                    k.dma("sp", vs[128 - T:128, NB - 1, :], src_n[0:T, h, :], writes=[vs])
                    if not isk:
                        k.op("pool", lambda e, vs=vs: e.tensor_copy(out=Vb[si][:], in_=vs[:]), reads=[vs], writes=[Vb[si]])
                    else:
                        for b0 in range(0, NB, 4):
                            nb_ = min(4, NB - b0)
                            for b in range(nb_):
                                k.tr(trs, trs[:, b * 128:(b + 1) * 128], vs[:, b0 + b, :], self.cst_f[:], reads=[vs, self.cst_f])
                            k.op("dve", lambda e, b0=b0, nb_=nb_: e.tensor_copy(out=kT[si][:, b0 * 128:(b0 + nb_) * 128], in_=trs[:, 0:nb_ * 128]), reads=[trs], writes=[kT[si]])

        G = []
        for h in range(8):
            for qs in range(nqs):
                q0 = qs * N
                grp = []
                if isp:
                    b = (q0 + N) // 128 - 1
                    while b >= 0:
                        m0 = (b - q0 // 128) if b >= q0 // 128 else None
                        m1 = ((b - 1) - q0 // 128) if (b - 1) >= q0 // 128 else None
                        grp.append((b, b - 1, m0, m1))
                        b -= 2
                else:
                    b = NB - 1
                    first_real = koff // 128
                    while b >= 0:
                        ms = []
                        for bb in (b, b - 1):
                            if bb == NB - 1:
                                ms.append(0)
                            elif bb == first_real and koff % 128:
                                ms.append(1)
                            elif bb < first_real:
                                ms.append(2)
                            else:
                                ms.append(None)
                        grp.append((b, b - 1, ms[0], ms[1]))
                        b -= 2
                ng = len(grp)
                for gi, (b0, b1, m0, m1) in enumerate(grp):
                    G.append(dict(h=h, si=h % NSET, q0=q0, b0=b0, b1=b1, m0=m0, m1=m1, first=(gi == 0), last=(gi == ng - 1),
                                  newhead=(qs == 0 and gi == 0)))
        NG = len(G)
        mtile = self.msk if isp else self.smsk

        def st_S(n):
            d = G[n]
            if n == 0:
                for hh in range(min(NSET, 8)):
                    load_head(hh)
            si, q0 = d["si"], d["q0"]
            S_ = k.ring("S", S)
            for i, (b, m) in enumerate(((d["b0"], d["m0"]), (d["b1"], d["m1"]))):
                pairs = [(kT[si][:, b * 128:(b + 1) * 128], qT[si][:, q0:q0 + N])]
                rd = [kT[si], qT[si]]
                if m is not None:
                    pairs.append((self.cst_b[:, 0, :], mtile[:, m, 0:N]))
                    rd += [self.cst_b, mtile]
                k.mm(S_, S_[:, i, 0:N], pairs, reads=rd)
            d["S"] = S_

        def st_exp1(n):
            d = G[n]
            e = k.ring("e", e_)
            S_ = d["S"]
            k.op("act", lambda en: en.activation(out=e[:], in_=S_[:, :, 0:N], func=AF.Exp, scale=sc), reads=[S_], writes=[e])
            d["e"] = e

        def st_ln(n):
            d = G[n]
            c = k.ring("c", c_)
            e = d["e"]
            k.op("act", lambda en: en.activation(out=c[:], in_=e[:], func=AF.Ln, bias=1.0), reads=[e], writes=[c])
            d["c"] = c

        def st_L(n):
            d = G[n]
            c = d["c"]
            k.mm(L, L[:, 0, 0:N], [(tri, c[:, 0, :])], reads=[c, self.cst_b])
            k.mm(L, L[:, 1, 0:N], [(tri, c[:, 1, :]), (ones, c[:, 0, :])], reads=[c, self.cst_b])
            if not d["first"]:
                Rt = d["R"]
                lr = k.ring("Lr", Lr)
                for i in range(2):
                    k.op("dve", lambda en, i=i: en.tensor_tensor(out=lr[:, i, :], in0=L[:, i, 0:N], in1=Rt[:], op=ALU.add), reads=[L, Rt], writes=[lr])
                d["src"], d["srct"] = lr[:], lr
            else:
                lr = k.ring("Lr", Lr)
                k.op("dve", lambda en: en.tensor_copy(out=lr[:], in_=L[:, :, 0:N]), reads=[L], writes=[lr])
                d["src"], d["srct"] = lr[:], lr
            if not d["last"]:
                k.mm(Racc, Racc[:, 0:N], [(ones, c[:, 0, :]), (ones, c[:, 1, :])], reads=[c, self.cst_b])
                Rn = k.ring("R", R)
                if d["first"]:
                    k.op("dve", lambda en: en.tensor_copy(out=Rn[:], in_=Racc[:, 0:N]), reads=[Racc], writes=[Rn])
                else:
                    Rt = d["R"]
                    k.op("dve", lambda en: en.tensor_tensor(out=Rn[:], in0=Racc[:, 0:N], in1=Rt[:], op=ALU.add), reads=[Racc, Rt], writes=[Rn])
                G[n + 1]["R"] = Rn

        def st_exp2(n):
            d = G[n]
            g = k.ring("g", g_)
            src, srct = d["src"], d["srct"]
            k.op("act", lambda en: en.activation(out=g[:], in_=src, func=AF.Exp, scale=-1.0), reads=[srct], writes=[g])
            d["g"] = g

        def st_a(n):
            d = G[n]
            a = k.ring("a", a_)
            e, g = d["e"], d["g"]
            k.op("pool", lambda en: en.tensor_tensor(out=a[:], in0=e[:], in1=g[:], op=ALU.mult), reads=[e, g], writes=[a])
            si, q0, h = d["si"], d["q0"], d["h"]
            k.mm(oacc, oacc[:, 0:N], [(Vb[si][:, d["b0"], :], a[:, 0, :]), (Vb[si][:, d["b1"], :], a[:, 1, :])], reads=[a, Vb[si]], start=d["first"], stop=d["last"])
            if d["last"]:
                y = k.ring("ysb", yst)
                k.op("dve", lambda en: en.tensor_tensor(out=y[:], in0=oacc[:, 0:N], in1=szT[si][:, q0:q0 + N], op=ALU.mult), reads=[oacc, szT[si]], writes=[y])
                k.dma("pool", YT[h, :, q0:q0 + N], y[:], reads=[y])
            for kk in ("S", "e", "c", "g", "src", "srct", "R"):
                d.pop(kk, None)
            if (n == NG - 1 or G[n + 1]["newhead"]) and d["h"] + NSET < 8:
                load_head(d["h"] + NSET)

        for t in range(-2, NG + 3):
            if 0 <= t + 2 < NG:
                st_S(t + 2)
            if 0 <= t + 1 < NG:
                st_exp1(t + 1)
            if 0 <= t < NG:
                st_ln(t)
            if 0 <= t - 1 < NG:
                st_L(t - 1)
            if 0 <= t - 2 < NG:
                st_exp2(t - 2)
            if 0 <= t - 3 < NG:
                st_a(t - 3)

    def p2_ml(self, stm, li, j, st):
        k = self.k
        I, O = self.I, self.O
        T = stm["T"]
        isp = stm["name"] == "p"
        FM, YT, TMs, GT = stm["FM"], stm["YT"], stm["TMs"], stm["GT"]
        LC = min(128, T)
        nch = T // LC
        SEG = min(2048, T)
        ig = k.sb("ml_ig", [4, T], F32, st)
        fg = k.sb("ml_fg", [4, T], F32, st)
        Bt = k.sb("ml_B", [4, T], F32, st)
        ones4 = k.sb("ml_ones", [4, SEG], F32, st)
        gcol = k.sb("ml_gcol", [4, 2], F32, st)
        minit = k.sb("ml_minit", [4, 1], F32, st)
        sel = k.sb("ml_sel", [4, 4, 128], F32, st)
        negm = k.sb("ml_negm", [128, 128], F32, st)
        ghr = k.sb("ml_ghr", [128, D], F32, st)
        skp = k.sb("ml_skip", [128, 8], F32, st)
        C = k.sb("ml_C", [128, 4, 2, 257], F32, st)
        Cb = k.sb("ml_Cb", [128, 4, 2, 257], BF16, st)
        MendB = k.sb("ml_MendB", [128, nch + 1, 4], F32, st)
        nMendB = k.sb("ml_nMendB", [128, nch + 1, 4], F32, st)
        decB = k.sb("ml_decB", [128, nch, 4], F32, st)
        k.op("pool", lambda e: e.memset(ones4[:], 1.0), writes=[ones4])
        k.dma("sp", gcol[:], I["gcol"][j], writes=[gcol])
        k.dma("sp", sel[:].rearrange("r h m -> r (h m)"), I["sel4"][:, :], writes=[sel])
        k.dma("sp", negm[:], I["negmask"][:, :], writes=[negm])
        k.dma("sp", ghr[:], I["ghead_r"][j], writes=[ghr])
        k.dma("sp", skp[:], I["skipT"][j], writes=[skp])
        if isp:
            k.op("pool", lambda e: e.memset(minit[:], 0.0), writes=[minit])
            k.op("pool", lambda e: e.memset(C[:], 0.0), writes=[C])
        else:
            k.dma("sp", minit[:], I["smm"][j].rearrange("(h o) -> h o", o=1), writes=[minit])
            for h in range(4):
                k.dma("sp", C[:, h, :, 0:256], I["smc"][j, h].rearrange("(c p) e -> p c e", p=128), writes=[C])
                k.dma("sp", C[:, h, :, 256:257], I["smn"][j, h].rearrange("(c p o) -> p c o", p=128, o=1), writes=[C], allow_slow_non_contiguous=True)
        k.op("act", lambda e: e.activation(out=Cb[:], in_=C[:], func=AF.Copy), reads=[C], writes=[Cb])
        k.dma("sp", ig[:], GT[0:4, :], writes=[ig])
        k.dma("sp", fg[:], GT[4:8, :], writes=[fg])
        k.op("dve", lambda e: e.tensor_scalar(out=ig[:], in0=ig[:], scalar1=gcol[:, 0:1], scalar2=None, op0=ALU.add), reads=[ig, gcol], writes=[ig])
        k.op("dve", lambda e: e.tensor_scalar(out=fg[:], in0=fg[:], scalar1=gcol[:, 1:2], scalar2=None, op0=ALU.add), reads=[fg, gcol], writes=[fg])
        k.op("act", lambda e: e.activation(out=fg[:], in_=fg[:], func=AF.Exp, scale=-1.0), reads=[fg], writes=[fg])
        k.op("act", lambda e: e.activation(out=fg[:], in_=fg[:], func=AF.Ln, bias=1.0), reads=[fg], writes=[fg])
        k.op("dve", lambda e: e.tensor_scalar(out=fg[:], in0=fg[:], scalar1=-1.0, scalar2=None, op0=ALU.mult), reads=[fg], writes=[fg])
        for s0 in range(0, T, SEG):
            n = min(SEG, T - s0)
            init = 0.0 if s0 == 0 else Bt[:, s0 - 1:s0]
            k.op("dve", lambda e, s0=s0, n=n, init=init: e.tensor_tensor_scan(out=Bt[:, s0:s0 + n], data0=ones4[:, 0:n], data1=fg[:, s0:s0 + n], initial=init, op0=ALU.mult, op1=ALU.add), reads=[ones4, fg, Bt], writes=[Bt])
        k.op("dve", lambda e: e.tensor_tensor(out=ig[:], in0=ig[:], in1=Bt[:], op=ALU.subtract), reads=[ig, Bt], writes=[ig])
        for s0 in range(0, T, SEG):
            n = min(SEG, T - s0)
            init = minit[:, 0:1] if s0 == 0 else fg[:, s0 - 1:s0]
            k.op("dve", lambda e, s0=s0, n=n, init=init: e.tensor_tensor_scan(out=fg[:, s0:s0 + n], data0=ones4[:, 0:n], data1=ig[:, s0:s0 + n], initial=init, op0=ALU.mult, op1=ALU.max), reads=[ones4, ig, fg, minit], writes=[fg])
        k.op("dve", lambda e: e.tensor_tensor(out=Bt[:], in0=Bt[:], in1=fg[:], op=ALU.add), reads=[Bt, fg], writes=[Bt])
        with contextlib.ExitStack() as s2:
            mps = k.ps("ml_mps", [128, 4, nch + 1], F32, s2)
            for h in range(4):
                k.mm(mps, mps[:, h, 0:1], [(sel[:, h, :], minit[:, 0:1])], reads=[sel, minit])
                k.mm(mps, mps[:, h, 1:nch + 1], [(sel[:, h, :], fg[:, LC - 1:T:LC])], reads=[sel, fg])
            k.op("dve", lambda e: e.tensor_copy(out=MendB[:], in_=mps[:].rearrange("p h c -> p c h")), reads=[mps], writes=[MendB])
            k.op("dve", lambda e: e.tensor_scalar(out=nMendB[:], in0=MendB[:], scalar1=-1.0, scalar2=None, op0=ALU.mult), reads=[MendB], writes=[nMendB])
            k.op("dve", lambda e: e.tensor_tensor(out=decB[:], in0=MendB[:, 0:nch, :], in1=MendB[:, 1:nch + 1, :], op=ALU.subtract), reads=[MendB], writes=[decB])
            k.op("act", lambda e: e.activation(out=decB[:], in_=decB[:], func=AF.Exp), reads=[decB], writes=[decB])
            k.barrier()
        qk = [k.sb("ml_qk%d" % i, [128, 16, LC], BF16, st) for i in range(2)]
        ex = [k.sb("ml_ex%d" % i, [128, 24, LC], BF16, st) for i in range(2)]
        ktm = [k.sb("ml_ktm%d" % i, [128, D], BF16, st) for i in range(2)]
        vaug = [k.sb("ml_vaug%d" % i, [128, 4, 257], BF16, st) for i in range(2)]
        for v in vaug:
            k.op("pool", lambda e, v=v: e.memset(v[:], 1.0), writes=[v])
        cols = [k.sb("ml_cols%d" % i, [128, 12], F32, st) for i in range(2)]
        sm4 = [k.sb("ml_sm4%d" % i, [128, 16], F32, st) for i in range(2)]
        negm4 = k.sb("ml_negm4", [128, 4, 128], F32, st)
        for h in range(4):
            k.op("dve", lambda e, h=h: e.tensor_copy(out=negm4[:, h, :], in_=negm[:]), reads=[negm], writes=[negm4])
        w4 = [k.sb("ml_w4%d" % i, [128, 4, 128], F32, st) for i in range(2)]
        smb4 = [k.sb("ml_sm4b%d" % i, [128, 4, 128], BF16, st) for i in range(2)]
        nbs = [k.sb("ml_nbs%d" % i, [128, 257], F32, st) for i in range(2)]
        nd4 = [k.sb("ml_nd4%d" % i, [128, 4, 257], F32, st) for i in range(2)]
        sc1 = [k.sb("ml_sc%d" % i, [128, 20], F32, st) for i in range(2)]
        junk = k.sb("ml_junk", [128, 256], BF16, st)
        hn4 = [k.sb("ml_hn4%d" % i, [128, 4, 256], BF16, st) for i in range(2)]
        gk4 = [k.sb("ml_gk4%d" % i, [128, 4, 256], BF16, st) for i in range(2)]
        y8 = [k.sb("ml_y8%d" % i, [128, 8, LC], F32, st) for i in range(2)]
        yst = [k.sb("ml_yst%d" % i, [128, 8, LC], BF16, st) for i in range(2)]
        cps = k.ps("ml_cps", [128, 12], F32, st)
        mb4 = k.ps("ml_mb", [128, 4, 128], F32, st)
        sps4 = k.ps("ml_sps", [128, 4, 128], F32, st)
        tps4 = k.ps("ml_tps", [128, 8, 128], BF16, st)
        numA = k.ps("ml_numA", [128, 257], F32, st)
        numB = k.ps("ml_numB", [128, 257], F32, st)
        cups = [k.ps("ml_cups%d" % i, [128, 257], F32, st) for i in range(2)]
        identb = self.cst_b
        def ml_loads(c):
            t0, t1 = c * LC, (c + 1) * LC
            qk_, ex_, kt_, va_ = qk[c % 2], ex[c % 2], ktm[c % 2], vaug[c % 2]
            k.dma("sp", qk_[:], FM[0:16, :, t0:t1].rearrange("c p t -> p c t"), writes=[qk_])
            k.dma("sp", ex_[:], FM[16:40, :, t0:t1].rearrange("c p t -> p c t"), writes=[ex_])
            k.dma("sp", kt_[0:LC, :], TMs[0, t0:t1, :], writes=[kt_])
            k.dma("sp", va_[0:LC, :, 0:256], TMs[1, t0:t1, :].rearrange("t (h e) -> t h e", h=4), writes=[va_])

        ml_loads(0)
        for c in range(nch):
            t0, t1 = c * LC, (c + 1) * LC
            qk_ = qk[c % 2]
            ex_ = ex[c % 2]
            kt_ = ktm[c % 2]
            va_ = vaug[c % 2]
            if c + 1 < nch:
                ml_loads(c + 1)
            co = cols[c % 2]
            for qi, src in enumerate((ig, fg, Bt)):
                k.tr(cps, cps[0:LC, qi * 4:(qi + 1) * 4], src[0:4, t0:t1], self.cst_f[0:4, 0:4], reads=[src, self.cst_f])
            k.op("dve", lambda e: e.tensor_copy(out=co[0:LC, :], in_=cps[0:LC, :]), reads=[cps], writes=[co])
            s4 = sm4[c % 2]
            k.op("dve", lambda e: e.tensor_tensor(out=s4[0:LC, 0:4], in0=MendB[0:LC, c, :], in1=co[0:LC, 4:8], op=ALU.subtract), reads=[MendB, co], writes=[s4])
            k.op("dve", lambda e: e.tensor_tensor(out=s4[0:LC, 4:8], in0=co[0:LC, 0:4], in1=nMendB[0:LC, c + 1, :], op=ALU.add), reads=[nMendB, co], writes=[s4])
            k.op("dve", lambda e: e.tensor_scalar(out=s4[0:LC, 8:12], in0=co[0:LC, 8:12], scalar1=-1.0, scalar2=None, op0=ALU.mult), reads=[co], writes=[s4])
            k.op("act", lambda e: e.activation(out=s4[0:LC, 0:12], in_=s4[0:LC, 0:12], func=AF.Exp), reads=[s4], writes=[s4])
            ys = yst[c % 2]
            w_, sm_, nd_, sc, hn_, gk_, y_ = w4[c % 2], smb4[c % 2], nd4[c % 2], sc1[c % 2], hn4[c % 2], gk4[c % 2], y8[c % 2]
            for h in range(4):
                k.mm(mb4, mb4[:, h, 0:LC], [(sel[:, h, :], fg[0:4, t0:t1])], reads=[sel, fg])
            k.op("dve", lambda e: e.tensor_tensor(out=w_[0:LC, :, 0:LC], in0=negm4[0:LC, :, 0:LC], in1=mb4[0:LC, :, 0:LC], op=ALU.subtract), reads=[negm4, mb4], writes=[w_])
            for h in range(4):
                k.op("act", lambda e, h=h: e.activation(out=w_[0:LC, h, 0:LC], in_=w_[0:LC, h, 0:LC], func=AF.Exp, bias=co[0:LC, h:h + 1]), reads=[w_, co], writes=[w_])
            for h in range(4):
                k.mm(sps4, sps4[0:LC, h, 0:LC], [(qk_[:, 8 + 2 * h + ec, :], qk_[:, 2 * h + ec, :]) for ec in range(2)], reads=[qk_])
            k.op("dve", lambda e: e.tensor_tensor(out=sm_[0:LC, :, 0:LC], in0=sps4[0:LC, :, 0:LC], in1=w_[0:LC, :, 0:LC], op=ALU.mult), reads=[sps4, w_], writes=[sm_])
            for h in range(4):
                k.op("act", lambda e, h=h: e.activation(out=gk_[0:LC, h, :], in_=kt_[0:LC, h * 256:(h + 1) * 256], func=AF.Copy, scale=s4[0:LC, 4 + h:5 + h]), reads=[kt_, s4], writes=[gk_])
            for h in range(4):
                k.mm(numA, numA[0:LC, :], [(sm_[0:LC, h, 0:LC], va_[0:LC, h, :])], reads=[sm_, va_])
                k.mm(numB, numB[0:LC, :], [(qk_[:, 2 * h + dc, :], Cb[:, h, dc, :]) for dc in range(2)], reads=[qk_, Cb])
                nb_ = k.ring("ml_nbs", nbs)
                k.op("act", lambda e, nb_=nb_, h=h: e.activation(out=nb_[0:LC, :], in_=numB[0:LC, :], func=AF.Copy, scale=s4[0:LC, h:h + 1]), reads=[numB, s4], writes=[nb_])
                k.op("dve", lambda e, nb_=nb_, h=h: e.tensor_tensor(out=nd_[0:LC, h, :], in0=numA[0:LC, :], in1=nb_[0:LC, :], op=ALU.add), reads=[numA, nb_], writes=[nd_])
            k.op("pool", lambda e: e.memset(sc[:], 0.0), writes=[sc])
            k.op("act", lambda e: e.activation(out=sc[0:LC, 0:4], in_=nd_[0:LC, :, 256], func=AF.Abs), reads=[nd_, sc], writes=[sc])
            k.op("dve", lambda e: e.tensor_tensor(out=sc[0:LC, 0:4], in0=sc[0:LC, 0:4], in1=s4[0:LC, 8:12], op=ALU.max), reads=[sc, s4], writes=[sc])
            k.op("dve", lambda e: e.reciprocal(out=sc[0:LC, 4:8], in_=sc[0:LC, 0:4]), reads=[sc], writes=[sc])
            for h in range(4):
                k.op("act", lambda e, h=h: e.activation(out=junk[0:LC, :], in_=nd_[0:LC, h, 0:256], func=AF.Square, scale=sc[0:LC, 4 + h:5 + h], accum_out=sc[0:LC, 8 + h:9 + h]), reads=[nd_, sc], writes=[junk, sc])
            k.op("act", lambda e: e.activation(out=sc[0:LC, 12:16], in_=sc[0:LC, 8:12], func=AF.Ln, scale=1.0 / 256.0, bias=EPS), reads=[sc], writes=[sc])
            k.op("act", lambda e: e.activation(out=sc[0:LC, 12:16], in_=sc[0:LC, 12:16], func=AF.Exp, scale=-0.5), reads=[sc], writes=[sc])
            k.op("dve", lambda e: e.tensor_tensor(out=sc[0:LC, 16:20], in0=sc[0:LC, 12:16], in1=sc[0:LC, 4:8], op=ALU.mult), reads=[sc], writes=[sc])
            for h in range(4):
                k.op("dve", lambda e, h=h: e.scalar_tensor_tensor(out=hn_[0:LC, h, :], in0=nd_[0:LC, h, 0:256], scalar=sc[0:LC, 16 + h:17 + h], in1=ghr[0:LC, h * 256:(h + 1) * 256], op0=ALU.mult, op1=ALU.mult), reads=[nd_, sc, ghr], writes=[hn_])
            for h in range(4):
                for dc in range(2):
                    cu = cups[dc]
                    k.mm(cu, cu[:, :], [(gk_[0:LC, h, dc * 128:(dc + 1) * 128], va_[0:LC, h, :])], reads=[gk_, va_])
                    k.op("dve", lambda e, dc=dc, cu=cu, h=h: e.scalar_tensor_tensor(out=C[:, h, dc, :], in0=C[:, h, dc, :], scalar=decB[:, c, h:h + 1], in1=cu[:, :], op0=ALU.mult, op1=ALU.add), reads=[C, decB, cu], writes=[C])
            k.op("act", lambda e: e.activation(out=Cb[:], in_=C[:], func=AF.Copy), reads=[C], writes=[Cb])
            for h in range(4):
                for ec in range(2):
                    k.tr(tps4, tps4[:, 2 * h + ec, 0:LC], hn_[0:LC, h, ec * 128:(ec + 1) * 128], identb[0:LC, 0, 0:LC], reads=[hn_, identb])
            k.op("dve", lambda e: e.tensor_tensor(out=y_[:], in0=tps4[:, :, 0:LC], in1=ex_[:, 8:16, :], op=ALU.mult), reads=[tps4, ex_], writes=[y_])
            k.op("pool", lambda e: e.tensor_tensor(out=y_[:], in0=y_[:], in1=ex_[:, 0:8, :], op=ALU.add), reads=[y_, ex_], writes=[y_])
            k.op("pool", lambda e: e.tensor_tensor(out=ys[:], in0=y_[:], in1=ex_[:, 16:24, :], op=ALU.mult), reads=[y_, ex_], writes=[ys])
            k.dma("pool", YT[0:8, :, t0:t1].rearrange("c p t -> p c t"), ys[:], reads=[ys])
        co_, no_, mo_ = (O["mlc_p"], O["mln_p"], O["mlm_p"]) if isp else (O["mlc_s"], O["mln_s"], O["mlm_s"])
        for h in range(4):
            k.dma("pool", co_[j, h].rearrange("(c p) e -> p c e", p=128), C[:, h, :, 0:256], reads=[C])
            k.dma("pool", no_[j, h].rearrange("(c p o) -> p c o", p=128, o=1), C[:, h, :, 256:257], reads=[C], allow_slow_non_contiguous=True)
        k.dma("pool", mo_[j].rearrange("(h o) -> h o", o=1), Bt[:, T - 1:T], reads=[Bt])


def make_consts():
    c = np.zeros((128, 128 * 5 + 2048), np.float32)
    idx = np.arange(128)
    c[:, 0:128] = np.eye(128)
    c[:, 128:256] = (idx[:, None] >= idx[None, :])
    c[:, 256:384] = 1.0
    c[:, 384:512] = (idx[:, None] <= idx[None, :])
    q = np.arange(512)
    for jj in range(4):
        c[:, 640 + jj * 512: 640 + (jj + 1) * 512] = np.where((128 * jj + idx[:, None]) < q[None, :], 0.0, -30000.0)
    return c


def make_smask(PAST, TS, koff):
    m = np.zeros((128, 3, 32), np.float32)
    idx = np.arange(128)
    q = np.arange(32)
    nb = (koff + PAST + TS) // 128
    key = (nb - 1) * 128 + idx - koff
    m[:, 0, :TS] = np.where((key[:, None] < PAST) | ((key[:, None] - PAST) < q[None, :TS]), 0.0, -30000.0)
    fr = koff // 128
    key = fr * 128 + idx - koff
    m[:, 1, :TS] = np.where(key[:, None] >= 0, 0.0, -30000.0)
    m[:, 2, :] = -30000.0
    return m.reshape(128, 96)


_CACHE = {}


def _prep(inp):
    x_prompt = np.asarray(inp["x_prompt"], np.float32)
    x_sample = np.asarray(inp["x_sample"], np.float32)
    B, T, _ = x_prompt.shape
    SBN, TS, _ = x_sample.shape
    DEPTH = inp["g_pre"].shape[0]
    PAST = inp["cache_sb_k"].shape[2]
    key = (T, TS, PAST, DEPTH)
    if key not in _CACHE:
        p = Prog(T, TS, PAST, DEPTH)
        p.build()
        _CACHE[key] = p
    p = _CACHE[key]
    NSB, NML, NCV = p.NSB, p.NML, p.NCV
    f = lambda a: np.ascontiguousarray(np.asarray(a, np.float32))

    def rep(a):
        a = f(a)
        return np.ascontiguousarray(np.broadcast_to(a[:, None, :], (a.shape[0], 128, a.shape[1])))

    def colT(a):
        a = f(a)
        return np.ascontiguousarray(a.reshape(a.shape[0], 8, 128).transpose(0, 2, 1))

    def convT(a):
        a = f(a)
        return np.ascontiguousarray(a.reshape(a.shape[0], a.shape[1], 8, 128).transpose(0, 3, 2, 1))

    NBs = (PAST + TS + 127) // 128
    if NBs % 2:
        NBs += 1
    koff = NBs * 128 - (PAST + TS)
    consts = make_consts()
    smask = make_smask(PAST, TS, koff)
    nml = max(NML, 1)
    ncv = max(NCV, 1)

    def orz(a, shape):
        a = f(a)
        if a.shape[0] == 0:
            return np.zeros(shape, np.float32)
        return a

    convml = np.concatenate([f(inp["conv_ml_w"]), f(inp["conv_ml_b"])[:, None, :]], axis=1) if NML else np.zeros((1, 5, D), np.float32)
    gcol = np.ascontiguousarray(np.stack([f(inp["b_ig_ml"]), f(inp["b_fg_ml"])], axis=2)) if NML else np.zeros((1, 4, 2), np.float32)
    ii = np.arange(128)
    negmask = np.where(ii[:, None] <= ii[None, :], 0.0, -30000.0).astype(np.float32)
    sel4 = np.zeros((4, 4, 128), np.float32)
    for hh in range(4):
        sel4[hh, hh, :] = 1.0
    sel4 = sel4.reshape(4, 512)
    common = {
        "memp": None, "gpre_r": rep(inp["g_pre"]), "gpost_r": rep(inp["g_post"]), "gmem_r": rep(inp["g_mem"]),
        "wmemkv": f(inp["w_mem_kv"]), "wout": f(inp["w_out"]), "winsb": f(inp["w_in_sb"]),
        "winml": orz(inp["w_in_ml"], (1, D, 5128)), "wincv": orz(inp["w_in_cv"], (1, D, 5120)),
        "convmlT": convT(convml), "wq": orz(inp["wq_ml"], (1, 4, 256, 256)), "wk": orz(inp["wk_ml"], (1, 4, 256, 256)),
        "gcol": gcol, "ghead_r": rep(orz(inp["g_head_ml"], (1, D))), "negmask": negmask, "sel4": sel4, "skipT": colT(orz(inp["skip_ml"], (1, D))),
        "convcvT": convT(orz(inp["conv_cv_w"], (1, 3, D))), "consts": consts, "smask": smask,
    }
    in_maps = []
    ncores = 8
    for c in range(ncores):
        bp = c % B
        bs = c % SBN
        m = dict(common)
        m["xp"] = f(x_prompt[bp])
        m["xs"] = f(x_sample[bs])
        m["csk"] = f(inp["cache_sb_k"][:, bs])
        m["csv"] = f(inp["cache_sb_v"][:, bs])
        m["smc"] = orz(np.asarray(inp["state_ml_c"])[:, bs], (1, 4, 256, 256))
        m["smn"] = orz(np.asarray(inp["state_ml_n"])[:, bs], (1, 4, 256))
        m["smm"] = orz(np.asarray(inp["state_ml_m"])[:, bs], (1, 4))
        m["smconvT"] = convT(orz(np.asarray(inp["state_ml_conv"])[:, bs], (1, 3, D)))
        m["scvT"] = convT(orz(np.asarray(inp["state_cv_conv"])[:, bs], (1, 2, D)))
        m["cmk"] = f(inp["cache_mem_k"][:, bs])
        m["cmv"] = f(inp["cache_mem_v"][:, bs])
        m["memp"] = f(inp["mem_prompt"][bp])
        in_maps.append(m)
    return p, in_maps, B, SBN


def kernel(**inp):
    p, in_maps, B, SBN = _prep(inp)
    NML, NCV = p.NML, p.NCV
    res = run_bass_kernel_spmd(p.nc, in_maps, core_ids=list(range(len(in_maps))))
    R = res.results

    def gp(name, axis_b):
        return np.stack([np.asarray(R[c][name], np.float32) for c in range(B)], axis=axis_b)

    def gs(name, axis_b):
        return np.stack([np.asarray(R[c][name], np.float32) for c in range(SBN)], axis=axis_b)

    outs = (
        gp("yp", 0), gs("ys", 0),
        gp("sbk_p", 1), gp("sbv_p", 1),
        gp("mlc_p", 1)[:NML], gp("mln_p", 1)[:NML], gp("mlm_p", 1)[:NML], gp("mlconv_p", 1)[:NML],
        gp("cv_p", 1)[:NCV],
        gp("memk_p", 1), gp("memv_p", 1),
        gs("sbk_s", 1), gs("sbv_s", 1),
        gs("mlc_s", 1)[:NML], gs("mln_s", 1)[:NML], gs("mlm_s", 1)[:NML], gs("mlconv_s", 1)[:NML],
        gs("cv_s", 1)[:NCV],
    )
    return outs
```

```python
import contextlib
import math
import numpy as np
import concourse.bass as bass
import concourse.mybir as mybir
from concourse.bass_utils import run_bass_kernel_spmd

F32 = mybir.dt.float32
BF16 = mybir.dt.bfloat16
AF = mybir.ActivationFunctionType
ALU = mybir.AluOpType
AX = mybir.AxisListType

NDS = 48
D = 1024
EPS = 1e-6


class Tile:
    __slots__ = ("ap", "w", "r", "name")

    def __init__(self, ap, name=""):
        self.ap = ap
        self.w = None
        self.r = {}
        self.name = name

    def __getitem__(self, idx):
        return self.ap[idx]


class KB:
    def __init__(self, nc, stack):
        self.nc = nc
        self.stack = stack
        self.E = {"pe": nc.tensor, "act": nc.scalar, "dve": nc.vector, "pool": nc.gpsimd, "sp": nc.sync}
        self.csem = {e: stack.enter_context(nc.semaphore("c_" + e)) for e in ("pe", "act", "dve", "pool")}
        self.ccnt = {e: 0 for e in self.csem}
        self.dsem = [stack.enter_context(nc.semaphore("d%d" % i)) for i in range(NDS)]
        self.dcnt = [0] * NDS
        self.dnext = {"sp": 0, "pool": NDS // 2}
        self.seen = {e: {} for e in self.E}
        self.ninst = 0
        self.rr = {}

    def _wait(self, eng, tok):
        sem, val, key, kind = tok
        if self.seen[eng].get(key, 0) >= val:
            return
        self.E[eng].wait_ge(sem, val)
        self.seen[eng][key] = val
        self.ninst += 1

    def _deps(self, eng, reads, writes):
        toks = []
        for t in reads:
            if t.w is not None:
                toks.append(t.w)
        for t in writes:
            if t.w is not None:
                toks.append(t.w)
            toks.extend(t.r.values())
        for tok in toks:
            if eng == "pe" and tok[3] == "pe":
                continue
            self._wait(eng, tok)

    def _mark(self, tok, reads, writes):
        for t in reads:
            old = t.r.get(tok[2])
            if old is None or old[1] < tok[1]:
                t.r[tok[2]] = tok
        for t in writes:
            t.w = tok
            t.r = {}

    def op(self, eng, fn, reads=(), writes=()):
        self._deps(eng, reads, writes)
        ins = fn(self.E[eng])
        self.ccnt[eng] += 1
        ins.then_inc(self.csem[eng], 1)
        tok = (self.csem[eng], self.ccnt[eng], eng, eng)
        self._mark(tok, reads, writes)
        self.ninst += 1
        return ins

    def mm(self, out_tile, out_ap, pairs, reads, start=True, stop=True):
        writes = [out_tile]
        self._deps("pe", reads, writes)
        n = len(pairs)
        ins = None
        for i, (l, r) in enumerate(pairs):
            ins = self.nc.tensor.matmul(out_ap, l, r, start=(start and i == 0), stop=(stop and i == n - 1))
            self.ninst += 1
        self.ccnt["pe"] += 1
        ins.then_inc(self.csem["pe"], 1)
        tok = (self.csem["pe"], self.ccnt["pe"], "pe", "pe")
        self._mark(tok, reads, writes)
        return ins

    def tr(self, out_tile, out_ap, in_ap, ident_ap, reads):
        self._deps("pe", reads, [out_tile])
        ins = self.nc.tensor.transpose(out_ap, in_ap, ident_ap)
        self.ccnt["pe"] += 1
        ins.then_inc(self.csem["pe"], 1)
        tok = (self.csem["pe"], self.ccnt["pe"], "pe", "pe")
        self._mark(tok, reads, [out_tile])
        self.ninst += 1
        return ins

    def dma(self, q, out_ap, in_ap, reads=(), writes=(), **kw):
        self._deps(q, reads, writes)
        i = self.dnext[q]
        base = 0 if q == "sp" else NDS // 2
        self.dnext[q] = base + (i - base + 1) % (NDS // 2)
        if self.dcnt[i] > 0:
            self._wait(q, (self.dsem[i], 16 * self.dcnt[i], ("d", i), "dma"))
        self.E[q].dma_start(out=out_ap, in_=in_ap, **kw).then_inc(self.dsem[i], 16)
        self.dcnt[i] += 1
        tok = (self.dsem[i], 16 * self.dcnt[i], ("d", i), "dma")
        self._mark(tok, reads, writes)
        self.ninst += 1

    def barrier(self, engines=("pe", "act", "dve", "pool", "sp")):
        for e in engines:
            for c in self.csem:
                if self.ccnt[c] > 0:
                    self._wait(e, (self.csem[c], self.ccnt[c], c, c))
            for i in range(NDS):
                if self.dcnt[i] > 0:
                    self._wait(e, (self.dsem[i], 16 * self.dcnt[i], ("d", i), "dma"))

    def sb(self, name, shape, dtype, stack=None):
        self.uid = getattr(self, "uid", 0) + 1
        name = "%s_u%d" % (name, self.uid)
        t = (stack or self.stack).enter_context(self.nc.sbuf_tensor(name, list(shape), dtype))
        return Tile(t, name)

    def ps(self, name, shape, dtype=F32, stack=None):
        self.uid = getattr(self, "uid", 0) + 1
        name = "%s_u%d" % (name, self.uid)
        t = (stack or self.stack).enter_context(self.nc.psum_tensor(name, list(shape), dtype))
        return Tile(t, name)

    def ring(self, key, tiles):
        i = self.rr.get(key, 0)
        self.rr[key] = i + 1
        return tiles[i % len(tiles)]


class Prog:
    def __init__(self, T, TS, PAST, DEPTH):
        self.T, self.TS, self.PAST, self.DEPTH = T, TS, PAST, DEPTH
        self.NSB, self.NML, self.NCV = (DEPTH + 2) // 3, (DEPTH + 1) // 3, DEPTH // 3
        self.nc = bass.Bass("TRN2", target_bir_lowering=False)
        self.in_shapes = {}
        self.out_shapes = {}

    def din(self, name, shape, dt=F32):
        self.in_shapes[name] = tuple(shape)
        return self.nc.dram_tensor(name, list(shape), dt, kind="ExternalInput").ap()

    def dout(self, name, shape, dt=F32):
        self.out_shapes[name] = tuple(shape)
        return self.nc.dram_tensor(name, list(shape), dt, kind="ExternalOutput").ap()

    def dscr(self, name, shape, dt=F32):
        return self.nc.dram_tensor(name, list(shape), dt, kind="Internal").ap()

    def build(self):
        T, TS, PAST, DEPTH = self.T, self.TS, self.PAST, self.DEPTH
        NSB, NML, NCV = self.NSB, self.NML, self.NCV
        I = {}
        I["xp"] = self.din("xp", [T, D])
        I["xs"] = self.din("xs", [TS, D])
        I["csk"] = self.din("csk", [NSB, PAST, 8, 128])
        I["csv"] = self.din("csv", [NSB, PAST, 8, 128])
        I["smc"] = self.din("smc", [max(NML, 1), 4, 256, 256])
        I["smn"] = self.din("smn", [max(NML, 1), 4, 256])
        I["smm"] = self.din("smm", [max(NML, 1), 4])
        I["smconvT"] = self.din("smconvT", [max(NML, 1), 128, 8, 3])
        I["scvT"] = self.din("scvT", [max(NCV, 1), 128, 8, 2])
        I["cmk"] = self.din("cmk", [DEPTH, 256, 4, 128])
        I["cmv"] = self.din("cmv", [DEPTH, 256, 4, 128])
        I["memp"] = self.din("memp", [256, D])
        I["gpre_r"] = self.din("gpre_r", [DEPTH, 128, D])
        I["gpost_r"] = self.din("gpost_r", [DEPTH, 128, D])
        I["gmem_r"] = self.din("gmem_r", [DEPTH, 128, D])
        I["wmemkv"] = self.din("wmemkv", [DEPTH, D, D])
        I["wout"] = self.din("wout", [DEPTH, 1536, D])
        I["winsb"] = self.din("winsb", [NSB, D, 5120])
        I["winml"] = self.din("winml", [max(NML, 1), D, 5128])
        I["wincv"] = self.din("wincv", [max(NCV, 1), D, 5120])
        I["convmlT"] = self.din("convmlT", [max(NML, 1), 128, 8, 5])
        I["wq"] = self.din("wq", [max(NML, 1), 4, 256, 256])
        I["wk"] = self.din("wk", [max(NML, 1), 4, 256, 256])
        I["gcol"] = self.din("gcol", [max(NML, 1), 4, 2])
        I["ghead_r"] = self.din("ghead_r", [max(NML, 1), 128, D])
        I["negmask"] = self.din("negmask", [128, 128])
        I["sel4"] = self.din("sel4", [4, 512])
        I["skipT"] = self.din("skipT", [max(NML, 1), 128, 8])
        I["convcvT"] = self.din("convcvT", [max(NCV, 1), 128, 8, 3])
        I["consts"] = self.din("consts", [128, 128 * 5 + 4 * 512 * 1])
        I["smask"] = self.din("smask", [128, 3 * 32])
        O = {}
        O["yp"] = self.dout("yp", [T, D])
        O["ys"] = self.dout("ys", [TS, D])
        O["sbk_p"] = self.dout("sbk_p", [NSB, T, 8, 128])
        O["sbv_p"] = self.dout("sbv_p", [NSB, T, 8, 128])
        O["mlc_p"] = self.dout("mlc_p", [max(NML, 1), 4, 256, 256])
        O["mln_p"] = self.dout("mln_p", [max(NML, 1), 4, 256])
        O["mlm_p"] = self.dout("mlm_p", [max(NML, 1), 4])
        O["mlconv_p"] = self.dout("mlconv_p", [max(NML, 1), 3, D])
        O["cv_p"] = self.dout("cv_p", [max(NCV, 1), 2, D])
        O["memk_p"] = self.dout("memk_p", [DEPTH, 256, 4, 128])
        O["memv_p"] = self.dout("memv_p", [DEPTH, 256, 4, 128])
        O["sbk_s"] = self.dout("sbk_s", [NSB, TS, 8, 128])
        O["sbv_s"] = self.dout("sbv_s", [NSB, TS, 8, 128])
        O["mlc_s"] = self.dout("mlc_s", [max(NML, 1), 4, 256, 256])
        O["mln_s"] = self.dout("mln_s", [max(NML, 1), 4, 256])
        O["mlm_s"] = self.dout("mlm_s", [max(NML, 1), 4])
        O["mlconv_s"] = self.dout("mlconv_s", [max(NML, 1), 3, D])
        O["cv_s"] = self.dout("cv_s", [max(NCV, 1), 2, D])
        self.I, self.O = I, O
        self.SP = dict(name="p", T=T, xin=I["xp"], yout=O["yp"], xres=self.dscr("xres_p", [T, D]),
                       FM=self.dscr("fm_p", [40, 128, T], BF16), YT=self.dscr("yt_p", [12, 128, T], BF16),
                       TMs=self.dscr("tm_p", [2, T, D], BF16), GT=self.dscr("gt_p", [8, T], F32))
        self.SS = dict(name="s", T=TS, xin=I["xs"], yout=O["ys"], xres=self.dscr("xres_s", [TS, D]),
                       FM=self.dscr("fm_s", [40, 128, TS], BF16), YT=self.dscr("yt_s", [12, 128, TS], BF16),
                       TMs=self.dscr("tm_s", [2, TS, D], BF16), GT=self.dscr("gt_s", [8, TS], F32))
        with contextlib.ExitStack() as st:
            k = KB(self.nc, st)
            self.k = k
            self.cst_f = k.sb("cst_f", [128, 128], F32)
            self.cst_b = k.sb("cst_b", [128, 4, 128], BF16)
            self.msk = k.sb("msk", [128, 4, 512], BF16)
            self.smsk = k.sb("smsk", [128, 3, 32], BF16)
            with contextlib.ExitStack() as s2:
                tmp = k.sb("cst_tmp", [128, 128 * 5 + 2048], F32, s2)
                tmp2 = k.sb("cst_tmp2", [128, 96], F32, s2)
                k.dma("sp", tmp[:], I["consts"][:, :], writes=[tmp])
                k.dma("sp", tmp2[:], I["smask"][:, :], writes=[tmp2])
                k.op("dve", lambda e: e.tensor_copy(out=self.cst_f[:], in_=tmp[:, 0:128]), reads=[tmp], writes=[self.cst_f])
                k.op("dve", lambda e: e.tensor_copy(out=self.cst_b[:].rearrange("p a b -> p (a b)"), in_=tmp[:, 0:512]), reads=[tmp], writes=[self.cst_b])
                k.op("dve", lambda e: e.tensor_copy(out=self.msk[:].rearrange("p a b -> p (a b)"), in_=tmp[:, 640:640 + 2048]), reads=[tmp], writes=[self.msk])
                k.op("dve", lambda e: e.tensor_copy(out=self.smsk[:].rearrange("p a b -> p (a b)"), in_=tmp2[:]), reads=[tmp2], writes=[self.smsk])
                k.barrier()
            for li in range(DEPTH):
                kind, j = li % 3, li // 3
                self.layer(li, kind, j)
            k.barrier()
        return self.nc

    def load_w_gen(self, k, st, dst, src, ncols, nkc, name, CP=256):
        stg = [k.sb("%s_stg%d" % (name, i), [128, nkc, CP], F32, st) for i in range(2)]
        srcv = src.rearrange("(c p) n -> p c n", p=128)
        c0 = 0
        i = 0
        while c0 < ncols:
            cw = min(CP, ncols - c0)
            s = stg[i % 2]
            k.dma("sp", s[:, :, 0:cw], srcv[:, :, c0:c0 + cw], writes=[s])
            eng = "pool" if i % 2 == 0 else "act"
            if eng == "pool":
                k.op("pool", lambda e, s=s, c0=c0, cw=cw: e.tensor_copy(out=dst[:, :, c0:c0 + cw], in_=s[:, :, 0:cw]), reads=[s], writes=[dst])
            else:
                k.op("act", lambda e, s=s, c0=c0, cw=cw: e.activation(out=dst[:, :, c0:c0 + cw], in_=s[:, :, 0:cw], func=AF.Copy), reads=[s], writes=[dst])
            c0 += cw
            i += 1
            yield

    def load_w(self, k, st, dst, src, ncols, nkc, name, CP=256):
        for _ in self.load_w_gen(k, st, dst, src, ncols, nkc, name, CP):
            pass

    def layer(self, li, kind, j):
        k = self.k
        I, O = self.I, self.O
        last = (li == self.DEPTH - 1)
        with contextlib.ExitStack() as st:
            mk = {"p": k.sb("mkT_p", [128, 4, 256], BF16, st), "s": k.sb("mkT_s", [128, 4, 256], BF16, st)}
            mv = {"p": k.sb("mv_p", [128, 2, 512], BF16, st), "s": k.sb("mv_s", [128, 2, 512], BF16, st)}
            self.memkv_phase(li, mk, mv)
            k.barrier()
            with contextlib.ExitStack() as s1:
                ncols = 5128 if kind == 1 else 5120
                if getattr(self, "wb_pref", None) is not None:
                    Wb = self.wb_pref
                else:
                    Wb = k.sb("Wb", [128, 8, ncols], BF16, s1)
                    wsrc = (I["winsb"], I["winml"], I["wincv"])[kind][j]
                    with contextlib.ExitStack() as sw:
                        self.load_w(k, sw, Wb, wsrc, ncols, 8, "win")
                        k.barrier()
                for stm in (self.SS, self.SP):
                    with contextlib.ExitStack() as s3:
                        self.p1(stm, li, kind, j, Wb, mk[stm["name"]], mv[stm["name"]], s3)
                        k.barrier()
        if getattr(self, "wb_pref", None) is not None:
            self.wb_stack.close()
            self.wb_pref = None
        if kind == 0:
            for stm in (self.SS, self.SP):
                with contextlib.ExitStack() as s3:
                    if stm is self.SS:
                        self.p2_sb_sample(stm, li, j, s3)
                    else:
                        self.p2_sb(stm, li, j, s3)
                    k.barrier()
        elif kind == 1:
            for stm in (self.SS, self.SP):
                with contextlib.ExitStack() as s3:
                    self.p2_ml(stm, li, j, s3)
                    k.barrier()
        bg = None
        if not last:
            nkind, nj = (li + 1) % 3, (li + 1) // 3
            nncols = 5128 if nkind == 1 else 5120
            self.wb_stack = contextlib.ExitStack()
            self.wb_pref = k.sb("Wbn", [128, 8, nncols], BF16, self.wb_stack)
            self.wb_stg_stack = contextlib.ExitStack()
            nsrc = (I["winsb"], I["winml"], I["wincv"])[nkind][nj]
            bg = self.load_w_gen(k, self.wb_stg_stack, self.wb_pref, nsrc, nncols, 8, "winn")
            next(bg, None)
        with contextlib.ExitStack() as s1:
            Wo = k.sb("Wo", [128, 12, D], BF16, s1)
            gpo = k.sb("gpo", [128, D], F32, s1)
            with contextlib.ExitStack() as sw:
                self.load_w(k, sw, Wo, I["wout"][li], D, 12, "wout")
                k.dma("sp", gpo[:], I["gpost_r"][li], writes=[gpo])
                k.barrier()
            for stm in (self.SS, self.SP):
                with contextlib.ExitStack() as s3:
                    self.p3(stm, li, Wo, gpo, last, s3, bg if stm is self.SP else None)
                    if stm is self.SP and bg is not None:
                        for _ in bg:
                            pass
                    k.barrier()
        if bg is not None:
            self.wb_stg_stack.close()

    def rms_front(self, k, xsrc_ap, np_, xt, junk, ss, hb, grep):
        k.dma("sp", xt[0:np_, :], xsrc_ap, writes=[xt])
        k.op("pool", lambda e: e.memset(ss[:], 0.0), writes=[ss])
        k.op("act", lambda e: e.activation(out=junk[0:np_, :], in_=xt[0:np_, :], func=AF.Square, accum_out=ss[0:np_, 0:1]), reads=[xt, ss], writes=[junk, ss])
        k.op("act", lambda e: e.activation(out=ss[0:np_, 1:2], in_=ss[0:np_, 0:1], func=AF.Ln, scale=1.0 / D, bias=EPS), reads=[ss], writes=[ss])
        k.op("act", lambda e: e.activation(out=ss[0:np_, 1:2], in_=ss[0:np_, 1:2], func=AF.Exp, scale=-0.5), reads=[ss], writes=[ss])
        k.op("dve", lambda e: e.scalar_tensor_tensor(out=hb[0:np_, :], in0=xt[0:np_, :], scalar=ss[0:np_, 1:2], in1=grep[0:np_, :], op0=ALU.mult, op1=ALU.mult), reads=[xt, ss, grep], writes=[hb])

    def to_fm(self, k, hb, np_, ptr, hT, col0):
        for kc in range(8):
            k.tr(ptr, ptr[:, kc, 0:np_], hb[0:np_, kc * 128:(kc + 1) * 128], self.cst_b[0:np_, 0, 0:np_], reads=[hb, self.cst_b])
        k.op("dve", lambda e: e.tensor_copy(out=hT[:, :, col0:col0 + np_], in_=ptr[:, :, 0:np_]), reads=[ptr], writes=[hT])

    def memkv_phase(self, li, mk, mv):
        k = self.k
        I, O = self.I, self.O
        with contextlib.ExitStack() as st:
            Wm = k.sb("Wm", [128, 8, D], BF16, st)
            gm = k.sb("gm", [128, D], F32, st)
            with contextlib.ExitStack() as sw:
                self.load_w(k, sw, Wm, I["wmemkv"][li], D, 8, "wmem")
                k.barrier()
            k.dma("sp", gm[:], I["gmem_r"][li], writes=[gm])
            xt = [k.sb("mxt%d" % i, [128, D], F32, st) for i in range(2)]
            junk = k.sb("mjunk", [128, D], BF16, st)
            ss = [k.sb("mss%d" % i, [128, 2], F32, st) for i in range(2)]
            hb = [k.sb("mhb%d" % i, [128, D], BF16, st) for i in range(2)]
            hT = k.sb("mhT", [128, 8, 256], BF16, st)
            ptr = k.ps("mptr", [128, 8, 128], BF16, st)
            acc = [k.ps("macc%d" % i, [128, 512], F32, st) for i in range(3)]
            og = [k.sb("mog%d" % i, [128, 512], F32, st) for i in range(2)]
            for s in range(2):
                self.rms_front(k, I["memp"][s * 128:(s + 1) * 128, :], 128, xt[s], junk, ss[s], hb[s], gm)
                self.to_fm(k, hb[s], 128, ptr, hT, s * 128)
            for s in range(2):
                for nb in range(2):
                    a = k.ring("macc", acc)
                    k.mm(a, a[:, :], [(hT[:, kc, s * 128:(s + 1) * 128], Wm[:, kc, nb * 512:(nb + 1) * 512]) for kc in range(8)], reads=[hT, Wm])
                    o = k.ring("mog", og)
                    k.op("act", lambda e, o=o, a=a: e.activation(out=o[:], in_=a[:], func=AF.Copy), reads=[a], writes=[o])
                    dst = (O["memk_p"], O["memv_p"])[nb][li, s * 128:(s + 1) * 128].rearrange("m h d -> m (h d)")
                    k.dma("pool", dst, o[:], reads=[o])
                    if nb == 1:
                        k.op("dve", lambda e, o=o, s=s: e.tensor_copy(out=mv["p"][:, s, :], in_=o[:]), reads=[o], writes=[mv["p"]])
            for h in range(4):
                a = k.ring("macc", acc)
                k.mm(a, a[:, 0:256], [(Wm[:, kc, h * 128:(h + 1) * 128], hT[:, kc, :]) for kc in range(8)], reads=[hT, Wm])
                k.op("dve", lambda e, a=a, h=h: e.tensor_copy(out=mk["p"][:, h, :], in_=a[:, 0:256]), reads=[a], writes=[mk["p"]])
            ck = k.sb("mck", [128, 2, 512], F32, st)
            cv = k.sb("mcv", [128, 2, 512], F32, st)
            k.dma("sp", ck[:], I["cmk"][li].rearrange("(s p) h d -> p s (h d)", p=128), writes=[ck])
            k.dma("sp", cv[:], I["cmv"][li].rearrange("(s p) h d -> p s (h d)", p=128), writes=[cv])
            k.op("dve", lambda e: e.tensor_copy(out=mv["s"][:], in_=cv[:]), reads=[cv], writes=[mv["s"]])
            for h in range(4):
                a = k.ring("macc", acc)
                for s in range(2):
                    k.tr(a, a[:, s * 128:(s + 1) * 128], ck[:, s, h * 128:(h + 1) * 128], self.cst_f[:], reads=[ck, self.cst_f])
                k.op("dve", lambda e, a=a, h=h: e.tensor_copy(out=mk["s"][:, h, :], in_=a[:, 0:256]), reads=[a], writes=[mk["s"]])

    def p1(self, stm, li, kind, j, Wb, mk, mv, st):
        k = self.k
        I, O = self.I, self.O
        T = stm["T"]
        isp = stm["name"] == "p"
        TT = min(256 if kind == 1 else 512, T)
        np_ = min(128, T)
        nsub = TT // np_
        ntt = T // TT
        xsrc = stm["xin"] if li == 0 else stm["xres"]
        FM, YT = stm["FM"], stm["YT"]
        if kind == 1:
            self.tmob = [k.sb("tmob%d" % i, [128, 512], BF16, st) for i in range(2)]
        gpre = k.sb("gpre", [128, D], F32, st)
        k.dma("sp", gpre[:], I["gpre_r"][li], writes=[gpre])
        xt = [k.sb("xt%d" % i, [128, D], F32, st) for i in range(3)]
        junk = k.sb("junk", [128, D], BF16, st)
        ss = [k.sb("ss%d" % i, [128, 2], F32, st) for i in range(3)]
        hb = [k.sb("hb%d" % i, [128, D], BF16, st) for i in range(2)]
        hT = [k.sb("hT%d" % i, [128, 8, TT], BF16, st) for i in range(2)]
        ptr = k.ps("ptr", [128, 8, 128], BF16, st)
        acc = [k.ps("acc%d" % i, [128, 512], F32, st) for i in range(4)]
        memS = k.ps("memS", [128, 2, 512], F32, st)
        nring = 2 if kind == 2 else 3
        stg = [k.sb("stg%d" % i, [128, 4, TT], BF16, st) for i in range(nring)]
        tmo = [k.sb("tmo%d" % i, [128, 512], F32, st) for i in range(nring)]
        mq = k.sb("mq", [128, 4, TT], BF16, st)
        zm = k.sb("zm", [128, 4, TT], BF16, st)
        eT = k.sb("eT", [128, 2, TT], BF16, st)
        rden = k.sb("rden", [128, TT], F32, st)
        ymem = k.sb("ymem", [128, 4, TT], BF16, st)
        if kind == 0:
            mqc, zc, zmc = 3072, 3584, 4608
        elif kind == 1:
            mqc, zc, zmc = 3080, 3592, 4616
        else:
            mqc, zc, zmc = 3072, 3584, 4608

        def fm_chunk(hTt, col):
            a = k.ring("acc", acc)
            k.mm(a, a[:, 0:TT], [(Wb[:, kc, col:col + 128], hTt[:, kc, :]) for kc in range(8)], reads=[hTt, Wb])
            return a

        def fm_group(hTt, col0, nch, func, tok0, dst_fm0=None, dst_tile=None, scale=1.0):
            t = dst_tile if dst_tile is not None else k.ring("stg", stg)
            for c in range(nch):
                a = fm_chunk(hTt, col0 + c * 128)
                if func is None:
                    eng = k.ring("evac", ["dve", "act"])
                    if eng == "dve":
                        k.op("dve", lambda e, a=a, c=c: e.tensor_copy(out=t[:, c, :], in_=a[:, 0:TT]), reads=[a], writes=[t])
                    else:
                        k.op("act", lambda e, a=a, c=c: e.activation(out=t[:, c, :], in_=a[:, 0:TT], func=AF.Copy), reads=[a], writes=[t])
                else:
                    k.op("act", lambda e, a=a, c=c: e.activation(out=t[:, c, :], in_=a[:, 0:TT], func=func, scale=scale), reads=[a], writes=[t])
            if dst_fm0 is not None:
                k.dma("pool", FM[dst_fm0:dst_fm0 + nch, :, tok0:tok0 + TT].rearrange("c p t -> p c t"), t[:, 0:nch, :], reads=[t])
            return t

        def tm_block(hTt, s, col0, ncols):
            a = k.ring("acc", acc)
            k.mm(a, a[0:np_, 0:ncols], [(hTt[:, kc, s * np_:(s + 1) * np_], Wb[:, kc, col0:col0 + ncols]) for kc in range(8)], reads=[hTt, Wb])
            return a

        def mem_attn(tok0):
            sc = 1.0 / math.sqrt(128.0)
            for h in range(4):
                for mb in range(2):
                    k.mm(memS, memS[:, mb, 0:TT], [(mk[:, h, mb * 128:(mb + 1) * 128], mq[:, h, :])], reads=[mk, mq])
                k.op("act", lambda e: e.activation(out=eT[:], in_=memS[:, :, 0:TT], func=AF.Exp, scale=sc), reads=[memS], writes=[eT])
                den = k.ring("acc", acc)
                k.mm(den, den[:, 0:TT], [(self.cst_b[:, 2, :], eT[:, mb, :]) for mb in range(2)], reads=[eT, self.cst_b])
                oT = k.ring("acc", acc)
                k.mm(oT, oT[:, 0:TT], [(mv[:, mb, h * 128:(h + 1) * 128], eT[:, mb, :]) for mb in range(2)], reads=[eT, mv])
                k.op("dve", lambda e, den=den: e.reciprocal(out=rden[:], in_=den[:, 0:TT]), reads=[den], writes=[rden])
                k.op("dve", lambda e: e.tensor_tensor(out=rden[:], in0=rden[:], in1=zm[:, h, :], op=ALU.mult), reads=[rden, zm], writes=[rden])
                k.op("dve", lambda e, oT=oT, h=h: e.tensor_tensor(out=ymem[:, h, :], in0=oT[:, 0:TT], in1=rden[:], op=ALU.mult), reads=[oT, rden], writes=[ymem])
            k.dma("pool", YT[8:12, :, tok0:tok0 + TT].rearrange("c p t -> p c t"), ymem[:], reads=[ymem])

        if kind == 1:
            cw = k.sb("cw", [128, 8, 5], F32, st)
            k.dma("sp", cw[:], I["convmlT"][j], writes=[cw])
            xm = k.sb("xm", [128, 8, 3 + TT], F32, st)
            if isp:
                k.op("pool", lambda e: e.memset(xm[:, :, 0:3], 0.0), writes=[xm])
            else:
                k.dma("sp", xm[:, :, 0:3], I["smconvT"][j], writes=[xm])
            cacc = [k.sb("cacc%d" % i, [128, TT], F32, st) for i in range(2)]
            xc = k.sb("xc", [128, 8, TT], BF16, st)
            wqb = k.sb("wqb", [128, 8, 256], BF16, st)
            wkb = k.sb("wkb", [128, 8, 256], BF16, st)
            with contextlib.ExitStack() as sw:
                self.load_w(k, sw, wqb, I["wq"][j].rearrange("h d e -> (h d) e"), 256, 8, "wq")
                self.load_w(k, sw, wkb, I["wk"][j].rearrange("h d e -> (h d) e"), 256, 8, "wk")
                k.barrier()
            gtl = [k.sb("gtl%d" % i, [8, TT], F32, st) for i in range(2)]
            self.xcs = [k.sb("xcs%d" % i, [128, 8, TT], BF16, st) for i in range(2)]
            self.skp1 = k.sb("skp1", [128, 8], F32, st)
            k.dma("sp", self.skp1[:], I["skipT"][j], writes=[self.skp1])
        if kind == 2:
            cw = k.sb("cw", [128, 8, 3], F32, st)
            k.dma("sp", cw[:], I["convcvT"][j], writes=[cw])
            ch = k.sb("ch", [128, 8, 2 + TT], F32, st)
            if isp:
                k.op("pool", lambda e: e.memset(ch[:, :, 0:2], 0.0), writes=[ch])
            else:
                k.dma("sp", ch[:, :, 0:2], I["scvT"][j], writes=[ch])
            bT = [k.sb("bT%d" % i, [128, 4, TT], BF16, st) for i in range(2)]
            cT = [k.sb("cT%d" % i, [128, 4, TT], BF16, st) for i in range(2)]
            cacc = [k.sb("cacc%d" % i, [128, TT], F32, st) for i in range(2)]
            yst = [k.sb("yst%d" % i, [128, 4, TT], BF16, st) for i in range(2)]

        for tt in range(ntt):
            tok0 = tt * TT
            hTt = hT[tt % 2]
            for s in range(nsub):
                x_ = k.ring("xt", xt)
                s_ = k.ring("ss", ss)
                h_ = k.ring("hb", hb)
                self.rms_front(k, xsrc[tok0 + s * np_: tok0 + (s + 1) * np_, :], np_, x_, junk, s_, h_, gpre)
                self.to_fm(k, h_, np_, ptr, hTt, s * np_)
            fm_group(hTt, mqc, 4, None, tok0, dst_tile=mq)
            fm_group(hTt, zmc, 4, AF.Silu, tok0, dst_tile=zm)
            mem_attn(tok0)
            if kind == 0:
                fm_group(hTt, 0, 4, None, tok0, dst_fm0=0)
                fm_group(hTt, 512, 4, None, tok0, dst_fm0=4)
                fm_group(hTt, 1024, 4, None, tok0, dst_fm0=8)
                fm_group(hTt, 1536, 4, None, tok0, dst_fm0=12)
                fm_group(hTt, zc, 4, AF.Silu, tok0, dst_fm0=16)
                fm_group(hTt, zc + 512, 4, AF.Silu, tok0, dst_fm0=20)
                ko = (O["sbk_p"] if isp else O["sbk_s"])[j].rearrange("t h d -> t (h d)")
                vo = (O["sbv_p"] if isp else O["sbv_s"])[j].rearrange("t h d -> t (h d)")
                for s in range(nsub):
                    for (dst, c0) in ((ko, 1024), (vo, 2048)):
                        for nb in range(2):
                            a = tm_block(hTt, s, c0 + nb * 512, 512)
                            o = k.ring("tmo", tmo)
                            eng = k.ring("evac", ["dve", "act"])
                            if eng == "dve":
                                k.op("dve", lambda e, a=a, o=o: e.tensor_copy(out=o[0:np_, :], in_=a[0:np_, :]), reads=[a], writes=[o])
                            else:
                                k.op("act", lambda e, a=a, o=o: e.activation(out=o[0:np_, :], in_=a[0:np_, :], func=AF.Copy), reads=[a], writes=[o])
                            k.dma("pool", dst[tok0 + s * np_: tok0 + (s + 1) * np_, nb * 512:(nb + 1) * 512], o[0:np_, :], reads=[o])
            elif kind == 2:
                for g in range(2):
                    bt = fm_group(hTt, 0 + g * 512, 4, None, tok0, dst_tile=bT[g])
                    ct = fm_group(hTt, 1024 + g * 512, 4, None, tok0, dst_tile=cT[g])
                    for c in range(4):
                        a = fm_chunk(hTt, 2048 + (g * 4 + c) * 128)
                        k.op("dve", lambda e, a=a, c=c, g=g, ct=ct: e.tensor_tensor(out=ch[:, g * 4 + c, 2:2 + TT], in0=a[:, 0:TT], in1=ct[:, c, :], op=ALU.mult), reads=[a, ct, ch], writes=[ch])
                    zt = fm_group(hTt, zc + g * 512, 4, AF.Silu, tok0, dst_tile=k.ring("stg", stg))
                    yt = yst[g]
                    for c in range(4):
                        cc = g * 4 + c
                        ca = k.ring("cacc", cacc)
                        k.op("dve", lambda e, ca=ca, cc=cc: e.tensor_scalar(out=ca[:], in0=ch[:, cc, 0:TT], scalar1=cw[:, cc, 0:1], scalar2=None, op0=ALU.mult), reads=[ch, cw], writes=[ca])
                        k.op("dve", lambda e, ca=ca, cc=cc: e.scalar_tensor_tensor(out=ca[:], in0=ch[:, cc, 1:1 + TT], scalar=cw[:, cc, 1:2], in1=ca[:], op0=ALU.mult, op1=ALU.add), reads=[ch, cw, ca], writes=[ca])
                        k.op("dve", lambda e, ca=ca, cc=cc: e.scalar_tensor_tensor(out=ca[:], in0=ch[:, cc, 2:2 + TT], scalar=cw[:, cc, 2:3], in1=ca[:], op0=ALU.mult, op1=ALU.add), reads=[ch, cw, ca], writes=[ca])
                        k.op("pool", lambda e, ca=ca, c=c, bt=bt: e.tensor_tensor(out=ca[:], in0=ca[:], in1=bt[:, c, :], op=ALU.mult), reads=[ca, bt], writes=[ca])
                        k.op("pool", lambda e, ca=ca, c=c, zt=zt, yt=yt: e.tensor_tensor(out=yt[:, c, :], in0=ca[:], in1=zt[:, c, :], op=ALU.mult), reads=[ca, zt], writes=[yt])
                    k.dma("pool", YT[g * 4:g * 4 + 4, :, tok0:tok0 + TT].rearrange("c p t -> p c t"), yt[:], reads=[yt])
                if tt == ntt - 1:
                    s = nsub - 1
                    cvo = (O["cv_p"] if isp else O["cv_s"])[j]
                    for nb in range(2):
                        a1 = tm_block(hTt, s, 1024 + nb * 512, 512)
                        o1 = k.ring("tmo", tmo)
                        k.op("act", lambda e, a1=a1, o1=o1: e.activation(out=o1[0:np_, :], in_=a1[0:np_, :], func=AF.Copy), reads=[a1], writes=[o1])
                        a2 = tm_block(hTt, s, 2048 + nb * 512, 512)
                        k.op("dve", lambda e, a2=a2, o1=o1: e.tensor_tensor(out=o1[0:np_, :], in0=a2[0:np_, :], in1=o1[0:np_, :], op=ALU.mult), reads=[a2, o1], writes=[o1])
                        k.dma("pool", cvo[:, nb * 512:(nb + 1) * 512], o1[np_ - 2:np_, :], reads=[o1])
                if tt < ntt - 1:
                    k.op("dve", lambda e: e.tensor_copy(out=ch[:, :, 0:2], in_=ch[:, :, TT:TT + 2]), reads=[ch], writes=[ch])
            else:
                self.p1_ml(stm, li, j, tt, ntt, tok0, TT, np_, nsub, hTt, fm_chunk, fm_group, tm_block, tmo, stg, acc, xm, cw, cacc, xc, wqb, wkb, gtl, zc, Wb)

    def p1_ml(self, stm, li, j, tt, ntt, tok0, TT, np_, nsub, hTt, fm_chunk, fm_group, tm_block, tmo, stg, acc, xm, cw, cacc, xc, wqb, wkb, gtl, zc, Wb):
        k = self.k
        I, O = self.I, self.O
        isp = stm["name"] == "p"
        FM, TMs, GT = stm["FM"], stm["TMs"], stm["GT"]
        for c in range(8):
            a = fm_chunk(hTt, c * 128)
            k.op("act", lambda e, a=a, c=c: e.activation(out=xm[:, c, 3:3 + TT], in_=a[:, 0:TT], func=AF.Copy), reads=[a, xm], writes=[xm])
        fm_group(hTt, 2048, 4, AF.Sigmoid, tok0, dst_fm0=24)
        fm_group(hTt, 2048 + 512, 4, AF.Sigmoid, tok0, dst_fm0=28)
        fm_group(hTt, zc, 4, AF.Silu, tok0, dst_fm0=32)
        fm_group(hTt, zc + 512, 4, AF.Silu, tok0, dst_fm0=36)
        a = k.ring("acc", acc)
        k.mm(a, a[0:8, 0:TT], [(Wb[:, kc, 3072:3080], hTt[:, kc, :]) for kc in range(8)], reads=[hTt, Wb])
        g = k.ring("gtl", gtl)
        k.op("dve", lambda e, a=a, g=g: e.tensor_copy(out=g[:, :], in_=a[0:8, 0:TT]), reads=[a], writes=[g])
        k.dma("pool", GT[:, tok0:tok0 + TT], g[:, :], reads=[g])

        for s in range(nsub):
            for nb in range(2):
                a = tm_block(hTt, s, 1024 + nb * 512, 512)
                o = k.ring("tmob", self.tmob)
                k.op("dve", lambda e, a=a, o=o: e.tensor_copy(out=o[0:np_, :], in_=a[0:np_, :]), reads=[a], writes=[o])
                k.dma("pool", TMs[1, tok0 + s * np_: tok0 + (s + 1) * np_, nb * 512:(nb + 1) * 512], o[0:np_, :], reads=[o])
        for c in range(8):
            ca = k.ring("cacc", cacc)
            k.op("dve", lambda e, ca=ca, c=c: e.tensor_scalar(out=ca[:], in0=xm[:, c, 0:TT], scalar1=cw[:, c, 0:1], scalar2=None, op0=ALU.mult), reads=[xm, cw], writes=[ca])
            for jj in range(1, 4):
                k.op("dve", lambda e, ca=ca, c=c, jj=jj: e.scalar_tensor_tensor(out=ca[:], in0=xm[:, c, jj:jj + TT], scalar=cw[:, c, jj:jj + 1], in1=ca[:], op0=ALU.mult, op1=ALU.add), reads=[xm, cw, ca], writes=[ca])
            k.op("act", lambda e, ca=ca, c=c: e.activation(out=xc[:, c, :], in_=ca[:], func=AF.Silu, bias=cw[:, c, 4:5]), reads=[ca, cw], writes=[xc])
        if tt == ntt - 1:
            s = nsub - 1
            mco = (O["mlconv_p"] if isp else O["mlconv_s"])[j]
            for nb in range(2):
                a1 = tm_block(hTt, s, nb * 512, 512)
                o1 = k.ring("tmo", tmo)
                k.op("act", lambda e, a1=a1, o1=o1: e.activation(out=o1[0:np_, :], in_=a1[0:np_, :], func=AF.Copy), reads=[a1], writes=[o1])
                k.dma("pool", mco[:, nb * 512:(nb + 1) * 512], o1[np_ - 3:np_, :], reads=[o1])
        if tt < ntt - 1:
            k.op("dve", lambda e: e.tensor_copy(out=xm[:, :, 0:3], in_=xm[:, :, TT:TT + 3]), reads=[xm], writes=[xm])
        xs_ = k.ring("stgx", self.xcs)
        for c in range(8):
            k.op("act", lambda e, c=c: e.activation(out=xs_[:, c, :], in_=xc[:, c, :], func=AF.Copy, scale=self.skp1[:, c:c + 1]), reads=[xc, self.skp1], writes=[xs_])
        k.dma("pool", FM[16:24, :, tok0:tok0 + TT].rearrange("c p t -> p c t"), xs_[:], reads=[xs_])
        for (wb, f0, scl) in ((wqb, 0, 1.0), (wkb, 8, 1.0 / 16.0)):
            for g in range(2):
                t = k.ring("stg", stg)
                for c in range(4):
                    hc = g * 4 + c
                    h, ec = hc // 2, hc % 2
                    a = k.ring("acc", acc)
                    k.mm(a, a[:, 0:TT], [(wb[:, 2 * h + dc, ec * 128:(ec + 1) * 128], xc[:, 2 * h + dc, :]) for dc in range(2)], reads=[xc, wb])
                    k.op("act", lambda e, a=a, c=c, t=t, scl=scl: e.activation(out=t[:, c, :], in_=a[:, 0:TT], func=AF.Copy, scale=scl), reads=[a], writes=[t])
                k.dma("pool", FM[f0 + g * 4:f0 + g * 4 + 4, :, tok0:tok0 + TT].rearrange("c p t -> p c t"), t[:], reads=[t])
        for s in range(nsub):
            for nb in range(2):
                a = k.ring("acc", acc)
                for hh in range(2):
                    h = nb * 2 + hh
                    k.mm(a, a[0:np_, hh * 256:(hh + 1) * 256], [(xc[:, 2 * h + dc, s * np_:(s + 1) * np_], wkb[:, 2 * h + dc, :]) for dc in range(2)], reads=[xc, wkb])
                o = k.ring("tmob", self.tmob)
                k.op("act", lambda e, a=a, o=o: e.activation(out=o[0:np_, :], in_=a[0:np_, :], func=AF.Copy, scale=1.0 / 16.0), reads=[a], writes=[o])
                k.dma("pool", TMs[0, tok0 + s * np_: tok0 + (s + 1) * np_, nb * 512:(nb + 1) * 512], o[0:np_, :], reads=[o])
    def p3(self, stm, li, Wo, gpo, last, st, bg=None):
        k = self.k
        T = stm["T"]
        TT = min(512, T)
        np_ = min(128, T)
        nsub = TT // np_
        ntt = T // TT
        xsrc = stm["xin"] if li == 0 else stm["xres"]
        xdst = stm["yout"] if last else stm["xres"]
        YT = stm["YT"]
        yt = [k.sb("yt%d" % i, [128, 12, TT], BF16, st) for i in range(2)]
        xt = [k.sb("p3x%d" % i, [128, D], F32, st) for i in range(3)]
        ot = [k.sb("p3o%d" % i, [128, D], F32, st) for i in range(2)]
        junk = k.sb("p3junk", [128, D], BF16, st)
        ss = [k.sb("p3ss%d" % i, [128, 2], F32, st) for i in range(3)]
        acc = [k.ps("p3acc%d" % i, [128, D], F32, st) for i in range(3)]
        for tt in range(ntt):
            tok0 = tt * TT
            y_ = yt[tt % 2]
            k.dma("sp", y_[:], YT[:, :, tok0:tok0 + TT].rearrange("c p t -> p c t"), writes=[y_])
            for s in range(nsub):
                if bg is not None and s % 3 != 2:
                    next(bg, None)
                x_ = k.ring("p3x", xt)
                k.dma("sp", x_[0:np_, :], xsrc[tok0 + s * np_: tok0 + (s + 1) * np_, :], writes=[x_])
                a = k.ring("p3acc", acc)
                for nb in range(2):
                    k.mm(a, a[0:np_, nb * 512:(nb + 1) * 512], [(y_[:, fc, s * np_:(s + 1) * np_], Wo[:, fc, nb * 512:(nb + 1) * 512]) for fc in range(12)], reads=[y_, Wo])
                s_ = k.ring("p3ss", ss)
                k.op("pool", lambda e, s_=s_: e.memset(s_[:], 0.0), writes=[s_])
                k.op("act", lambda e, a=a, s_=s_: e.activation(out=junk[0:np_, :], in_=a[0:np_, :], func=AF.Square, accum_out=s_[0:np_, 0:1]), reads=[a, s_], writes=[junk, s_])
                k.op("act", lambda e, s_=s_: e.activation(out=s_[0:np_, 1:2], in_=s_[0:np_, 0:1], func=AF.Ln, scale=1.0 / D, bias=EPS), reads=[s_], writes=[s_])
                k.op("act", lambda e, s_=s_: e.activation(out=s_[0:np_, 1:2], in_=s_[0:np_, 1:2], func=AF.Exp, scale=-0.5), reads=[s_], writes=[s_])
                o_ = k.ring("p3o", ot)
                k.op("dve", lambda e, a=a, s_=s_, o_=o_: e.scalar_tensor_tensor(out=o_[0:np_, :], in0=a[0:np_, :], scalar=s_[0:np_, 1:2], in1=gpo[0:np_, :], op0=ALU.mult, op1=ALU.mult), reads=[a, s_, gpo], writes=[o_])
                k.op("pool", lambda e, o_=o_, x_=x_: e.tensor_tensor(out=o_[0:np_, :], in0=o_[0:np_, :], in1=x_[0:np_, :], op=ALU.add), reads=[o_, x_], writes=[o_])
                k.dma("pool", xdst[tok0 + s * np_: tok0 + (s + 1) * np_, :], o_[0:np_, :], reads=[o_])

    def p2_sb(self, stm, li, j, st):
        k = self.k
        I, O = self.I, self.O
        T = stm["T"]
        isp = stm["name"] == "p"
        FM, YT = stm["FM"], stm["YT"]
        N = min(512, T)
        sc = 1.0 / math.sqrt(128.0)
        PAST = self.PAST
        if isp:
            NB = T // 128
            koff = 0
        else:
            NB = (PAST + T + 127) // 128
            if NB % 2:
                NB += 1
            koff = NB * 128 - (PAST + T)
        nqs = T // N
        NSET = 2 if isp else 4
        early = not isp
        qT = [k.sb("qT%d" % i, [128, T], BF16, st) for i in range(NSET)]
        kT = [k.sb("kT%d" % i, [128, NB * 128], BF16, st) for i in range(NSET)]
        szT = [k.sb("szT%d" % i, [128, T], BF16, st) for i in range(NSET)]
        Vb = [k.sb("Vb%d" % i, [128, NB, 128], BF16, st) for i in range(NSET)]
        VST = 16
        vstg = [k.sb("vstg%d" % i, [128, VST, 128], F32, st) for i in range(2)] if isp else None
        vfull = [k.sb("vfull%d" % i, [128, NB, 128], F32, st) for i in range(2)] if not isp else None
        S = [k.ps("S%d" % i, [128, 2, N], F32, st) for i in range(2)]
        L = k.ps("L", [128, 2, N], F32, st)
        Racc = k.ps("Racc", [128, N], F32, st)
        oacc = k.ps("oacc", [128, N], F32, st)
        trs = k.ps("trs", [128, 512], F32, st) if not isp else None
        e_ = [k.sb("e%d" % i, [128, 2, N], BF16, st) for i in range(5)]
        c_ = [k.sb("c%d" % i, [128, 2, N], BF16, st) for i in range(3)]
        g_ = [k.sb("g%d" % i, [128, 2, N], BF16, st) for i in range(2)]
        a_ = [k.sb("a%d" % i, [128, 2, N], BF16, st) for i in range(3)]
        Lr = [k.sb("Lr%d" % i, [128, 2, N], F32, st) for i in range(2)]
        R = [k.sb("R%d" % i, [128, N], F32, st) for i in range(3)]
        yst = [k.sb("ysb%d" % i, [128, N], BF16, st) for i in range(2)]
        tri, ones = self.cst_b[:, 1, :], self.cst_b[:, 2, :]
        vout = (O["sbv_p"] if isp else O["sbv_s"])[j]
        kout = (O["sbk_p"] if isp else O["sbk_s"])[j]

        def load_head(h):
            si = h % NSET
            k.dma("sp", qT[si][:], FM[h], writes=[qT[si]])
            k.dma("sp", szT[si][:], FM[16 + h], writes=[szT[si]])
            if isp:
                k.dma("sp", kT[si][:], FM[8 + h], writes=[kT[si]])
                for b0 in range(0, NB, VST):
                    nb_ = min(VST, NB - b0)
                    vs = k.ring("vstg", vstg)
                    k.dma("sp", vs[:, 0:nb_, :], vout[b0 * 128:(b0 + nb_) * 128, h, :].rearrange("(b p) d -> p b d", p=128), writes=[vs])
                    k.op("pool", lambda e, vs=vs, b0=b0, nb_=nb_: e.tensor_copy(out=Vb[si][:, b0:b0 + nb_, :], in_=vs[:, 0:nb_, :]), reads=[vs], writes=[Vb[si]])
            else:
                for (src_c, src_n, isk) in ((I["csk"][j], kout, True), (I["csv"][j], vout, False)):
                    vs = k.ring("vfull", vfull)
                    k.op("pool", lambda e, vs=vs: e.memset(vs[:], 0.0), writes=[vs])
                    p0 = koff % 128
                    bq = koff // 128
                    t0 = (128 - p0) % 128
                    if t0:
                        k.dma("sp", vs[p0:128, bq, :], src_c[0:t0, h, :], writes=[vs])
                        bq += 1
                    nfull = (PAST - t0) // 128
                    if nfull:
                        k.dma("sp", vs[:, bq:bq + nfull, :], src_c[t0:t0 + nfull * 128, h, :].rearrange("(b p) d -> p b d", p=128), writes=[vs])
                    rem = PAST - t0 - nfull * 128
                    if rem:
                        k.dma("sp", vs[0:rem, bq + nfull, :], src_c[t0 + nfull * 128:PAST, h, :], writes=[vs])
                    k.dma("sp", vs[128 - T:128, NB - 1, :], src_n[0:T, h, :], writes=[vs])
                    if not isk:
                        k.op("pool", lambda e, vs=vs: e.tensor_copy(out=Vb[si][:], in_=vs[:]), reads=[vs], writes=[Vb[si]])
                    else:
                        for b0 in range(0, NB, 4):
                            nb_ = min(4, NB - b0)
                            for b in range(nb_):
                                k.tr(trs, trs[:, b * 128:(b + 1) * 128], vs[:, b0 + b, :], self.cst_f[:], reads=[vs, self.cst_f])
                            k.op("dve", lambda e, b0=b0, nb_=nb_: e.tensor_copy(out=kT[si][:, b0 * 128:(b0 + nb_) * 128], in_=trs[:, 0:nb_ * 128]), reads=[trs], writes=[kT[si]])

        G = []
        for h in range(8):
            for qs in range(nqs):
                q0 = qs * N
                grp = []
                if isp:
                    b = (q0 + N) // 128 - 1
                    while b >= 0:
                        m0 = (b - q0 // 128) if b >= q0 // 128 else None
                        m1 = ((b - 1) - q0 // 128) if (b - 1) >= q0 // 128 else None
                        grp.append((b, b - 1, m0, m1))
                        b -= 2
                else:
                    b = NB - 1
                    first_real = koff // 128
                    while b >= 0:
                        ms = []
                        for bb in (b, b - 1):
                            if bb == NB - 1:
                                ms.append(0)
                            elif bb == first_real and koff % 128:
                                ms.append(1)
                            elif bb < first_real:
                                ms.append(2)
                            else:
                                ms.append(None)
                        grp.append((b, b - 1, ms[0], ms[1]))
                        b -= 2
                ng = len(grp)
                for gi, (b0, b1, m0, m1) in enumerate(grp):
                    G.append(dict(h=h, si=h % NSET, q0=q0, b0=b0, b1=b1, m0=m0, m1=m1, first=(gi == 0), last=(gi == ng - 1),
                                  newhead=(qs == 0 and gi == 0)))
        NG = len(G)
        mtile = self.msk if isp else self.smsk

        def st_S(n):
            d = G[n]
            if n == 0:
                for hh in range(min(NSET, 8)):
                    load_head(hh)
            si, q0 = d["si"], d["q0"]
            S_ = k.ring("S", S)
            for i, (b, m) in enumerate(((d["b0"], d["m0"]), (d["b1"], d["m1"]))):
                pairs = [(kT[si][:, b * 128:(b + 1) * 128], qT[si][:, q0:q0 + N])]
                rd = [kT[si], qT[si]]
                if m is not None:
                    pairs.append((self.cst_b[:, 0, :], mtile[:, m, 0:N]))
                    rd += [self.cst_b, mtile]
                k.mm(S_, S_[:, i, 0:N], pairs, reads=rd)
            d["S"] = S_

        def st_exp1(n):
            d = G[n]
            e = k.ring("e", e_)
            S_ = d["S"]
            k.op("act", lambda en: en.activation(out=e[:], in_=S_[:, :, 0:N], func=AF.Exp, scale=sc), reads=[S_], writes=[e])
            d["e"] = e

        def st_ln(n):
            d = G[n]
            c = k.ring("c", c_)
            e = d["e"]
            k.op("act", lambda en: en.activation(out=c[:], in_=e[:], func=AF.Ln, bias=1.0), reads=[e], writes=[c])
            d["c"] = c

        def st_L(n):
            d = G[n]
            c = d["c"]
            k.mm(L, L[:, 0, 0:N], [(tri, c[:, 0, :])], reads=[c, self.cst_b])
            k.mm(L, L[:, 1, 0:N], [(tri, c[:, 1, :]), (ones, c[:, 0, :])], reads=[c, self.cst_b])
            if not d["first"]:
                Rt = d["R"]
                lr = k.ring("Lr", Lr)
                for i in range(2):
                    k.op("dve", lambda en, i=i: en.tensor_tensor(out=lr[:, i, :], in0=L[:, i, 0:N], in1=Rt[:], op=ALU.add), reads=[L, Rt], writes=[lr])
                d["src"], d["srct"] = lr[:], lr
            else:
                lr = k.ring("Lr", Lr)
                k.op("dve", lambda en: en.tensor_copy(out=lr[:], in_=L[:, :, 0:N]), reads=[L], writes=[lr])
                d["src"], d["srct"] = lr[:], lr
            if not d["last"]:
                k.mm(Racc, Racc[:, 0:N], [(ones, c[:, 0, :]), (ones, c[:, 1, :])], reads=[c, self.cst_b])
                Rn = k.ring("R", R)
                if d["first"]:
                    k.op("dve", lambda en: en.tensor_copy(out=Rn[:], in_=Racc[:, 0:N]), reads=[Racc], writes=[Rn])
                else:
                    Rt = d["R"]
                    k.op("dve", lambda en: en.tensor_tensor(out=Rn[:], in0=Racc[:, 0:N], in1=Rt[:], op=ALU.add), reads=[Racc, Rt], writes=[Rn])
                G[n + 1]["R"] = Rn

        def st_exp2(n):
            d = G[n]
            g = k.ring("g", g_)
            src, srct = d["src"], d["srct"]
            k.op("act", lambda en: en.activation(out=g[:], in_=src, func=AF.Exp, scale=-1.0), reads=[srct], writes=[g])
            d["g"] = g

        def st_a(n):
            d = G[n]
            a = k.ring("a", a_)
            e, g = d["e"], d["g"]
            k.op("pool", lambda en: en.tensor_tensor(out=a[:], in0=e[:], in1=g[:], op=ALU.mult), reads=[e, g], writes=[a])
            si, q0, h = d["si"], d["q0"], d["h"]
            k.mm(oacc, oacc[:, 0:N], [(Vb[si][:, d["b0"], :], a[:, 0, :]), (Vb[si][:, d["b1"], :], a[:, 1, :])], reads=[a, Vb[si]], start=d["first"], stop=d["last"])
            if d["last"]:
                y = k.ring("ysb", yst)
                k.op("dve", lambda en: en.tensor_tensor(out=y[:], in0=oacc[:, 0:N], in1=szT[si][:, q0:q0 + N], op=ALU.mult), reads=[oacc, szT[si]], writes=[y])
                k.dma("pool", YT[h, :, q0:q0 + N], y[:], reads=[y])
            for kk in ("S", "e", "c", "g", "src", "srct", "R"):
                d.pop(kk, None)
            if (n == NG - 1 or G[n + 1]["newhead"]) and d["h"] + NSET < 8:
                load_head(d["h"] + NSET)

        for t in range(-2, NG + 3):
            if 0 <= t + 2 < NG:
                st_S(t + 2)
            if 0 <= t + 1 < NG:
                st_exp1(t + 1)
            if 0 <= t < NG:
                st_ln(t)
            if 0 <= t - 1 < NG:
                st_L(t - 1)
            if 0 <= t - 2 < NG:
                st_exp2(t - 2)
            if 0 <= t - 3 < NG:
                st_a(t - 3)

    def p2_sb_sample(self, stm, li, j, st):
        k = self.k
        I, O = self.I, self.O
        T = stm["T"]
        FM, YT = stm["FM"], stm["YT"]
        N = T
        H = 8
        sc = 1.0 / math.sqrt(128.0)
        PAST = self.PAST
        NB = (PAST + T + 127) // 128
        if NB % 2:
            NB += 1
        koff = NB * 128 - (PAST + T)
        qT = k.sb("sqT", [128, H, N], BF16, st)
        szT = k.sb("sszT", [128, H, N], BF16, st)
        kT = [k.sb("skT%d" % h, [128, NB * 128], BF16, st) for h in range(H)]
        Vb = [k.sb("sVb%d" % h, [128, NB, 128], BF16, st) for h in range(H)]
        vfull = [k.sb("svfull%d" % i, [128, NB, 128], F32, st) for i in range(2)]
        S = [k.ps("sS%d" % i, [128, H, 2, N], F32, st) for i in range(2)]
        L = k.ps("sL", [128, H, 2, N], F32, st)
        Racc = k.ps("sRacc", [128, H, N], F32, st)
        oacc = k.ps("soacc", [128, H, N], F32, st)
        trs = k.ps("strs", [128, 512], F32, st)
        e_ = [k.sb("se%d" % i, [128, H, 2, N], BF16, st) for i in range(5)]
        c_ = [k.sb("sc%d" % i, [128, H, 2, N], BF16, st) for i in range(3)]
        g_ = [k.sb("sg%d" % i, [128, H, 2, N], BF16, st) for i in range(2)]
        a_ = [k.sb("sa%d" % i, [128, H, 2, N], BF16, st) for i in range(3)]
        Lr = [k.sb("sLr%d" % i, [128, H, 2, N], F32, st) for i in range(2)]
        R = [k.sb("sR%d" % i, [128, H, N], F32, st) for i in range(3)]
        yst = k.sb("sysb", [128, H, N], BF16, st)
        osb = k.sb("sosb", [128, H, N], F32, st)
        ident, tri, ones = self.cst_b[:, 0, :], self.cst_b[:, 1, :], self.cst_b[:, 2, :]
        vout = O["sbv_s"][j]
        kout = O["sbk_s"][j]
        k.dma("sp", qT[:], FM[0:8, :, :].rearrange("c p t -> p c t"), writes=[qT])
        k.dma("sp", szT[:], FM[16:24, :, :].rearrange("c p t -> p c t"), writes=[szT])
        for h in range(H):
            for (src_c, src_n, isk) in ((I["csk"][j], kout, True), (I["csv"][j], vout, False)):
                vs = k.ring("svfull", vfull)
                k.op("pool", lambda e, vs=vs: e.memset(vs[:], 0.0), writes=[vs])
                p0 = koff % 128
                bq = koff // 128
                t0 = (128 - p0) % 128
                if t0:
                    k.dma("sp", vs[p0:128, bq, :], src_c[0:t0, h, :], writes=[vs])
                    bq += 1
                nfull = (PAST - t0) // 128
                if nfull:
                    k.dma("sp", vs[:, bq:bq + nfull, :], src_c[t0:t0 + nfull * 128, h, :].rearrange("(b p) d -> p b d", p=128), writes=[vs])
                rem = PAST - t0 - nfull * 128
                if rem:
                    k.dma("sp", vs[0:rem, bq + nfull, :], src_c[t0 + nfull * 128:PAST, h, :], writes=[vs])
                k.dma("sp", vs[128 - T:128, NB - 1, :], src_n[0:T, h, :], writes=[vs])
                if not isk:
                    k.op("pool", lambda e, vs=vs, h=h: e.tensor_copy(out=Vb[h][:], in_=vs[:]), reads=[vs], writes=[Vb[h]])
                else:
                    for b0 in range(0, NB, 4):
                        nb_ = min(4, NB - b0)
                        for b in range(nb_):
                            k.tr(trs, trs[:, b * 128:(b + 1) * 128], vs[:, b0 + b, :], self.cst_f[:], reads=[vs, self.cst_f])
                        eng = k.ring("sevac", ["dve", "act"])
                        if eng == "dve":
                            k.op("dve", lambda e, b0=b0, nb_=nb_, h=h: e.tensor_copy(out=kT[h][:, b0 * 128:(b0 + nb_) * 128], in_=trs[:, 0:nb_ * 128]), reads=[trs], writes=[kT[h]])
                        else:
                            k.op("act", lambda e, b0=b0, nb_=nb_, h=h: e.activation(out=kT[h][:, b0 * 128:(b0 + nb_) * 128], in_=trs[:, 0:nb_ * 128], func=AF.Copy), reads=[trs], writes=[kT[h]])
        G = []
        b = NB - 1
        first_real = koff // 128
        while b >= 0:
            ms = []
            for bb in (b, b - 1):
                if bb == NB - 1:
                    ms.append(0)
                elif bb == first_real and koff % 128:
                    ms.append(1)
                elif bb < first_real:
                    ms.append(2)
                else:
                    ms.append(None)
            G.append(dict(b0=b, b1=b - 1, m0=ms[0], m1=ms[1]))
            b -= 2
        NG = len(G)
        for n, d in enumerate(G):
            d["first"], d["last"] = (n == 0), (n == NG - 1)
        mtile = self.smsk

        def st_S(n):
            d = G[n]
            S_ = k.ring("sS", S)
            for h in range(H):
                for i, (b, m) in enumerate(((d["b0"], d["m0"]), (d["b1"], d["m1"]))):
                    pairs = [(kT[h][:, b * 128:(b + 1) * 128], qT[:, h, :])]
                    rd = [kT[h], qT]
                    if m is not None:
                        pairs.append((ident, mtile[:, m, 0:N]))
                        rd += [self.cst_b, mtile]
                    k.mm(S_, S_[:, h, i, :], pairs, reads=rd)
            d["S"] = S_

        def st_exp1(n):
            d = G[n]
            e = k.ring("se", e_)
            S_ = d["S"]
            k.op("act", lambda en: en.activation(out=e[:], in_=S_[:], func=AF.Exp, scale=sc), reads=[S_], writes=[e])
            d["e"] = e

        def st_ln(n):
            d = G[n]
            c = k.ring("sc", c_)
            e = d["e"]
            k.op("act", lambda en: en.activation(out=c[:], in_=e[:], func=AF.Ln, bias=1.0), reads=[e], writes=[c])
            d["c"] = c

        def st_L(n):
            d = G[n]
            c = d["c"]
            for h in range(H):
                k.mm(L, L[:, h, 0, :], [(tri, c[:, h, 0, :])], reads=[c, self.cst_b])
                k.mm(L, L[:, h, 1, :], [(tri, c[:, h, 1, :]), (ones, c[:, h, 0, :])], reads=[c, self.cst_b])
            lr = k.ring("sLr", Lr)
            if not d["first"]:
                Rt = d["R"]
                for i in range(2):
                    k.op("dve", lambda en, i=i: en.tensor_tensor(out=lr[:, :, i, :], in0=L[:, :, i, :], in1=Rt[:], op=ALU.add), reads=[L, Rt], writes=[lr])
            else:
                k.op("dve", lambda en: en.tensor_copy(out=lr[:], in_=L[:]), reads=[L], writes=[lr])
            d["lr"] = lr
            if not d["last"]:
                for h in range(H):
                    k.mm(Racc, Racc[:, h, :], [(ones, c[:, h, 0, :]), (ones, c[:, h, 1, :])], reads=[c, self.cst_b])
                Rn = k.ring("sR", R)
                if d["first"]:
                    k.op("dve", lambda en: en.tensor_copy(out=Rn[:], in_=Racc[:]), reads=[Racc], writes=[Rn])
                else:
                    Rt = d["R"]
                    k.op("dve", lambda en: en.tensor_tensor(out=Rn[:], in0=Racc[:], in1=Rt[:], op=ALU.add), reads=[Racc, Rt], writes=[Rn])
                G[n + 1]["R"] = Rn

        def st_exp2(n):
            d = G[n]
            g = k.ring("sg", g_)
            lr = d["lr"]
            k.op("act", lambda en: en.activation(out=g[:], in_=lr[:], func=AF.Exp, scale=-1.0), reads=[lr], writes=[g])
            d["g"] = g

        def st_a(n):
            d = G[n]
            a = k.ring("sa", a_)
            e, g = d["e"], d["g"]
            k.op("pool", lambda en: en.tensor_tensor(out=a[:], in0=e[:], in1=g[:], op=ALU.mult), reads=[e, g], writes=[a])
            for h in range(H):
                k.mm(oacc, oacc[:, h, :], [(Vb[h][:, d["b0"], :], a[:, h, 0, :]), (Vb[h][:, d["b1"], :], a[:, h, 1, :])], reads=[a, Vb[h]])
            if d["first"]:
                k.op("dve", lambda en: en.tensor_copy(out=osb[:], in_=oacc[:]), reads=[oacc], writes=[osb])
            else:
                k.op("dve", lambda en: en.tensor_tensor(out=osb[:], in0=oacc[:], in1=osb[:], op=ALU.add), reads=[oacc, osb], writes=[osb])
            if d["last"]:
                k.op("dve", lambda en: en.tensor_tensor(out=yst[:], in0=osb[:], in1=szT[:], op=ALU.mult), reads=[osb, szT], writes=[yst])
                k.dma("pool", YT[0:8, :, :].rearrange("c p t -> p c t"), yst[:], reads=[yst])

        for t in range(-2, NG + 3):
            if 0 <= t + 2 < NG:
                st_S(t + 2)
            if 0 <= t + 1 < NG:
                st_exp1(t + 1)
            if 0 <= t < NG:
                st_ln(t)
            if 0 <= t - 1 < NG:
                st_L(t - 1)
            if 0 <= t - 2 < NG:
                st_exp2(t - 2)
            if 0 <= t - 3 < NG:
                st_a(t - 3)

    def p2_ml(self, stm, li, j, st):
        k = self.k
        I, O = self.I, self.O
        T = stm["T"]
        isp = stm["name"] == "p"
        FM, YT, TMs, GT = stm["FM"], stm["YT"], stm["TMs"], stm["GT"]
        LC = min(128, T)
        nch = T // LC
        SEG = min(2048, T)
        ig = k.sb("ml_ig", [4, T], F32, st)
        fg = k.sb("ml_fg", [4, T], F32, st)
        Bt = k.sb("ml_B", [4, T], F32, st)
        ones4 = k.sb("ml_ones", [4, SEG], F32, st)
        gcol = k.sb("ml_gcol", [4, 2], F32, st)
        minit = k.sb("ml_minit", [4, 1], F32, st)
        sel = k.sb("ml_sel", [4, 4, 128], F32, st)
        negm = k.sb("ml_negm", [128, 128], F32, st)
        ghr = k.sb("ml_ghr", [128, D], F32, st)
        skp = k.sb("ml_skip", [128, 8], F32, st)
        C = k.sb("ml_C", [128, 4, 2, 257], F32, st)
        Cb = k.sb("ml_Cb", [128, 4, 2, 257], BF16, st)
        MendB = k.sb("ml_MendB", [128, nch + 1, 4], F32, st)
        nMendB = k.sb("ml_nMendB", [128, nch + 1, 4], F32, st)
        decB = k.sb("ml_decB", [128, nch, 4], F32, st)
        k.op("pool", lambda e: e.memset(ones4[:], 1.0), writes=[ones4])
        k.dma("sp", gcol[:], I["gcol"][j], writes=[gcol])
        k.dma("sp", sel[:].rearrange("r h m -> r (h m)"), I["sel4"][:, :], writes=[sel])
        k.dma("sp", negm[:], I["negmask"][:, :], writes=[negm])
        k.dma("sp", ghr[:], I["ghead_r"][j], writes=[ghr])
        k.dma("sp", skp[:], I["skipT"][j], writes=[skp])
        if isp:
            k.op("pool", lambda e: e.memset(minit[:], 0.0), writes=[minit])
            k.op("pool", lambda e: e.memset(C[:], 0.0), writes=[C])
        else:
            k.dma("sp", minit[:], I["smm"][j].rearrange("(h o) -> h o", o=1), writes=[minit])
            for h in range(4):
                k.dma("sp", C[:, h, :, 0:256], I["smc"][j, h].rearrange("(c p) e -> p c e", p=128), writes=[C])
                k.dma("sp", C[:, h, :, 256:257], I["smn"][j, h].rearrange("(c p o) -> p c o", p=128, o=1), writes=[C], allow_slow_non_contiguous=True)
        k.op("act", lambda e: e.activation(out=Cb[:], in_=C[:], func=AF.Copy), reads=[C], writes=[Cb])
        k.dma("sp", ig[:], GT[0:4, :], writes=[ig])
        k.dma("sp", fg[:], GT[4:8, :], writes=[fg])
        k.op("dve", lambda e: e.tensor_scalar(out=ig[:], in0=ig[:], scalar1=gcol[:, 0:1], scalar2=None, op0=ALU.add), reads=[ig, gcol], writes=[ig])
        k.op("dve", lambda e: e.tensor_scalar(out=fg[:], in0=fg[:], scalar1=gcol[:, 1:2], scalar2=None, op0=ALU.add), reads=[fg, gcol], writes=[fg])
        k.op("act", lambda e: e.activation(out=fg[:], in_=fg[:], func=AF.Exp, scale=-1.0), reads=[fg], writes=[fg])
        k.op("act", lambda e: e.activation(out=fg[:], in_=fg[:], func=AF.Ln, bias=1.0), reads=[fg], writes=[fg])
        k.op("dve", lambda e: e.tensor_scalar(out=fg[:], in0=fg[:], scalar1=-1.0, scalar2=None, op0=ALU.mult), reads=[fg], writes=[fg])
        for s0 in range(0, T, SEG):
            n = min(SEG, T - s0)
            init = 0.0 if s0 == 0 else Bt[:, s0 - 1:s0]
            k.op("dve", lambda e, s0=s0, n=n, init=init: e.tensor_tensor_scan(out=Bt[:, s0:s0 + n], data0=ones4[:, 0:n], data1=fg[:, s0:s0 + n], initial=init, op0=ALU.mult, op1=ALU.add), reads=[ones4, fg, Bt], writes=[Bt])
        k.op("dve", lambda e: e.tensor_tensor(out=ig[:], in0=ig[:], in1=Bt[:], op=ALU.subtract), reads=[ig, Bt], writes=[ig])
        for s0 in range(0, T, SEG):
            n = min(SEG, T - s0)
            init = minit[:, 0:1] if s0 == 0 else fg[:, s0 - 1:s0]
            k.op("dve", lambda e, s0=s0, n=n, init=init: e.tensor_tensor_scan(out=fg[:, s0:s0 + n], data0=ones4[:, 0:n], data1=ig[:, s0:s0 + n], initial=init, op0=ALU.mult, op1=ALU.max), reads=[ones4, ig, fg, minit], writes=[fg])
        k.op("dve", lambda e: e.tensor_tensor(out=Bt[:], in0=Bt[:], in1=fg[:], op=ALU.add), reads=[Bt, fg], writes=[Bt])
        with contextlib.ExitStack() as s2:
            mps = k.ps("ml_mps", [128, 4, nch + 1], F32, s2)
            for h in range(4):
                k.mm(mps, mps[:, h, 0:1], [(sel[:, h, :], minit[:, 0:1])], reads=[sel, minit])
                k.mm(mps, mps[:, h, 1:nch + 1], [(sel[:, h, :], fg[:, LC - 1:T:LC])], reads=[sel, fg])
            k.op("dve", lambda e: e.tensor_copy(out=MendB[:], in_=mps[:].rearrange("p h c -> p c h")), reads=[mps], writes=[MendB])
            k.op("dve", lambda e: e.tensor_scalar(out=nMendB[:], in0=MendB[:], scalar1=-1.0, scalar2=None, op0=ALU.mult), reads=[MendB], writes=[nMendB])
            k.op("dve", lambda e: e.tensor_tensor(out=decB[:], in0=MendB[:, 0:nch, :], in1=MendB[:, 1:nch + 1, :], op=ALU.subtract), reads=[MendB], writes=[decB])
            k.op("act", lambda e: e.activation(out=decB[:], in_=decB[:], func=AF.Exp), reads=[decB], writes=[decB])
            k.barrier()
        qk = [k.sb("ml_qk%d" % i, [128, 16, LC], BF16, st) for i in range(2)]
        ex = [k.sb("ml_ex%d" % i, [128, 24, LC], BF16, st) for i in range(2)]
        ktm = [k.sb("ml_ktm%d" % i, [128, D], BF16, st) for i in range(2)]
        vaug = [k.sb("ml_vaug%d" % i, [128, 4, 257], BF16, st) for i in range(2)]
        for v in vaug:
            k.op("pool", lambda e, v=v: e.memset(v[:], 1.0), writes=[v])
        cols = [k.sb("ml_cols%d" % i, [128, 12], F32, st) for i in range(2)]
        sm4 = [k.sb("ml_sm4%d" % i, [128, 16], F32, st) for i in range(2)]
        negm4 = k.sb("ml_negm4", [128, 4, 128], F32, st)
        for h in range(4):
            k.op("dve", lambda e, h=h: e.tensor_copy(out=negm4[:, h, :], in_=negm[:]), reads=[negm], writes=[negm4])
        w4 = [k.sb("ml_w4%d" % i, [128, 4, 128], F32, st) for i in range(2)]
        smb4 = [k.sb("ml_sm4b%d" % i, [128, 4, 128], BF16, st) for i in range(2)]
        nbs = [k.sb("ml_nbs%d" % i, [128, 257], F32, st) for i in range(2)]
        nd4 = [k.sb("ml_nd4%d" % i, [128, 4, 257], F32, st) for i in range(2)]
        sc1 = [k.sb("ml_sc%d" % i, [128, 20], F32, st) for i in range(2)]
        junk = k.sb("ml_junk", [128, 256], BF16, st)
        hn4 = [k.sb("ml_hn4%d" % i, [128, 4, 256], BF16, st) for i in range(2)]
        gk4 = [k.sb("ml_gk4%d" % i, [128, 4, 256], BF16, st) for i in range(2)]
        y8 = [k.sb("ml_y8%d" % i, [128, 8, LC], F32, st) for i in range(2)]
        yst = [k.sb("ml_yst%d" % i, [128, 8, LC], BF16, st) for i in range(2)]
        cps = k.ps("ml_cps", [128, 12], F32, st)
        mb4 = k.ps("ml_mb", [128, 4, 128], F32, st)
        sps4 = k.ps("ml_sps", [128, 4, 128], F32, st)
        tps4 = k.ps("ml_tps", [128, 8, 128], BF16, st)
        numA = k.ps("ml_numA", [128, 257], F32, st)
        numB = k.ps("ml_numB", [128, 257], F32, st)
        cups = [k.ps("ml_cups%d" % i, [128, 257], F32, st) for i in range(2)]
        identb = self.cst_b
        def ml_loads(c):
            t0, t1 = c * LC, (c + 1) * LC
            qk_, ex_, kt_, va_ = qk[c % 2], ex[c % 2], ktm[c % 2], vaug[c % 2]
            k.dma("sp", qk_[:], FM[0:16, :, t0:t1].rearrange("c p t -> p c t"), writes=[qk_])
            k.dma("sp", ex_[:], FM[16:40, :, t0:t1].rearrange("c p t -> p c t"), writes=[ex_])
            k.dma("sp", kt_[0:LC, :], TMs[0, t0:t1, :], writes=[kt_])
            k.dma("sp", va_[0:LC, :, 0:256], TMs[1, t0:t1, :].rearrange("t (h e) -> t h e", h=4), writes=[va_])

        ml_loads(0)
        for c in range(nch):
            t0, t1 = c * LC, (c + 1) * LC
            qk_ = qk[c % 2]
            ex_ = ex[c % 2]
            kt_ = ktm[c % 2]
            va_ = vaug[c % 2]
            if c + 1 < nch:
                ml_loads(c + 1)
            co = cols[c % 2]
            for qi, src in enumerate((ig, fg, Bt)):
                k.tr(cps, cps[0:LC, qi * 4:(qi + 1) * 4], src[0:4, t0:t1], self.cst_f[0:4, 0:4], reads=[src, self.cst_f])
            k.op("dve", lambda e: e.tensor_copy(out=co[0:LC, :], in_=cps[0:LC, :]), reads=[cps], writes=[co])
            s4 = sm4[c % 2]
            k.op("dve", lambda e: e.tensor_tensor(out=s4[0:LC, 0:4], in0=MendB[0:LC, c, :], in1=co[0:LC, 4:8], op=ALU.subtract), reads=[MendB, co], writes=[s4])
            k.op("dve", lambda e: e.tensor_tensor(out=s4[0:LC, 4:8], in0=co[0:LC, 0:4], in1=nMendB[0:LC, c + 1, :], op=ALU.add), reads=[nMendB, co], writes=[s4])
            k.op("dve", lambda e: e.tensor_scalar(out=s4[0:LC, 8:12], in0=co[0:LC, 8:12], scalar1=-1.0, scalar2=None, op0=ALU.mult), reads=[co], writes=[s4])
            k.op("act", lambda e: e.activation(out=s4[0:LC, 0:12], in_=s4[0:LC, 0:12], func=AF.Exp), reads=[s4], writes=[s4])
            ys = yst[c % 2]
            w_, sm_, nd_, sc, hn_, gk_, y_ = w4[c % 2], smb4[c % 2], nd4[c % 2], sc1[c % 2], hn4[c % 2], gk4[c % 2], y8[c % 2]
            for h in range(4):
                k.mm(mb4, mb4[:, h, 0:LC], [(sel[:, h, :], fg[0:4, t0:t1])], reads=[sel, fg])
            k.op("dve", lambda e: e.tensor_tensor(out=w_[0:LC, :, 0:LC], in0=negm4[0:LC, :, 0:LC], in1=mb4[0:LC, :, 0:LC], op=ALU.subtract), reads=[negm4, mb4], writes=[w_])
            for h in range(4):
                k.op("act", lambda e, h=h: e.activation(out=w_[0:LC, h, 0:LC], in_=w_[0:LC, h, 0:LC], func=AF.Exp, bias=co[0:LC, h:h + 1]), reads=[w_, co], writes=[w_])
            for h in range(4):
                k.mm(sps4, sps4[0:LC, h, 0:LC], [(qk_[:, 8 + 2 * h + ec, :], qk_[:, 2 * h + ec, :]) for ec in range(2)], reads=[qk_])
            k.op("dve", lambda e: e.tensor_tensor(out=sm_[0:LC, :, 0:LC], in0=sps4[0:LC, :, 0:LC], in1=w_[0:LC, :, 0:LC], op=ALU.mult), reads=[sps4, w_], writes=[sm_])
            for h in range(4):
                k.op("act", lambda e, h=h: e.activation(out=gk_[0:LC, h, :], in_=kt_[0:LC, h * 256:(h + 1) * 256], func=AF.Copy, scale=s4[0:LC, 4 + h:5 + h]), reads=[kt_, s4], writes=[gk_])
            for h in range(4):
                k.mm(numA, numA[0:LC, :], [(sm_[0:LC, h, 0:LC], va_[0:LC, h, :])], reads=[sm_, va_])
                k.mm(numB, numB[0:LC, :], [(qk_[:, 2 * h + dc, :], Cb[:, h, dc, :]) for dc in range(2)], reads=[qk_, Cb])
                nb_ = k.ring("ml_nbs", nbs)
                k.op("act", lambda e, nb_=nb_, h=h: e.activation(out=nb_[0:LC, :], in_=numB[0:LC, :], func=AF.Copy, scale=s4[0:LC, h:h + 1]), reads=[numB, s4], writes=[nb_])
                k.op("dve", lambda e, nb_=nb_, h=h: e.tensor_tensor(out=nd_[0:LC, h, :], in0=numA[0:LC, :], in1=nb_[0:LC, :], op=ALU.add), reads=[numA, nb_], writes=[nd_])
            k.op("pool", lambda e: e.memset(sc[:], 0.0), writes=[sc])
            k.op("act", lambda e: e.activation(out=sc[0:LC, 0:4], in_=nd_[0:LC, :, 256], func=AF.Abs), reads=[nd_, sc], writes=[sc])
            k.op("dve", lambda e: e.tensor_tensor(out=sc[0:LC, 0:4], in0=sc[0:LC, 0:4], in1=s4[0:LC, 8:12], op=ALU.max), reads=[sc, s4], writes=[sc])
            k.op("dve", lambda e: e.reciprocal(out=sc[0:LC, 4:8], in_=sc[0:LC, 0:4]), reads=[sc], writes=[sc])
            for h in range(4):
                k.op("act", lambda e, h=h: e.activation(out=junk[0:LC, :], in_=nd_[0:LC, h, 0:256], func=AF.Square, scale=sc[0:LC, 4 + h:5 + h], accum_out=sc[0:LC, 8 + h:9 + h]), reads=[nd_, sc], writes=[junk, sc])
            k.op("act", lambda e: e.activation(out=sc[0:LC, 12:16], in_=sc[0:LC, 8:12], func=AF.Ln, scale=1.0 / 256.0, bias=EPS), reads=[sc], writes=[sc])
            k.op("act", lambda e: e.activation(out=sc[0:LC, 12:16], in_=sc[0:LC, 12:16], func=AF.Exp, scale=-0.5), reads=[sc], writes=[sc])
            k.op("dve", lambda e: e.tensor_tensor(out=sc[0:LC, 16:20], in0=sc[0:LC, 12:16], in1=sc[0:LC, 4:8], op=ALU.mult), reads=[sc], writes=[sc])
            for h in range(4):
                k.op("dve", lambda e, h=h: e.scalar_tensor_tensor(out=hn_[0:LC, h, :], in0=nd_[0:LC, h, 0:256], scalar=sc[0:LC, 16 + h:17 + h], in1=ghr[0:LC, h * 256:(h + 1) * 256], op0=ALU.mult, op1=ALU.mult), reads=[nd_, sc, ghr], writes=[hn_])
            for h in range(4):
                for dc in range(2):
                    cu = cups[dc]
                    k.mm(cu, cu[:, :], [(gk_[0:LC, h, dc * 128:(dc + 1) * 128], va_[0:LC, h, :])], reads=[gk_, va_])
                    k.op("dve", lambda e, dc=dc, cu=cu, h=h: e.scalar_tensor_tensor(out=C[:, h, dc, :], in0=C[:, h, dc, :], scalar=decB[:, c, h:h + 1], in1=cu[:, :], op0=ALU.mult, op1=ALU.add), reads=[C, decB, cu], writes=[C])
            k.op("act", lambda e: e.activation(out=Cb[:], in_=C[:], func=AF.Copy), reads=[C], writes=[Cb])
            for h in range(4):
                for ec in range(2):
                    k.tr(tps4, tps4[:, 2 * h + ec, 0:LC], hn_[0:LC, h, ec * 128:(ec + 1) * 128], identb[0:LC, 0, 0:LC], reads=[hn_, identb])
            k.op("dve", lambda e: e.tensor_tensor(out=y_[:], in0=tps4[:, :, 0:LC], in1=ex_[:, 8:16, :], op=ALU.mult), reads=[tps4, ex_], writes=[y_])
            k.op("pool", lambda e: e.tensor_tensor(out=y_[:], in0=y_[:], in1=ex_[:, 0:8, :], op=ALU.add), reads=[y_, ex_], writes=[y_])
            k.op("pool", lambda e: e.tensor_tensor(out=ys[:], in0=y_[:], in1=ex_[:, 16:24, :], op=ALU.mult), reads=[y_, ex_], writes=[ys])
            k.dma("pool", YT[0:8, :, t0:t1].rearrange("c p t -> p c t"), ys[:], reads=[ys])
        co_, no_, mo_ = (O["mlc_p"], O["mln_p"], O["mlm_p"]) if isp else (O["mlc_s"], O["mln_s"], O["mlm_s"])
        for h in range(4):
            k.dma("pool", co_[j, h].rearrange("(c p) e -> p c e", p=128), C[:, h, :, 0:256], reads=[C])
            k.dma("pool", no_[j, h].rearrange("(c p o) -> p c o", p=128, o=1), C[:, h, :, 256:257], reads=[C], allow_slow_non_contiguous=True)
        k.dma("pool", mo_[j].rearrange("(h o) -> h o", o=1), Bt[:, T - 1:T], reads=[Bt])


def make_consts():
    c = np.zeros((128, 128 * 5 + 2048), np.float32)
    idx = np.arange(128)
    c[:, 0:128] = np.eye(128)
    c[:, 128:256] = (idx[:, None] >= idx[None, :])
    c[:, 256:384] = 1.0
    c[:, 384:512] = (idx[:, None] <= idx[None, :])
    q = np.arange(512)
    for jj in range(4):
        c[:, 640 + jj * 512: 640 + (jj + 1) * 512] = np.where((128 * jj + idx[:, None]) < q[None, :], 0.0, -30000.0)
    return c


def make_smask(PAST, TS, koff):
    m = np.zeros((128, 3, 32), np.float32)
    idx = np.arange(128)
    q = np.arange(32)
    nb = (koff + PAST + TS) // 128
    key = (nb - 1) * 128 + idx - koff
    m[:, 0, :TS] = np.where((key[:, None] < PAST) | ((key[:, None] - PAST) < q[None, :TS]), 0.0, -30000.0)
    fr = koff // 128
    key = fr * 128 + idx - koff
    m[:, 1, :TS] = np.where(key[:, None] >= 0, 0.0, -30000.0)
    m[:, 2, :] = -30000.0
    return m.reshape(128, 96)


_CACHE = {}


def _prep(inp):
    x_prompt = np.asarray(inp["x_prompt"], np.float32)
    x_sample = np.asarray(inp["x_sample"], np.float32)
    B, T, _ = x_prompt.shape
    SBN, TS, _ = x_sample.shape
    DEPTH = inp["g_pre"].shape[0]
    PAST = inp["cache_sb_k"].shape[2]
    key = (T, TS, PAST, DEPTH)
    if key not in _CACHE:
        p = Prog(T, TS, PAST, DEPTH)
        p.build()
        _CACHE[key] = p
    p = _CACHE[key]
    NSB, NML, NCV = p.NSB, p.NML, p.NCV
    f = lambda a: np.ascontiguousarray(np.asarray(a, np.float32))

    def rep(a):
        a = f(a)
        return np.ascontiguousarray(np.broadcast_to(a[:, None, :], (a.shape[0], 128, a.shape[1])))

    def colT(a):
        a = f(a)
        return np.ascontiguousarray(a.reshape(a.shape[0], 8, 128).transpose(0, 2, 1))

    def convT(a):
        a = f(a)
        return np.ascontiguousarray(a.reshape(a.shape[0], a.shape[1], 8, 128).transpose(0, 3, 2, 1))

    NBs = (PAST + TS + 127) // 128
    if NBs % 2:
        NBs += 1
    koff = NBs * 128 - (PAST + TS)
    consts = make_consts()
    smask = make_smask(PAST, TS, koff)
    nml = max(NML, 1)
    ncv = max(NCV, 1)

    def orz(a, shape):
        a = f(a)
        if a.shape[0] == 0:
            return np.zeros(shape, np.float32)
        return a

    convml = np.concatenate([f(inp["conv_ml_w"]), f(inp["conv_ml_b"])[:, None, :]], axis=1) if NML else np.zeros((1, 5, D), np.float32)
    gcol = np.ascontiguousarray(np.stack([f(inp["b_ig_ml"]), f(inp["b_fg_ml"])], axis=2)) if NML else np.zeros((1, 4, 2), np.float32)
    ii = np.arange(128)
    negmask = np.where(ii[:, None] <= ii[None, :], 0.0, -30000.0).astype(np.float32)
    sel4 = np.zeros((4, 4, 128), np.float32)
    for hh in range(4):
        sel4[hh, hh, :] = 1.0
    sel4 = sel4.reshape(4, 512)
    common = {
        "memp": None, "gpre_r": rep(inp["g_pre"]), "gpost_r": rep(inp["g_post"]), "gmem_r": rep(inp["g_mem"]),
        "wmemkv": f(inp["w_mem_kv"]), "wout": f(inp["w_out"]), "winsb": f(inp["w_in_sb"]),
        "winml": orz(inp["w_in_ml"], (1, D, 5128)), "wincv": orz(inp["w_in_cv"], (1, D, 5120)),
        "convmlT": convT(convml), "wq": orz(inp["wq_ml"], (1, 4, 256, 256)), "wk": orz(inp["wk_ml"], (1, 4, 256, 256)),
        "gcol": gcol, "ghead_r": rep(orz(inp["g_head_ml"], (1, D))), "negmask": negmask, "sel4": sel4, "skipT": colT(orz(inp["skip_ml"], (1, D))),
        "convcvT": convT(orz(inp["conv_cv_w"], (1, 3, D))), "consts": consts, "smask": smask,
    }
    in_maps = []
    ncores = 8
    for c in range(ncores):
        bp = c % B
        bs = c % SBN
        m = dict(common)
        m["xp"] = f(x_prompt[bp])
        m["xs"] = f(x_sample[bs])
        m["csk"] = f(inp["cache_sb_k"][:, bs])
        m["csv"] = f(inp["cache_sb_v"][:, bs])
        m["smc"] = orz(np.asarray(inp["state_ml_c"])[:, bs], (1, 4, 256, 256))
        m["smn"] = orz(np.asarray(inp["state_ml_n"])[:, bs], (1, 4, 256))
        m["smm"] = orz(np.asarray(inp["state_ml_m"])[:, bs], (1, 4))
        m["smconvT"] = convT(orz(np.asarray(inp["state_ml_conv"])[:, bs], (1, 3, D)))
        m["scvT"] = convT(orz(np.asarray(inp["state_cv_conv"])[:, bs], (1, 2, D)))
        m["cmk"] = f(inp["cache_mem_k"][:, bs])
        m["cmv"] = f(inp["cache_mem_v"][:, bs])
        m["memp"] = f(inp["mem_prompt"][bp])
        in_maps.append(m)
    return p, in_maps, B, SBN


def kernel(**inp):
    p, in_maps, B, SBN = _prep(inp)
    NML, NCV = p.NML, p.NCV
    res = run_bass_kernel_spmd(p.nc, in_maps, core_ids=list(range(len(in_maps))))
    R = res.results

    def gp(name, axis_b):
        return np.stack([np.asarray(R[c][name], np.float32) for c in range(B)], axis=axis_b)

    def gs(name, axis_b):
        return np.stack([np.asarray(R[c][name], np.float32) for c in range(SBN)], axis=axis_b)

    outs = (
        gp("yp", 0), gs("ys", 0),
        gp("sbk_p", 1), gp("sbv_p", 1),
        gp("mlc_p", 1)[:NML], gp("mln_p", 1)[:NML], gp("mlm_p", 1)[:NML], gp("mlconv_p", 1)[:NML],
        gp("cv_p", 1)[:NCV],
        gp("memk_p", 1), gp("memv_p", 1),
        gs("sbk_s", 1), gs("sbv_s", 1),
        gs("mlc_s", 1)[:NML], gs("mln_s", 1)[:NML], gs("mlm_s", 1)[:NML], gs("mlconv_s", 1)[:NML],
        gs("cv_s", 1)[:NCV],
    )
    return outs
```

```python
import contextlib
import math
import numpy as np
import concourse.bass as bass
import concourse.mybir as mybir
from concourse.bass_utils import run_bass_kernel_spmd

F32 = mybir.dt.float32
BF16 = mybir.dt.bfloat16
AF = mybir.ActivationFunctionType
ALU = mybir.AluOpType
AX = mybir.AxisListType

NDS = 48
D = 1024
EPS = 1e-6


class Tile:
    __slots__ = ("ap", "w", "r", "name")

    def __init__(self, ap, name=""):
        self.ap = ap
        self.w = None
        self.r = {}
        self.name = name

    def __getitem__(self, idx):
        return self.ap[idx]


class KB:
    def __init__(self, nc, stack):
        self.nc = nc
        self.stack = stack
        self.E = {"pe": nc.tensor, "act": nc.scalar, "dve": nc.vector, "pool": nc.gpsimd, "sp": nc.sync}
        self.csem = {e: stack.enter_context(nc.semaphore("c_" + e)) for e in ("pe", "act", "dve", "pool")}
        self.ccnt = {e: 0 for e in self.csem}
        self.dsem = [stack.enter_context(nc.semaphore("d%d" % i)) for i in range(NDS)]
        self.dcnt = [0] * NDS
        self.dnext = {"sp": 0, "pool": NDS // 2}
        self.seen = {e: {} for e in self.E}
        self.ninst = 0
        self.rr = {}

    def _wait(self, eng, tok):
        sem, val, key, kind = tok
        if self.seen[eng].get(key, 0) >= val:
            return
        self.E[eng].wait_ge(sem, val)
        self.seen[eng][key] = val
        self.ninst += 1

    def _deps(self, eng, reads, writes):
        toks = []
        for t in reads:
            if t.w is not None:
                toks.append(t.w)
        for t in writes:
            if t.w is not None:
                toks.append(t.w)
            toks.extend(t.r.values())
        for tok in toks:
            if eng == "pe" and tok[3] == "pe":
                continue
            self._wait(eng, tok)

    def _mark(self, tok, reads, writes):
        for t in reads:
            old = t.r.get(tok[2])
            if old is None or old[1] < tok[1]:
                t.r[tok[2]] = tok
        for t in writes:
            t.w = tok
            t.r = {}

    def op(self, eng, fn, reads=(), writes=()):
        self._deps(eng, reads, writes)
        ins = fn(self.E[eng])
        self.ccnt[eng] += 1
        ins.then_inc(self.csem[eng], 1)
        tok = (self.csem[eng], self.ccnt[eng], eng, eng)
        self._mark(tok, reads, writes)
        self.ninst += 1
        return ins

    def mm(self, out_tile, out_ap, pairs, reads, start=True, stop=True):
        writes = [out_tile]
        self._deps("pe", reads, writes)
        n = len(pairs)
        ins = None
        for i, (l, r) in enumerate(pairs):
            ins = self.nc.tensor.matmul(out_ap, l, r, start=(start and i == 0), stop=(stop and i == n - 1))
            self.ninst += 1
        self.ccnt["pe"] += 1
        ins.then_inc(self.csem["pe"], 1)
        tok = (self.csem["pe"], self.ccnt["pe"], "pe", "pe")
        self._mark(tok, reads, writes)
        return ins

    def tr(self, out_tile, out_ap, in_ap, ident_ap, reads):
        self._deps("pe", reads, [out_tile])
        ins = self.nc.tensor.transpose(out_ap, in_ap, ident_ap)
        self.ccnt["pe"] += 1
        ins.then_inc(self.csem["pe"], 1)
        tok = (self.csem["pe"], self.ccnt["pe"], "pe", "pe")
        self._mark(tok, reads, [out_tile])
        self.ninst += 1
        return ins

    def dma(self, q, out_ap, in_ap, reads=(), writes=(), **kw):
        self._deps(q, reads, writes)
        i = self.dnext[q]
        base = 0 if q == "sp" else NDS // 2
        self.dnext[q] = base + (i - base + 1) % (NDS // 2)
        if self.dcnt[i] > 0:
            self._wait(q, (self.dsem[i], 16 * self.dcnt[i], ("d", i), "dma"))
        self.E[q].dma_start(out=out_ap, in_=in_ap, **kw).then_inc(self.dsem[i], 16)
        self.dcnt[i] += 1
        tok = (self.dsem[i], 16 * self.dcnt[i], ("d", i), "dma")
        self._mark(tok, reads, writes)
        self.ninst += 1

    def barrier(self, engines=("pe", "act", "dve", "pool", "sp")):
        for e in engines:
            for c in self.csem:
                if self.ccnt[c] > 0:
                    self._wait(e, (self.csem[c], self.ccnt[c], c, c))
            for i in range(NDS):
                if self.dcnt[i] > 0:
                    self._wait(e, (self.dsem[i], 16 * self.dcnt[i], ("d", i), "dma"))

    def sb(self, name, shape, dtype, stack=None):
        self.uid = getattr(self, "uid", 0) + 1
        name = "%s_u%d" % (name, self.uid)
        t = (stack or self.stack).enter_context(self.nc.sbuf_tensor(name, list(shape), dtype))
        return Tile(t, name)

    def ps(self, name, shape, dtype=F32, stack=None):
        self.uid = getattr(self, "uid", 0) + 1
        name = "%s_u%d" % (name, self.uid)
        t = (stack or self.stack).enter_context(self.nc.psum_tensor(name, list(shape), dtype))
        return Tile(t, name)

    def ring(self, key, tiles):
        i = self.rr.get(key, 0)
        self.rr[key] = i + 1
        return tiles[i % len(tiles)]


class Prog:
    def __init__(self, T, TS, PAST, DEPTH):
        self.T, self.TS, self.PAST, self.DEPTH = T, TS, PAST, DEPTH
        self.NSB, self.NML, self.NCV = (DEPTH + 2) // 3, (DEPTH + 1) // 3, DEPTH // 3
        self.nc = bass.Bass("TRN2", target_bir_lowering=False)
        self.in_shapes = {}
        self.out_shapes = {}

    def din(self, name, shape, dt=F32):
        self.in_shapes[name] = tuple(shape)
        return self.nc.dram_tensor(name, list(shape), dt, kind="ExternalInput").ap()

    def dout(self, name, shape, dt=F32):
        self.out_shapes[name] = tuple(shape)
        return self.nc.dram_tensor(name, list(shape), dt, kind="ExternalOutput").ap()

    def dscr(self, name, shape, dt=F32):
        return self.nc.dram_tensor(name, list(shape), dt, kind="Internal").ap()

    def build(self):
        T, TS, PAST, DEPTH = self.T, self.TS, self.PAST, self.DEPTH
        NSB, NML, NCV = self.NSB, self.NML, self.NCV
        I = {}
        I["xp"] = self.din("xp", [T, D])
        I["xs"] = self.din("xs", [TS, D])
        I["csk"] = self.din("csk", [NSB, PAST, 8, 128])
        I["csv"] = self.din("csv", [NSB, PAST, 8, 128])
        I["smc"] = self.din("smc", [max(NML, 1), 4, 256, 256])
        I["smn"] = self.din("smn", [max(NML, 1), 4, 256])
        I["smm"] = self.din("smm", [max(NML, 1), 4])
        I["smconvT"] = self.din("smconvT", [max(NML, 1), 128, 8, 3])
        I["scvT"] = self.din("scvT", [max(NCV, 1), 128, 8, 2])
        I["cmk"] = self.din("cmk", [DEPTH, 256, 4, 128])
        I["cmv"] = self.din("cmv", [DEPTH, 256, 4, 128])
        I["memp"] = self.din("memp", [256, D])
        I["gpre_r"] = self.din("gpre_r", [DEPTH, 128, D])
        I["gpost_r"] = self.din("gpost_r", [DEPTH, 128, D])
        I["gmem_r"] = self.din("gmem_r", [DEPTH, 128, D])
        I["wmemkv"] = self.din("wmemkv", [DEPTH, D, D])
        I["wout"] = self.din("wout", [DEPTH, 1536, D])
        I["winsb"] = self.din("winsb", [NSB, D, 5120])
        I["winml"] = self.din("winml", [max(NML, 1), D, 5128])
        I["wincv"] = self.din("wincv", [max(NCV, 1), D, 5120])
        I["convmlT"] = self.din("convmlT", [max(NML, 1), 128, 8, 5])
        I["wq"] = self.din("wq", [max(NML, 1), 4, 256, 256])
        I["wk"] = self.din("wk", [max(NML, 1), 4, 256, 256])
        I["gcol"] = self.din("gcol", [max(NML, 1), 4, 2])
        I["ghead_r"] = self.din("ghead_r", [max(NML, 1), 128, D])
        I["negmask"] = self.din("negmask", [128, 128])
        I["sel4"] = self.din("sel4", [4, 512])
        I["skipT"] = self.din("skipT", [max(NML, 1), 128, 8])
        I["convcvT"] = self.din("convcvT", [max(NCV, 1), 128, 8, 3])
        I["consts"] = self.din("consts", [128, 128 * 5 + 4 * 512 * 1])
        I["smask"] = self.din("smask", [128, 3 * 32])
        O = {}
        O["yp"] = self.dout("yp", [T, D])
        O["ys"] = self.dout("ys", [TS, D])
        O["sbk_p"] = self.dout("sbk_p", [NSB, T, 8, 128])
        O["sbv_p"] = self.dout("sbv_p", [NSB, T, 8, 128])
        O["mlc_p"] = self.dout("mlc_p", [max(NML, 1), 4, 256, 256])
        O["mln_p"] = self.dout("mln_p", [max(NML, 1), 4, 256])
        O["mlm_p"] = self.dout("mlm_p", [max(NML, 1), 4])
        O["mlconv_p"] = self.dout("mlconv_p", [max(NML, 1), 3, D])
        O["cv_p"] = self.dout("cv_p", [max(NCV, 1), 2, D])
        O["memk_p"] = self.dout("memk_p", [DEPTH, 256, 4, 128])
        O["memv_p"] = self.dout("memv_p", [DEPTH, 256, 4, 128])
        O["sbk_s"] = self.dout("sbk_s", [NSB, TS, 8, 128])
        O["sbv_s"] = self.dout("sbv_s", [NSB, TS, 8, 128])
        O["mlc_s"] = self.dout("mlc_s", [max(NML, 1), 4, 256, 256])
        O["mln_s"] = self.dout("mln_s", [max(NML, 1), 4, 256])
        O["mlm_s"] = self.dout("mlm_s", [max(NML, 1), 4])
        O["mlconv_s"] = self.dout("mlconv_s", [max(NML, 1), 3, D])
        O["cv_s"] = self.dout("cv_s", [max(NCV, 1), 2, D])
        self.I, self.O = I, O
        self.SP = dict(name="p", T=T, xin=I["xp"], yout=O["yp"], xres=self.dscr("xres_p", [T, D]),
                       FM=self.dscr("fm_p", [40, 128, T], BF16), YT=self.dscr("yt_p", [12, 128, T], BF16),
                       TMs=self.dscr("tm_p", [2, T, D], BF16), GT=self.dscr("gt_p", [8, T], F32))
        self.SS = dict(name="s", T=TS, xin=I["xs"], yout=O["ys"], xres=self.dscr("xres_s", [TS, D]),
                       FM=self.dscr("fm_s", [40, 128, TS], BF16), YT=self.dscr("yt_s", [12, 128, TS], BF16),
                       TMs=self.dscr("tm_s", [2, TS, D], BF16), GT=self.dscr("gt_s", [8, TS], F32))
        with contextlib.ExitStack() as st:
            k = KB(self.nc, st)
            self.k = k
            self.cst_f = k.sb("cst_f", [128, 128], F32)
            self.cst_b = k.sb("cst_b", [128, 4, 128], BF16)
            self.msk = k.sb("msk", [128, 4, 512], BF16)
            self.smsk = k.sb("smsk", [128, 3, 32], BF16)
            with contextlib.ExitStack() as s2:
                tmp = k.sb("cst_tmp", [128, 128 * 5 + 2048], F32, s2)
                tmp2 = k.sb("cst_tmp2", [128, 96], F32, s2)
                k.dma("sp", tmp[:], I["consts"][:, :], writes=[tmp])
                k.dma("sp", tmp2[:], I["smask"][:, :], writes=[tmp2])
                k.op("dve", lambda e: e.tensor_copy(out=self.cst_f[:], in_=tmp[:, 0:128]), reads=[tmp], writes=[self.cst_f])
                k.op("dve", lambda e: e.tensor_copy(out=self.cst_b[:].rearrange("p a b -> p (a b)"), in_=tmp[:, 0:512]), reads=[tmp], writes=[self.cst_b])
                k.op("dve", lambda e: e.tensor_copy(out=self.msk[:].rearrange("p a b -> p (a b)"), in_=tmp[:, 640:640 + 2048]), reads=[tmp], writes=[self.msk])
                k.op("dve", lambda e: e.tensor_copy(out=self.smsk[:].rearrange("p a b -> p (a b)"), in_=tmp2[:]), reads=[tmp2], writes=[self.smsk])
                k.barrier()
            for li in range(DEPTH):
                kind, j = li % 3, li // 3
                self.layer(li, kind, j)
            k.barrier()
        return self.nc

    def load_w_gen(self, k, st, dst, src, ncols, nkc, name, CP=256):
        stg = [k.sb("%s_stg%d" % (name, i), [128, nkc, CP], F32, st) for i in range(2)]
        srcv = src.rearrange("(c p) n -> p c n", p=128)
        c0 = 0
        i = 0
        while c0 < ncols:
            cw = min(CP, ncols - c0)
            s = stg[i % 2]
            k.dma("sp", s[:, :, 0:cw], srcv[:, :, c0:c0 + cw], writes=[s])
            eng = "pool" if i % 2 == 0 else "act"
            if eng == "pool":
                k.op("pool", lambda e, s=s, c0=c0, cw=cw: e.tensor_copy(out=dst[:, :, c0:c0 + cw], in_=s[:, :, 0:cw]), reads=[s], writes=[dst])
            else:
                k.op("act", lambda e, s=s, c0=c0, cw=cw: e.activation(out=dst[:, :, c0:c0 + cw], in_=s[:, :, 0:cw], func=AF.Copy), reads=[s], writes=[dst])
            c0 += cw
            i += 1
            yield

    def load_w(self, k, st, dst, src, ncols, nkc, name, CP=256):
        for _ in self.load_w_gen(k, st, dst, src, ncols, nkc, name, CP):
            pass

    def layer(self, li, kind, j):
        k = self.k
        I, O = self.I, self.O
        last = (li == self.DEPTH - 1)
        with contextlib.ExitStack() as st:
            mk = {"p": k.sb("mkT_p", [128, 4, 256], BF16, st), "s": k.sb("mkT_s", [128, 4, 256], BF16, st)}
            mv = {"p": k.sb("mv_p", [128, 2, 512], BF16, st), "s": k.sb("mv_s", [128, 2, 512], BF16, st)}
            self.memkv_phase(li, mk, mv)
            k.barrier()
            with contextlib.ExitStack() as s1:
                ncols = 5128 if kind == 1 else 5120
                if getattr(self, "wb_pref", None) is not None:
                    Wb = self.wb_pref
                else:
                    Wb = k.sb("Wb", [128, 8, ncols], BF16, s1)
                    wsrc = (I["winsb"], I["winml"], I["wincv"])[kind][j]
                    with contextlib.ExitStack() as sw:
                        self.load_w(k, sw, Wb, wsrc, ncols, 8, "win")
                        k.barrier()
                for stm in (self.SS, self.SP):
                    with contextlib.ExitStack() as s3:
                        self.p1(stm, li, kind, j, Wb, mk[stm["name"]], mv[stm["name"]], s3)
                        k.barrier()
        if getattr(self, "wb_pref", None) is not None:
            self.wb_stack.close()
            self.wb_pref = None
        if kind == 0:
            for stm in (self.SS, self.SP):
                with contextlib.ExitStack() as s3:
                    if stm is self.SS:
                        self.p2_sb_sample(stm, li, j, s3)
                    else:
                        self.p2_sb(stm, li, j, s3)
                    k.barrier()
        elif kind == 1:
            for stm in (self.SS, self.SP):
                with contextlib.ExitStack() as s3:
                    self.p2_ml(stm, li, j, s3)
                    k.barrier()
        bg = None
        if not last:
            nkind, nj = (li + 1) % 3, (li + 1) // 3
            nncols = 5128 if nkind == 1 else 5120
            self.wb_stack = contextlib.ExitStack()
            self.wb_pref = k.sb("Wbn", [128, 8, nncols], BF16, self.wb_stack)
            self.wb_stg_stack = contextlib.ExitStack()
            nsrc = (I["winsb"], I["winml"], I["wincv"])[nkind][nj]
            bg = self.load_w_gen(k, self.wb_stg_stack, self.wb_pref, nsrc, nncols, 8, "winn")
            next(bg, None)
        with contextlib.ExitStack() as s1:
            Wo = k.sb("Wo", [128, 12, D], BF16, s1)
            gpo = k.sb("gpo", [128, D], F32, s1)
            with contextlib.ExitStack() as sw:
                self.load_w(k, sw, Wo, I["wout"][li], D, 12, "wout")
                k.dma("sp", gpo[:], I["gpost_r"][li], writes=[gpo])
                k.barrier()
            for stm in (self.SS, self.SP):
                with contextlib.ExitStack() as s3:
                    self.p3(stm, li, Wo, gpo, last, s3, bg if stm is self.SP else None)
                    if stm is self.SP and bg is not None:
                        for _ in bg:
                            pass
                    k.barrier()
        if bg is not None:
            self.wb_stg_stack.close()

    def rms_front(self, k, xsrc_ap, np_, xt, junk, ss, hb, grep):
        k.dma("sp", xt[0:np_, :], xsrc_ap, writes=[xt])
        k.op("pool", lambda e: e.memset(ss[:], 0.0), writes=[ss])
        k.op("act", lambda e: e.activation(out=junk[0:np_, :], in_=xt[0:np_, :], func=AF.Square, accum_out=ss[0:np_, 0:1]), reads=[xt, ss], writes=[junk, ss])
        k.op("act", lambda e: e.activation(out=ss[0:np_, 1:2], in_=ss[0:np_, 0:1], func=AF.Ln, scale=1.0 / D, bias=EPS), reads=[ss], writes=[ss])
        k.op("act", lambda e: e.activation(out=ss[0:np_, 1:2], in_=ss[0:np_, 1:2], func=AF.Exp, scale=-0.5), reads=[ss], writes=[ss])
        k.op("dve", lambda e: e.scalar_tensor_tensor(out=hb[0:np_, :], in0=xt[0:np_, :], scalar=ss[0:np_, 1:2], in1=grep[0:np_, :], op0=ALU.mult, op1=ALU.mult), reads=[xt, ss, grep], writes=[hb])

    def to_fm(self, k, hb, np_, ptr, hT, col0):
        for kc in range(8):
            k.tr(ptr, ptr[:, kc, 0:np_], hb[0:np_, kc * 128:(kc + 1) * 128], self.cst_b[0:np_, 0, 0:np_], reads=[hb, self.cst_b])
        k.op("dve", lambda e: e.tensor_copy(out=hT[:, :, col0:col0 + np_], in_=ptr[:, :, 0:np_]), reads=[ptr], writes=[hT])

    def memkv_phase(self, li, mk, mv):
        k = self.k
        I, O = self.I, self.O
        with contextlib.ExitStack() as st:
            Wm = k.sb("Wm", [128, 8, D], BF16, st)
            gm = k.sb("gm", [128, D], F32, st)
            with contextlib.ExitStack() as sw:
                self.load_w(k, sw, Wm, I["wmemkv"][li], D, 8, "wmem")
                k.barrier()
            k.dma("sp", gm[:], I["gmem_r"][li], writes=[gm])
            xt = [k.sb("mxt%d" % i, [128, D], F32, st) for i in range(2)]
            junk = k.sb("mjunk", [128, D], BF16, st)
            ss = [k.sb("mss%d" % i, [128, 2], F32, st) for i in range(2)]
            hb = [k.sb("mhb%d" % i, [128, D], BF16, st) for i in range(2)]
            hT = k.sb("mhT", [128, 8, 256], BF16, st)
            ptr = k.ps("mptr", [128, 8, 128], BF16, st)
            acc = [k.ps("macc%d" % i, [128, 512], F32, st) for i in range(3)]
            og = [k.sb("mog%d" % i, [128, 512], F32, st) for i in range(2)]
            for s in range(2):
                self.rms_front(k, I["memp"][s * 128:(s + 1) * 128, :], 128, xt[s], junk, ss[s], hb[s], gm)
                self.to_fm(k, hb[s], 128, ptr, hT, s * 128)
            for s in range(2):
                for nb in range(2):
                    a = k.ring("macc", acc)
                    k.mm(a, a[:, :], [(hT[:, kc, s * 128:(s + 1) * 128], Wm[:, kc, nb * 512:(nb + 1) * 512]) for kc in range(8)], reads=[hT, Wm])
                    o = k.ring("mog", og)
                    k.op("act", lambda e, o=o, a=a: e.activation(out=o[:], in_=a[:], func=AF.Copy), reads=[a], writes=[o])
                    dst = (O["memk_p"], O["memv_p"])[nb][li, s * 128:(s + 1) * 128].rearrange("m h d -> m (h d)")
                    k.dma("pool", dst, o[:], reads=[o])
                    if nb == 1:
                        k.op("dve", lambda e, o=o, s=s: e.tensor_copy(out=mv["p"][:, s, :], in_=o[:]), reads=[o], writes=[mv["p"]])
            for h in range(4):
                a = k.ring("macc", acc)
                k.mm(a, a[:, 0:256], [(Wm[:, kc, h * 128:(h + 1) * 128], hT[:, kc, :]) for kc in range(8)], reads=[hT, Wm])
                k.op("dve", lambda e, a=a, h=h: e.tensor_copy(out=mk["p"][:, h, :], in_=a[:, 0:256]), reads=[a], writes=[mk["p"]])
            ck = k.sb("mck", [128, 2, 512], F32, st)
            cv = k.sb("mcv", [128, 2, 512], F32, st)
            k.dma("sp", ck[:], I["cmk"][li].rearrange("(s p) h d -> p s (h d)", p=128), writes=[ck])
            k.dma("sp", cv[:], I["cmv"][li].rearrange("(s p) h d -> p s (h d)", p=128), writes=[cv])
            k.op("dve", lambda e: e.tensor_copy(out=mv["s"][:], in_=cv[:]), reads=[cv], writes=[mv["s"]])
            for h in range(4):
                a = k.ring("macc", acc)
                for s in range(2):
                    k.tr(a, a[:, s * 128:(s + 1) * 128], ck[:, s, h * 128:(h + 1) * 128], self.cst_f[:], reads=[ck, self.cst_f])
                k.op("dve", lambda e, a=a, h=h: e.tensor_copy(out=mk["s"][:, h, :], in_=a[:, 0:256]), reads=[a], writes=[mk["s"]])

    def p1(self, stm, li, kind, j, Wb, mk, mv, st):
        k = self.k
        I, O = self.I, self.O
        T = stm["T"]
        isp = stm["name"] == "p"
        TT = min(256 if kind == 1 else 512, T)
        np_ = min(128, T)
        nsub = TT // np_
        ntt = T // TT
        xsrc = stm["xin"] if li == 0 else stm["xres"]
        FM, YT = stm["FM"], stm["YT"]
        if kind == 1:
            self.tmob = [k.sb("tmob%d" % i, [128, 512], BF16, st) for i in range(2)]
        gpre = k.sb("gpre", [128, D], F32, st)
        k.dma("sp", gpre[:], I["gpre_r"][li], writes=[gpre])
        xt = [k.sb("xt%d" % i, [128, D], F32, st) for i in range(3)]
        junk = k.sb("junk", [128, D], BF16, st)
        ss = [k.sb("ss%d" % i, [128, 2], F32, st) for i in range(3)]
        hb = [k.sb("hb%d" % i, [128, D], BF16, st) for i in range(2)]
        hT = [k.sb("hT%d" % i, [128, 8, TT], BF16, st) for i in range(2)]
        ptr = k.ps("ptr", [128, 8, 128], BF16, st)
        acc = [k.ps("acc%d" % i, [128, 512], F32, st) for i in range(4)]
        memS = k.ps("memS", [128, 2, 512], F32, st)
        nring = 2 if kind == 2 else 3
        stg = [k.sb("stg%d" % i, [128, 4, TT], BF16, st) for i in range(nring)]
        tmo = [k.sb("tmo%d" % i, [128, 512], F32, st) for i in range(nring)]
        mq = k.sb("mq", [128, 4, TT], BF16, st)
        zm = k.sb("zm", [128, 4, TT], BF16, st)
        eT = k.sb("eT", [128, 2, TT], BF16, st)
        rden = k.sb("rden", [128, TT], F32, st)
        ymem = k.sb("ymem", [128, 4, TT], BF16, st)
        if kind == 0:
            mqc, zc, zmc = 3072, 3584, 4608
        elif kind == 1:
            mqc, zc, zmc = 3080, 3592, 4616
        else:
            mqc, zc, zmc = 3072, 3584, 4608

        def fm_chunk(hTt, col):
            a = k.ring("acc", acc)
            k.mm(a, a[:, 0:TT], [(Wb[:, kc, col:col + 128], hTt[:, kc, :]) for kc in range(8)], reads=[hTt, Wb])
            return a

        def fm_group(hTt, col0, nch, func, tok0, dst_fm0=None, dst_tile=None, scale=1.0):
            t = dst_tile if dst_tile is not None else k.ring("stg", stg)
            for c in range(nch):
                a = fm_chunk(hTt, col0 + c * 128)
                if func is None:
                    eng = k.ring("evac", ["dve", "act"])
                    if eng == "dve":
                        k.op("dve", lambda e, a=a, c=c: e.tensor_copy(out=t[:, c, :], in_=a[:, 0:TT]), reads=[a], writes=[t])
                    else:
                        k.op("act", lambda e, a=a, c=c: e.activation(out=t[:, c, :], in_=a[:, 0:TT], func=AF.Copy), reads=[a], writes=[t])
                else:
                    k.op("act", lambda e, a=a, c=c: e.activation(out=t[:, c, :], in_=a[:, 0:TT], func=func, scale=scale), reads=[a], writes=[t])
            if dst_fm0 is not None:
                k.dma("pool", FM[dst_fm0:dst_fm0 + nch, :, tok0:tok0 + TT].rearrange("c p t -> p c t"), t[:, 0:nch, :], reads=[t])
            return t

        def tm_block(hTt, s, col0, ncols):
            a = k.ring("acc", acc)
            k.mm(a, a[0:np_, 0:ncols], [(hTt[:, kc, s * np_:(s + 1) * np_], Wb[:, kc, col0:col0 + ncols]) for kc in range(8)], reads=[hTt, Wb])
            return a

        def mem_attn(tok0):
            sc = 1.0 / math.sqrt(128.0)
            for h in range(4):
                for mb in range(2):
                    k.mm(memS, memS[:, mb, 0:TT], [(mk[:, h, mb * 128:(mb + 1) * 128], mq[:, h, :])], reads=[mk, mq])
                k.op("act", lambda e: e.activation(out=eT[:], in_=memS[:, :, 0:TT], func=AF.Exp, scale=sc), reads=[memS], writes=[eT])
                den = k.ring("acc", acc)
                k.mm(den, den[:, 0:TT], [(self.cst_b[:, 2, :], eT[:, mb, :]) for mb in range(2)], reads=[eT, self.cst_b])
                oT = k.ring("acc", acc)
                k.mm(oT, oT[:, 0:TT], [(mv[:, mb, h * 128:(h + 1) * 128], eT[:, mb, :]) for mb in range(2)], reads=[eT, mv])
                k.op("dve", lambda e, den=den: e.reciprocal(out=rden[:], in_=den[:, 0:TT]), reads=[den], writes=[rden])
                k.op("dve", lambda e: e.tensor_tensor(out=rden[:], in0=rden[:], in1=zm[:, h, :], op=ALU.mult), reads=[rden, zm], writes=[rden])
                k.op("dve", lambda e, oT=oT, h=h: e.tensor_tensor(out=ymem[:, h, :], in0=oT[:, 0:TT], in1=rden[:], op=ALU.mult), reads=[oT, rden], writes=[ymem])
            k.dma("pool", YT[8:12, :, tok0:tok0 + TT].rearrange("c p t -> p c t"), ymem[:], reads=[ymem])

        if kind == 1:
            cw = k.sb("cw", [128, 8, 5], F32, st)
            k.dma("sp", cw[:], I["convmlT"][j], writes=[cw])
            xm = k.sb("xm", [128, 8, 3 + TT], F32, st)
            if isp:
                k.op("pool", lambda e: e.memset(xm[:, :, 0:3], 0.0), writes=[xm])
            else:
                k.dma("sp", xm[:, :, 0:3], I["smconvT"][j], writes=[xm])
            cacc = [k.sb("cacc%d" % i, [128, TT], F32, st) for i in range(2)]
            xc = k.sb("xc", [128, 8, TT], BF16, st)
            wqb = k.sb("wqb", [128, 8, 256], BF16, st)
            wkb = k.sb("wkb", [128, 8, 256], BF16, st)
            with contextlib.ExitStack() as sw:
                self.load_w(k, sw, wqb, I["wq"][j].rearrange("h d e -> (h d) e"), 256, 8, "wq")
                self.load_w(k, sw, wkb, I["wk"][j].rearrange("h d e -> (h d) e"), 256, 8, "wk")
                k.barrier()
            gtl = [k.sb("gtl%d" % i, [8, TT], F32, st) for i in range(2)]
            self.xcs = [k.sb("xcs%d" % i, [128, 8, TT], BF16, st) for i in range(2)]
            self.skp1 = k.sb("skp1", [128, 8], F32, st)
            k.dma("sp", self.skp1[:], I["skipT"][j], writes=[self.skp1])
        if kind == 2:
            cw = k.sb("cw", [128, 8, 3], F32, st)
            k.dma("sp", cw[:], I["convcvT"][j], writes=[cw])
            ch = k.sb("ch", [128, 8, 2 + TT], F32, st)
            if isp:
                k.op("pool", lambda e: e.memset(ch[:, :, 0:2], 0.0), writes=[ch])
            else:
                k.dma("sp", ch[:, :, 0:2], I["scvT"][j], writes=[ch])
            bT = [k.sb("bT%d" % i, [128, 4, TT], BF16, st) for i in range(2)]
            cT = [k.sb("cT%d" % i, [128, 4, TT], BF16, st) for i in range(2)]
            cacc = [k.sb("cacc%d" % i, [128, TT], F32, st) for i in range(2)]
            yst = [k.sb("yst%d" % i, [128, 4, TT], BF16, st) for i in range(2)]

        for tt in range(ntt):
            tok0 = tt * TT
            hTt = hT[tt % 2]
            for s in range(nsub):
                x_ = k.ring("xt", xt)
                s_ = k.ring("ss", ss)
                h_ = k.ring("hb", hb)
                self.rms_front(k, xsrc[tok0 + s * np_: tok0 + (s + 1) * np_, :], np_, x_, junk, s_, h_, gpre)
                self.to_fm(k, h_, np_, ptr, hTt, s * np_)
            fm_group(hTt, mqc, 4, None, tok0, dst_tile=mq)
            fm_group(hTt, zmc, 4, AF.Silu, tok0, dst_tile=zm)
            mem_attn(tok0)
            if kind == 0:
                fm_group(hTt, 0, 4, None, tok0, dst_fm0=0)
                fm_group(hTt, 512, 4, None, tok0, dst_fm0=4)
                fm_group(hTt, 1024, 4, None, tok0, dst_fm0=8)
                fm_group(hTt, 1536, 4, None, tok0, dst_fm0=12)
                fm_group(hTt, zc, 4, AF.Silu, tok0, dst_fm0=16)
                fm_group(hTt, zc + 512, 4, AF.Silu, tok0, dst_fm0=20)
                ko = (O["sbk_p"] if isp else O["sbk_s"])[j].rearrange("t h d -> t (h d)")
                vo = (O["sbv_p"] if isp else O["sbv_s"])[j].rearrange("t h d -> t (h d)")
                for s in range(nsub):
                    for (dst, c0) in ((ko, 1024), (vo, 2048)):
                        for nb in range(2):
                            a = tm_block(hTt, s, c0 + nb * 512, 512)
                            o = k.ring("tmo", tmo)
                            eng = k.ring("evac", ["dve", "act"])
                            if eng == "dve":
                                k.op("dve", lambda e, a=a, o=o: e.tensor_copy(out=o[0:np_, :], in_=a[0:np_, :]), reads=[a], writes=[o])
                            else:
                                k.op("act", lambda e, a=a, o=o: e.activation(out=o[0:np_, :], in_=a[0:np_, :], func=AF.Copy), reads=[a], writes=[o])
                            k.dma("pool", dst[tok0 + s * np_: tok0 + (s + 1) * np_, nb * 512:(nb + 1) * 512], o[0:np_, :], reads=[o])
            elif kind == 2:
                for g in range(2):
                    bt = fm_group(hTt, 0 + g * 512, 4, None, tok0, dst_tile=bT[g])
                    ct = fm_group(hTt, 1024 + g * 512, 4, None, tok0, dst_tile=cT[g])
                    for c in range(4):
                        a = fm_chunk(hTt, 2048 + (g * 4 + c) * 128)
                        k.op("dve", lambda e, a=a, c=c, g=g, ct=ct: e.tensor_tensor(out=ch[:, g * 4 + c, 2:2 + TT], in0=a[:, 0:TT], in1=ct[:, c, :], op=ALU.mult), reads=[a, ct, ch], writes=[ch])
                    zt = fm_group(hTt, zc + g * 512, 4, AF.Silu, tok0, dst_tile=k.ring("stg", stg))
                    yt = yst[g]
                    for c in range(4):
                        cc = g * 4 + c
                        ca = k.ring("cacc", cacc)
                        k.op("dve", lambda e, ca=ca, cc=cc: e.tensor_scalar(out=ca[:], in0=ch[:, cc, 0:TT], scalar1=cw[:, cc, 0:1], scalar2=None, op0=ALU.mult), reads=[ch, cw], writes=[ca])
                        k.op("dve", lambda e, ca=ca, cc=cc: e.scalar_tensor_tensor(out=ca[:], in0=ch[:, cc, 1:1 + TT], scalar=cw[:, cc, 1:2], in1=ca[:], op0=ALU.mult, op1=ALU.add), reads=[ch, cw, ca], writes=[ca])
                        k.op("dve", lambda e, ca=ca, cc=cc: e.scalar_tensor_tensor(out=ca[:], in0=ch[:, cc, 2:2 + TT], scalar=cw[:, cc, 2:3], in1=ca[:], op0=ALU.mult, op1=ALU.add), reads=[ch, cw, ca], writes=[ca])
                        k.op("pool", lambda e, ca=ca, c=c, bt=bt: e.tensor_tensor(out=ca[:], in0=ca[:], in1=bt[:, c, :], op=ALU.mult), reads=[ca, bt], writes=[ca])
                        k.op("pool", lambda e, ca=ca, c=c, zt=zt, yt=yt: e.tensor_tensor(out=yt[:, c, :], in0=ca[:], in1=zt[:, c, :], op=ALU.mult), reads=[ca, zt], writes=[yt])
                    k.dma("pool", YT[g * 4:g * 4 + 4, :, tok0:tok0 + TT].rearrange("c p t -> p c t"), yt[:], reads=[yt])
                if tt == ntt - 1:
                    s = nsub - 1
                    cvo = (O["cv_p"] if isp else O["cv_s"])[j]
                    for nb in range(2):
                        a1 = tm_block(hTt, s, 1024 + nb * 512, 512)
                        o1 = k.ring("tmo", tmo)
                        k.op("act", lambda e, a1=a1, o1=o1: e.activation(out=o1[0:np_, :], in_=a1[0:np_, :], func=AF.Copy), reads=[a1], writes=[o1])
                        a2 = tm_block(hTt, s, 2048 + nb * 512, 512)
                        k.op("dve", lambda e, a2=a2, o1=o1: e.tensor_tensor(out=o1[0:np_, :], in0=a2[0:np_, :], in1=o1[0:np_, :], op=ALU.mult), reads=[a2, o1], writes=[o1])
                        k.dma("pool", cvo[:, nb * 512:(nb + 1) * 512], o1[np_ - 2:np_, :], reads=[o1])
                if tt < ntt - 1:
                    k.op("dve", lambda e: e.tensor_copy(out=ch[:, :, 0:2], in_=ch[:, :, TT:TT + 2]), reads=[ch], writes=[ch])
            else:
                self.p1_ml(stm, li, j, tt, ntt, tok0, TT, np_, nsub, hTt, fm_chunk, fm_group, tm_block, tmo, stg, acc, xm, cw, cacc, xc, wqb, wkb, gtl, zc, Wb)

    def p1_ml(self, stm, li, j, tt, ntt, tok0, TT, np_, nsub, hTt, fm_chunk, fm_group, tm_block, tmo, stg, acc, xm, cw, cacc, xc, wqb, wkb, gtl, zc, Wb):
        k = self.k
        I, O = self.I, self.O
        isp = stm["name"] == "p"
        FM, TMs, GT = stm["FM"], stm["TMs"], stm["GT"]
        for c in range(8):
            a = fm_chunk(hTt, c * 128)
            k.op("act", lambda e, a=a, c=c: e.activation(out=xm[:, c, 3:3 + TT], in_=a[:, 0:TT], func=AF.Copy), reads=[a, xm], writes=[xm])
        fm_group(hTt, 2048, 4, AF.Sigmoid, tok0, dst_fm0=24)
        fm_group(hTt, 2048 + 512, 4, AF.Sigmoid, tok0, dst_fm0=28)
        fm_group(hTt, zc, 4, AF.Silu, tok0, dst_fm0=32)
        fm_group(hTt, zc + 512, 4, AF.Silu, tok0, dst_fm0=36)
        a = k.ring("acc", acc)
        k.mm(a, a[0:8, 0:TT], [(Wb[:, kc, 3072:3080], hTt[:, kc, :]) for kc in range(8)], reads=[hTt, Wb])
        g = k.ring("gtl", gtl)
        k.op("dve", lambda e, a=a, g=g: e.tensor_copy(out=g[:, :], in_=a[0:8, 0:TT]), reads=[a], writes=[g])
        k.dma("pool", GT[:, tok0:tok0 + TT], g[:, :], reads=[g])

        for s in range(nsub):
            for nb in range(2):
                a = tm_block(hTt, s, 1024 + nb * 512, 512)
                o = k.ring("tmob", self.tmob)
                k.op("dve", lambda e, a=a, o=o: e.tensor_copy(out=o[0:np_, :], in_=a[0:np_, :]), reads=[a], writes=[o])
                k.dma("pool", TMs[1, tok0 + s * np_: tok0 + (s + 1) * np_, nb * 512:(nb + 1) * 512], o[0:np_, :], reads=[o])
        for c in range(8):
            ca = k.ring("cacc", cacc)
            k.op("dve", lambda e, ca=ca, c=c: e.tensor_scalar(out=ca[:], in0=xm[:, c, 0:TT], scalar1=cw[:, c, 0:1], scalar2=None, op0=ALU.mult), reads=[xm, cw], writes=[ca])
            for jj in range(1, 4):
                k.op("dve", lambda e, ca=ca, c=c, jj=jj: e.scalar_tensor_tensor(out=ca[:], in0=xm[:, c, jj:jj + TT], scalar=cw[:, c, jj:jj + 1], in1=ca[:], op0=ALU.mult, op1=ALU.add), reads=[xm, cw, ca], writes=[ca])
            k.op("act", lambda e, ca=ca, c=c: e.activation(out=xc[:, c, :], in_=ca[:], func=AF.Silu, bias=cw[:, c, 4:5]), reads=[ca, cw], writes=[xc])
        if tt == ntt - 1:
            s = nsub - 1
            mco = (O["mlconv_p"] if isp else O["mlconv_s"])[j]
            for nb in range(2):
                a1 = tm_block(hTt, s, nb * 512, 512)
                o1 = k.ring("tmo", tmo)
                k.op("act", lambda e, a1=a1, o1=o1: e.activation(out=o1[0:np_, :], in_=a1[0:np_, :], func=AF.Copy), reads=[a1], writes=[o1])
                k.dma("pool", mco[:, nb * 512:(nb + 1) * 512], o1[np_ - 3:np_, :], reads=[o1])
        if tt < ntt - 1:
            k.op("dve", lambda e: e.tensor_copy(out=xm[:, :, 0:3], in_=xm[:, :, TT:TT + 3]), reads=[xm], writes=[xm])
        xs_ = k.ring("stgx", self.xcs)
        for c in range(8):
            k.op("act", lambda e, c=c: e.activation(out=xs_[:, c, :], in_=xc[:, c, :], func=AF.Copy, scale=self.skp1[:, c:c + 1]), reads=[xc, self.skp1], writes=[xs_])
        k.dma("pool", FM[16:24, :, tok0:tok0 + TT].rearrange("c p t -> p c t"), xs_[:], reads=[xs_])
        for (wb, f0, scl) in ((wqb, 0, 1.0), (wkb, 8, 1.0 / 16.0)):
            for g in range(2):
                t = k.ring("stg", stg)
                for c in range(4):
                    hc = g * 4 + c
                    h, ec = hc // 2, hc % 2
                    a = k.ring("acc", acc)
                    k.mm(a, a[:, 0:TT], [(wb[:, 2 * h + dc, ec * 128:(ec + 1) * 128], xc[:, 2 * h + dc, :]) for dc in range(2)], reads=[xc, wb])
                    k.op("act", lambda e, a=a, c=c, t=t, scl=scl: e.activation(out=t[:, c, :], in_=a[:, 0:TT], func=AF.Copy, scale=scl), reads=[a], writes=[t])
                k.dma("pool", FM[f0 + g * 4:f0 + g * 4 + 4, :, tok0:tok0 + TT].rearrange("c p t -> p c t"), t[:], reads=[t])
        for s in range(nsub):
            for nb in range(2):
                a = k.ring("acc", acc)
                for hh in range(2):
                    h = nb * 2 + hh
                    k.mm(a, a[0:np_, hh * 256:(hh + 1) * 256], [(xc[:, 2 * h + dc, s * np_:(s + 1) * np_], wkb[:, 2 * h + dc, :]) for dc in range(2)], reads=[xc, wkb])
                o = k.ring("tmob", self.tmob)
                k.op("act", lambda e, a=a, o=o: e.activation(out=o[0:np_, :], in_=a[0:np_, :], func=AF.Copy, scale=1.0 / 16.0), reads=[a], writes=[o])
                k.dma("pool", TMs[0, tok0 + s * np_: tok0 + (s + 1) * np_, nb * 512:(nb + 1) * 512], o[0:np_, :], reads=[o])
    def p3(self, stm, li, Wo, gpo, last, st, bg=None):
        k = self.k
        T = stm["T"]
        TT = min(512, T)
        np_ = min(128, T)
        nsub = TT // np_
        ntt = T // TT
        xsrc = stm["xin"] if li == 0 else stm["xres"]
        xdst = stm["yout"] if last else stm["xres"]
        YT = stm["YT"]
        yt = [k.sb("yt%d" % i, [128, 12, TT], BF16, st) for i in range(2)]
        xt = [k.sb("p3x%d" % i, [128, D], F32, st) for i in range(3)]
        ot = [k.sb("p3o%d" % i, [128, D], F32, st) for i in range(2)]
        junk = k.sb("p3junk", [128, D], BF16, st)
        ss = [k.sb("p3ss%d" % i, [128, 2], F32, st) for i in range(3)]
        acc = [k.ps("p3acc%d" % i, [128, D], F32, st) for i in range(3)]
        for tt in range(ntt):
            tok0 = tt * TT
            y_ = yt[tt % 2]
            k.dma("sp", y_[:], YT[:, :, tok0:tok0 + TT].rearrange("c p t -> p c t"), writes=[y_])
            for s in range(nsub):
                if bg is not None and s % 3 != 2:
                    next(bg, None)
                x_ = k.ring("p3x", xt)
                k.dma("sp", x_[0:np_, :], xsrc[tok0 + s * np_: tok0 + (s + 1) * np_, :], writes=[x_])
                a = k.ring("p3acc", acc)
                for nb in range(2):
                    k.mm(a, a[0:np_, nb * 512:(nb + 1) * 512], [(y_[:, fc, s * np_:(s + 1) * np_], Wo[:, fc, nb * 512:(nb + 1) * 512]) for fc in range(12)], reads=[y_, Wo])
                s_ = k.ring("p3ss", ss)
                k.op("pool", lambda e, s_=s_: e.memset(s_[:], 0.0), writes=[s_])
                k.op("act", lambda e, a=a, s_=s_: e.activation(out=junk[0:np_, :], in_=a[0:np_, :], func=AF.Square, accum_out=s_[0:np_, 0:1]), reads=[a, s_], writes=[junk, s_])
                k.op("act", lambda e, s_=s_: e.activation(out=s_[0:np_, 1:2], in_=s_[0:np_, 0:1], func=AF.Ln, scale=1.0 / D, bias=EPS), reads=[s_], writes=[s_])
                k.op("act", lambda e, s_=s_: e.activation(out=s_[0:np_, 1:2], in_=s_[0:np_, 1:2], func=AF.Exp, scale=-0.5), reads=[s_], writes=[s_])
                o_ = k.ring("p3o", ot)
                k.op("dve", lambda e, a=a, s_=s_, o_=o_: e.scalar_tensor_tensor(out=o_[0:np_, :], in0=a[0:np_, :], scalar=s_[0:np_, 1:2], in1=gpo[0:np_, :], op0=ALU.mult, op1=ALU.mult), reads=[a, s_, gpo], writes=[o_])
                k.op("pool", lambda e, o_=o_, x_=x_: e.tensor_tensor(out=o_[0:np_, :], in0=o_[0:np_, :], in1=x_[0:np_, :], op=ALU.add), reads=[o_, x_], writes=[o_])
                k.dma("pool", xdst[tok0 + s * np_: tok0 + (s + 1) * np_, :], o_[0:np_, :], reads=[o_])

    def p2_sb(self, stm, li, j, st):
        k = self.k
        I, O = self.I, self.O
        T = stm["T"]
        isp = stm["name"] == "p"
        FM, YT = stm["FM"], stm["YT"]
        N = min(512, T)
        sc = 1.0 / math.sqrt(128.0)
        PAST = self.PAST
        if isp:
            NB = T // 128
            koff = 0
        else:
            NB = (PAST + T + 127) // 128
            if NB % 2:
                NB += 1
            koff = NB * 128 - (PAST + T)
        nqs = T // N
        NSET = 2 if isp else 4
        early = not isp
        qT = [k.sb("qT%d" % i, [128, T], BF16, st) for i in range(NSET)]
        kT = [k.sb("kT%d" % i, [128, NB * 128], BF16, st) for i in range(NSET)]
        szT = [k.sb("szT%d" % i, [128, T], BF16, st) for i in range(NSET)]
        Vb = [k.sb("Vb%d" % i, [128, NB, 128], BF16, st) for i in range(NSET)]
        VST = 16
        vstg = [k.sb("vstg%d" % i, [128, VST, 128], F32, st) for i in range(2)] if isp else None
        vfull = [k.sb("vfull%d" % i, [128, NB, 128], F32, st) for i in range(2)] if not isp else None
        S = [k.ps("S%d" % i, [128, 2, N], F32, st) for i in range(2)]
        L = k.ps("L", [128, 2, N], F32, st)
        Racc = k.ps("Racc", [128, N], F32, st)
        oacc = k.ps("oacc", [128, N], F32, st)
        trs = k.ps("trs", [128, 512], F32, st) if not isp else None
        e_ = [k.sb("e%d" % i, [128, 2, N], BF16, st) for i in range(5)]
        c_ = [k.sb("c%d" % i, [128, 2, N], BF16, st) for i in range(3)]
        g_ = [k.sb("g%d" % i, [128, 2, N], BF16, st) for i in range(2)]
        a_ = [k.sb("a%d" % i, [128, 2, N], BF16, st) for i in range(3)]
        Lr = [k.sb("Lr%d" % i, [128, 2, N], F32, st) for i in range(2)]
        R = [k.sb("R%d" % i, [128, N], F32, st) for i in range(3)]
        yst = [k.sb("ysb%d" % i, [128, N], BF16, st) for i in range(2)]
        tri, ones = self.cst_b[:, 1, :], self.cst_b[:, 2, :]
        vout = (O["sbv_p"] if isp else O["sbv_s"])[j]
        kout = (O["sbk_p"] if isp else O["sbk_s"])[j]

        def load_head(h):
            si = h % NSET
            k.dma("sp", qT[si][:], FM[h], writes=[qT[si]])
            k.dma("sp", szT[si][:], FM[16 + h], writes=[szT[si]])
            if isp:
                k.dma("sp", kT[si][:], FM[8 + h], writes=[kT[si]])
                for b0 in range(0, NB, VST):
                    nb_ = min(VST, NB - b0)
                    vs = k.ring("vstg", vstg)
                    k.dma("sp", vs[:, 0:nb_, :], vout[b0 * 128:(b0 + nb_) * 128, h, :].rearrange("(b p) d -> p b d", p=128), writes=[vs])
                    k.op("pool", lambda e, vs=vs, b0=b0, nb_=nb_: e.tensor_copy(out=Vb[si][:, b0:b0 + nb_, :], in_=vs[:, 0:nb_, :]), reads=[vs], writes=[Vb[si]])
            else:
                for (src_c, src_n, isk) in ((I["csk"][j], kout, True), (I["csv"][j], vout, False)):
                    vs = k.ring("vfull", vfull)
                    k.op("pool", lambda e, vs=vs: e.memset(vs[:], 0.0), writes=[vs])
                    p0 = koff % 128
                    bq = koff // 128
                    t0 = (128 - p0) % 128
                    if t0:
                        k.dma("sp", vs[p0:128, bq, :], src_c[0:t0, h, :], writes=[vs])
                        bq += 1
                    nfull = (PAST - t0) // 128
                    if nfull:
                        k.dma("sp", vs[:, bq:bq + nfull, :], src_c[t0:t0 + nfull * 128, h, :].rearrange("(b p) d -> p b d", p=128), writes=[vs])
                    rem = PAST - t0 - nfull * 128
                    if rem:
                        k.dma("sp", vs[0:rem, bq + nfull, :], src_c[t0 + nfull * 128:PAST, h, :], writes=[vs])
                    k.dma("sp", vs[128 - T:128, NB - 1, :], src_n[0:T, h, :], writes=[vs])
                    if not isk:
                        k.op("pool", lambda e, vs=vs: e.tensor_copy(out=Vb[si][:], in_=vs[:]), reads=[vs], writes=[Vb[si]])
                    else:
                        for b0 in range(0, NB, 4):
                            nb_ = min(4, NB - b0)
                            for b in range(nb_):
                                k.tr(trs, trs[:, b * 128:(b + 1) * 128], vs[:, b0 + b, :], self.cst_f[:], reads=[vs, self.cst_f])
                            k.op("dve", lambda e, b0=b0, nb_=nb_: e.tensor_copy(out=kT[si][:, b0 * 128:(b0 + nb_) * 128], in_=trs[:, 0:nb_ * 128]), reads=[trs], writes=[kT[si]])

        G = []
        for h in range(8):
            for qs in range(nqs):
                q0 = qs * N
                grp = []
                if isp:
                    b = (q0 + N) // 128 - 1
                    while b >= 0:
                        m0 = (b - q0 // 128) if b >= q0 // 128 else None
                        m1 = ((b - 1) - q0 // 128) if (b - 1) >= q0 // 128 else None
                        grp.append((b, b - 1, m0, m1))
                        b -= 2
                else:
                    b = NB - 1
                    first_real = koff // 128
                    while b >= 0:
                        ms = []
                        for bb in (b, b - 1):
                            if bb == NB - 1:
                                ms.append(0)
                            elif bb == first_real and koff % 128:
                                ms.append(1)
                            elif bb < first_real:
                                ms.append(2)
                            else:
                                ms.append(None)
                        grp.append((b, b - 1, ms[0], ms[1]))
                        b -= 2
                ng = len(grp)
                for gi, (b0, b1, m0, m1) in enumerate(grp):
                    G.append(dict(h=h, si=h % NSET, q0=q0, b0=b0, b1=b1, m0=m0, m1=m1, first=(gi == 0), last=(gi == ng - 1),
                                  newhead=(qs == 0 and gi == 0)))
        NG = len(G)
        mtile = self.msk if isp else self.smsk

        def st_S(n):
            d = G[n]
            if n == 0:
                for hh in range(min(NSET, 8)):
                    load_head(hh)
            si, q0 = d["si"], d["q0"]
            S_ = k.ring("S", S)
            for i, (b, m) in enumerate(((d["b0"], d["m0"]), (d["b1"], d["m1"]))):
                pairs = [(kT[si][:, b * 128:(b + 1) * 128], qT[si][:, q0:q0 + N])]
                rd = [kT[si], qT[si]]
                if m is not None:
                    pairs.append((self.cst_b[:, 0, :], mtile[:, m, 0:N]))
                    rd += [self.cst_b, mtile]
                k.mm(S_, S_[:, i, 0:N], pairs, reads=rd)
            d["S"] = S_

        def st_exp1(n):
            d = G[n]
            e = k.ring("e", e_)
            S_ = d["S"]
            k.op("act", lambda en: en.activation(out=e[:], in_=S_[:, :, 0:N], func=AF.Exp, scale=sc), reads=[S_], writes=[e])
            d["e"] = e

        def st_ln(n):
            d = G[n]
            c = k.ring("c", c_)
            e = d["e"]
            k.op("act", lambda en: en.activation(out=c[:], in_=e[:], func=AF.Ln, bias=1.0), reads=[e], writes=[c])
            d["c"] = c

        def st_L(n):
            d = G[n]
            c = d["c"]
            k.mm(L, L[:, 0, 0:N], [(tri, c[:, 0, :])], reads=[c, self.cst_b])
            k.mm(L, L[:, 1, 0:N], [(tri, c[:, 1, :]), (ones, c[:, 0, :])], reads=[c, self.cst_b])
            if not d["first"]:
                Rt = d["R"]
                lr = k.ring("Lr", Lr)
                for i in range(2):
                    k.op("dve", lambda en, i=i: en.tensor_tensor(out=lr[:, i, :], in0=L[:, i, 0:N], in1=Rt[:], op=ALU.add), reads=[L, Rt], writes=[lr])
                d["src"], d["srct"] = lr[:], lr
            else:
                lr = k.ring("Lr", Lr)
                k.op("dve", lambda en: en.tensor_copy(out=lr[:], in_=L[:, :, 0:N]), reads=[L], writes=[lr])
                d["src"], d["srct"] = lr[:], lr
            if not d["last"]:
                k.mm(Racc, Racc[:, 0:N], [(ones, c[:, 0, :]), (ones, c[:, 1, :])], reads=[c, self.cst_b])
                Rn = k.ring("R", R)
                if d["first"]:
                    k.op("dve", lambda en: en.tensor_copy(out=Rn[:], in_=Racc[:, 0:N]), reads=[Racc], writes=[Rn])
                else:
                    Rt = d["R"]
                    k.op("dve", lambda en: en.tensor_tensor(out=Rn[:], in0=Racc[:, 0:N], in1=Rt[:], op=ALU.add), reads=[Racc, Rt], writes=[Rn])
                G[n + 1]["R"] = Rn

        def st_exp2(n):
            d = G[n]
            g = k.ring("g", g_)
            src, srct = d["src"], d["srct"]
            k.op("act", lambda en: en.activation(out=g[:], in_=src, func=AF.Exp, scale=-1.0), reads=[srct], writes=[g])
            d["g"] = g

        def st_a(n):
            d = G[n]
            a = k.ring("a", a_)
            e, g = d["e"], d["g"]
            k.op("pool", lambda en: en.tensor_tensor(out=a[:], in0=e[:], in1=g[:], op=ALU.mult), reads=[e, g], writes=[a])
            si, q0, h = d["si"], d["q0"], d["h"]
            k.mm(oacc, oacc[:, 0:N], [(Vb[si][:, d["b0"], :], a[:, 0, :]), (Vb[si][:, d["b1"], :], a[:, 1, :])], reads=[a, Vb[si]], start=d["first"], stop=d["last"])
            if d["last"]:
                y = k.ring("ysb", yst)
                k.op("dve", lambda en: en.tensor_tensor(out=y[:], in0=oacc[:, 0:N], in1=szT[si][:, q0:q0 + N], op=ALU.mult), reads=[oacc, szT[si]], writes=[y])
                k.dma("pool", YT[h, :, q0:q0 + N], y[:], reads=[y])
            for kk in ("S", "e", "c", "g", "src", "srct", "R"):
                d.pop(kk, None)
            if (n == NG - 1 or G[n + 1]["newhead"]) and d["h"] + NSET < 8:
                load_head(d["h"] + NSET)

        for t in range(-2, NG + 3):
            if 0 <= t + 2 < NG:
                st_S(t + 2)
            if 0 <= t + 1 < NG:
                st_exp1(t + 1)
            if 0 <= t < NG:
                st_ln(t)
            if 0 <= t - 1 < NG:
                st_L(t - 1)
            if 0 <= t - 2 < NG:
                st_exp2(t - 2)
            if 0 <= t - 3 < NG:
                st_a(t - 3)

    def p2_sb_sample(self, stm, li, j, st):
        k = self.k
        I, O = self.I, self.O
        T = stm["T"]
        FM, YT = stm["FM"], stm["YT"]
        N = T
        H = 8
        sc = 1.0 / math.sqrt(128.0)
        PAST = self.PAST
        NB = (PAST + T + 127) // 128
        if NB % 2:
            NB += 1
        koff = NB * 128 - (PAST + T)
        qT = k.sb("sqT", [128, H, N], BF16, st)
        szT = k.sb("sszT", [128, H, N], BF16, st)
        kT = [k.sb("skT%d" % h, [128, NB * 128], BF16, st) for h in range(H)]
        Vb = [k.sb("sVb%d" % h, [128, NB, 128], BF16, st) for h in range(H)]
        vfull = [k.sb("svfull%d" % i, [128, NB, 128], F32, st) for i in range(2)]
        S = [k.ps("sS%d" % i, [128, H, 2, N], F32, st) for i in range(2)]
        L = k.ps("sL", [128, H, 2, N], F32, st)
        Racc = k.ps("sRacc", [128, H, N], F32, st)
        oacc = k.ps("soacc", [128, H, N], F32, st)
        trs = k.ps("strs", [128, 512], F32, st)
        e_ = [k.sb("se%d" % i, [128, H, 2, N], BF16, st) for i in range(5)]
        c_ = [k.sb("sc%d" % i, [128, H, 2, N], BF16, st) for i in range(3)]
        g_ = [k.sb("sg%d" % i, [128, H, 2, N], BF16, st) for i in range(2)]
        a_ = [k.sb("sa%d" % i, [128, H, 2, N], BF16, st) for i in range(3)]
        Lr = [k.sb("sLr%d" % i, [128, H, 2, N], F32, st) for i in range(2)]
        R = [k.sb("sR%d" % i, [128, H, N], F32, st) for i in range(3)]
        yst = k.sb("sysb", [128, H, N], BF16, st)
        osb = k.sb("sosb", [128, H, N], F32, st)
        ident, tri, ones = self.cst_b[:, 0, :], self.cst_b[:, 1, :], self.cst_b[:, 2, :]
        vout = O["sbv_s"][j]
        kout = O["sbk_s"][j]
        k.dma("sp", qT[:], FM[0:8, :, :].rearrange("c p t -> p c t"), writes=[qT])
        k.dma("sp", szT[:], FM[16:24, :, :].rearrange("c p t -> p c t"), writes=[szT])
        for h in range(H):
            for (src_c, src_n, isk) in ((I["csk"][j], kout, True), (I["csv"][j], vout, False)):
                vs = k.ring("svfull", vfull)
                nz = koff // 128 + (1 if koff % 128 else 0)
                if nz:
                    k.op("pool", lambda e, vs=vs, nz=nz: e.memset(vs[:, 0:nz, :], 0.0), writes=[vs])
                p0 = koff % 128
                bq = koff // 128
                t0 = (128 - p0) % 128
                if t0:
                    k.dma("sp", vs[p0:128, bq, :], src_c[0:t0, h, :], writes=[vs])
                    bq += 1
                nfull = (PAST - t0) // 128
                if nfull:
                    k.dma("sp", vs[:, bq:bq + nfull, :], src_c[t0:t0 + nfull * 128, h, :].rearrange("(b p) d -> p b d", p=128), writes=[vs])
                rem = PAST - t0 - nfull * 128
                if rem:
                    k.dma("sp", vs[0:rem, bq + nfull, :], src_c[t0 + nfull * 128:PAST, h, :], writes=[vs])
                k.dma("sp", vs[128 - T:128, NB - 1, :], src_n[0:T, h, :], writes=[vs])
                if not isk:
                    if h % 2 == 0:
                        k.op("pool", lambda e, vs=vs, h=h: e.tensor_copy(out=Vb[h][:], in_=vs[:]), reads=[vs], writes=[Vb[h]])
                    else:
                        k.op("act", lambda e, vs=vs, h=h: e.activation(out=Vb[h][:], in_=vs[:], func=AF.Copy), reads=[vs], writes=[Vb[h]])
                else:
                    for b0 in range(0, NB, 4):
                        nb_ = min(4, NB - b0)
                        for b in range(nb_):
                            k.tr(trs, trs[:, b * 128:(b + 1) * 128], vs[:, b0 + b, :], self.cst_f[:], reads=[vs, self.cst_f])
                        eng = k.ring("sevac", ["dve", "act"])
                        if eng == "dve":
                            k.op("dve", lambda e, b0=b0, nb_=nb_, h=h: e.tensor_copy(out=kT[h][:, b0 * 128:(b0 + nb_) * 128], in_=trs[:, 0:nb_ * 128]), reads=[trs], writes=[kT[h]])
                        else:
                            k.op("act", lambda e, b0=b0, nb_=nb_, h=h: e.activation(out=kT[h][:, b0 * 128:(b0 + nb_) * 128], in_=trs[:, 0:nb_ * 128], func=AF.Copy), reads=[trs], writes=[kT[h]])
        G = []
        b = NB - 1
        first_real = koff // 128
        while b >= 0:
            ms = []
            for bb in (b, b - 1):
                if bb == NB - 1:
                    ms.append(0)
                elif bb == first_real and koff % 128:
                    ms.append(1)
                elif bb < first_real:
                    ms.append(2)
                else:
                    ms.append(None)
            G.append(dict(b0=b, b1=b - 1, m0=ms[0], m1=ms[1]))
            b -= 2
        NG = len(G)
        for n, d in enumerate(G):
            d["first"], d["last"] = (n == 0), (n == NG - 1)
        mtile = self.smsk

        def st_S(n):
            d = G[n]
            S_ = k.ring("sS", S)
            for h in range(H):
                for i, (b, m) in enumerate(((d["b0"], d["m0"]), (d["b1"], d["m1"]))):
                    pairs = [(kT[h][:, b * 128:(b + 1) * 128], qT[:, h, :])]
                    rd = [kT[h], qT]
                    if m is not None:
                        pairs.append((ident, mtile[:, m, 0:N]))
                        rd += [self.cst_b, mtile]
                    k.mm(S_, S_[:, h, i, :], pairs, reads=rd)
            d["S"] = S_

        def st_exp1(n):
            d = G[n]
            e = k.ring("se", e_)
            S_ = d["S"]
            k.op("act", lambda en: en.activation(out=e[:], in_=S_[:], func=AF.Exp, scale=sc), reads=[S_], writes=[e])
            d["e"] = e

        def st_ln(n):
            d = G[n]
            c = k.ring("sc", c_)
            e = d["e"]
            k.op("act", lambda en: en.activation(out=c[:], in_=e[:], func=AF.Ln, bias=1.0), reads=[e], writes=[c])
            d["c"] = c

        def st_L(n):
            d = G[n]
            c = d["c"]
            for h in range(H):
                k.mm(L, L[:, h, 0, :], [(tri, c[:, h, 0, :])], reads=[c, self.cst_b])
                k.mm(L, L[:, h, 1, :], [(tri, c[:, h, 1, :]), (ones, c[:, h, 0, :])], reads=[c, self.cst_b])
            lr = k.ring("sLr", Lr)
            if not d["first"]:
                Rt = d["R"]
                for i in range(2):
                    k.op("dve", lambda en, i=i: en.tensor_tensor(out=lr[:, :, i, :], in0=L[:, :, i, :], in1=Rt[:], op=ALU.add), reads=[L, Rt], writes=[lr])
            else:
                k.op("dve", lambda en: en.tensor_copy(out=lr[:], in_=L[:]), reads=[L], writes=[lr])
            d["lr"] = lr
            if not d["last"]:
                for h in range(H):
                    k.mm(Racc, Racc[:, h, :], [(ones, c[:, h, 0, :]), (ones, c[:, h, 1, :])], reads=[c, self.cst_b])
                Rn = k.ring("sR", R)
                if d["first"]:
                    k.op("dve", lambda en: en.tensor_copy(out=Rn[:], in_=Racc[:]), reads=[Racc], writes=[Rn])
                else:
                    Rt = d["R"]
                    k.op("dve", lambda en: en.tensor_tensor(out=Rn[:], in0=Racc[:], in1=Rt[:], op=ALU.add), reads=[Racc, Rt], writes=[Rn])
                G[n + 1]["R"] = Rn

        def st_exp2(n):
            d = G[n]
            g = k.ring("sg", g_)
            lr = d["lr"]
            k.op("act", lambda en: en.activation(out=g[:], in_=lr[:], func=AF.Exp, scale=-1.0), reads=[lr], writes=[g])
            d["g"] = g

        def st_a(n):
            d = G[n]
            a = k.ring("sa", a_)
            e, g = d["e"], d["g"]
            k.op("pool", lambda en: en.tensor_tensor(out=a[:], in0=e[:], in1=g[:], op=ALU.mult), reads=[e, g], writes=[a])
            for h in range(H):
                k.mm(oacc, oacc[:, h, :], [(Vb[h][:, d["b0"], :], a[:, h, 0, :]), (Vb[h][:, d["b1"], :], a[:, h, 1, :])], reads=[a, Vb[h]])
            if d["first"]:
                k.op("dve", lambda en: en.tensor_copy(out=osb[:], in_=oacc[:]), reads=[oacc], writes=[osb])
            else:
                k.op("dve", lambda en: en.tensor_tensor(out=osb[:], in0=oacc[:], in1=osb[:], op=ALU.add), reads=[oacc, osb], writes=[osb])
            if d["last"]:
                k.op("dve", lambda en: en.tensor_tensor(out=yst[:], in0=osb[:], in1=szT[:], op=ALU.mult), reads=[osb, szT], writes=[yst])
                k.dma("pool", YT[0:8, :, :].rearrange("c p t -> p c t"), yst[:], reads=[yst])

        for t in range(-2, NG + 3):
            if 0 <= t + 2 < NG:
                st_S(t + 2)
            if 0 <= t + 1 < NG:
                st_exp1(t + 1)
            if 0 <= t < NG:
                st_ln(t)
            if 0 <= t - 1 < NG:
                st_L(t - 1)
            if 0 <= t - 2 < NG:
                st_exp2(t - 2)
            if 0 <= t - 3 < NG:
                st_a(t - 3)

    def p2_ml(self, stm, li, j, st):
        k = self.k
        I, O = self.I, self.O
        T = stm["T"]
        isp = stm["name"] == "p"
        FM, YT, TMs, GT = stm["FM"], stm["YT"], stm["TMs"], stm["GT"]
        LC = min(128, T)
        nch = T // LC
        SEG = min(2048, T)
        ig = k.sb("ml_ig", [4, T], F32, st)
        fg = k.sb("ml_fg", [4, T], F32, st)
        Bt = k.sb("ml_B", [4, T], F32, st)
        ones4 = k.sb("ml_ones", [4, SEG], F32, st)
        gcol = k.sb("ml_gcol", [4, 2], F32, st)
        minit = k.sb("ml_minit", [4, 1], F32, st)
        sel = k.sb("ml_sel", [4, 4, 128], F32, st)
        negm = k.sb("ml_negm", [128, 128], F32, st)
        ghr = k.sb("ml_ghr", [128, D], F32, st)
        skp = k.sb("ml_skip", [128, 8], F32, st)
        C = k.sb("ml_C", [128, 4, 2, 257], F32, st)
        Cb = k.sb("ml_Cb", [128, 4, 2, 257], BF16, st)
        MendB = k.sb("ml_MendB", [128, nch + 1, 4], F32, st)
        nMendB = k.sb("ml_nMendB", [128, nch + 1, 4], F32, st)
        decB = k.sb("ml_decB", [128, nch, 4], F32, st)
        k.op("pool", lambda e: e.memset(ones4[:], 1.0), writes=[ones4])
        k.dma("sp", gcol[:], I["gcol"][j], writes=[gcol])
        k.dma("sp", sel[:].rearrange("r h m -> r (h m)"), I["sel4"][:, :], writes=[sel])
        k.dma("sp", negm[:], I["negmask"][:, :], writes=[negm])
        k.dma("sp", ghr[:], I["ghead_r"][j], writes=[ghr])
        k.dma("sp", skp[:], I["skipT"][j], writes=[skp])
        if isp:
            k.op("pool", lambda e: e.memset(minit[:], 0.0), writes=[minit])
            k.op("pool", lambda e: e.memset(C[:], 0.0), writes=[C])
        else:
            k.dma("sp", minit[:], I["smm"][j].rearrange("(h o) -> h o", o=1), writes=[minit])
            for h in range(4):
                k.dma("sp", C[:, h, :, 0:256], I["smc"][j, h].rearrange("(c p) e -> p c e", p=128), writes=[C])
                k.dma("sp", C[:, h, :, 256:257], I["smn"][j, h].rearrange("(c p o) -> p c o", p=128, o=1), writes=[C], allow_slow_non_contiguous=True)
        k.op("act", lambda e: e.activation(out=Cb[:], in_=C[:], func=AF.Copy), reads=[C], writes=[Cb])
        k.dma("sp", ig[:], GT[0:4, :], writes=[ig])
        k.dma("sp", fg[:], GT[4:8, :], writes=[fg])
        k.op("dve", lambda e: e.tensor_scalar(out=ig[:], in0=ig[:], scalar1=gcol[:, 0:1], scalar2=None, op0=ALU.add), reads=[ig, gcol], writes=[ig])
        k.op("dve", lambda e: e.tensor_scalar(out=fg[:], in0=fg[:], scalar1=gcol[:, 1:2], scalar2=None, op0=ALU.add), reads=[fg, gcol], writes=[fg])
        k.op("act", lambda e: e.activation(out=fg[:], in_=fg[:], func=AF.Exp, scale=-1.0), reads=[fg], writes=[fg])
        k.op("act", lambda e: e.activation(out=fg[:], in_=fg[:], func=AF.Ln, bias=1.0), reads=[fg], writes=[fg])
        k.op("dve", lambda e: e.tensor_scalar(out=fg[:], in0=fg[:], scalar1=-1.0, scalar2=None, op0=ALU.mult), reads=[fg], writes=[fg])
        for s0 in range(0, T, SEG):
            n = min(SEG, T - s0)
            init = 0.0 if s0 == 0 else Bt[:, s0 - 1:s0]
            k.op("dve", lambda e, s0=s0, n=n, init=init: e.tensor_tensor_scan(out=Bt[:, s0:s0 + n], data0=ones4[:, 0:n], data1=fg[:, s0:s0 + n], initial=init, op0=ALU.mult, op1=ALU.add), reads=[ones4, fg, Bt], writes=[Bt])
        k.op("dve", lambda e: e.tensor_tensor(out=ig[:], in0=ig[:], in1=Bt[:], op=ALU.subtract), reads=[ig, Bt], writes=[ig])
        for s0 in range(0, T, SEG):
            n = min(SEG, T - s0)
            init = minit[:, 0:1] if s0 == 0 else fg[:, s0 - 1:s0]
            k.op("dve", lambda e, s0=s0, n=n, init=init: e.tensor_tensor_scan(out=fg[:, s0:s0 + n], data0=ones4[:, 0:n], data1=ig[:, s0:s0 + n], initial=init, op0=ALU.mult, op1=ALU.max), reads=[ones4, ig, fg, minit], writes=[fg])
        k.op("dve", lambda e: e.tensor_tensor(out=Bt[:], in0=Bt[:], in1=fg[:], op=ALU.add), reads=[Bt, fg], writes=[Bt])
        with contextlib.ExitStack() as s2:
            mps = k.ps("ml_mps", [128, 4, nch + 1], F32, s2)
            for h in range(4):
                k.mm(mps, mps[:, h, 0:1], [(sel[:, h, :], minit[:, 0:1])], reads=[sel, minit])
                k.mm(mps, mps[:, h, 1:nch + 1], [(sel[:, h, :], fg[:, LC - 1:T:LC])], reads=[sel, fg])
            k.op("dve", lambda e: e.tensor_copy(out=MendB[:], in_=mps[:].rearrange("p h c -> p c h")), reads=[mps], writes=[MendB])
            k.op("dve", lambda e: e.tensor_scalar(out=nMendB[:], in0=MendB[:], scalar1=-1.0, scalar2=None, op0=ALU.mult), reads=[MendB], writes=[nMendB])
            k.op("dve", lambda e: e.tensor_tensor(out=decB[:], in0=MendB[:, 0:nch, :], in1=MendB[:, 1:nch + 1, :], op=ALU.subtract), reads=[MendB], writes=[decB])
            k.op("act", lambda e: e.activation(out=decB[:], in_=decB[:], func=AF.Exp), reads=[decB], writes=[decB])
            k.barrier()
        qk = [k.sb("ml_qk%d" % i, [128, 16, LC], BF16, st) for i in range(2)]
        ex = [k.sb("ml_ex%d" % i, [128, 24, LC], BF16, st) for i in range(2)]
        ktm = [k.sb("ml_ktm%d" % i, [128, D], BF16, st) for i in range(2)]
        vaug = [k.sb("ml_vaug%d" % i, [128, 4, 257], BF16, st) for i in range(2)]
        for v in vaug:
            k.op("pool", lambda e, v=v: e.memset(v[:], 1.0), writes=[v])
        cols = [k.sb("ml_cols%d" % i, [128, 12], F32, st) for i in range(2)]
        sm4 = [k.sb("ml_sm4%d" % i, [128, 16], F32, st) for i in range(2)]
        negm4 = k.sb("ml_negm4", [128, 4, 128], F32, st)
        for h in range(4):
            k.op("dve", lambda e, h=h: e.tensor_copy(out=negm4[:, h, :], in_=negm[:]), reads=[negm], writes=[negm4])
        w4 = [k.sb("ml_w4%d" % i, [128, 4, 128], F32, st) for i in range(2)]
        smb4 = [k.sb("ml_sm4b%d" % i, [128, 4, 128], BF16, st) for i in range(2)]
        nbs = [k.sb("ml_nbs%d" % i, [128, 257], F32, st) for i in range(2)]
        nd4 = [k.sb("ml_nd4%d" % i, [128, 4, 257], F32, st) for i in range(2)]
        sc1 = [k.sb("ml_sc%d" % i, [128, 20], F32, st) for i in range(2)]
        junk = k.sb("ml_junk", [128, 256], BF16, st)
        hn4 = [k.sb("ml_hn4%d" % i, [128, 4, 256], BF16, st) for i in range(2)]
        gk4 = [k.sb("ml_gk4%d" % i, [128, 4, 256], BF16, st) for i in range(2)]
        y8 = [k.sb("ml_y8%d" % i, [128, 8, LC], F32, st) for i in range(2)]
        yst = [k.sb("ml_yst%d" % i, [128, 8, LC], BF16, st) for i in range(2)]
        cps = k.ps("ml_cps", [128, 12], F32, st)
        mb4 = k.ps("ml_mb", [128, 4, 128], F32, st)
        sps4 = k.ps("ml_sps", [128, 4, 128], F32, st)
        tps4 = k.ps("ml_tps", [128, 8, 128], BF16, st)
        numA = k.ps("ml_numA", [128, 257], F32, st)
        numB = k.ps("ml_numB", [128, 257], F32, st)
        cups = [k.ps("ml_cups%d" % i, [128, 257], F32, st) for i in range(2)]
        identb = self.cst_b
        def ml_loads(c):
            t0, t1 = c * LC, (c + 1) * LC
            qk_, ex_, kt_, va_ = qk[c % 2], ex[c % 2], ktm[c % 2], vaug[c % 2]
            k.dma("sp", qk_[:], FM[0:16, :, t0:t1].rearrange("c p t -> p c t"), writes=[qk_])
            k.dma("sp", ex_[:], FM[16:40, :, t0:t1].rearrange("c p t -> p c t"), writes=[ex_])
            k.dma("sp", kt_[0:LC, :], TMs[0, t0:t1, :], writes=[kt_])
            k.dma("sp", va_[0:LC, :, 0:256], TMs[1, t0:t1, :].rearrange("t (h e) -> t h e", h=4), writes=[va_])

        ml_loads(0)
        for c in range(nch):
            t0, t1 = c * LC, (c + 1) * LC
            qk_ = qk[c % 2]
            ex_ = ex[c % 2]
            kt_ = ktm[c % 2]
            va_ = vaug[c % 2]
            if c + 1 < nch:
                ml_loads(c + 1)
            co = cols[c % 2]
            for qi, src in enumerate((ig, fg, Bt)):
                k.tr(cps, cps[0:LC, qi * 4:(qi + 1) * 4], src[0:4, t0:t1], self.cst_f[0:4, 0:4], reads=[src, self.cst_f])
            k.op("dve", lambda e: e.tensor_copy(out=co[0:LC, :], in_=cps[0:LC, :]), reads=[cps], writes=[co])
            s4 = sm4[c % 2]
            k.op("dve", lambda e: e.tensor_tensor(out=s4[0:LC, 0:4], in0=MendB[0:LC, c, :], in1=co[0:LC, 4:8], op=ALU.subtract), reads=[MendB, co], writes=[s4])
            k.op("dve", lambda e: e.tensor_tensor(out=s4[0:LC, 4:8], in0=co[0:LC, 0:4], in1=nMendB[0:LC, c + 1, :], op=ALU.add), reads=[nMendB, co], writes=[s4])
            k.op("dve", lambda e: e.tensor_scalar(out=s4[0:LC, 8:12], in0=co[0:LC, 8:12], scalar1=-1.0, scalar2=None, op0=ALU.mult), reads=[co], writes=[s4])
            k.op("act", lambda e: e.activation(out=s4[0:LC, 0:12], in_=s4[0:LC, 0:12], func=AF.Exp), reads=[s4], writes=[s4])
            ys = yst[c % 2]
            w_, sm_, nd_, sc, hn_, gk_, y_ = w4[c % 2], smb4[c % 2], nd4[c % 2], sc1[c % 2], hn4[c % 2], gk4[c % 2], y8[c % 2]
            for h in range(4):
                k.mm(mb4, mb4[:, h, 0:LC], [(sel[:, h, :], fg[0:4, t0:t1])], reads=[sel, fg])
            k.op("dve", lambda e: e.tensor_tensor(out=w_[0:LC, :, 0:LC], in0=negm4[0:LC, :, 0:LC], in1=mb4[0:LC, :, 0:LC], op=ALU.subtract), reads=[negm4, mb4], writes=[w_])
            for h in range(4):
                k.op("act", lambda e, h=h: e.activation(out=w_[0:LC, h, 0:LC], in_=w_[0:LC, h, 0:LC], func=AF.Exp, bias=co[0:LC, h:h + 1]), reads=[w_, co], writes=[w_])
            for h in range(4):
                k.mm(sps4, sps4[0:LC, h, 0:LC], [(qk_[:, 8 + 2 * h + ec, :], qk_[:, 2 * h + ec, :]) for ec in range(2)], reads=[qk_])
            k.op("dve", lambda e: e.tensor_tensor(out=sm_[0:LC, :, 0:LC], in0=sps4[0:LC, :, 0:LC], in1=w_[0:LC, :, 0:LC], op=ALU.mult), reads=[sps4, w_], writes=[sm_])
            for h in range(4):
                k.op("act", lambda e, h=h: e.activation(out=gk_[0:LC, h, :], in_=kt_[0:LC, h * 256:(h + 1) * 256], func=AF.Copy, scale=s4[0:LC, 4 + h:5 + h]), reads=[kt_, s4], writes=[gk_])
            for h in range(4):
                k.mm(numA, numA[0:LC, :], [(sm_[0:LC, h, 0:LC], va_[0:LC, h, :])], reads=[sm_, va_])
                k.mm(numB, numB[0:LC, :], [(qk_[:, 2 * h + dc, :], Cb[:, h, dc, :]) for dc in range(2)], reads=[qk_, Cb])
                nb_ = k.ring("ml_nbs", nbs)
                k.op("act", lambda e, nb_=nb_, h=h: e.activation(out=nb_[0:LC, :], in_=numB[0:LC, :], func=AF.Copy, scale=s4[0:LC, h:h + 1]), reads=[numB, s4], writes=[nb_])
                k.op("dve", lambda e, nb_=nb_, h=h: e.tensor_tensor(out=nd_[0:LC, h, :], in0=numA[0:LC, :], in1=nb_[0:LC, :], op=ALU.add), reads=[numA, nb_], writes=[nd_])
            k.op("pool", lambda e: e.memset(sc[:], 0.0), writes=[sc])
            k.op("act", lambda e: e.activation(out=sc[0:LC, 0:4], in_=nd_[0:LC, :, 256], func=AF.Abs), reads=[nd_, sc], writes=[sc])
            k.op("dve", lambda e: e.tensor_tensor(out=sc[0:LC, 0:4], in0=sc[0:LC, 0:4], in1=s4[0:LC, 8:12], op=ALU.max), reads=[sc, s4], writes=[sc])
            k.op("dve", lambda e: e.reciprocal(out=sc[0:LC, 4:8], in_=sc[0:LC, 0:4]), reads=[sc], writes=[sc])
            for h in range(4):
                k.op("act", lambda e, h=h: e.activation(out=junk[0:LC, :], in_=nd_[0:LC, h, 0:256], func=AF.Square, scale=sc[0:LC, 4 + h:5 + h], accum_out=sc[0:LC, 8 + h:9 + h]), reads=[nd_, sc], writes=[junk, sc])
            k.op("act", lambda e: e.activation(out=sc[0:LC, 12:16], in_=sc[0:LC, 8:12], func=AF.Ln, scale=1.0 / 256.0, bias=EPS), reads=[sc], writes=[sc])
            k.op("act", lambda e: e.activation(out=sc[0:LC, 12:16], in_=sc[0:LC, 12:16], func=AF.Exp, scale=-0.5), reads=[sc], writes=[sc])
            k.op("dve", lambda e: e.tensor_tensor(out=sc[0:LC, 16:20], in0=sc[0:LC, 12:16], in1=sc[0:LC, 4:8], op=ALU.mult), reads=[sc], writes=[sc])
            for h in range(4):
                k.op("dve", lambda e, h=h: e.scalar_tensor_tensor(out=hn_[0:LC, h, :], in0=nd_[0:LC, h, 0:256], scalar=sc[0:LC, 16 + h:17 + h], in1=ghr[0:LC, h * 256:(h + 1) * 256], op0=ALU.mult, op1=ALU.mult), reads=[nd_, sc, ghr], writes=[hn_])
            for h in range(4):
                for dc in range(2):
                    cu = cups[dc]
                    k.mm(cu, cu[:, :], [(gk_[0:LC, h, dc * 128:(dc + 1) * 128], va_[0:LC, h, :])], reads=[gk_, va_])
                    k.op("dve", lambda e, dc=dc, cu=cu, h=h: e.scalar_tensor_tensor(out=C[:, h, dc, :], in0=C[:, h, dc, :], scalar=decB[:, c, h:h + 1], in1=cu[:, :], op0=ALU.mult, op1=ALU.add), reads=[C, decB, cu], writes=[C])
            k.op("act", lambda e: e.activation(out=Cb[:], in_=C[:], func=AF.Copy), reads=[C], writes=[Cb])
            for h in range(4):
                for ec in range(2):
                    k.tr(tps4, tps4[:, 2 * h + ec, 0:LC], hn_[0:LC, h, ec * 128:(ec + 1) * 128], identb[0:LC, 0, 0:LC], reads=[hn_, identb])
            k.op("dve", lambda e: e.tensor_tensor(out=y_[:], in0=tps4[:, :, 0:LC], in1=ex_[:, 8:16, :], op=ALU.mult), reads=[tps4, ex_], writes=[y_])
            k.op("pool", lambda e: e.tensor_tensor(out=y_[:], in0=y_[:], in1=ex_[:, 0:8, :], op=ALU.add), reads=[y_, ex_], writes=[y_])
            k.op("pool", lambda e: e.tensor_tensor(out=ys[:], in0=y_[:], in1=ex_[:, 16:24, :], op=ALU.mult), reads=[y_, ex_], writes=[ys])
            k.dma("pool", YT[0:8, :, t0:t1].rearrange("c p t -> p c t"), ys[:], reads=[ys])
        co_, no_, mo_ = (O["mlc_p"], O["mln_p"], O["mlm_p"]) if isp else (O["mlc_s"], O["mln_s"], O["mlm_s"])
        for h in range(4):
            k.dma("pool", co_[j, h].rearrange("(c p) e -> p c e", p=128), C[:, h, :, 0:256], reads=[C])
            k.dma("pool", no_[j, h].rearrange("(c p o) -> p c o", p=128, o=1), C[:, h, :, 256:257], reads=[C], allow_slow_non_contiguous=True)
        k.dma("pool", mo_[j].rearrange("(h o) -> h o", o=1), Bt[:, T - 1:T], reads=[Bt])


def make_consts():
    c = np.zeros((128, 128 * 5 + 2048), np.float32)
    idx = np.arange(128)
    c[:, 0:128] = np.eye(128)
    c[:, 128:256] = (idx[:, None] >= idx[None, :])
    c[:, 256:384] = 1.0
    c[:, 384:512] = (idx[:, None] <= idx[None, :])
    q = np.arange(512)
    for jj in range(4):
        c[:, 640 + jj * 512: 640 + (jj + 1) * 512] = np.where((128 * jj + idx[:, None]) < q[None, :], 0.0, -30000.0)
    return c


def make_smask(PAST, TS, koff):
    m = np.zeros((128, 3, 32), np.float32)
    idx = np.arange(128)
    q = np.arange(32)
    nb = (koff + PAST + TS) // 128
    key = (nb - 1) * 128 + idx - koff
    m[:, 0, :TS] = np.where((key[:, None] < PAST) | ((key[:, None] - PAST) < q[None, :TS]), 0.0, -30000.0)
    fr = koff // 128
    key = fr * 128 + idx - koff
    m[:, 1, :TS] = np.where(key[:, None] >= 0, 0.0, -30000.0)
    m[:, 2, :] = -30000.0
    return m.reshape(128, 96)


_CACHE = {}


def _prep(inp):
    x_prompt = np.asarray(inp["x_prompt"], np.float32)
    x_sample = np.asarray(inp["x_sample"], np.float32)
    B, T, _ = x_prompt.shape
    SBN, TS, _ = x_sample.shape
    DEPTH = inp["g_pre"].shape[0]
    PAST = inp["cache_sb_k"].shape[2]
    key = (T, TS, PAST, DEPTH)
    if key not in _CACHE:
        p = Prog(T, TS, PAST, DEPTH)
        p.build()
        _CACHE[key] = p
    p = _CACHE[key]
    NSB, NML, NCV = p.NSB, p.NML, p.NCV
    f = lambda a: np.ascontiguousarray(np.asarray(a, np.float32))

    def rep(a):
        a = f(a)
        return np.ascontiguousarray(np.broadcast_to(a[:, None, :], (a.shape[0], 128, a.shape[1])))

    def colT(a):
        a = f(a)
        return np.ascontiguousarray(a.reshape(a.shape[0], 8, 128).transpose(0, 2, 1))

    def convT(a):
        a = f(a)
        return np.ascontiguousarray(a.reshape(a.shape[0], a.shape[1], 8, 128).transpose(0, 3, 2, 1))

    NBs = (PAST + TS + 127) // 128
    if NBs % 2:
        NBs += 1
    koff = NBs * 128 - (PAST + TS)
    consts = make_consts()
    smask = make_smask(PAST, TS, koff)
    nml = max(NML, 1)
    ncv = max(NCV, 1)

    def orz(a, shape):
        a = f(a)
        if a.shape[0] == 0:
            return np.zeros(shape, np.float32)
        return a

    convml = np.concatenate([f(inp["conv_ml_w"]), f(inp["conv_ml_b"])[:, None, :]], axis=1) if NML else np.zeros((1, 5, D), np.float32)
    gcol = np.ascontiguousarray(np.stack([f(inp["b_ig_ml"]), f(inp["b_fg_ml"])], axis=2)) if NML else np.zeros((1, 4, 2), np.float32)
    ii = np.arange(128)
    negmask = np.where(ii[:, None] <= ii[None, :], 0.0, -30000.0).astype(np.float32)
    sel4 = np.zeros((4, 4, 128), np.float32)
    for hh in range(4):
        sel4[hh, hh, :] = 1.0
    sel4 = sel4.reshape(4, 512)
    common = {
        "memp": None, "gpre_r": rep(inp["g_pre"]), "gpost_r": rep(inp["g_post"]), "gmem_r": rep(inp["g_mem"]),
        "wmemkv": f(inp["w_mem_kv"]), "wout": f(inp["w_out"]), "winsb": f(inp["w_in_sb"]),
        "winml": orz(inp["w_in_ml"], (1, D, 5128)), "wincv": orz(inp["w_in_cv"], (1, D, 5120)),
        "convmlT": convT(convml), "wq": orz(inp["wq_ml"], (1, 4, 256, 256)), "wk": orz(inp["wk_ml"], (1, 4, 256, 256)),
        "gcol": gcol, "ghead_r": rep(orz(inp["g_head_ml"], (1, D))), "negmask": negmask, "sel4": sel4, "skipT": colT(orz(inp["skip_ml"], (1, D))),
        "convcvT": convT(orz(inp["conv_cv_w"], (1, 3, D))), "consts": consts, "smask": smask,
    }
    in_maps = []
    ncores = 8
    for c in range(ncores):
        bp = c % B
        bs = c % SBN
        m = dict(common)
        m["xp"] = f(x_prompt[bp])
        m["xs"] = f(x_sample[bs])
        m["csk"] = f(inp["cache_sb_k"][:, bs])
        m["csv"] = f(inp["cache_sb_v"][:, bs])
        m["smc"] = orz(np.asarray(inp["state_ml_c"])[:, bs], (1, 4, 256, 256))
        m["smn"] = orz(np.asarray(inp["state_ml_n"])[:, bs], (1, 4, 256))
        m["smm"] = orz(np.asarray(inp["state_ml_m"])[:, bs], (1, 4))
        m["smconvT"] = convT(orz(np.asarray(inp["state_ml_conv"])[:, bs], (1, 3, D)))
        m["scvT"] = convT(orz(np.asarray(inp["state_cv_conv"])[:, bs], (1, 2, D)))
        m["cmk"] = f(inp["cache_mem_k"][:, bs])
        m["cmv"] = f(inp["cache_mem_v"][:, bs])
        m["memp"] = f(inp["mem_prompt"][bp])
        in_maps.append(m)
    return p, in_maps, B, SBN


def kernel(**inp):
    p, in_maps, B, SBN = _prep(inp)
    NML, NCV = p.NML, p.NCV
    res = run_bass_kernel_spmd(p.nc, in_maps, core_ids=list(range(len(in_maps))))
    R = res.results

    def gp(name, axis_b):
        return np.stack([np.asarray(R[c][name], np.float32) for c in range(B)], axis=axis_b)

    def gs(name, axis_b):
        return np.stack([np.asarray(R[c][name], np.float32) for c in range(SBN)], axis=axis_b)

    outs = (
        gp("yp", 0), gs("ys", 0),
        gp("sbk_p", 1), gp("sbv_p", 1),
        gp("mlc_p", 1)[:NML], gp("mln_p", 1)[:NML], gp("mlm_p", 1)[:NML], gp("mlconv_p", 1)[:NML],
        gp("cv_p", 1)[:NCV],
        gp("memk_p", 1), gp("memv_p", 1),
        gs("sbk_s", 1), gs("sbv_s", 1),
        gs("mlc_s", 1)[:NML], gs("mln_s", 1)[:NML], gs("mlm_s", 1)[:NML], gs("mlconv_s", 1)[:NML],
        gs("cv_s", 1)[:NCV],
    )
    return outs
```

```python
import contextlib
import math
import numpy as np
import concourse.bass as bass
import concourse.mybir as mybir
from concourse.bass_utils import run_bass_kernel_spmd

F32 = mybir.dt.float32
BF16 = mybir.dt.bfloat16
AF = mybir.ActivationFunctionType
ALU = mybir.AluOpType
AX = mybir.AxisListType

NDS = 48
D = 1024
EPS = 1e-6


class Tile:
    __slots__ = ("ap", "w", "r", "name")

    def __init__(self, ap, name=""):
        self.ap = ap
        self.w = None
        self.r = {}
        self.name = name

    def __getitem__(self, idx):
        return self.ap[idx]


class KB:
    def __init__(self, nc, stack):
        self.nc = nc
        self.stack = stack
        self.E = {"pe": nc.tensor, "act": nc.scalar, "dve": nc.vector, "pool": nc.gpsimd, "sp": nc.sync}
        self.csem = {e: stack.enter_context(nc.semaphore("c_" + e)) for e in ("pe", "act", "dve", "pool")}
        self.ccnt = {e: 0 for e in self.csem}
        self.dsem = [stack.enter_context(nc.semaphore("d%d" % i)) for i in range(NDS)]
        self.dcnt = [0] * NDS
        self.dnext = {"sp": 0, "pool": NDS // 2}
        self.seen = {e: {} for e in self.E}
        self.ninst = 0
        self.rr = {}

    def _wait(self, eng, tok):
        sem, val, key, kind = tok
        if self.seen[eng].get(key, 0) >= val:
            return
        self.E[eng].wait_ge(sem, val)
        self.seen[eng][key] = val
        self.ninst += 1

    def _deps(self, eng, reads, writes):
        toks = []
        for t in reads:
            if t.w is not None:
                toks.append(t.w)
        for t in writes:
            if t.w is not None:
                toks.append(t.w)
            toks.extend(t.r.values())
        for tok in toks:
            if eng == "pe" and tok[3] == "pe":
                continue
            self._wait(eng, tok)

    def _mark(self, tok, reads, writes):
        for t in reads:
            old = t.r.get(tok[2])
            if old is None or old[1] < tok[1]:
                t.r[tok[2]] = tok
        for t in writes:
            t.w = tok
            t.r = {}

    def op(self, eng, fn, reads=(), writes=()):
        self._deps(eng, reads, writes)
        ins = fn(self.E[eng])
        self.ccnt[eng] += 1
        ins.then_inc(self.csem[eng], 1)
        tok = (self.csem[eng], self.ccnt[eng], eng, eng)
        self._mark(tok, reads, writes)
        self.ninst += 1
        return ins

    def mm(self, out_tile, out_ap, pairs, reads, start=True, stop=True):
        writes = [out_tile]
        self._deps("pe", reads, writes)
        n = len(pairs)
        ins = None
        for i, (l, r) in enumerate(pairs):
            ins = self.nc.tensor.matmul(out_ap, l, r, start=(start and i == 0), stop=(stop and i == n - 1))
            self.ninst += 1
        self.ccnt["pe"] += 1
        ins.then_inc(self.csem["pe"], 1)
        tok = (self.csem["pe"], self.ccnt["pe"], "pe", "pe")
        self._mark(tok, reads, writes)
        return ins

    def tr(self, out_tile, out_ap, in_ap, ident_ap, reads):
        self._deps("pe", reads, [out_tile])
        ins = self.nc.tensor.transpose(out_ap, in_ap, ident_ap)
        self.ccnt["pe"] += 1
        ins.then_inc(self.csem["pe"], 1)
        tok = (self.csem["pe"], self.ccnt["pe"], "pe", "pe")
        self._mark(tok, reads, [out_tile])
        self.ninst += 1
        return ins

    def dma(self, q, out_ap, in_ap, reads=(), writes=(), **kw):
        self._deps(q, reads, writes)
        i = self.dnext[q]
        base = 0 if q == "sp" else NDS // 2
        self.dnext[q] = base + (i - base + 1) % (NDS // 2)
        if self.dcnt[i] > 0:
            self._wait(q, (self.dsem[i], 16 * self.dcnt[i], ("d", i), "dma"))
        self.E[q].dma_start(out=out_ap, in_=in_ap, **kw).then_inc(self.dsem[i], 16)
        self.dcnt[i] += 1
        tok = (self.dsem[i], 16 * self.dcnt[i], ("d", i), "dma")
        self._mark(tok, reads, writes)
        self.ninst += 1

    def barrier(self, engines=("pe", "act", "dve", "pool", "sp")):
        for e in engines:
            for c in self.csem:
                if self.ccnt[c] > 0:
                    self._wait(e, (self.csem[c], self.ccnt[c], c, c))
            for i in range(NDS):
                if self.dcnt[i] > 0:
                    self._wait(e, (self.dsem[i], 16 * self.dcnt[i], ("d", i), "dma"))

    def sb(self, name, shape, dtype, stack=None):
        self.uid = getattr(self, "uid", 0) + 1
        name = "%s_u%d" % (name, self.uid)
        t = (stack or self.stack).enter_context(self.nc.sbuf_tensor(name, list(shape), dtype))
        return Tile(t, name)

    def ps(self, name, shape, dtype=F32, stack=None):
        self.uid = getattr(self, "uid", 0) + 1
        name = "%s_u%d" % (name, self.uid)
        t = (stack or self.stack).enter_context(self.nc.psum_tensor(name, list(shape), dtype))
        return Tile(t, name)

    def ring(self, key, tiles):
        i = self.rr.get(key, 0)
        self.rr[key] = i + 1
        return tiles[i % len(tiles)]


class Prog:
    def __init__(self, T, TS, PAST, DEPTH):
        self.T, self.TS, self.PAST, self.DEPTH = T, TS, PAST, DEPTH
        self.NSB, self.NML, self.NCV = (DEPTH + 2) // 3, (DEPTH + 1) // 3, DEPTH // 3
        self.nc = bass.Bass("TRN2", target_bir_lowering=False)
        self.in_shapes = {}
        self.out_shapes = {}

    def din(self, name, shape, dt=F32):
        self.in_shapes[name] = tuple(shape)
        return self.nc.dram_tensor(name, list(shape), dt, kind="ExternalInput").ap()

    def dout(self, name, shape, dt=F32):
        self.out_shapes[name] = tuple(shape)
        return self.nc.dram_tensor(name, list(shape), dt, kind="ExternalOutput").ap()

    def dscr(self, name, shape, dt=F32):
        return self.nc.dram_tensor(name, list(shape), dt, kind="Internal").ap()

    def build(self):
        T, TS, PAST, DEPTH = self.T, self.TS, self.PAST, self.DEPTH
        NSB, NML, NCV = self.NSB, self.NML, self.NCV
        I = {}
        I["xp"] = self.din("xp", [T, D])
        I["xs"] = self.din("xs", [TS, D])
        I["csk"] = self.din("csk", [NSB, PAST, 8, 128])
        I["csv"] = self.din("csv", [NSB, PAST, 8, 128])
        I["smc"] = self.din("smc", [max(NML, 1), 4, 256, 256])
        I["smn"] = self.din("smn", [max(NML, 1), 4, 256])
        I["smm"] = self.din("smm", [max(NML, 1), 4])
        I["smconvT"] = self.din("smconvT", [max(NML, 1), 128, 8, 3])
        I["scvT"] = self.din("scvT", [max(NCV, 1), 128, 8, 2])
        I["cmk"] = self.din("cmk", [DEPTH, 256, 4, 128])
        I["cmv"] = self.din("cmv", [DEPTH, 256, 4, 128])
        I["memp"] = self.din("memp", [256, D])
        I["gpre_r"] = self.din("gpre_r", [DEPTH, 128, D])
        I["gpost_r"] = self.din("gpost_r", [DEPTH, 128, D])
        I["gmem_r"] = self.din("gmem_r", [DEPTH, 128, D])
        I["wmemkv"] = self.din("wmemkv", [DEPTH, D, D])
        I["wout"] = self.din("wout", [DEPTH, 1536, D])
        I["winsb"] = self.din("winsb", [NSB, D, 5120])
        I["winml"] = self.din("winml", [max(NML, 1), D, 5128])
        I["wincv"] = self.din("wincv", [max(NCV, 1), D, 5120])
        I["convmlT"] = self.din("convmlT", [max(NML, 1), 128, 8, 5])
        I["wq"] = self.din("wq", [max(NML, 1), 4, 256, 256])
        I["wk"] = self.din("wk", [max(NML, 1), 4, 256, 256])
        I["gcol"] = self.din("gcol", [max(NML, 1), 4, 2])
        I["ghead_r"] = self.din("ghead_r", [max(NML, 1), 128, D])
        I["negmask"] = self.din("negmask", [128, 128])
        I["sel4"] = self.din("sel4", [4, 512])
        I["skipT"] = self.din("skipT", [max(NML, 1), 128, 8])
        I["convcvT"] = self.din("convcvT", [max(NCV, 1), 128, 8, 3])
        I["consts"] = self.din("consts", [128, 128 * 5 + 4 * 512 * 1])
        I["smask"] = self.din("smask", [128, 3 * 32])
        O = {}
        O["yp"] = self.dout("yp", [T, D])
        O["ys"] = self.dout("ys", [TS, D])
        O["sbk_p"] = self.dout("sbk_p", [NSB, T, 8, 128])
        O["sbv_p"] = self.dout("sbv_p", [NSB, T, 8, 128])
        O["mlc_p"] = self.dout("mlc_p", [max(NML, 1), 4, 256, 256])
        O["mln_p"] = self.dout("mln_p", [max(NML, 1), 4, 256])
        O["mlm_p"] = self.dout("mlm_p", [max(NML, 1), 4])
        O["mlconv_p"] = self.dout("mlconv_p", [max(NML, 1), 3, D])
        O["cv_p"] = self.dout("cv_p", [max(NCV, 1), 2, D])
        O["memk_p"] = self.dout("memk_p", [DEPTH, 256, 4, 128])
        O["memv_p"] = self.dout("memv_p", [DEPTH, 256, 4, 128])
        O["sbk_s"] = self.dout("sbk_s", [NSB, TS, 8, 128])
        O["sbv_s"] = self.dout("sbv_s", [NSB, TS, 8, 128])
        O["mlc_s"] = self.dout("mlc_s", [max(NML, 1), 4, 256, 256])
        O["mln_s"] = self.dout("mln_s", [max(NML, 1), 4, 256])
        O["mlm_s"] = self.dout("mlm_s", [max(NML, 1), 4])
        O["mlconv_s"] = self.dout("mlconv_s", [max(NML, 1), 3, D])
        O["cv_s"] = self.dout("cv_s", [max(NCV, 1), 2, D])
        self.I, self.O = I, O
        self.SP = dict(name="p", T=T, xin=I["xp"], yout=O["yp"], xres=self.dscr("xres_p", [T, D]),
                       FM=self.dscr("fm_p", [40, 128, T], BF16), YT=self.dscr("yt_p", [12, 128, T], BF16),
                       TMs=self.dscr("tm_p", [2, T, D], BF16), GT=self.dscr("gt_p", [8, T], F32))
        self.SS = dict(name="s", T=TS, xin=I["xs"], yout=O["ys"], xres=self.dscr("xres_s", [TS, D]),
                       FM=self.dscr("fm_s", [40, 128, TS], BF16), YT=self.dscr("yt_s", [12, 128, TS], BF16),
                       TMs=self.dscr("tm_s", [2, TS, D], BF16), GT=self.dscr("gt_s", [8, TS], F32))
        with contextlib.ExitStack() as st:
            k = KB(self.nc, st)
            self.k = k
            self.cst_f = k.sb("cst_f", [128, 128], F32)
            self.cst_b = k.sb("cst_b", [128, 4, 128], BF16)
            self.msk = k.sb("msk", [128, 4, 512], BF16)
            self.smsk = k.sb("smsk", [128, 3, 32], BF16)
            with contextlib.ExitStack() as s2:
                tmp = k.sb("cst_tmp", [128, 128 * 5 + 2048], F32, s2)
                tmp2 = k.sb("cst_tmp2", [128, 96], F32, s2)
                k.dma("sp", tmp[:], I["consts"][:, :], writes=[tmp])
                k.dma("sp", tmp2[:], I["smask"][:, :], writes=[tmp2])
                k.op("dve", lambda e: e.tensor_copy(out=self.cst_f[:], in_=tmp[:, 0:128]), reads=[tmp], writes=[self.cst_f])
                k.op("dve", lambda e: e.tensor_copy(out=self.cst_b[:].rearrange("p a b -> p (a b)"), in_=tmp[:, 0:512]), reads=[tmp], writes=[self.cst_b])
                k.op("dve", lambda e: e.tensor_copy(out=self.msk[:].rearrange("p a b -> p (a b)"), in_=tmp[:, 640:640 + 2048]), reads=[tmp], writes=[self.msk])
                k.op("dve", lambda e: e.tensor_copy(out=self.smsk[:].rearrange("p a b -> p (a b)"), in_=tmp2[:]), reads=[tmp2], writes=[self.smsk])
                k.barrier()
            for li in range(DEPTH):
                kind, j = li % 3, li // 3
                self.layer(li, kind, j)
            k.barrier()
        return self.nc

    def load_w_gen(self, k, st, dst, src, ncols, nkc, name, CP=256):
        stg = [k.sb("%s_stg%d" % (name, i), [128, nkc, CP], F32, st) for i in range(2)]
        srcv = src.rearrange("(c p) n -> p c n", p=128)
        c0 = 0
        i = 0
        while c0 < ncols:
            cw = min(CP, ncols - c0)
            s = stg[i % 2]
            k.dma("sp", s[:, :, 0:cw], srcv[:, :, c0:c0 + cw], writes=[s])
            eng = "pool" if i % 2 == 0 else "act"
            if eng == "pool":
                k.op("pool", lambda e, s=s, c0=c0, cw=cw: e.tensor_copy(out=dst[:, :, c0:c0 + cw], in_=s[:, :, 0:cw]), reads=[s], writes=[dst])
            else:
                k.op("act", lambda e, s=s, c0=c0, cw=cw: e.activation(out=dst[:, :, c0:c0 + cw], in_=s[:, :, 0:cw], func=AF.Copy), reads=[s], writes=[dst])
            c0 += cw
            i += 1
            yield

    def load_w(self, k, st, dst, src, ncols, nkc, name, CP=256):
        for _ in self.load_w_gen(k, st, dst, src, ncols, nkc, name, CP):
            pass

    def layer(self, li, kind, j):
        k = self.k
        I, O = self.I, self.O
        last = (li == self.DEPTH - 1)
        with contextlib.ExitStack() as st:
            mk = {"p": k.sb("mkT_p", [128, 4, 256], BF16, st), "s": k.sb("mkT_s", [128, 4, 256], BF16, st)}
            mv = {"p": k.sb("mv_p", [128, 2, 512], BF16, st), "s": k.sb("mv_s", [128, 2, 512], BF16, st)}
            self.memkv_phase(li, mk, mv)
            k.barrier()
            with contextlib.ExitStack() as s1:
                ncols = 5128 if kind == 1 else 5120
                if getattr(self, "wb_pref", None) is not None:
                    Wb = self.wb_pref
                else:
                    Wb = k.sb("Wb", [128, 8, ncols], BF16, s1)
                    wsrc = (I["winsb"], I["winml"], I["wincv"])[kind][j]
                    with contextlib.ExitStack() as sw:
                        self.load_w(k, sw, Wb, wsrc, ncols, 8, "win")
                        k.barrier()
                for stm in (self.SS, self.SP):
                    with contextlib.ExitStack() as s3:
                        self.p1(stm, li, kind, j, Wb, mk[stm["name"]], mv[stm["name"]], s3)
                        k.barrier()
        if getattr(self, "wb_pref", None) is not None:
            self.wb_stack.close()
            self.wb_pref = None
        if kind == 0:
            for stm in (self.SS, self.SP):
                with contextlib.ExitStack() as s3:
                    if stm is self.SS:
                        self.p2_sb_sample(stm, li, j, s3)
                    else:
                        self.p2_sb(stm, li, j, s3)
                    k.barrier()
        elif kind == 1:
            for stm in (self.SS, self.SP):
                with contextlib.ExitStack() as s3:
                    self.p2_ml(stm, li, j, s3)
                    k.barrier()
        bg = None
        if not last:
            nkind, nj = (li + 1) % 3, (li + 1) // 3
            nncols = 5128 if nkind == 1 else 5120
            self.wb_stack = contextlib.ExitStack()
            self.wb_pref = k.sb("Wbn", [128, 8, nncols], BF16, self.wb_stack)
            self.wb_stg_stack = contextlib.ExitStack()
            nsrc = (I["winsb"], I["winml"], I["wincv"])[nkind][nj]
            bg = self.load_w_gen(k, self.wb_stg_stack, self.wb_pref, nsrc, nncols, 8, "winn")
            next(bg, None)
        with contextlib.ExitStack() as s1:
            Wo = k.sb("Wo", [128, 12, D], BF16, s1)
            gpo = k.sb("gpo", [128, D], F32, s1)
            with contextlib.ExitStack() as sw:
                self.load_w(k, sw, Wo, I["wout"][li], D, 12, "wout")
                k.dma("sp", gpo[:], I["gpost_r"][li], writes=[gpo])
                k.barrier()
            for stm in (self.SS, self.SP):
                with contextlib.ExitStack() as s3:
                    self.p3(stm, li, Wo, gpo, last, s3, bg if stm is self.SP else None)
                    if stm is self.SP and bg is not None:
                        for _ in bg:
                            pass
                    k.barrier()
        if bg is not None:
            self.wb_stg_stack.close()

    def rms_front(self, k, xsrc_ap, np_, xt, junk, ss, hb, grep):
        k.dma("sp", xt[0:np_, :], xsrc_ap, writes=[xt])
        k.op("pool", lambda e: e.memset(ss[:], 0.0), writes=[ss])
        k.op("act", lambda e: e.activation(out=junk[0:np_, :], in_=xt[0:np_, :], func=AF.Square, accum_out=ss[0:np_, 0:1]), reads=[xt, ss], writes=[junk, ss])
        k.op("act", lambda e: e.activation(out=ss[0:np_, 1:2], in_=ss[0:np_, 0:1], func=AF.Ln, scale=1.0 / D, bias=EPS), reads=[ss], writes=[ss])
        k.op("act", lambda e: e.activation(out=ss[0:np_, 1:2], in_=ss[0:np_, 1:2], func=AF.Exp, scale=-0.5), reads=[ss], writes=[ss])
        k.op("dve", lambda e: e.scalar_tensor_tensor(out=hb[0:np_, :], in0=xt[0:np_, :], scalar=ss[0:np_, 1:2], in1=grep[0:np_, :], op0=ALU.mult, op1=ALU.mult), reads=[xt, ss, grep], writes=[hb])

    def to_fm(self, k, hb, np_, ptr, hT, col0):
        for kc in range(8):
            k.tr(ptr, ptr[:, kc, 0:np_], hb[0:np_, kc * 128:(kc + 1) * 128], self.cst_b[0:np_, 0, 0:np_], reads=[hb, self.cst_b])
        k.op("dve", lambda e: e.tensor_copy(out=hT[:, :, col0:col0 + np_], in_=ptr[:, :, 0:np_]), reads=[ptr], writes=[hT])

    def memkv_phase(self, li, mk, mv):
        k = self.k
        I, O = self.I, self.O
        with contextlib.ExitStack() as st:
            Wm = k.sb("Wm", [128, 8, D], BF16, st)
            gm = k.sb("gm", [128, D], F32, st)
            with contextlib.ExitStack() as sw:
                self.load_w(k, sw, Wm, I["wmemkv"][li], D, 8, "wmem")
                k.barrier()
            k.dma("sp", gm[:], I["gmem_r"][li], writes=[gm])
            xt = [k.sb("mxt%d" % i, [128, D], F32, st) for i in range(2)]
            junk = k.sb("mjunk", [128, D], BF16, st)
            ss = [k.sb("mss%d" % i, [128, 2], F32, st) for i in range(2)]
            hb = [k.sb("mhb%d" % i, [128, D], BF16, st) for i in range(2)]
            hT = k.sb("mhT", [128, 8, 256], BF16, st)
            ptr = k.ps("mptr", [128, 8, 128], BF16, st)
            acc = [k.ps("macc%d" % i, [128, 512], F32, st) for i in range(3)]
            og = [k.sb("mog%d" % i, [128, 512], F32, st) for i in range(2)]
            for s in range(2):
                self.rms_front(k, I["memp"][s * 128:(s + 1) * 128, :], 128, xt[s], junk, ss[s], hb[s], gm)
                self.to_fm(k, hb[s], 128, ptr, hT, s * 128)
            for s in range(2):
                for nb in range(2):
                    a = k.ring("macc", acc)
                    k.mm(a, a[:, :], [(hT[:, kc, s * 128:(s + 1) * 128], Wm[:, kc, nb * 512:(nb + 1) * 512]) for kc in range(8)], reads=[hT, Wm])
                    o = k.ring("mog", og)
                    k.op("act", lambda e, o=o, a=a: e.activation(out=o[:], in_=a[:], func=AF.Copy), reads=[a], writes=[o])
                    dst = (O["memk_p"], O["memv_p"])[nb][li, s * 128:(s + 1) * 128].rearrange("m h d -> m (h d)")
                    k.dma("pool", dst, o[:], reads=[o])
                    if nb == 1:
                        k.op("dve", lambda e, o=o, s=s: e.tensor_copy(out=mv["p"][:, s, :], in_=o[:]), reads=[o], writes=[mv["p"]])
            for h in range(4):
                a = k.ring("macc", acc)
                k.mm(a, a[:, 0:256], [(Wm[:, kc, h * 128:(h + 1) * 128], hT[:, kc, :]) for kc in range(8)], reads=[hT, Wm])
                k.op("dve", lambda e, a=a, h=h: e.tensor_copy(out=mk["p"][:, h, :], in_=a[:, 0:256]), reads=[a], writes=[mk["p"]])
            ck = k.sb("mck", [128, 2, 512], F32, st)
            cv = k.sb("mcv", [128, 2, 512], F32, st)
            k.dma("sp", ck[:], I["cmk"][li].rearrange("(s p) h d -> p s (h d)", p=128), writes=[ck])
            k.dma("sp", cv[:], I["cmv"][li].rearrange("(s p) h d -> p s (h d)", p=128), writes=[cv])
            k.op("dve", lambda e: e.tensor_copy(out=mv["s"][:], in_=cv[:]), reads=[cv], writes=[mv["s"]])
            for h in range(4):
                a = k.ring("macc", acc)
                for s in range(2):
                    k.tr(a, a[:, s * 128:(s + 1) * 128], ck[:, s, h * 128:(h + 1) * 128], self.cst_f[:], reads=[ck, self.cst_f])
                k.op("dve", lambda e, a=a, h=h: e.tensor_copy(out=mk["s"][:, h, :], in_=a[:, 0:256]), reads=[a], writes=[mk["s"]])

    def p1(self, stm, li, kind, j, Wb, mk, mv, st):
        k = self.k
        I, O = self.I, self.O
        T = stm["T"]
        isp = stm["name"] == "p"
        TT = min(256 if kind == 1 else 512, T)
        np_ = min(128, T)
        nsub = TT // np_
        ntt = T // TT
        xsrc = stm["xin"] if li == 0 else stm["xres"]
        FM, YT = stm["FM"], stm["YT"]
        if kind == 1:
            self.tmob = [k.sb("tmob%d" % i, [128, 512], BF16, st) for i in range(2)]
        gpre = k.sb("gpre", [128, D], F32, st)
        k.dma("sp", gpre[:], I["gpre_r"][li], writes=[gpre])
        xt = [k.sb("xt%d" % i, [128, D], F32, st) for i in range(3)]
        junk = k.sb("junk", [128, D], BF16, st)
        ss = [k.sb("ss%d" % i, [128, 2], F32, st) for i in range(3)]
        hb = [k.sb("hb%d" % i, [128, D], BF16, st) for i in range(2)]
        hT = [k.sb("hT%d" % i, [128, 8, TT], BF16, st) for i in range(2)]
        ptr = k.ps("ptr", [128, 8, 128], BF16, st)
        acc = [k.ps("acc%d" % i, [128, 512], F32, st) for i in range(4)]
        memS = k.ps("memS", [128, 2, 512], F32, st)
        nring = 2 if kind == 2 else 3
        stg = [k.sb("stg%d" % i, [128, 4, TT], BF16, st) for i in range(nring)]
        tmo = [k.sb("tmo%d" % i, [128, 512], F32, st) for i in range(nring)]
        mq = k.sb("mq", [128, 4, TT], BF16, st)
        zm = k.sb("zm", [128, 4, TT], BF16, st)
        eT = k.sb("eT", [128, 2, TT], BF16, st)
        rden = k.sb("rden", [128, TT], F32, st)
        ymem = k.sb("ymem", [128, 4, TT], BF16, st)
        if kind == 0:
            mqc, zc, zmc = 3072, 3584, 4608
        elif kind == 1:
            mqc, zc, zmc = 3080, 3592, 4616
        else:
            mqc, zc, zmc = 3072, 3584, 4608

        def fm_chunk(hTt, col):
            a = k.ring("acc", acc)
            k.mm(a, a[:, 0:TT], [(Wb[:, kc, col:col + 128], hTt[:, kc, :]) for kc in range(8)], reads=[hTt, Wb])
            return a

        def fm_group(hTt, col0, nch, func, tok0, dst_fm0=None, dst_tile=None, scale=1.0):
            t = dst_tile if dst_tile is not None else k.ring("stg", stg)
            for c in range(nch):
                a = fm_chunk(hTt, col0 + c * 128)
                if func is None:
                    eng = k.ring("evac", ["dve", "act"])
                    if eng == "dve":
                        k.op("dve", lambda e, a=a, c=c: e.tensor_copy(out=t[:, c, :], in_=a[:, 0:TT]), reads=[a], writes=[t])
                    else:
                        k.op("act", lambda e, a=a, c=c: e.activation(out=t[:, c, :], in_=a[:, 0:TT], func=AF.Copy), reads=[a], writes=[t])
                else:
                    k.op("act", lambda e, a=a, c=c: e.activation(out=t[:, c, :], in_=a[:, 0:TT], func=func, scale=scale), reads=[a], writes=[t])
            if dst_fm0 is not None:
                k.dma("pool", FM[dst_fm0:dst_fm0 + nch, :, tok0:tok0 + TT].rearrange("c p t -> p c t"), t[:, 0:nch, :], reads=[t])
            return t

        def tm_block(hTt, s, col0, ncols):
            a = k.ring("acc", acc)
            k.mm(a, a[0:np_, 0:ncols], [(hTt[:, kc, s * np_:(s + 1) * np_], Wb[:, kc, col0:col0 + ncols]) for kc in range(8)], reads=[hTt, Wb])
            return a

        def mem_attn(tok0):
            sc = 1.0 / math.sqrt(128.0)
            for h in range(4):
                for mb in range(2):
                    k.mm(memS, memS[:, mb, 0:TT], [(mk[:, h, mb * 128:(mb + 1) * 128], mq[:, h, :])], reads=[mk, mq])
                k.op("act", lambda e: e.activation(out=eT[:], in_=memS[:, :, 0:TT], func=AF.Exp, scale=sc), reads=[memS], writes=[eT])
                den = k.ring("acc", acc)
                k.mm(den, den[:, 0:TT], [(self.cst_b[:, 2, :], eT[:, mb, :]) for mb in range(2)], reads=[eT, self.cst_b])
                oT = k.ring("acc", acc)
                k.mm(oT, oT[:, 0:TT], [(mv[:, mb, h * 128:(h + 1) * 128], eT[:, mb, :]) for mb in range(2)], reads=[eT, mv])
                k.op("dve", lambda e, den=den: e.reciprocal(out=rden[:], in_=den[:, 0:TT]), reads=[den], writes=[rden])
                k.op("dve", lambda e: e.tensor_tensor(out=rden[:], in0=rden[:], in1=zm[:, h, :], op=ALU.mult), reads=[rden, zm], writes=[rden])
                k.op("dve", lambda e, oT=oT, h=h: e.tensor_tensor(out=ymem[:, h, :], in0=oT[:, 0:TT], in1=rden[:], op=ALU.mult), reads=[oT, rden], writes=[ymem])
            k.dma("pool", YT[8:12, :, tok0:tok0 + TT].rearrange("c p t -> p c t"), ymem[:], reads=[ymem])

        if kind == 1:
            cw = k.sb("cw", [128, 8, 5], F32, st)
            k.dma("sp", cw[:], I["convmlT"][j], writes=[cw])
            xm = k.sb("xm", [128, 8, 3 + TT], F32, st)
            if isp:
                k.op("pool", lambda e: e.memset(xm[:, :, 0:3], 0.0), writes=[xm])
            else:
                k.dma("sp", xm[:, :, 0:3], I["smconvT"][j], writes=[xm])
            cacc = [k.sb("cacc%d" % i, [128, TT], F32, st) for i in range(2)]
            xc = k.sb("xc", [128, 8, TT], BF16, st)
            wqb = k.sb("wqb", [128, 8, 256], BF16, st)
            wkb = k.sb("wkb", [128, 8, 256], BF16, st)
            with contextlib.ExitStack() as sw:
                self.load_w(k, sw, wqb, I["wq"][j].rearrange("h d e -> (h d) e"), 256, 8, "wq")
                self.load_w(k, sw, wkb, I["wk"][j].rearrange("h d e -> (h d) e"), 256, 8, "wk")
                k.barrier()
            gtl = [k.sb("gtl%d" % i, [8, TT], F32, st) for i in range(2)]
            self.xcs = [k.sb("xcs%d" % i, [128, 8, TT], BF16, st) for i in range(2)]
            self.skp1 = k.sb("skp1", [128, 8], F32, st)
            k.dma("sp", self.skp1[:], I["skipT"][j], writes=[self.skp1])
        if kind == 2:
            cw = k.sb("cw", [128, 8, 3], F32, st)
            k.dma("sp", cw[:], I["convcvT"][j], writes=[cw])
            ch = k.sb("ch", [128, 8, 2 + TT], F32, st)
            if isp:
                k.op("pool", lambda e: e.memset(ch[:, :, 0:2], 0.0), writes=[ch])
            else:
                k.dma("sp", ch[:, :, 0:2], I["scvT"][j], writes=[ch])
            bT = [k.sb("bT%d" % i, [128, 4, TT], BF16, st) for i in range(2)]
            cT = [k.sb("cT%d" % i, [128, 4, TT], BF16, st) for i in range(2)]
            cacc = [k.sb("cacc%d" % i, [128, TT], F32, st) for i in range(2)]
            yst = [k.sb("yst%d" % i, [128, 4, TT], BF16, st) for i in range(2)]

        for tt in range(ntt):
            tok0 = tt * TT
            hTt = hT[tt % 2]
            for s in range(nsub):
                x_ = k.ring("xt", xt)
                s_ = k.ring("ss", ss)
                h_ = k.ring("hb", hb)
                self.rms_front(k, xsrc[tok0 + s * np_: tok0 + (s + 1) * np_, :], np_, x_, junk, s_, h_, gpre)
                self.to_fm(k, h_, np_, ptr, hTt, s * np_)
            fm_group(hTt, mqc, 4, None, tok0, dst_tile=mq)
            fm_group(hTt, zmc, 4, AF.Silu, tok0, dst_tile=zm)
            mem_attn(tok0)
            if kind == 0:
                fm_group(hTt, 0, 4, None, tok0, dst_fm0=0)
                fm_group(hTt, 512, 4, None, tok0, dst_fm0=4)
                fm_group(hTt, 1024, 4, None, tok0, dst_fm0=8)
                fm_group(hTt, 1536, 4, None, tok0, dst_fm0=12)
                fm_group(hTt, zc, 4, AF.Silu, tok0, dst_fm0=16)
                fm_group(hTt, zc + 512, 4, AF.Silu, tok0, dst_fm0=20)
                ko = (O["sbk_p"] if isp else O["sbk_s"])[j].rearrange("t h d -> t (h d)")
                vo = (O["sbv_p"] if isp else O["sbv_s"])[j].rearrange("t h d -> t (h d)")
                for s in range(nsub):
                    for (dst, c0) in ((ko, 1024), (vo, 2048)):
                        for nb in range(2):
                            a = tm_block(hTt, s, c0 + nb * 512, 512)
                            o = k.ring("tmo", tmo)
                            eng = k.ring("evac", ["dve", "act"])
                            if eng == "dve":
                                k.op("dve", lambda e, a=a, o=o: e.tensor_copy(out=o[0:np_, :], in_=a[0:np_, :]), reads=[a], writes=[o])
                            else:
                                k.op("act", lambda e, a=a, o=o: e.activation(out=o[0:np_, :], in_=a[0:np_, :], func=AF.Copy), reads=[a], writes=[o])
                            k.dma("pool", dst[tok0 + s * np_: tok0 + (s + 1) * np_, nb * 512:(nb + 1) * 512], o[0:np_, :], reads=[o])
            elif kind == 2:
                for g in range(2):
                    bt = fm_group(hTt, 0 + g * 512, 4, None, tok0, dst_tile=bT[g])
                    ct = fm_group(hTt, 1024 + g * 512, 4, None, tok0, dst_tile=cT[g])
                    for c in range(4):
                        a = fm_chunk(hTt, 2048 + (g * 4 + c) * 128)
                        k.op("dve", lambda e, a=a, c=c, g=g, ct=ct: e.tensor_tensor(out=ch[:, g * 4 + c, 2:2 + TT], in0=a[:, 0:TT], in1=ct[:, c, :], op=ALU.mult), reads=[a, ct, ch], writes=[ch])
                    zt = fm_group(hTt, zc + g * 512, 4, AF.Silu, tok0, dst_tile=k.ring("stg", stg))
                    yt = yst[g]
                    for c in range(4):
                        cc = g * 4 + c
                        ca = k.ring("cacc", cacc)
                        k.op("dve", lambda e, ca=ca, cc=cc: e.tensor_scalar(out=ca[:], in0=ch[:, cc, 0:TT], scalar1=cw[:, cc, 0:1], scalar2=None, op0=ALU.mult), reads=[ch, cw], writes=[ca])
                        k.op("dve", lambda e, ca=ca, cc=cc: e.scalar_tensor_tensor(out=ca[:], in0=ch[:, cc, 1:1 + TT], scalar=cw[:, cc, 1:2], in1=ca[:], op0=ALU.mult, op1=ALU.add), reads=[ch, cw, ca], writes=[ca])
                        k.op("dve", lambda e, ca=ca, cc=cc: e.scalar_tensor_tensor(out=ca[:], in0=ch[:, cc, 2:2 + TT], scalar=cw[:, cc, 2:3], in1=ca[:], op0=ALU.mult, op1=ALU.add), reads=[ch, cw, ca], writes=[ca])
                        k.op("pool", lambda e, ca=ca, c=c, bt=bt: e.tensor_tensor(out=ca[:], in0=ca[:], in1=bt[:, c, :], op=ALU.mult), reads=[ca, bt], writes=[ca])
                        k.op("pool", lambda e, ca=ca, c=c, zt=zt, yt=yt: e.tensor_tensor(out=yt[:, c, :], in0=ca[:], in1=zt[:, c, :], op=ALU.mult), reads=[ca, zt], writes=[yt])
                    k.dma("pool", YT[g * 4:g * 4 + 4, :, tok0:tok0 + TT].rearrange("c p t -> p c t"), yt[:], reads=[yt])
                if tt == ntt - 1:
                    s = nsub - 1
                    cvo = (O["cv_p"] if isp else O["cv_s"])[j]
                    for nb in range(2):
                        a1 = tm_block(hTt, s, 1024 + nb * 512, 512)
                        o1 = k.ring("tmo", tmo)
                        k.op("act", lambda e, a1=a1, o1=o1: e.activation(out=o1[0:np_, :], in_=a1[0:np_, :], func=AF.Copy), reads=[a1], writes=[o1])
                        a2 = tm_block(hTt, s, 2048 + nb * 512, 512)
                        k.op("dve", lambda e, a2=a2, o1=o1: e.tensor_tensor(out=o1[0:np_, :], in0=a2[0:np_, :], in1=o1[0:np_, :], op=ALU.mult), reads=[a2, o1], writes=[o1])
                        k.dma("pool", cvo[:, nb * 512:(nb + 1) * 512], o1[np_ - 2:np_, :], reads=[o1])
                if tt < ntt - 1:
                    k.op("dve", lambda e: e.tensor_copy(out=ch[:, :, 0:2], in_=ch[:, :, TT:TT + 2]), reads=[ch], writes=[ch])
            else:
                self.p1_ml(stm, li, j, tt, ntt, tok0, TT, np_, nsub, hTt, fm_chunk, fm_group, tm_block, tmo, stg, acc, xm, cw, cacc, xc, wqb, wkb, gtl, zc, Wb)

    def p1_ml(self, stm, li, j, tt, ntt, tok0, TT, np_, nsub, hTt, fm_chunk, fm_group, tm_block, tmo, stg, acc, xm, cw, cacc, xc, wqb, wkb, gtl, zc, Wb):
        k = self.k
        I, O = self.I, self.O
        isp = stm["name"] == "p"
        FM, TMs, GT = stm["FM"], stm["TMs"], stm["GT"]
        for c in range(8):
            a = fm_chunk(hTt, c * 128)
            k.op("act", lambda e, a=a, c=c: e.activation(out=xm[:, c, 3:3 + TT], in_=a[:, 0:TT], func=AF.Copy), reads=[a, xm], writes=[xm])
        fm_group(hTt, 2048, 4, AF.Sigmoid, tok0, dst_fm0=24)
        fm_group(hTt, 2048 + 512, 4, AF.Sigmoid, tok0, dst_fm0=28)
        fm_group(hTt, zc, 4, AF.Silu, tok0, dst_fm0=32)
        fm_group(hTt, zc + 512, 4, AF.Silu, tok0, dst_fm0=36)
        a = k.ring("acc", acc)
        k.mm(a, a[0:8, 0:TT], [(Wb[:, kc, 3072:3080], hTt[:, kc, :]) for kc in range(8)], reads=[hTt, Wb])
        g = k.ring("gtl", gtl)
        k.op("dve", lambda e, a=a, g=g: e.tensor_copy(out=g[:, :], in_=a[0:8, 0:TT]), reads=[a], writes=[g])
        k.dma("pool", GT[:, tok0:tok0 + TT], g[:, :], reads=[g])

        for s in range(nsub):
            for nb in range(2):
                a = tm_block(hTt, s, 1024 + nb * 512, 512)
                o = k.ring("tmob", self.tmob)
                k.op("dve", lambda e, a=a, o=o: e.tensor_copy(out=o[0:np_, :], in_=a[0:np_, :]), reads=[a], writes=[o])
                k.dma("pool", TMs[1, tok0 + s * np_: tok0 + (s + 1) * np_, nb * 512:(nb + 1) * 512], o[0:np_, :], reads=[o])
        for c in range(8):
            ca = k.ring("cacc", cacc)
            k.op("dve", lambda e, ca=ca, c=c: e.tensor_scalar(out=ca[:], in0=xm[:, c, 0:TT], scalar1=cw[:, c, 0:1], scalar2=None, op0=ALU.mult), reads=[xm, cw], writes=[ca])
            for jj in range(1, 4):
                k.op("dve", lambda e, ca=ca, c=c, jj=jj: e.scalar_tensor_tensor(out=ca[:], in0=xm[:, c, jj:jj + TT], scalar=cw[:, c, jj:jj + 1], in1=ca[:], op0=ALU.mult, op1=ALU.add), reads=[xm, cw, ca], writes=[ca])
            k.op("act", lambda e, ca=ca, c=c: e.activation(out=xc[:, c, :], in_=ca[:], func=AF.Silu, bias=cw[:, c, 4:5]), reads=[ca, cw], writes=[xc])
        if tt == ntt - 1:
            s = nsub - 1
            mco = (O["mlconv_p"] if isp else O["mlconv_s"])[j]
            for nb in range(2):
                a1 = tm_block(hTt, s, nb * 512, 512)
                o1 = k.ring("tmo", tmo)
                k.op("act", lambda e, a1=a1, o1=o1: e.activation(out=o1[0:np_, :], in_=a1[0:np_, :], func=AF.Copy), reads=[a1], writes=[o1])
                k.dma("pool", mco[:, nb * 512:(nb + 1) * 512], o1[np_ - 3:np_, :], reads=[o1])
        if tt < ntt - 1:
            k.op("dve", lambda e: e.tensor_copy(out=xm[:, :, 0:3], in_=xm[:, :, TT:TT + 3]), reads=[xm], writes=[xm])
        xs_ = k.ring("stgx", self.xcs)
        for c in range(8):
            k.op("act", lambda e, c=c: e.activation(out=xs_[:, c, :], in_=xc[:, c, :], func=AF.Copy, scale=self.skp1[:, c:c + 1]), reads=[xc, self.skp1], writes=[xs_])
        k.dma("pool", FM[16:24, :, tok0:tok0 + TT].rearrange("c p t -> p c t"), xs_[:], reads=[xs_])
        for (wb, f0, scl) in ((wqb, 0, 1.0), (wkb, 8, 1.0 / 16.0)):
            for g in range(2):
                t = k.ring("stg", stg)
                for c in range(4):
                    hc = g * 4 + c
                    h, ec = hc // 2, hc % 2
                    a = k.ring("acc", acc)
                    k.mm(a, a[:, 0:TT], [(wb[:, 2 * h + dc, ec * 128:(ec + 1) * 128], xc[:, 2 * h + dc, :]) for dc in range(2)], reads=[xc, wb])
                    k.op("act", lambda e, a=a, c=c, t=t, scl=scl: e.activation(out=t[:, c, :], in_=a[:, 0:TT], func=AF.Copy, scale=scl), reads=[a], writes=[t])
                k.dma("pool", FM[f0 + g * 4:f0 + g * 4 + 4, :, tok0:tok0 + TT].rearrange("c p t -> p c t"), t[:], reads=[t])
        for s in range(nsub):
            for nb in range(2):
                a = k.ring("acc", acc)
                for hh in range(2):
                    h = nb * 2 + hh
                    k.mm(a, a[0:np_, hh * 256:(hh + 1) * 256], [(xc[:, 2 * h + dc, s * np_:(s + 1) * np_], wkb[:, 2 * h + dc, :]) for dc in range(2)], reads=[xc, wkb])
                o = k.ring("tmob", self.tmob)
                k.op("act", lambda e, a=a, o=o: e.activation(out=o[0:np_, :], in_=a[0:np_, :], func=AF.Copy, scale=1.0 / 16.0), reads=[a], writes=[o])
                k.dma("pool", TMs[0, tok0 + s * np_: tok0 + (s + 1) * np_, nb * 512:(nb + 1) * 512], o[0:np_, :], reads=[o])
    def p3(self, stm, li, Wo, gpo, last, st, bg=None):
        k = self.k
        T = stm["T"]
        TT = min(512, T)
        np_ = min(128, T)
        nsub = TT // np_
        ntt = T // TT
        xsrc = stm["xin"] if li == 0 else stm["xres"]
        xdst = stm["yout"] if last else stm["xres"]
        YT = stm["YT"]
        yt = [k.sb("yt%d" % i, [128, 12, TT], BF16, st) for i in range(2)]
        xt = [k.sb("p3x%d" % i, [128, D], F32, st) for i in range(3)]
        ot = [k.sb("p3o%d" % i, [128, D], F32, st) for i in range(2)]
        junk = k.sb("p3junk", [128, D], BF16, st)
        ss = [k.sb("p3ss%d" % i, [128, 2], F32, st) for i in range(3)]
        acc = [k.ps("p3acc%d" % i, [128, D], F32, st) for i in range(3)]
        for tt in range(ntt):
            tok0 = tt * TT
            y_ = yt[tt % 2]
            k.dma("sp", y_[:], YT[:, :, tok0:tok0 + TT].rearrange("c p t -> p c t"), writes=[y_])
            for s in range(nsub):
                if bg is not None and s % 3 != 2:
                    next(bg, None)
                x_ = k.ring("p3x", xt)
                k.dma("sp", x_[0:np_, :], xsrc[tok0 + s * np_: tok0 + (s + 1) * np_, :], writes=[x_])
                a = k.ring("p3acc", acc)
                for nb in range(2):
                    k.mm(a, a[0:np_, nb * 512:(nb + 1) * 512], [(y_[:, fc, s * np_:(s + 1) * np_], Wo[:, fc, nb * 512:(nb + 1) * 512]) for fc in range(12)], reads=[y_, Wo])
                s_ = k.ring("p3ss", ss)
                k.op("pool", lambda e, s_=s_: e.memset(s_[:], 0.0), writes=[s_])
                k.op("act", lambda e, a=a, s_=s_: e.activation(out=junk[0:np_, :], in_=a[0:np_, :], func=AF.Square, accum_out=s_[0:np_, 0:1]), reads=[a, s_], writes=[junk, s_])
                k.op("act", lambda e, s_=s_: e.activation(out=s_[0:np_, 1:2], in_=s_[0:np_, 0:1], func=AF.Ln, scale=1.0 / D, bias=EPS), reads=[s_], writes=[s_])
                k.op("act", lambda e, s_=s_: e.activation(out=s_[0:np_, 1:2], in_=s_[0:np_, 1:2], func=AF.Exp, scale=-0.5), reads=[s_], writes=[s_])
                o_ = k.ring("p3o", ot)
                k.op("dve", lambda e, a=a, s_=s_, o_=o_: e.scalar_tensor_tensor(out=o_[0:np_, :], in0=a[0:np_, :], scalar=s_[0:np_, 1:2], in1=gpo[0:np_, :], op0=ALU.mult, op1=ALU.mult), reads=[a, s_, gpo], writes=[o_])
                k.op("pool", lambda e, o_=o_, x_=x_: e.tensor_tensor(out=o_[0:np_, :], in0=o_[0:np_, :], in1=x_[0:np_, :], op=ALU.add), reads=[o_, x_], writes=[o_])
                k.dma("pool", xdst[tok0 + s * np_: tok0 + (s + 1) * np_, :], o_[0:np_, :], reads=[o_])

    def p2_sb(self, stm, li, j, st):
        k = self.k
        I, O = self.I, self.O
        T = stm["T"]
        isp = stm["name"] == "p"
        FM, YT = stm["FM"], stm["YT"]
        N = min(512, T)
        sc = 1.0 / math.sqrt(128.0)
        PAST = self.PAST
        if isp:
            NB = T // 128
            koff = 0
        else:
            NB = (PAST + T + 127) // 128
            if NB % 2:
                NB += 1
            koff = NB * 128 - (PAST + T)
        nqs = T // N
        NSET = 2 if isp else 4
        early = not isp
        qT = [k.sb("qT%d" % i, [128, T], BF16, st) for i in range(NSET)]
        kT = [k.sb("kT%d" % i, [128, NB * 128], BF16, st) for i in range(NSET)]
        szT = [k.sb("szT%d" % i, [128, T], BF16, st) for i in range(NSET)]
        Vb = [k.sb("Vb%d" % i, [128, NB, 128], BF16, st) for i in range(NSET)]
        VST = 16
        vstg = [k.sb("vstg%d" % i, [128, VST, 128], F32, st) for i in range(2)] if isp else None
        vfull = [k.sb("vfull%d" % i, [128, NB, 128], F32, st) for i in range(2)] if not isp else None
        S = [k.ps("S%d" % i, [128, 2, N], F32, st) for i in range(2)]
        L = k.ps("L", [128, 2, N], F32, st)
        Racc = k.ps("Racc", [128, N], F32, st)
        oacc = k.ps("oacc", [128, N], F32, st)
        trs = k.ps("trs", [128, 512], F32, st) if not isp else None
        e_ = [k.sb("e%d" % i, [128, 2, N], BF16, st) for i in range(5)]
        c_ = [k.sb("c%d" % i, [128, 2, N], BF16, st) for i in range(3)]
        g_ = [k.sb("g%d" % i, [128, 2, N], BF16, st) for i in range(2)]
        a_ = [k.sb("a%d" % i, [128, 2, N], BF16, st) for i in range(3)]
        Lr = [k.sb("Lr%d" % i, [128, 2, N], F32, st) for i in range(2)]
        R = [k.sb("R%d" % i, [128, N], F32, st) for i in range(3)]
        yst = [k.sb("ysb%d" % i, [128, N], BF16, st) for i in range(2)]
        tri, ones = self.cst_b[:, 1, :], self.cst_b[:, 2, :]
        vout = (O["sbv_p"] if isp else O["sbv_s"])[j]
        kout = (O["sbk_p"] if isp else O["sbk_s"])[j]

        def load_head(h):
            si = h % NSET
            k.dma("sp", qT[si][:], FM[h], writes=[qT[si]])
            k.dma("sp", szT[si][:], FM[16 + h], writes=[szT[si]])
            if isp:
                k.dma("sp", kT[si][:], FM[8 + h], writes=[kT[si]])
                for b0 in range(0, NB, VST):
                    nb_ = min(VST, NB - b0)
                    vs = k.ring("vstg", vstg)
                    k.dma("sp", vs[:, 0:nb_, :], vout[b0 * 128:(b0 + nb_) * 128, h, :].rearrange("(b p) d -> p b d", p=128), writes=[vs])
                    k.op("pool", lambda e, vs=vs, b0=b0, nb_=nb_: e.tensor_copy(out=Vb[si][:, b0:b0 + nb_, :], in_=vs[:, 0:nb_, :]), reads=[vs], writes=[Vb[si]])
            else:
                for (src_c, src_n, isk) in ((I["csk"][j], kout, True), (I["csv"][j], vout, False)):
                    vs = k.ring("vfull", vfull)
                    k.op("pool", lambda e, vs=vs: e.memset(vs[:], 0.0), writes=[vs])
                    p0 = koff % 128
                    bq = koff // 128
                    t0 = (128 - p0) % 128
                    if t0:
                        k.dma("sp", vs[p0:128, bq, :], src_c[0:t0, h, :], writes=[vs])
                        bq += 1
                    nfull = (PAST - t0) // 128
                    if nfull:
                        k.dma("sp", vs[:, bq:bq + nfull, :], src_c[t0:t0 + nfull * 128, h, :].rearrange("(b p) d -> p b d", p=128), writes=[vs])
                    rem = PAST - t0 - nfull * 128
                    if rem:
                        k.dma("sp", vs[0:rem, bq + nfull, :], src_c[t0 + nfull * 128:PAST, h, :], writes=[vs])
                    k.dma("sp", vs[128 - T:128, NB - 1, :], src_n[0:T, h, :], writes=[vs])
                    if not isk:
                        k.op("pool", lambda e, vs=vs: e.tensor_copy(out=Vb[si][:], in_=vs[:]), reads=[vs], writes=[Vb[si]])
                    else:
                        for b0 in range(0, NB, 4):
                            nb_ = min(4, NB - b0)
                            for b in range(nb_):
                                k.tr(trs, trs[:, b * 128:(b + 1) * 128], vs[:, b0 + b, :], self.cst_f[:], reads=[vs, self.cst_f])
                            k.op("dve", lambda e, b0=b0, nb_=nb_: e.tensor_copy(out=kT[si][:, b0 * 128:(b0 + nb_) * 128], in_=trs[:, 0:nb_ * 128]), reads=[trs], writes=[kT[si]])

        G = []
        for h in range(8):
            for qs in range(nqs):
                q0 = qs * N
                grp = []
                if isp:
                    b = (q0 + N) // 128 - 1
                    while b >= 0:
                        m0 = (b - q0 // 128) if b >= q0 // 128 else None
                        m1 = ((b - 1) - q0 // 128) if (b - 1) >= q0 // 128 else None
                        grp.append((b, b - 1, m0, m1))
                        b -= 2
                else:
                    b = NB - 1
                    first_real = koff // 128
                    while b >= 0:
                        ms = []
                        for bb in (b, b - 1):
                            if bb == NB - 1:
                                ms.append(0)
                            elif bb == first_real and koff % 128:
                                ms.append(1)
                            elif bb < first_real:
                                ms.append(2)
                            else:
                                ms.append(None)
                        grp.append((b, b - 1, ms[0], ms[1]))
                        b -= 2
                ng = len(grp)
                for gi, (b0, b1, m0, m1) in enumerate(grp):
                    qlo = 0
                    if isp and gi == 0 and N == 512:
                        qlo = (b1 - q0 // 128) * 128
                    G.append(dict(h=h, si=h % NSET, q0=q0, b0=b0, b1=b1, m0=m0, m1=m1, first=(gi == 0), last=(gi == ng - 1),
                                  newhead=(qs == 0 and gi == 0), qlo=qlo))
        NG = len(G)
        mtile = self.msk if isp else self.smsk

        def st_S(n):
            d = G[n]
            if n == 0:
                for hh in range(min(NSET, 8)):
                    load_head(hh)
            si, q0 = d["si"], d["q0"]
            S_ = k.ring("S", S)
            for i, (b, m) in enumerate(((d["b0"], d["m0"]), (d["b1"], d["m1"]))):
                ql = d["qlo"]
                pairs = [(kT[si][:, b * 128:(b + 1) * 128], qT[si][:, q0 + ql:q0 + N])]
                rd = [kT[si], qT[si]]
                if m is not None:
                    pairs.append((self.cst_b[:, 0, :], mtile[:, m, ql:N]))
                    rd += [self.cst_b, mtile]
                k.mm(S_, S_[:, i, ql:N], pairs, reads=rd)
            d["S"] = S_

        def st_exp1(n):
            d = G[n]
            e = k.ring("e", e_)
            S_ = d["S"]
            ql = d["qlo"]
            k.op("act", lambda en: en.activation(out=e[:, :, ql:N], in_=S_[:, :, ql:N], func=AF.Exp, scale=sc), reads=[S_], writes=[e])
            d["e"] = e

        def st_ln(n):
            d = G[n]
            c = k.ring("c", c_)
            e = d["e"]
            ql = d["qlo"]
            k.op("act", lambda en: en.activation(out=c[:, :, ql:N], in_=e[:, :, ql:N], func=AF.Ln, bias=1.0), reads=[e], writes=[c])
            d["c"] = c

        def st_L(n):
            d = G[n]
            c = d["c"]
            ql = d["qlo"]
            k.mm(L, L[:, 0, ql:N], [(tri, c[:, 0, ql:N])], reads=[c, self.cst_b])
            k.mm(L, L[:, 1, ql:N], [(tri, c[:, 1, ql:N]), (ones, c[:, 0, ql:N])], reads=[c, self.cst_b])
            if not d["first"]:
                Rt = d["R"]
                lr = k.ring("Lr", Lr)
                for i in range(2):
                    k.op("dve", lambda en, i=i: en.tensor_tensor(out=lr[:, i, :], in0=L[:, i, 0:N], in1=Rt[:], op=ALU.add), reads=[L, Rt], writes=[lr])
                d["src"], d["srct"] = lr[:], lr
            else:
                lr = k.ring("Lr", Lr)
                k.op("dve", lambda en: en.tensor_copy(out=lr[:, :, ql:N], in_=L[:, :, ql:N]), reads=[L], writes=[lr])
                d["src"], d["srct"] = lr[:, :, ql:N], lr
            if not d["last"]:
                k.mm(Racc, Racc[:, ql:N], [(ones, c[:, 0, ql:N]), (ones, c[:, 1, ql:N])], reads=[c, self.cst_b])
                Rn = k.ring("R", R)
                if d["first"]:
                    if ql:
                        k.op("pool", lambda en: en.memset(Rn[:, 0:ql], 0.0), writes=[Rn])
                    k.op("dve", lambda en: en.tensor_copy(out=Rn[:, ql:N], in_=Racc[:, ql:N]), reads=[Racc], writes=[Rn])
                else:
                    Rt = d["R"]
                    k.op("dve", lambda en: en.tensor_tensor(out=Rn[:], in0=Racc[:, 0:N], in1=Rt[:], op=ALU.add), reads=[Racc, Rt], writes=[Rn])
                G[n + 1]["R"] = Rn

        def st_exp2(n):
            d = G[n]
            g = k.ring("g", g_)
            src, srct = d["src"], d["srct"]
            ql = d["qlo"]
            k.op("act", lambda en: en.activation(out=g[:, :, ql:N], in_=src, func=AF.Exp, scale=-1.0), reads=[srct], writes=[g])
            d["g"] = g

        def st_a(n):
            d = G[n]
            a = k.ring("a", a_)
            e, g = d["e"], d["g"]
            ql = d["qlo"]
            if ql:
                k.op("pool", lambda en: en.memset(a[:, :, 0:ql], 0.0), writes=[a])
            k.op("pool", lambda en: en.tensor_tensor(out=a[:, :, ql:N], in0=e[:, :, ql:N], in1=g[:, :, ql:N], op=ALU.mult), reads=[e, g], writes=[a])
            si, q0, h = d["si"], d["q0"], d["h"]
            k.mm(oacc, oacc[:, 0:N], [(Vb[si][:, d["b0"], :], a[:, 0, :]), (Vb[si][:, d["b1"], :], a[:, 1, :])], reads=[a, Vb[si]], start=d["first"], stop=d["last"])
            if d["last"]:
                y = k.ring("ysb", yst)
                k.op("dve", lambda en: en.tensor_tensor(out=y[:], in0=oacc[:, 0:N], in1=szT[si][:, q0:q0 + N], op=ALU.mult), reads=[oacc, szT[si]], writes=[y])
                k.dma("pool", YT[h, :, q0:q0 + N], y[:], reads=[y])
            for kk in ("S", "e", "c", "g", "src", "srct", "R"):
                d.pop(kk, None)
            if (n == NG - 1 or G[n + 1]["newhead"]) and d["h"] + NSET < 8:
                load_head(d["h"] + NSET)

        for t in range(-2, NG + 3):
            if 0 <= t + 2 < NG:
                st_S(t + 2)
            if 0 <= t + 1 < NG:
                st_exp1(t + 1)
            if 0 <= t < NG:
                st_ln(t)
            if 0 <= t - 1 < NG:
                st_L(t - 1)
            if 0 <= t - 2 < NG:
                st_exp2(t - 2)
            if 0 <= t - 3 < NG:
                st_a(t - 3)

    def p2_sb_sample(self, stm, li, j, st):
        k = self.k
        I, O = self.I, self.O
        T = stm["T"]
        FM, YT = stm["FM"], stm["YT"]
        N = T
        H = 8
        sc = 1.0 / math.sqrt(128.0)
        PAST = self.PAST
        NB = (PAST + T + 127) // 128
        if NB % 2:
            NB += 1
        koff = NB * 128 - (PAST + T)
        qT = k.sb("sqT", [128, H, N], BF16, st)
        szT = k.sb("sszT", [128, H, N], BF16, st)
        kT = [k.sb("skT%d" % h, [128, NB * 128], BF16, st) for h in range(H)]
        Vb = [k.sb("sVb%d" % h, [128, NB, 128], BF16, st) for h in range(H)]
        vfull = [k.sb("svfull%d" % i, [128, NB, 128], F32, st) for i in range(2)]
        S = [k.ps("sS%d" % i, [128, H, 2, N], F32, st) for i in range(2)]
        L = k.ps("sL", [128, H, 2, N], F32, st)
        Racc = k.ps("sRacc", [128, H, N], F32, st)
        oacc = k.ps("soacc", [128, H, N], F32, st)
        trs = k.ps("strs", [128, 512], F32, st)
        e_ = [k.sb("se%d" % i, [128, H, 2, N], BF16, st) for i in range(5)]
        c_ = [k.sb("sc%d" % i, [128, H, 2, N], BF16, st) for i in range(3)]
        g_ = [k.sb("sg%d" % i, [128, H, 2, N], BF16, st) for i in range(2)]
        a_ = [k.sb("sa%d" % i, [128, H, 2, N], BF16, st) for i in range(3)]
        Lr = [k.sb("sLr%d" % i, [128, H, 2, N], F32, st) for i in range(2)]
        R = [k.sb("sR%d" % i, [128, H, N], F32, st) for i in range(3)]
        yst = k.sb("sysb", [128, H, N], BF16, st)
        osb = k.sb("sosb", [128, H, N], F32, st)
        ident, tri, ones = self.cst_b[:, 0, :], self.cst_b[:, 1, :], self.cst_b[:, 2, :]
        vout = O["sbv_s"][j]
        kout = O["sbk_s"][j]
        k.dma("sp", qT[:], FM[0:8, :, :].rearrange("c p t -> p c t"), writes=[qT])
        k.dma("sp", szT[:], FM[16:24, :, :].rearrange("c p t -> p c t"), writes=[szT])
        for h in range(H):
            for (src_c, src_n, isk) in ((I["csk"][j], kout, True), (I["csv"][j], vout, False)):
                vs = k.ring("svfull", vfull)
                nz = koff // 128 + (1 if koff % 128 else 0)
                if nz:
                    k.op("pool", lambda e, vs=vs, nz=nz: e.memset(vs[:, 0:nz, :], 0.0), writes=[vs])
                p0 = koff % 128
                bq = koff // 128
                t0 = (128 - p0) % 128
                if t0:
                    k.dma("sp", vs[p0:128, bq, :], src_c[0:t0, h, :], writes=[vs])
                    bq += 1
                nfull = (PAST - t0) // 128
                if nfull:
                    k.dma("sp", vs[:, bq:bq + nfull, :], src_c[t0:t0 + nfull * 128, h, :].rearrange("(b p) d -> p b d", p=128), writes=[vs])
                rem = PAST - t0 - nfull * 128
                if rem:
                    k.dma("sp", vs[0:rem, bq + nfull, :], src_c[t0 + nfull * 128:PAST, h, :], writes=[vs])
                k.dma("sp", vs[128 - T:128, NB - 1, :], src_n[0:T, h, :], writes=[vs])
                if not isk:
                    if h % 2 == 0:
                        k.op("pool", lambda e, vs=vs, h=h: e.tensor_copy(out=Vb[h][:], in_=vs[:]), reads=[vs], writes=[Vb[h]])
                    else:
                        k.op("act", lambda e, vs=vs, h=h: e.activation(out=Vb[h][:], in_=vs[:], func=AF.Copy), reads=[vs], writes=[Vb[h]])
                else:
                    for b0 in range(0, NB, 4):
                        nb_ = min(4, NB - b0)
                        for b in range(nb_):
                            k.tr(trs, trs[:, b * 128:(b + 1) * 128], vs[:, b0 + b, :], self.cst_f[:], reads=[vs, self.cst_f])
                        eng = k.ring("sevac", ["dve", "act"])
                        if eng == "dve":
                            k.op("dve", lambda e, b0=b0, nb_=nb_, h=h: e.tensor_copy(out=kT[h][:, b0 * 128:(b0 + nb_) * 128], in_=trs[:, 0:nb_ * 128]), reads=[trs], writes=[kT[h]])
                        else:
                            k.op("act", lambda e, b0=b0, nb_=nb_, h=h: e.activation(out=kT[h][:, b0 * 128:(b0 + nb_) * 128], in_=trs[:, 0:nb_ * 128], func=AF.Copy), reads=[trs], writes=[kT[h]])
        G = []
        b = NB - 1
        first_real = koff // 128
        while b >= 0:
            ms = []
            for bb in (b, b - 1):
                if bb == NB - 1:
                    ms.append(0)
                elif bb == first_real and koff % 128:
                    ms.append(1)
                elif bb < first_real:
                    ms.append(2)
                else:
                    ms.append(None)
            G.append(dict(b0=b, b1=b - 1, m0=ms[0], m1=ms[1]))
            b -= 2
        NG = len(G)
        for n, d in enumerate(G):
            d["first"], d["last"] = (n == 0), (n == NG - 1)
        mtile = self.smsk

        def st_S(n):
            d = G[n]
            S_ = k.ring("sS", S)
            for h in range(H):
                for i, (b, m) in enumerate(((d["b0"], d["m0"]), (d["b1"], d["m1"]))):
                    pairs = [(kT[h][:, b * 128:(b + 1) * 128], qT[:, h, :])]
                    rd = [kT[h], qT]
                    if m is not None:
                        pairs.append((ident, mtile[:, m, 0:N]))
                        rd += [self.cst_b, mtile]
                    k.mm(S_, S_[:, h, i, :], pairs, reads=rd)
            d["S"] = S_

        def st_exp1(n):
            d = G[n]
            e = k.ring("se", e_)
            S_ = d["S"]
            k.op("act", lambda en: en.activation(out=e[:], in_=S_[:], func=AF.Exp, scale=sc), reads=[S_], writes=[e])
            d["e"] = e

        def st_ln(n):
            d = G[n]
            c = k.ring("sc", c_)
            e = d["e"]
            k.op("act", lambda en: en.activation(out=c[:], in_=e[:], func=AF.Ln, bias=1.0), reads=[e], writes=[c])
            d["c"] = c

        def st_L(n):
            d = G[n]
            c = d["c"]
            for h in range(H):
                k.mm(L, L[:, h, 0, :], [(tri, c[:, h, 0, :])], reads=[c, self.cst_b])
                k.mm(L, L[:, h, 1, :], [(tri, c[:, h, 1, :]), (ones, c[:, h, 0, :])], reads=[c, self.cst_b])
            lr = k.ring("sLr", Lr)
            if not d["first"]:
                Rt = d["R"]
                for i in range(2):
                    k.op("dve", lambda en, i=i: en.tensor_tensor(out=lr[:, :, i, :], in0=L[:, :, i, :], in1=Rt[:], op=ALU.add), reads=[L, Rt], writes=[lr])
            else:
                k.op("dve", lambda en: en.tensor_copy(out=lr[:], in_=L[:]), reads=[L], writes=[lr])
            d["lr"] = lr
            if not d["last"]:
                for h in range(H):
                    k.mm(Racc, Racc[:, h, :], [(ones, c[:, h, 0, :]), (ones, c[:, h, 1, :])], reads=[c, self.cst_b])
                Rn = k.ring("sR", R)
                if d["first"]:
                    k.op("dve", lambda en: en.tensor_copy(out=Rn[:], in_=Racc[:]), reads=[Racc], writes=[Rn])
                else:
                    Rt = d["R"]
                    k.op("dve", lambda en: en.tensor_tensor(out=Rn[:], in0=Racc[:], in1=Rt[:], op=ALU.add), reads=[Racc, Rt], writes=[Rn])
                G[n + 1]["R"] = Rn

        def st_exp2(n):
            d = G[n]
            g = k.ring("sg", g_)
            lr = d["lr"]
            k.op("act", lambda en: en.activation(out=g[:], in_=lr[:], func=AF.Exp, scale=-1.0), reads=[lr], writes=[g])
            d["g"] = g

        def st_a(n):
            d = G[n]
            a = k.ring("sa", a_)
            e, g = d["e"], d["g"]
            k.op("pool", lambda en: en.tensor_tensor(out=a[:], in0=e[:], in1=g[:], op=ALU.mult), reads=[e, g], writes=[a])
            for h in range(H):
                k.mm(oacc, oacc[:, h, :], [(Vb[h][:, d["b0"], :], a[:, h, 0, :]), (Vb[h][:, d["b1"], :], a[:, h, 1, :])], reads=[a, Vb[h]])
            if d["first"]:
                k.op("dve", lambda en: en.tensor_copy(out=osb[:], in_=oacc[:]), reads=[oacc], writes=[osb])
            else:
                k.op("dve", lambda en: en.tensor_tensor(out=osb[:], in0=oacc[:], in1=osb[:], op=ALU.add), reads=[oacc, osb], writes=[osb])
            if d["last"]:
                k.op("dve", lambda en: en.tensor_tensor(out=yst[:], in0=osb[:], in1=szT[:], op=ALU.mult), reads=[osb, szT], writes=[yst])
                k.dma("pool", YT[0:8, :, :].rearrange("c p t -> p c t"), yst[:], reads=[yst])

        for t in range(-2, NG + 3):
            if 0 <= t + 2 < NG:
                st_S(t + 2)
            if 0 <= t + 1 < NG:
                st_exp1(t + 1)
            if 0 <= t < NG:
                st_ln(t)
            if 0 <= t - 1 < NG:
                st_L(t - 1)
            if 0 <= t - 2 < NG:
                st_exp2(t - 2)
            if 0 <= t - 3 < NG:
                st_a(t - 3)

    def p2_ml(self, stm, li, j, st):
        k = self.k
        I, O = self.I, self.O
        T = stm["T"]
        isp = stm["name"] == "p"
        FM, YT, TMs, GT = stm["FM"], stm["YT"], stm["TMs"], stm["GT"]
        LC = min(128, T)
        nch = T // LC
        SEG = min(2048, T)
        ig = k.sb("ml_ig", [4, T], F32, st)
        fg = k.sb("ml_fg", [4, T], F32, st)
        Bt = k.sb("ml_B", [4, T], F32, st)
        ones4 = k.sb("ml_ones", [4, SEG], F32, st)
        gcol = k.sb("ml_gcol", [4, 2], F32, st)
        minit = k.sb("ml_minit", [4, 1], F32, st)
        sel = k.sb("ml_sel", [4, 4, 128], F32, st)
        negm = k.sb("ml_negm", [128, 128], F32, st)
        ghr = k.sb("ml_ghr", [128, D], F32, st)
        skp = k.sb("ml_skip", [128, 8], F32, st)
        C = k.sb("ml_C", [128, 4, 2, 257], F32, st)
        Cb = k.sb("ml_Cb", [128, 4, 2, 257], BF16, st)
        MendB = k.sb("ml_MendB", [128, nch + 1, 4], F32, st)
        nMendB = k.sb("ml_nMendB", [128, nch + 1, 4], F32, st)
        decB = k.sb("ml_decB", [128, nch, 4], F32, st)
        k.op("pool", lambda e: e.memset(ones4[:], 1.0), writes=[ones4])
        k.dma("sp", gcol[:], I["gcol"][j], writes=[gcol])
        k.dma("sp", sel[:].rearrange("r h m -> r (h m)"), I["sel4"][:, :], writes=[sel])
        k.dma("sp", negm[:], I["negmask"][:, :], writes=[negm])
        k.dma("sp", ghr[:], I["ghead_r"][j], writes=[ghr])
        k.dma("sp", skp[:], I["skipT"][j], writes=[skp])
        if isp:
            k.op("pool", lambda e: e.memset(minit[:], 0.0), writes=[minit])
            k.op("pool", lambda e: e.memset(C[:], 0.0), writes=[C])
        else:
            k.dma("sp", minit[:], I["smm"][j].rearrange("(h o) -> h o", o=1), writes=[minit])
            for h in range(4):
                k.dma("sp", C[:, h, :, 0:256], I["smc"][j, h].rearrange("(c p) e -> p c e", p=128), writes=[C])
                k.dma("sp", C[:, h, :, 256:257], I["smn"][j, h].rearrange("(c p o) -> p c o", p=128, o=1), writes=[C], allow_slow_non_contiguous=True)
        k.op("act", lambda e: e.activation(out=Cb[:], in_=C[:], func=AF.Copy), reads=[C], writes=[Cb])
        k.dma("sp", ig[:], GT[0:4, :], writes=[ig])
        k.dma("sp", fg[:], GT[4:8, :], writes=[fg])
        k.op("dve", lambda e: e.tensor_scalar(out=ig[:], in0=ig[:], scalar1=gcol[:, 0:1], scalar2=None, op0=ALU.add), reads=[ig, gcol], writes=[ig])
        k.op("dve", lambda e: e.tensor_scalar(out=fg[:], in0=fg[:], scalar1=gcol[:, 1:2], scalar2=None, op0=ALU.add), reads=[fg, gcol], writes=[fg])
        k.op("act", lambda e: e.activation(out=fg[:], in_=fg[:], func=AF.Exp, scale=-1.0), reads=[fg], writes=[fg])
        k.op("act", lambda e: e.activation(out=fg[:], in_=fg[:], func=AF.Ln, bias=1.0), reads=[fg], writes=[fg])
        k.op("dve", lambda e: e.tensor_scalar(out=fg[:], in0=fg[:], scalar1=-1.0, scalar2=None, op0=ALU.mult), reads=[fg], writes=[fg])
        for s0 in range(0, T, SEG):
            n = min(SEG, T - s0)
            init = 0.0 if s0 == 0 else Bt[:, s0 - 1:s0]
            k.op("dve", lambda e, s0=s0, n=n, init=init: e.tensor_tensor_scan(out=Bt[:, s0:s0 + n], data0=ones4[:, 0:n], data1=fg[:, s0:s0 + n], initial=init, op0=ALU.mult, op1=ALU.add), reads=[ones4, fg, Bt], writes=[Bt])
        k.op("dve", lambda e: e.tensor_tensor(out=ig[:], in0=ig[:], in1=Bt[:], op=ALU.subtract), reads=[ig, Bt], writes=[ig])
        for s0 in range(0, T, SEG):
            n = min(SEG, T - s0)
            init = minit[:, 0:1] if s0 == 0 else fg[:, s0 - 1:s0]
            k.op("dve", lambda e, s0=s0, n=n, init=init: e.tensor_tensor_scan(out=fg[:, s0:s0 + n], data0=ones4[:, 0:n], data1=ig[:, s0:s0 + n], initial=init, op0=ALU.mult, op1=ALU.max), reads=[ones4, ig, fg, minit], writes=[fg])
        k.op("dve", lambda e: e.tensor_tensor(out=Bt[:], in0=Bt[:], in1=fg[:], op=ALU.add), reads=[Bt, fg], writes=[Bt])
        with contextlib.ExitStack() as s2:
            mps = k.ps("ml_mps", [128, 4, nch + 1], F32, s2)
            for h in range(4):
                k.mm(mps, mps[:, h, 0:1], [(sel[:, h, :], minit[:, 0:1])], reads=[sel, minit])
                k.mm(mps, mps[:, h, 1:nch + 1], [(sel[:, h, :], fg[:, LC - 1:T:LC])], reads=[sel, fg])
            k.op("dve", lambda e: e.tensor_copy(out=MendB[:], in_=mps[:].rearrange("p h c -> p c h")), reads=[mps], writes=[MendB])
            k.op("dve", lambda e: e.tensor_scalar(out=nMendB[:], in0=MendB[:], scalar1=-1.0, scalar2=None, op0=ALU.mult), reads=[MendB], writes=[nMendB])
            k.op("dve", lambda e: e.tensor_tensor(out=decB[:], in0=MendB[:, 0:nch, :], in1=MendB[:, 1:nch + 1, :], op=ALU.subtract), reads=[MendB], writes=[decB])
            k.op("act", lambda e: e.activation(out=decB[:], in_=decB[:], func=AF.Exp), reads=[decB], writes=[decB])
            k.barrier()
        qk = [k.sb("ml_qk%d" % i, [128, 16, LC], BF16, st) for i in range(2)]
        ex = [k.sb("ml_ex%d" % i, [128, 24, LC], BF16, st) for i in range(2)]
        ktm = [k.sb("ml_ktm%d" % i, [128, D], BF16, st) for i in range(2)]
        vaug = [k.sb("ml_vaug%d" % i, [128, 4, 257], BF16, st) for i in range(2)]
        for v in vaug:
            k.op("pool", lambda e, v=v: e.memset(v[:], 1.0), writes=[v])
        cols = [k.sb("ml_cols%d" % i, [128, 12], F32, st) for i in range(2)]
        sm4 = [k.sb("ml_sm4%d" % i, [128, 16], F32, st) for i in range(2)]
        negm4 = k.sb("ml_negm4", [128, 4, 128], F32, st)
        for h in range(4):
            k.op("dve", lambda e, h=h: e.tensor_copy(out=negm4[:, h, :], in_=negm[:]), reads=[negm], writes=[negm4])
        w4 = [k.sb("ml_w4%d" % i, [128, 4, 128], F32, st) for i in range(2)]
        smb4 = [k.sb("ml_sm4b%d" % i, [128, 4, 128], BF16, st) for i in range(2)]
        nbs = [k.sb("ml_nbs%d" % i, [128, 257], F32, st) for i in range(2)]
        nd4 = [k.sb("ml_nd4%d" % i, [128, 4, 257], F32, st) for i in range(2)]
        sc1 = [k.sb("ml_sc%d" % i, [128, 20], F32, st) for i in range(2)]
        junk = k.sb("ml_junk", [128, 256], BF16, st)
        hn4 = [k.sb("ml_hn4%d" % i, [128, 4, 256], BF16, st) for i in range(2)]
        gk4 = [k.sb("ml_gk4%d" % i, [128, 4, 256], BF16, st) for i in range(2)]
        y8 = [k.sb("ml_y8%d" % i, [128, 8, LC], F32, st) for i in range(2)]
        yst = [k.sb("ml_yst%d" % i, [128, 8, LC], BF16, st) for i in range(2)]
        cps = k.ps("ml_cps", [128, 12], F32, st)
        mb4 = k.ps("ml_mb", [128, 4, 128], F32, st)
        sps4 = k.ps("ml_sps", [128, 4, 128], F32, st)
        tps4 = k.ps("ml_tps", [128, 8, 128], BF16, st)
        numA = k.ps("ml_numA", [128, 257], F32, st)
        numB = k.ps("ml_numB", [128, 257], F32, st)
        cups = [k.ps("ml_cups%d" % i, [128, 257], F32, st) for i in range(2)]
        identb = self.cst_b
        def ml_loads(c):
            t0, t1 = c * LC, (c + 1) * LC
            qk_, ex_, kt_, va_ = qk[c % 2], ex[c % 2], ktm[c % 2], vaug[c % 2]
            k.dma("sp", qk_[:], FM[0:16, :, t0:t1].rearrange("c p t -> p c t"), writes=[qk_])
            k.dma("sp", ex_[:], FM[16:40, :, t0:t1].rearrange("c p t -> p c t"), writes=[ex_])
            k.dma("sp", kt_[0:LC, :], TMs[0, t0:t1, :], writes=[kt_])
            k.dma("sp", va_[0:LC, :, 0:256], TMs[1, t0:t1, :].rearrange("t (h e) -> t h e", h=4), writes=[va_])

        ml_loads(0)
        for c in range(nch):
            t0, t1 = c * LC, (c + 1) * LC
            qk_ = qk[c % 2]
            ex_ = ex[c % 2]
            kt_ = ktm[c % 2]
            va_ = vaug[c % 2]
            if c + 1 < nch:
                ml_loads(c + 1)
            co = cols[c % 2]
            for qi, src in enumerate((ig, fg, Bt)):
                k.tr(cps, cps[0:LC, qi * 4:(qi + 1) * 4], src[0:4, t0:t1], self.cst_f[0:4, 0:4], reads=[src, self.cst_f])
            k.op("dve", lambda e: e.tensor_copy(out=co[0:LC, :], in_=cps[0:LC, :]), reads=[cps], writes=[co])
            s4 = sm4[c % 2]
            k.op("dve", lambda e: e.tensor_tensor(out=s4[0:LC, 0:4], in0=MendB[0:LC, c, :], in1=co[0:LC, 4:8], op=ALU.subtract), reads=[MendB, co], writes=[s4])
            k.op("dve", lambda e: e.tensor_tensor(out=s4[0:LC, 4:8], in0=co[0:LC, 0:4], in1=nMendB[0:LC, c + 1, :], op=ALU.add), reads=[nMendB, co], writes=[s4])
            k.op("dve", lambda e: e.tensor_scalar(out=s4[0:LC, 8:12], in0=co[0:LC, 8:12], scalar1=-1.0, scalar2=None, op0=ALU.mult), reads=[co], writes=[s4])
            k.op("act", lambda e: e.activation(out=s4[0:LC, 0:12], in_=s4[0:LC, 0:12], func=AF.Exp), reads=[s4], writes=[s4])
            ys = yst[c % 2]
            w_, sm_, nd_, sc, hn_, gk_, y_ = w4[c % 2], smb4[c % 2], nd4[c % 2], sc1[c % 2], hn4[c % 2], gk4[c % 2], y8[c % 2]
            for h in range(4):
                k.mm(mb4, mb4[:, h, 0:LC], [(sel[:, h, :], fg[0:4, t0:t1])], reads=[sel, fg])
            k.op("dve", lambda e: e.tensor_tensor(out=w_[0:LC, :, 0:LC], in0=negm4[0:LC, :, 0:LC], in1=mb4[0:LC, :, 0:LC], op=ALU.subtract), reads=[negm4, mb4], writes=[w_])
            for h in range(4):
                k.op("act", lambda e, h=h: e.activation(out=w_[0:LC, h, 0:LC], in_=w_[0:LC, h, 0:LC], func=AF.Exp, bias=co[0:LC, h:h + 1]), reads=[w_, co], writes=[w_])
            for h in range(4):
                k.mm(sps4, sps4[0:LC, h, 0:LC], [(qk_[:, 8 + 2 * h + ec, :], qk_[:, 2 * h + ec, :]) for ec in range(2)], reads=[qk_])
            k.op("dve", lambda e: e.tensor_tensor(out=sm_[0:LC, :, 0:LC], in0=sps4[0:LC, :, 0:LC], in1=w_[0:LC, :, 0:LC], op=ALU.mult), reads=[sps4, w_], writes=[sm_])
            for h in range(4):
                k.op("act", lambda e, h=h: e.activation(out=gk_[0:LC, h, :], in_=kt_[0:LC, h * 256:(h + 1) * 256], func=AF.Copy, scale=s4[0:LC, 4 + h:5 + h]), reads=[kt_, s4], writes=[gk_])
            for h in range(4):
                k.mm(numA, numA[0:LC, :], [(sm_[0:LC, h, 0:LC], va_[0:LC, h, :])], reads=[sm_, va_])
                k.mm(numB, numB[0:LC, :], [(qk_[:, 2 * h + dc, :], Cb[:, h, dc, :]) for dc in range(2)], reads=[qk_, Cb])
                nb_ = k.ring("ml_nbs", nbs)
                k.op("act", lambda e, nb_=nb_, h=h: e.activation(out=nb_[0:LC, :], in_=numB[0:LC, :], func=AF.Copy, scale=s4[0:LC, h:h + 1]), reads=[numB, s4], writes=[nb_])
                k.op("dve", lambda e, nb_=nb_, h=h: e.tensor_tensor(out=nd_[0:LC, h, :], in0=numA[0:LC, :], in1=nb_[0:LC, :], op=ALU.add), reads=[numA, nb_], writes=[nd_])
            k.op("pool", lambda e: e.memset(sc[:], 0.0), writes=[sc])
            k.op("act", lambda e: e.activation(out=sc[0:LC, 0:4], in_=nd_[0:LC, :, 256], func=AF.Abs), reads=[nd_, sc], writes=[sc])
            k.op("dve", lambda e: e.tensor_tensor(out=sc[0:LC, 0:4], in0=sc[0:LC, 0:4], in1=s4[0:LC, 8:12], op=ALU.max), reads=[sc, s4], writes=[sc])
            k.op("dve", lambda e: e.reciprocal(out=sc[0:LC, 4:8], in_=sc[0:LC, 0:4]), reads=[sc], writes=[sc])
            for h in range(4):
                k.op("act", lambda e, h=h: e.activation(out=junk[0:LC, :], in_=nd_[0:LC, h, 0:256], func=AF.Square, scale=sc[0:LC, 4 + h:5 + h], accum_out=sc[0:LC, 8 + h:9 + h]), reads=[nd_, sc], writes=[junk, sc])
            k.op("act", lambda e: e.activation(out=sc[0:LC, 12:16], in_=sc[0:LC, 8:12], func=AF.Ln, scale=1.0 / 256.0, bias=EPS), reads=[sc], writes=[sc])
            k.op("act", lambda e: e.activation(out=sc[0:LC, 12:16], in_=sc[0:LC, 12:16], func=AF.Exp, scale=-0.5), reads=[sc], writes=[sc])
            k.op("dve", lambda e: e.tensor_tensor(out=sc[0:LC, 16:20], in0=sc[0:LC, 12:16], in1=sc[0:LC, 4:8], op=ALU.mult), reads=[sc], writes=[sc])
            for h in range(4):
                k.op("dve", lambda e, h=h: e.scalar_tensor_tensor(out=hn_[0:LC, h, :], in0=nd_[0:LC, h, 0:256], scalar=sc[0:LC, 16 + h:17 + h], in1=ghr[0:LC, h * 256:(h + 1) * 256], op0=ALU.mult, op1=ALU.mult), reads=[nd_, sc, ghr], writes=[hn_])
            for h in range(4):
                for dc in range(2):
                    cu = cups[dc]
                    k.mm(cu, cu[:, :], [(gk_[0:LC, h, dc * 128:(dc + 1) * 128], va_[0:LC, h, :])], reads=[gk_, va_])
                    k.op("dve", lambda e, dc=dc, cu=cu, h=h: e.scalar_tensor_tensor(out=C[:, h, dc, :], in0=C[:, h, dc, :], scalar=decB[:, c, h:h + 1], in1=cu[:, :], op0=ALU.mult, op1=ALU.add), reads=[C, decB, cu], writes=[C])
            k.op("act", lambda e: e.activation(out=Cb[:], in_=C[:], func=AF.Copy), reads=[C], writes=[Cb])
            for h in range(4):
                for ec in range(2):
                    k.tr(tps4, tps4[:, 2 * h + ec, 0:LC], hn_[0:LC, h, ec * 128:(ec + 1) * 128], identb[0:LC, 0, 0:LC], reads=[hn_, identb])
            k.op("dve", lambda e: e.tensor_tensor(out=y_[:], in0=tps4[:, :, 0:LC], in1=ex_[:, 8:16, :], op=ALU.mult), reads=[tps4, ex_], writes=[y_])
            k.op("pool", lambda e: e.tensor_tensor(out=y_[:], in0=y_[:], in1=ex_[:, 0:8, :], op=ALU.add), reads=[y_, ex_], writes=[y_])
            k.op("pool", lambda e: e.tensor_tensor(out=ys[:], in0=y_[:], in1=ex_[:, 16:24, :], op=ALU.mult), reads=[y_, ex_], writes=[ys])
            k.dma("pool", YT[0:8, :, t0:t1].rearrange("c p t -> p c t"), ys[:], reads=[ys])
        co_, no_, mo_ = (O["mlc_p"], O["mln_p"], O["mlm_p"]) if isp else (O["mlc_s"], O["mln_s"], O["mlm_s"])
        for h in range(4):
            k.dma("pool", co_[j, h].rearrange("(c p) e -> p c e", p=128), C[:, h, :, 0:256], reads=[C])
            k.dma("pool", no_[j, h].rearrange("(c p o) -> p c o", p=128, o=1), C[:, h, :, 256:257], reads=[C], allow_slow_non_contiguous=True)
        k.dma("pool", mo_[j].rearrange("(h o) -> h o", o=1), Bt[:, T - 1:T], reads=[Bt])


def make_consts():
    c = np.zeros((128, 128 * 5 + 2048), np.float32)
    idx = np.arange(128)
    c[:, 0:128] = np.eye(128)
    c[:, 128:256] = (idx[:, None] >= idx[None, :])
    c[:, 256:384] = 1.0
    c[:, 384:512] = (idx[:, None] <= idx[None, :])
    q = np.arange(512)
    for jj in range(4):
        c[:, 640 + jj * 512: 640 + (jj + 1) * 512] = np.where((128 * jj + idx[:, None]) < q[None, :], 0.0, -30000.0)
    return c


def make_smask(PAST, TS, koff):
    m = np.zeros((128, 3, 32), np.float32)
    idx = np.arange(128)
    q = np.arange(32)
    nb = (koff + PAST + TS) // 128
    key = (nb - 1) * 128 + idx - koff
    m[:, 0, :TS] = np.where((key[:, None] < PAST) | ((key[:, None] - PAST) < q[None, :TS]), 0.0, -30000.0)
    fr = koff // 128
    key = fr * 128 + idx - koff
    m[:, 1, :TS] = np.where(key[:, None] >= 0, 0.0, -30000.0)
    m[:, 2, :] = -30000.0
    return m.reshape(128, 96)


_CACHE = {}


def _prep(inp):
    x_prompt = np.asarray(inp["x_prompt"], np.float32)
    x_sample = np.asarray(inp["x_sample"], np.float32)
    B, T, _ = x_prompt.shape
    SBN, TS, _ = x_sample.shape
    DEPTH = inp["g_pre"].shape[0]
    PAST = inp["cache_sb_k"].shape[2]
    key = (T, TS, PAST, DEPTH)
    if key not in _CACHE:
        p = Prog(T, TS, PAST, DEPTH)
        p.build()
        _CACHE[key] = p
    p = _CACHE[key]
    NSB, NML, NCV = p.NSB, p.NML, p.NCV
    f = lambda a: np.ascontiguousarray(np.asarray(a, np.float32))

    def rep(a):
        a = f(a)
        return np.ascontiguousarray(np.broadcast_to(a[:, None, :], (a.shape[0], 128, a.shape[1])))

    def colT(a):
        a = f(a)
        return np.ascontiguousarray(a.reshape(a.shape[0], 8, 128).transpose(0, 2, 1))

    def convT(a):
        a = f(a)
        return np.ascontiguousarray(a.reshape(a.shape[0], a.shape[1], 8, 128).transpose(0, 3, 2, 1))

    NBs = (PAST + TS + 127) // 128
    if NBs % 2:
        NBs += 1
    koff = NBs * 128 - (PAST + TS)
    consts = make_consts()
    smask = make_smask(PAST, TS, koff)
    nml = max(NML, 1)
    ncv = max(NCV, 1)

    def orz(a, shape):
        a = f(a)
        if a.shape[0] == 0:
            return np.zeros(shape, np.float32)
        return a

    convml = np.concatenate([f(inp["conv_ml_w"]), f(inp["conv_ml_b"])[:, None, :]], axis=1) if NML else np.zeros((1, 5, D), np.float32)
    gcol = np.ascontiguousarray(np.stack([f(inp["b_ig_ml"]), f(inp["b_fg_ml"])], axis=2)) if NML else np.zeros((1, 4, 2), np.float32)
    ii = np.arange(128)
    negmask = np.where(ii[:, None] <= ii[None, :], 0.0, -30000.0).astype(np.float32)
    sel4 = np.zeros((4, 4, 128), np.float32)
    for hh in range(4):
        sel4[hh, hh, :] = 1.0
    sel4 = sel4.reshape(4, 512)
    common = {
        "memp": None, "gpre_r": rep(inp["g_pre"]), "gpost_r": rep(inp["g_post"]), "gmem_r": rep(inp["g_mem"]),
        "wmemkv": f(inp["w_mem_kv"]), "wout": f(inp["w_out"]), "winsb": f(inp["w_in_sb"]),
        "winml": orz(inp["w_in_ml"], (1, D, 5128)), "wincv": orz(inp["w_in_cv"], (1, D, 5120)),
        "convmlT": convT(convml), "wq": orz(inp["wq_ml"], (1, 4, 256, 256)), "wk": orz(inp["wk_ml"], (1, 4, 256, 256)),
        "gcol": gcol, "ghead_r": rep(orz(inp["g_head_ml"], (1, D))), "negmask": negmask, "sel4": sel4, "skipT": colT(orz(inp["skip_ml"], (1, D))),
        "convcvT": convT(orz(inp["conv_cv_w"], (1, 3, D))), "consts": consts, "smask": smask,
    }
    in_maps = []
    ncores = 8
    for c in range(ncores):
        bp = c % B
        bs = c % SBN
        m = dict(common)
        m["xp"] = f(x_prompt[bp])
        m["xs"] = f(x_sample[bs])
        m["csk"] = f(inp["cache_sb_k"][:, bs])
        m["csv"] = f(inp["cache_sb_v"][:, bs])
        m["smc"] = orz(np.asarray(inp["state_ml_c"])[:, bs], (1, 4, 256, 256))
        m["smn"] = orz(np.asarray(inp["state_ml_n"])[:, bs], (1, 4, 256))
        m["smm"] = orz(np.asarray(inp["state_ml_m"])[:, bs], (1, 4))
        m["smconvT"] = convT(orz(np.asarray(inp["state_ml_conv"])[:, bs], (1, 3, D)))
        m["scvT"] = convT(orz(np.asarray(inp["state_cv_conv"])[:, bs], (1, 2, D)))
        m["cmk"] = f(inp["cache_mem_k"][:, bs])
        m["cmv"] = f(inp["cache_mem_v"][:, bs])
        m["memp"] = f(inp["mem_prompt"][bp])
        in_maps.append(m)
    return p, in_maps, B, SBN


def kernel(**inp):
    p, in_maps, B, SBN = _prep(inp)
    NML, NCV = p.NML, p.NCV
    res = run_bass_kernel_spmd(p.nc, in_maps, core_ids=list(range(len(in_maps))))
    R = res.results

    def gp(name, axis_b):
        return np.stack([np.asarray(R[c][name], np.float32) for c in range(B)], axis=axis_b)

    def gs(name, axis_b):
        return np.stack([np.asarray(R[c][name], np.float32) for c in range(SBN)], axis=axis_b)

    outs = (
        gp("yp", 0), gs("ys", 0),
        gp("sbk_p", 1), gp("sbv_p", 1),
        gp("mlc_p", 1)[:NML], gp("mln_p", 1)[:NML], gp("mlm_p", 1)[:NML], gp("mlconv_p", 1)[:NML],
        gp("cv_p", 1)[:NCV],
        gp("memk_p", 1), gp("memv_p", 1),
        gs("sbk_s", 1), gs("sbv_s", 1),
        gs("mlc_s", 1)[:NML], gs("mln_s", 1)[:NML], gs("mlm_s", 1)[:NML], gs("mlconv_s", 1)[:NML],
        gs("cv_s", 1)[:NCV],
    )
    return outs
```

```python
import contextlib
import math
import numpy as np
import concourse.bass as bass
import concourse.mybir as mybir
from concourse.bass_utils import run_bass_kernel_spmd

F32 = mybir.dt.float32
BF16 = mybir.dt.bfloat16
AF = mybir.ActivationFunctionType
ALU = mybir.AluOpType
AX = mybir.AxisListType

NDS = 48
D = 1024
EPS = 1e-6


class Tile:
    __slots__ = ("ap", "w", "r", "name")

    def __init__(self, ap, name=""):
        self.ap = ap
        self.w = None
        self.r = {}
        self.name = name

    def __getitem__(self, idx):
        return self.ap[idx]


class KB:
    def __init__(self, nc, stack):
        self.nc = nc
        self.stack = stack
        self.E = {"pe": nc.tensor, "act": nc.scalar, "dve": nc.vector, "pool": nc.gpsimd, "sp": nc.sync}
        self.csem = {e: stack.enter_context(nc.semaphore("c_" + e)) for e in ("pe", "act", "dve", "pool")}
        self.ccnt = {e: 0 for e in self.csem}
        self.dsem = [stack.enter_context(nc.semaphore("d%d" % i)) for i in range(NDS)]
        self.dcnt = [0] * NDS
        self.dnext = {"sp": 0, "pool": NDS // 2}
        self.seen = {e: {} for e in self.E}
        self.ninst = 0
        self.rr = {}

    def _wait(self, eng, tok):
        sem, val, key, kind = tok
        if self.seen[eng].get(key, 0) >= val:
            return
        self.E[eng].wait_ge(sem, val)
        self.seen[eng][key] = val
        self.ninst += 1

    def _deps(self, eng, reads, writes):
        toks = []
        for t in reads:
            if t.w is not None:
                toks.append(t.w)
        for t in writes:
            if t.w is not None:
                toks.append(t.w)
            toks.extend(t.r.values())
        for tok in toks:
            if eng == "pe" and tok[3] == "pe":
                continue
            self._wait(eng, tok)

    def _mark(self, tok, reads, writes):
        for t in reads:
            old = t.r.get(tok[2])
            if old is None or old[1] < tok[1]:
                t.r[tok[2]] = tok
        for t in writes:
            t.w = tok
            t.r = {}

    def op(self, eng, fn, reads=(), writes=()):
        self._deps(eng, reads, writes)
        ins = fn(self.E[eng])
        self.ccnt[eng] += 1
        ins.then_inc(self.csem[eng], 1)
        tok = (self.csem[eng], self.ccnt[eng], eng, eng)
        self._mark(tok, reads, writes)
        self.ninst += 1
        return ins

    def mm(self, out_tile, out_ap, pairs, reads, start=True, stop=True):
        writes = [out_tile]
        self._deps("pe", reads, writes)
        n = len(pairs)
        ins = None
        for i, (l, r) in enumerate(pairs):
            ins = self.nc.tensor.matmul(out_ap, l, r, start=(start and i == 0), stop=(stop and i == n - 1))
            self.ninst += 1
        self.ccnt["pe"] += 1
        ins.then_inc(self.csem["pe"], 1)
        tok = (self.csem["pe"], self.ccnt["pe"], "pe", "pe")
        self._mark(tok, reads, writes)
        return ins

    def tr(self, out_tile, out_ap, in_ap, ident_ap, reads):
        self._deps("pe", reads, [out_tile])
        ins = self.nc.tensor.transpose(out_ap, in_ap, ident_ap)
        self.ccnt["pe"] += 1
        ins.then_inc(self.csem["pe"], 1)
        tok = (self.csem["pe"], self.ccnt["pe"], "pe", "pe")
        self._mark(tok, reads, [out_tile])
        self.ninst += 1
        return ins

    def dma(self, q, out_ap, in_ap, reads=(), writes=(), **kw):
        self._deps(q, reads, writes)
        i = self.dnext[q]
        base = 0 if q == "sp" else NDS // 2
        self.dnext[q] = base + (i - base + 1) % (NDS // 2)
        if self.dcnt[i] > 0:
            self._wait(q, (self.dsem[i], 16 * self.dcnt[i], ("d", i), "dma"))
        self.E[q].dma_start(out=out_ap, in_=in_ap, **kw).then_inc(self.dsem[i], 16)
        self.dcnt[i] += 1
        tok = (self.dsem[i], 16 * self.dcnt[i], ("d", i), "dma")
        self._mark(tok, reads, writes)
        self.ninst += 1

    def barrier(self, engines=("pe", "act", "dve", "pool", "sp")):
        for e in engines:
            for c in self.csem:
                if self.ccnt[c] > 0:
                    self._wait(e, (self.csem[c], self.ccnt[c], c, c))
            for i in range(NDS):
                if self.dcnt[i] > 0:
                    self._wait(e, (self.dsem[i], 16 * self.dcnt[i], ("d", i), "dma"))

    def sb(self, name, shape, dtype, stack=None):
        self.uid = getattr(self, "uid", 0) + 1
        name = "%s_u%d" % (name, self.uid)
        t = (stack or self.stack).enter_context(self.nc.sbuf_tensor(name, list(shape), dtype))
        return Tile(t, name)

    def ps(self, name, shape, dtype=F32, stack=None):
        self.uid = getattr(self, "uid", 0) + 1
        name = "%s_u%d" % (name, self.uid)
        t = (stack or self.stack).enter_context(self.nc.psum_tensor(name, list(shape), dtype))
        return Tile(t, name)

    def ring(self, key, tiles):
        i = self.rr.get(key, 0)
        self.rr[key] = i + 1
        return tiles[i % len(tiles)]


class Prog:
    def __init__(self, T, TS, PAST, DEPTH):
        self.T, self.TS, self.PAST, self.DEPTH = T, TS, PAST, DEPTH
        self.NSB, self.NML, self.NCV = (DEPTH + 2) // 3, (DEPTH + 1) // 3, DEPTH // 3
        self.nc = bass.Bass("TRN2", target_bir_lowering=False)
        self.in_shapes = {}
        self.out_shapes = {}

    def din(self, name, shape, dt=F32):
        self.in_shapes[name] = tuple(shape)
        return self.nc.dram_tensor(name, list(shape), dt, kind="ExternalInput").ap()

    def dout(self, name, shape, dt=F32):
        self.out_shapes[name] = tuple(shape)
        return self.nc.dram_tensor(name, list(shape), dt, kind="ExternalOutput").ap()

    def dscr(self, name, shape, dt=F32):
        return self.nc.dram_tensor(name, list(shape), dt, kind="Internal").ap()

    def build(self):
        T, TS, PAST, DEPTH = self.T, self.TS, self.PAST, self.DEPTH
        NSB, NML, NCV = self.NSB, self.NML, self.NCV
        I = {}
        I["xp"] = self.din("xp", [T, D])
        I["xs"] = self.din("xs", [TS, D])
        I["csk"] = self.din("csk", [NSB, PAST, 8, 128])
        I["csv"] = self.din("csv", [NSB, PAST, 8, 128])
        I["smc"] = self.din("smc", [max(NML, 1), 4, 256, 256])
        I["smn"] = self.din("smn", [max(NML, 1), 4, 256])
        I["smm"] = self.din("smm", [max(NML, 1), 4])
        I["smconvT"] = self.din("smconvT", [max(NML, 1), 128, 8, 3])
        I["scvT"] = self.din("scvT", [max(NCV, 1), 128, 8, 2])
        I["cmk"] = self.din("cmk", [DEPTH, 256, 4, 128])
        I["cmv"] = self.din("cmv", [DEPTH, 256, 4, 128])
        I["memp"] = self.din("memp", [256, D])
        I["gpre_r"] = self.din("gpre_r", [DEPTH, 128, D])
        I["gpost_r"] = self.din("gpost_r", [DEPTH, 128, D])
        I["gmem_r"] = self.din("gmem_r", [DEPTH, 128, D])
        I["wmemkv"] = self.din("wmemkv", [DEPTH, D, D])
        I["wout"] = self.din("wout", [DEPTH, 1536, D])
        I["winsb"] = self.din("winsb", [NSB, D, 5120])
        I["winml"] = self.din("winml", [max(NML, 1), D, 5128])
        I["wincv"] = self.din("wincv", [max(NCV, 1), D, 5120])
        I["convmlT"] = self.din("convmlT", [max(NML, 1), 128, 8, 5])
        I["wq"] = self.din("wq", [max(NML, 1), 4, 256, 256])
        I["wk"] = self.din("wk", [max(NML, 1), 4, 256, 256])
        I["gcol"] = self.din("gcol", [max(NML, 1), 4, 2])
        I["ghead_r"] = self.din("ghead_r", [max(NML, 1), 128, D])
        I["negmask"] = self.din("negmask", [128, 128])
        I["sel4"] = self.din("sel4", [4, 512])
        I["skipT"] = self.din("skipT", [max(NML, 1), 128, 8])
        I["convcvT"] = self.din("convcvT", [max(NCV, 1), 128, 8, 3])
        I["consts"] = self.din("consts", [128, 128 * 5 + 4 * 512 * 1])
        I["smask"] = self.din("smask", [128, 3 * 32])
        O = {}
        O["yp"] = self.dout("yp", [T, D])
        O["ys"] = self.dout("ys", [TS, D])
        O["sbk_p"] = self.dout("sbk_p", [NSB, T, 8, 128])
        O["sbv_p"] = self.dout("sbv_p", [NSB, T, 8, 128])
        O["mlc_p"] = self.dout("mlc_p", [max(NML, 1), 4, 256, 256])
        O["mln_p"] = self.dout("mln_p", [max(NML, 1), 4, 256])
        O["mlm_p"] = self.dout("mlm_p", [max(NML, 1), 4])
        O["mlconv_p"] = self.dout("mlconv_p", [max(NML, 1), 3, D])
        O["cv_p"] = self.dout("cv_p", [max(NCV, 1), 2, D])
        O["memk_p"] = self.dout("memk_p", [DEPTH, 256, 4, 128])
        O["memv_p"] = self.dout("memv_p", [DEPTH, 256, 4, 128])
        O["sbk_s"] = self.dout("sbk_s", [NSB, TS, 8, 128])
        O["sbv_s"] = self.dout("sbv_s", [NSB, TS, 8, 128])
        O["mlc_s"] = self.dout("mlc_s", [max(NML, 1), 4, 256, 256])
        O["mln_s"] = self.dout("mln_s", [max(NML, 1), 4, 256])
        O["mlm_s"] = self.dout("mlm_s", [max(NML, 1), 4])
        O["mlconv_s"] = self.dout("mlconv_s", [max(NML, 1), 3, D])
        O["cv_s"] = self.dout("cv_s", [max(NCV, 1), 2, D])
        self.I, self.O = I, O
        self.SP = dict(name="p", T=T, xin=I["xp"], yout=O["yp"], xres=self.dscr("xres_p", [T, D]),
                       FM=self.dscr("fm_p", [40, 128, T], BF16), YT=self.dscr("yt_p", [12, 128, T], BF16),
                       TMs=self.dscr("tm_p", [2, T, D], BF16), GT=self.dscr("gt_p", [8, T], F32))
        self.SS = dict(name="s", T=TS, xin=I["xs"], yout=O["ys"], xres=self.dscr("xres_s", [TS, D]),
                       FM=self.dscr("fm_s", [40, 128, TS], BF16), YT=self.dscr("yt_s", [12, 128, TS], BF16),
                       TMs=self.dscr("tm_s", [2, TS, D], BF16), GT=self.dscr("gt_s", [8, TS], F32))
        with contextlib.ExitStack() as st:
            k = KB(self.nc, st)
            self.k = k
            self.cst_f = k.sb("cst_f", [128, 128], F32)
            self.cst_b = k.sb("cst_b", [128, 4, 128], BF16)
            self.msk = k.sb("msk", [128, 4, 512], BF16)
            self.smsk = k.sb("smsk", [128, 3, 32], BF16)
            with contextlib.ExitStack() as s2:
                tmp = k.sb("cst_tmp", [128, 128 * 5 + 2048], F32, s2)
                tmp2 = k.sb("cst_tmp2", [128, 96], F32, s2)
                k.dma("sp", tmp[:], I["consts"][:, :], writes=[tmp])
                k.dma("sp", tmp2[:], I["smask"][:, :], writes=[tmp2])
                k.op("dve", lambda e: e.tensor_copy(out=self.cst_f[:], in_=tmp[:, 0:128]), reads=[tmp], writes=[self.cst_f])
                k.op("dve", lambda e: e.tensor_copy(out=self.cst_b[:].rearrange("p a b -> p (a b)"), in_=tmp[:, 0:512]), reads=[tmp], writes=[self.cst_b])
                k.op("dve", lambda e: e.tensor_copy(out=self.msk[:].rearrange("p a b -> p (a b)"), in_=tmp[:, 640:640 + 2048]), reads=[tmp], writes=[self.msk])
                k.op("dve", lambda e: e.tensor_copy(out=self.smsk[:].rearrange("p a b -> p (a b)"), in_=tmp2[:]), reads=[tmp2], writes=[self.smsk])
                k.barrier()
            for li in range(DEPTH):
                kind, j = li % 3, li // 3
                self.layer(li, kind, j)
            k.barrier()
        return self.nc

    def load_w_gen(self, k, st, dst, src, ncols, nkc, name, CP=256):
        stg = [k.sb("%s_stg%d" % (name, i), [128, nkc, CP], F32, st) for i in range(2)]
        srcv = src.rearrange("(c p) n -> p c n", p=128)
        c0 = 0
        i = 0
        while c0 < ncols:
            cw = min(CP, ncols - c0)
            s = stg[i % 2]
            k.dma("sp", s[:, :, 0:cw], srcv[:, :, c0:c0 + cw], writes=[s])
            eng = "pool" if i % 2 == 0 else "act"
            if eng == "pool":
                k.op("pool", lambda e, s=s, c0=c0, cw=cw: e.tensor_copy(out=dst[:, :, c0:c0 + cw], in_=s[:, :, 0:cw]), reads=[s], writes=[dst])
            else:
                k.op("act", lambda e, s=s, c0=c0, cw=cw: e.activation(out=dst[:, :, c0:c0 + cw], in_=s[:, :, 0:cw], func=AF.Copy), reads=[s], writes=[dst])
            c0 += cw
            i += 1
            yield

    def load_w(self, k, st, dst, src, ncols, nkc, name, CP=256):
        for _ in self.load_w_gen(k, st, dst, src, ncols, nkc, name, CP):
            pass

    def layer(self, li, kind, j):
        k = self.k
        I, O = self.I, self.O
        last = (li == self.DEPTH - 1)
        with contextlib.ExitStack() as st:
            mk = {"p": k.sb("mkT_p", [128, 4, 256], BF16, st), "s": k.sb("mkT_s", [128, 4, 256], BF16, st)}
            mv = {"p": k.sb("mv_p", [128, 2, 512], BF16, st), "s": k.sb("mv_s", [128, 2, 512], BF16, st)}
            self.memkv_phase(li, mk, mv)
            k.barrier()
            with contextlib.ExitStack() as s1:
                ncols = 5128 if kind == 1 else 5120
                if getattr(self, "wb_pref", None) is not None:
                    Wb = self.wb_pref
                else:
                    Wb = k.sb("Wb", [128, 8, ncols], BF16, s1)
                    wsrc = (I["winsb"], I["winml"], I["wincv"])[kind][j]
                    with contextlib.ExitStack() as sw:
                        self.load_w(k, sw, Wb, wsrc, ncols, 8, "win")
                        k.barrier()
                for stm in (self.SS, self.SP):
                    with contextlib.ExitStack() as s3:
                        self.p1(stm, li, kind, j, Wb, mk[stm["name"]], mv[stm["name"]], s3)
                        k.barrier()
        if getattr(self, "wb_pref", None) is not None:
            self.wb_stack.close()
            self.wb_pref = None
        if kind == 0:
            for stm in (self.SS, self.SP):
                with contextlib.ExitStack() as s3:
                    if stm is self.SS:
                        self.p2_sb_sample(stm, li, j, s3)
                    else:
                        self.p2_sb(stm, li, j, s3)
                    k.barrier()
        elif kind == 1:
            for stm in (self.SS, self.SP):
                with contextlib.ExitStack() as s3:
                    self.p2_ml(stm, li, j, s3)
                    k.barrier()
        bg = None
        if not last:
            nkind, nj = (li + 1) % 3, (li + 1) // 3
            nncols = 5128 if nkind == 1 else 5120
            self.wb_stack = contextlib.ExitStack()
            self.wb_pref = k.sb("Wbn", [128, 8, nncols], BF16, self.wb_stack)
            self.wb_stg_stack = contextlib.ExitStack()
            nsrc = (I["winsb"], I["winml"], I["wincv"])[nkind][nj]
            bg = self.load_w_gen(k, self.wb_stg_stack, self.wb_pref, nsrc, nncols, 8, "winn")
            next(bg, None)
        with contextlib.ExitStack() as s1:
            Wo = k.sb("Wo", [128, 12, D], BF16, s1)
            gpo = k.sb("gpo", [128, D], F32, s1)
            with contextlib.ExitStack() as sw:
                self.load_w(k, sw, Wo, I["wout"][li], D, 12, "wout")
                k.dma("sp", gpo[:], I["gpost_r"][li], writes=[gpo])
                k.barrier()
            for stm in (self.SS, self.SP):
                with contextlib.ExitStack() as s3:
                    self.p3(stm, li, Wo, gpo, last, s3, bg if stm is self.SP else None)
                    if stm is self.SP and bg is not None:
                        for _ in bg:
                            pass
                    k.barrier()
        if bg is not None:
            self.wb_stg_stack.close()

    def rms_front(self, k, xsrc_ap, np_, xt, junk, ss, hb, grep):
        k.dma("sp", xt[0:np_, :], xsrc_ap, writes=[xt])
        k.op("pool", lambda e: e.memset(ss[:], 0.0), writes=[ss])
        k.op("act", lambda e: e.activation(out=junk[0:np_, :], in_=xt[0:np_, :], func=AF.Square, accum_out=ss[0:np_, 0:1]), reads=[xt, ss], writes=[junk, ss])
        k.op("act", lambda e: e.activation(out=ss[0:np_, 1:2], in_=ss[0:np_, 0:1], func=AF.Ln, scale=1.0 / D, bias=EPS), reads=[ss], writes=[ss])
        k.op("act", lambda e: e.activation(out=ss[0:np_, 1:2], in_=ss[0:np_, 1:2], func=AF.Exp, scale=-0.5), reads=[ss], writes=[ss])
        k.op("dve", lambda e: e.scalar_tensor_tensor(out=hb[0:np_, :], in0=xt[0:np_, :], scalar=ss[0:np_, 1:2], in1=grep[0:np_, :], op0=ALU.mult, op1=ALU.mult), reads=[xt, ss, grep], writes=[hb])

    def to_fm(self, k, hb, np_, ptr, hT, col0):
        for kc in range(8):
            k.tr(ptr, ptr[:, kc, 0:np_], hb[0:np_, kc * 128:(kc + 1) * 128], self.cst_b[0:np_, 0, 0:np_], reads=[hb, self.cst_b])
        k.op("dve", lambda e: e.tensor_copy(out=hT[:, :, col0:col0 + np_], in_=ptr[:, :, 0:np_]), reads=[ptr], writes=[hT])

    def memkv_phase(self, li, mk, mv):
        k = self.k
        I, O = self.I, self.O
        with contextlib.ExitStack() as st:
            Wm = k.sb("Wm", [128, 8, D], BF16, st)
            gm = k.sb("gm", [128, D], F32, st)
            with contextlib.ExitStack() as sw:
                self.load_w(k, sw, Wm, I["wmemkv"][li], D, 8, "wmem")
                k.barrier()
            k.dma("sp", gm[:], I["gmem_r"][li], writes=[gm])
            xt = [k.sb("mxt%d" % i, [128, D], F32, st) for i in range(2)]
            junk = k.sb("mjunk", [128, D], BF16, st)
            ss = [k.sb("mss%d" % i, [128, 2], F32, st) for i in range(2)]
            hb = [k.sb("mhb%d" % i, [128, D], BF16, st) for i in range(2)]
            hT = k.sb("mhT", [128, 8, 256], BF16, st)
            ptr = k.ps("mptr", [128, 8, 128], BF16, st)
            acc = [k.ps("macc%d" % i, [128, 512], F32, st) for i in range(3)]
            og = [k.sb("mog%d" % i, [128, 512], F32, st) for i in range(2)]
            for s in range(2):
                self.rms_front(k, I["memp"][s * 128:(s + 1) * 128, :], 128, xt[s], junk, ss[s], hb[s], gm)
                self.to_fm(k, hb[s], 128, ptr, hT, s * 128)
            for s in range(2):
                for nb in range(2):
                    a = k.ring("macc", acc)
                    k.mm(a, a[:, :], [(hT[:, kc, s * 128:(s + 1) * 128], Wm[:, kc, nb * 512:(nb + 1) * 512]) for kc in range(8)], reads=[hT, Wm])
                    o = k.ring("mog", og)
                    k.op("act", lambda e, o=o, a=a: e.activation(out=o[:], in_=a[:], func=AF.Copy), reads=[a], writes=[o])
                    dst = (O["memk_p"], O["memv_p"])[nb][li, s * 128:(s + 1) * 128].rearrange("m h d -> m (h d)")
                    k.dma("pool", dst, o[:], reads=[o])
                    if nb == 1:
                        k.op("dve", lambda e, o=o, s=s: e.tensor_copy(out=mv["p"][:, s, :], in_=o[:]), reads=[o], writes=[mv["p"]])
            for h in range(4):
                a = k.ring("macc", acc)
                k.mm(a, a[:, 0:256], [(Wm[:, kc, h * 128:(h + 1) * 128], hT[:, kc, :]) for kc in range(8)], reads=[hT, Wm])
                k.op("dve", lambda e, a=a, h=h: e.tensor_copy(out=mk["p"][:, h, :], in_=a[:, 0:256]), reads=[a], writes=[mk["p"]])
            ck = k.sb("mck", [128, 2, 512], F32, st)
            cv = k.sb("mcv", [128, 2, 512], F32, st)
            k.dma("sp", ck[:], I["cmk"][li].rearrange("(s p) h d -> p s (h d)", p=128), writes=[ck])
            k.dma("sp", cv[:], I["cmv"][li].rearrange("(s p) h d -> p s (h d)", p=128), writes=[cv])
            k.op("dve", lambda e: e.tensor_copy(out=mv["s"][:], in_=cv[:]), reads=[cv], writes=[mv["s"]])
            for h in range(4):
                a = k.ring("macc", acc)
                for s in range(2):
                    k.tr(a, a[:, s * 128:(s + 1) * 128], ck[:, s, h * 128:(h + 1) * 128], self.cst_f[:], reads=[ck, self.cst_f])
                k.op("dve", lambda e, a=a, h=h: e.tensor_copy(out=mk["s"][:, h, :], in_=a[:, 0:256]), reads=[a], writes=[mk["s"]])

    def p1(self, stm, li, kind, j, Wb, mk, mv, st):
        k = self.k
        I, O = self.I, self.O
        T = stm["T"]
        isp = stm["name"] == "p"
        TT = min(256 if kind == 1 else 512, T)
        np_ = min(128, T)
        nsub = TT // np_
        ntt = T // TT
        xsrc = stm["xin"] if li == 0 else stm["xres"]
        FM, YT = stm["FM"], stm["YT"]
        if kind == 1:
            self.tmob = [k.sb("tmob%d" % i, [128, 512], BF16, st) for i in range(2)]
        gpre = k.sb("gpre", [128, D], F32, st)
        k.dma("sp", gpre[:], I["gpre_r"][li], writes=[gpre])
        xt = [k.sb("xt%d" % i, [128, D], F32, st) for i in range(3)]
        junk = k.sb("junk", [128, D], BF16, st)
        ss = [k.sb("ss%d" % i, [128, 2], F32, st) for i in range(3)]
        hb = [k.sb("hb%d" % i, [128, D], BF16, st) for i in range(2)]
        hT = [k.sb("hT%d" % i, [128, 8, TT], BF16, st) for i in range(2)]
        ptr = k.ps("ptr", [128, 8, 128], BF16, st)
        acc = [k.ps("acc%d" % i, [128, 512], F32, st) for i in range(4)]
        memS = k.ps("memS", [128, 2, 512], F32, st)
        nring = 2 if kind == 2 else 3
        stg = [k.sb("stg%d" % i, [128, 4, TT], BF16, st) for i in range(nring)]
        tmo = [k.sb("tmo%d" % i, [128, 512], F32, st) for i in range(nring)]
        mq = k.sb("mq", [128, 4, TT], BF16, st)
        zm = k.sb("zm", [128, 4, TT], BF16, st)
        eT = k.sb("eT", [128, 2, TT], BF16, st)
        rden = k.sb("rden", [128, TT], F32, st)
        ymem = k.sb("ymem", [128, 4, TT], BF16, st)
        if kind == 0:
            mqc, zc, zmc = 3072, 3584, 4608
        elif kind == 1:
            mqc, zc, zmc = 3080, 3592, 4616
        else:
            mqc, zc, zmc = 3072, 3584, 4608

        def fm_chunk(hTt, col):
            a = k.ring("acc", acc)
            k.mm(a, a[:, 0:TT], [(Wb[:, kc, col:col + 128], hTt[:, kc, :]) for kc in range(8)], reads=[hTt, Wb])
            return a

        def fm_group(hTt, col0, nch, func, tok0, dst_fm0=None, dst_tile=None, scale=1.0):
            t = dst_tile if dst_tile is not None else k.ring("stg", stg)
            for c in range(nch):
                a = fm_chunk(hTt, col0 + c * 128)
                if func is None:
                    eng = k.ring("evac", ["dve", "act"])
                    if eng == "dve":
                        k.op("dve", lambda e, a=a, c=c: e.tensor_copy(out=t[:, c, :], in_=a[:, 0:TT]), reads=[a], writes=[t])
                    else:
                        k.op("act", lambda e, a=a, c=c: e.activation(out=t[:, c, :], in_=a[:, 0:TT], func=AF.Copy), reads=[a], writes=[t])
                else:
                    k.op("act", lambda e, a=a, c=c: e.activation(out=t[:, c, :], in_=a[:, 0:TT], func=func, scale=scale), reads=[a], writes=[t])
            if dst_fm0 is not None:
                k.dma("pool", FM[dst_fm0:dst_fm0 + nch, :, tok0:tok0 + TT].rearrange("c p t -> p c t"), t[:, 0:nch, :], reads=[t])
            return t

        def tm_block(hTt, s, col0, ncols):
            a = k.ring("acc", acc)
            k.mm(a, a[0:np_, 0:ncols], [(hTt[:, kc, s * np_:(s + 1) * np_], Wb[:, kc, col0:col0 + ncols]) for kc in range(8)], reads=[hTt, Wb])
            return a

        def mem_attn(tok0):
            sc = 1.0 / math.sqrt(128.0)
            for h in range(4):
                for mb in range(2):
                    k.mm(memS, memS[:, mb, 0:TT], [(mk[:, h, mb * 128:(mb + 1) * 128], mq[:, h, :])], reads=[mk, mq])
                k.op("act", lambda e: e.activation(out=eT[:], in_=memS[:, :, 0:TT], func=AF.Exp, scale=sc), reads=[memS], writes=[eT])
                den = k.ring("acc", acc)
                k.mm(den, den[:, 0:TT], [(self.cst_b[:, 2, :], eT[:, mb, :]) for mb in range(2)], reads=[eT, self.cst_b])
                oT = k.ring("acc", acc)
                k.mm(oT, oT[:, 0:TT], [(mv[:, mb, h * 128:(h + 1) * 128], eT[:, mb, :]) for mb in range(2)], reads=[eT, mv])
                k.op("dve", lambda e, den=den: e.reciprocal(out=rden[:], in_=den[:, 0:TT]), reads=[den], writes=[rden])
                k.op("dve", lambda e: e.tensor_tensor(out=rden[:], in0=rden[:], in1=zm[:, h, :], op=ALU.mult), reads=[rden, zm], writes=[rden])
                k.op("dve", lambda e, oT=oT, h=h: e.tensor_tensor(out=ymem[:, h, :], in0=oT[:, 0:TT], in1=rden[:], op=ALU.mult), reads=[oT, rden], writes=[ymem])
            k.dma("pool", YT[8:12, :, tok0:tok0 + TT].rearrange("c p t -> p c t"), ymem[:], reads=[ymem])

        if kind == 1:
            cw = k.sb("cw", [128, 8, 5], F32, st)
            k.dma("sp", cw[:], I["convmlT"][j], writes=[cw])
            xm = k.sb("xm", [128, 8, 3 + TT], F32, st)
            if isp:
                k.op("pool", lambda e: e.memset(xm[:, :, 0:3], 0.0), writes=[xm])
            else:
                k.dma("sp", xm[:, :, 0:3], I["smconvT"][j], writes=[xm])
            cacc = [k.sb("cacc%d" % i, [128, TT], F32, st) for i in range(2)]
            xc = k.sb("xc", [128, 8, TT], BF16, st)
            wqb = k.sb("wqb", [128, 8, 256], BF16, st)
            wkb = k.sb("wkb", [128, 8, 256], BF16, st)
            with contextlib.ExitStack() as sw:
                self.load_w(k, sw, wqb, I["wq"][j].rearrange("h d e -> (h d) e"), 256, 8, "wq")
                self.load_w(k, sw, wkb, I["wk"][j].rearrange("h d e -> (h d) e"), 256, 8, "wk")
                k.barrier()
            gtl = [k.sb("gtl%d" % i, [8, TT], F32, st) for i in range(2)]
            self.xcs = [k.sb("xcs%d" % i, [128, 8, TT], BF16, st) for i in range(2)]
            self.skp1 = k.sb("skp1", [128, 8], F32, st)
            k.dma("sp", self.skp1[:], I["skipT"][j], writes=[self.skp1])
        if kind == 2:
            cw = k.sb("cw", [128, 8, 3], F32, st)
            k.dma("sp", cw[:], I["convcvT"][j], writes=[cw])
            ch = k.sb("ch", [128, 8, 2 + TT], F32, st)
            if isp:
                k.op("pool", lambda e: e.memset(ch[:, :, 0:2], 0.0), writes=[ch])
            else:
                k.dma("sp", ch[:, :, 0:2], I["scvT"][j], writes=[ch])
            bT = [k.sb("bT%d" % i, [128, 4, TT], BF16, st) for i in range(2)]
            cT = [k.sb("cT%d" % i, [128, 4, TT], BF16, st) for i in range(2)]
            cacc = [k.sb("cacc%d" % i, [128, TT], F32, st) for i in range(2)]
            yst = [k.sb("yst%d" % i, [128, 4, TT], BF16, st) for i in range(2)]

        for tt in range(ntt):
            tok0 = tt * TT
            hTt = hT[tt % 2]
            for s in range(nsub):
                x_ = k.ring("xt", xt)
                s_ = k.ring("ss", ss)
                h_ = k.ring("hb", hb)
                self.rms_front(k, xsrc[tok0 + s * np_: tok0 + (s + 1) * np_, :], np_, x_, junk, s_, h_, gpre)
                self.to_fm(k, h_, np_, ptr, hTt, s * np_)
            fm_group(hTt, mqc, 4, None, tok0, dst_tile=mq)
            fm_group(hTt, zmc, 4, AF.Silu, tok0, dst_tile=zm)
            mem_attn(tok0)
            if kind == 0:
                fm_group(hTt, 0, 4, None, tok0, dst_fm0=0)
                fm_group(hTt, 512, 4, None, tok0, dst_fm0=4)
                fm_group(hTt, 1024, 4, None, tok0, dst_fm0=8)
                fm_group(hTt, 1536, 4, None, tok0, dst_fm0=12)
                fm_group(hTt, zc, 4, AF.Silu, tok0, dst_fm0=16)
                fm_group(hTt, zc + 512, 4, AF.Silu, tok0, dst_fm0=20)
                ko = (O["sbk_p"] if isp else O["sbk_s"])[j].rearrange("t h d -> t (h d)")
                vo = (O["sbv_p"] if isp else O["sbv_s"])[j].rearrange("t h d -> t (h d)")
                for s in range(nsub):
                    for (dst, c0) in ((ko, 1024), (vo, 2048)):
                        for nb in range(2):
                            a = tm_block(hTt, s, c0 + nb * 512, 512)
                            o = k.ring("tmo", tmo)
                            eng = k.ring("evac", ["dve", "act"])
                            if eng == "dve":
                                k.op("dve", lambda e, a=a, o=o: e.tensor_copy(out=o[0:np_, :], in_=a[0:np_, :]), reads=[a], writes=[o])
                            else:
                                k.op("act", lambda e, a=a, o=o: e.activation(out=o[0:np_, :], in_=a[0:np_, :], func=AF.Copy), reads=[a], writes=[o])
                            k.dma("pool", dst[tok0 + s * np_: tok0 + (s + 1) * np_, nb * 512:(nb + 1) * 512], o[0:np_, :], reads=[o])
            elif kind == 2:
                for g in range(2):
                    bt = fm_group(hTt, 0 + g * 512, 4, None, tok0, dst_tile=bT[g])
                    ct = fm_group(hTt, 1024 + g * 512, 4, None, tok0, dst_tile=cT[g])
                    for c in range(4):
                        a = fm_chunk(hTt, 2048 + (g * 4 + c) * 128)
                        k.op("dve", lambda e, a=a, c=c, g=g, ct=ct: e.tensor_tensor(out=ch[:, g * 4 + c, 2:2 + TT], in0=a[:, 0:TT], in1=ct[:, c, :], op=ALU.mult), reads=[a, ct, ch], writes=[ch])
                    zt = fm_group(hTt, zc + g * 512, 4, AF.Silu, tok0, dst_tile=k.ring("stg", stg))
                    yt = yst[g]
                    for c in range(4):
                        cc = g * 4 + c
                        ca = k.ring("cacc", cacc)
                        k.op("dve", lambda e, ca=ca, cc=cc: e.tensor_scalar(out=ca[:], in0=ch[:, cc, 0:TT], scalar1=cw[:, cc, 0:1], scalar2=None, op0=ALU.mult), reads=[ch, cw], writes=[ca])
                        k.op("dve", lambda e, ca=ca, cc=cc: e.scalar_tensor_tensor(out=ca[:], in0=ch[:, cc, 1:1 + TT], scalar=cw[:, cc, 1:2], in1=ca[:], op0=ALU.mult, op1=ALU.add), reads=[ch, cw, ca], writes=[ca])
                        k.op("dve", lambda e, ca=ca, cc=cc: e.scalar_tensor_tensor(out=ca[:], in0=ch[:, cc, 2:2 + TT], scalar=cw[:, cc, 2:3], in1=ca[:], op0=ALU.mult, op1=ALU.add), reads=[ch, cw, ca], writes=[ca])
                        k.op("pool", lambda e, ca=ca, c=c, bt=bt: e.tensor_tensor(out=ca[:], in0=ca[:], in1=bt[:, c, :], op=ALU.mult), reads=[ca, bt], writes=[ca])
                        k.op("pool", lambda e, ca=ca, c=c, zt=zt, yt=yt: e.tensor_tensor(out=yt[:, c, :], in0=ca[:], in1=zt[:, c, :], op=ALU.mult), reads=[ca, zt], writes=[yt])
                    k.dma("pool", YT[g * 4:g * 4 + 4, :, tok0:tok0 + TT].rearrange("c p t -> p c t"), yt[:], reads=[yt])
                if tt == ntt - 1:
                    s = nsub - 1
                    cvo = (O["cv_p"] if isp else O["cv_s"])[j]
                    for nb in range(2):
                        a1 = tm_block(hTt, s, 1024 + nb * 512, 512)
                        o1 = k.ring("tmo", tmo)
                        k.op("act", lambda e, a1=a1, o1=o1: e.activation(out=o1[0:np_, :], in_=a1[0:np_, :], func=AF.Copy), reads=[a1], writes=[o1])
                        a2 = tm_block(hTt, s, 2048 + nb * 512, 512)
                        k.op("dve", lambda e, a2=a2, o1=o1: e.tensor_tensor(out=o1[0:np_, :], in0=a2[0:np_, :], in1=o1[0:np_, :], op=ALU.mult), reads=[a2, o1], writes=[o1])
                        k.dma("pool", cvo[:, nb * 512:(nb + 1) * 512], o1[np_ - 2:np_, :], reads=[o1])
                if tt < ntt - 1:
                    k.op("dve", lambda e: e.tensor_copy(out=ch[:, :, 0:2], in_=ch[:, :, TT:TT + 2]), reads=[ch], writes=[ch])
            else:
                self.p1_ml(stm, li, j, tt, ntt, tok0, TT, np_, nsub, hTt, fm_chunk, fm_group, tm_block, tmo, stg, acc, xm, cw, cacc, xc, wqb, wkb, gtl, zc, Wb)

    def p1_ml(self, stm, li, j, tt, ntt, tok0, TT, np_, nsub, hTt, fm_chunk, fm_group, tm_block, tmo, stg, acc, xm, cw, cacc, xc, wqb, wkb, gtl, zc, Wb):
        k = self.k
        I, O = self.I, self.O
        isp = stm["name"] == "p"
        FM, TMs, GT = stm["FM"], stm["TMs"], stm["GT"]
        for c in range(8):
            a = fm_chunk(hTt, c * 128)
            k.op("act", lambda e, a=a, c=c: e.activation(out=xm[:, c, 3:3 + TT], in_=a[:, 0:TT], func=AF.Copy), reads=[a, xm], writes=[xm])
        fm_group(hTt, 2048, 4, AF.Sigmoid, tok0, dst_fm0=24)
        fm_group(hTt, 2048 + 512, 4, AF.Sigmoid, tok0, dst_fm0=28)
        fm_group(hTt, zc, 4, AF.Silu, tok0, dst_fm0=32)
        fm_group(hTt, zc + 512, 4, AF.Silu, tok0, dst_fm0=36)
        a = k.ring("acc", acc)
        k.mm(a, a[0:8, 0:TT], [(Wb[:, kc, 3072:3080], hTt[:, kc, :]) for kc in range(8)], reads=[hTt, Wb])
        g = k.ring("gtl", gtl)
        k.op("dve", lambda e, a=a, g=g: e.tensor_copy(out=g[:, :], in_=a[0:8, 0:TT]), reads=[a], writes=[g])
        k.dma("pool", GT[:, tok0:tok0 + TT], g[:, :], reads=[g])

        for s in range(nsub):
            for nb in range(2):
                a = tm_block(hTt, s, 1024 + nb * 512, 512)
                o = k.ring("tmob", self.tmob)
                k.op("dve", lambda e, a=a, o=o: e.tensor_copy(out=o[0:np_, :], in_=a[0:np_, :]), reads=[a], writes=[o])
                k.dma("pool", TMs[1, tok0 + s * np_: tok0 + (s + 1) * np_, nb * 512:(nb + 1) * 512], o[0:np_, :], reads=[o])
        for c in range(8):
            ca = k.ring("cacc", cacc)
            k.op("dve", lambda e, ca=ca, c=c: e.tensor_scalar(out=ca[:], in0=xm[:, c, 0:TT], scalar1=cw[:, c, 0:1], scalar2=None, op0=ALU.mult), reads=[xm, cw], writes=[ca])
            for jj in range(1, 4):
                k.op("dve", lambda e, ca=ca, c=c, jj=jj: e.scalar_tensor_tensor(out=ca[:], in0=xm[:, c, jj:jj + TT], scalar=cw[:, c, jj:jj + 1], in1=ca[:], op0=ALU.mult, op1=ALU.add), reads=[xm, cw, ca], writes=[ca])
            k.op("act", lambda e, ca=ca, c=c: e.activation(out=xc[:, c, :], in_=ca[:], func=AF.Silu, bias=cw[:, c, 4:5]), reads=[ca, cw], writes=[xc])
        if tt == ntt - 1:
            s = nsub - 1
            mco = (O["mlconv_p"] if isp else O["mlconv_s"])[j]
            for nb in range(2):
                a1 = tm_block(hTt, s, nb * 512, 512)
                o1 = k.ring("tmo", tmo)
                k.op("act", lambda e, a1=a1, o1=o1: e.activation(out=o1[0:np_, :], in_=a1[0:np_, :], func=AF.Copy), reads=[a1], writes=[o1])
                k.dma("pool", mco[:, nb * 512:(nb + 1) * 512], o1[np_ - 3:np_, :], reads=[o1])
        if tt < ntt - 1:
            k.op("dve", lambda e: e.tensor_copy(out=xm[:, :, 0:3], in_=xm[:, :, TT:TT + 3]), reads=[xm], writes=[xm])
        xs_ = k.ring("stgx", self.xcs)
        for c in range(8):
            k.op("act", lambda e, c=c: e.activation(out=xs_[:, c, :], in_=xc[:, c, :], func=AF.Copy, scale=self.skp1[:, c:c + 1]), reads=[xc, self.skp1], writes=[xs_])
        k.dma("pool", FM[16:24, :, tok0:tok0 + TT].rearrange("c p t -> p c t"), xs_[:], reads=[xs_])
        for (wb, f0, scl) in ((wqb, 0, 1.0), (wkb, 8, 1.0 / 16.0)):
            for g in range(2):
                t = k.ring("stg", stg)
                for c in range(4):
                    hc = g * 4 + c
                    h, ec = hc // 2, hc % 2
                    a = k.ring("acc", acc)
                    k.mm(a, a[:, 0:TT], [(wb[:, 2 * h + dc, ec * 128:(ec + 1) * 128], xc[:, 2 * h + dc, :]) for dc in range(2)], reads=[xc, wb])
                    k.op("act", lambda e, a=a, c=c, t=t, scl=scl: e.activation(out=t[:, c, :], in_=a[:, 0:TT], func=AF.Copy, scale=scl), reads=[a], writes=[t])
                k.dma("pool", FM[f0 + g * 4:f0 + g * 4 + 4, :, tok0:tok0 + TT].rearrange("c p t -> p c t"), t[:], reads=[t])
        for s in range(nsub):
            for nb in range(2):
                a = k.ring("acc", acc)
                for hh in range(2):
                    h = nb * 2 + hh
                    k.mm(a, a[0:np_, hh * 256:(hh + 1) * 256], [(xc[:, 2 * h + dc, s * np_:(s + 1) * np_], wkb[:, 2 * h + dc, :]) for dc in range(2)], reads=[xc, wkb])
                o = k.ring("tmob", self.tmob)
                k.op("act", lambda e, a=a, o=o: e.activation(out=o[0:np_, :], in_=a[0:np_, :], func=AF.Copy, scale=1.0 / 16.0), reads=[a], writes=[o])
                k.dma("pool", TMs[0, tok0 + s * np_: tok0 + (s + 1) * np_, nb * 512:(nb + 1) * 512], o[0:np_, :], reads=[o])
    def p3(self, stm, li, Wo, gpo, last, st, bg=None):
        k = self.k
        T = stm["T"]
        TT = min(512, T)
        np_ = min(128, T)
        nsub = TT // np_
        ntt = T // TT
        xsrc = stm["xin"] if li == 0 else stm["xres"]
        xdst = stm["yout"] if last else stm["xres"]
        YT = stm["YT"]
        yt = [k.sb("yt%d" % i, [128, 12, TT], BF16, st) for i in range(2)]
        xt = [k.sb("p3x%d" % i, [128, D], F32, st) for i in range(3)]
        ot = [k.sb("p3o%d" % i, [128, D], F32, st) for i in range(2)]
        junk = k.sb("p3junk", [128, D], BF16, st)
        ss = [k.sb("p3ss%d" % i, [128, 2], F32, st) for i in range(3)]
        acc = [k.ps("p3acc%d" % i, [128, D], F32, st) for i in range(3)]
        for tt in range(ntt):
            tok0 = tt * TT
            y_ = yt[tt % 2]
            k.dma("sp", y_[:], YT[:, :, tok0:tok0 + TT].rearrange("c p t -> p c t"), writes=[y_])
            for s in range(nsub):
                if bg is not None and s % 3 != 2:
                    next(bg, None)
                x_ = k.ring("p3x", xt)
                k.dma("sp", x_[0:np_, :], xsrc[tok0 + s * np_: tok0 + (s + 1) * np_, :], writes=[x_])
                a = k.ring("p3acc", acc)
                for nb in range(2):
                    k.mm(a, a[0:np_, nb * 512:(nb + 1) * 512], [(y_[:, fc, s * np_:(s + 1) * np_], Wo[:, fc, nb * 512:(nb + 1) * 512]) for fc in range(12)], reads=[y_, Wo])
                s_ = k.ring("p3ss", ss)
                k.op("pool", lambda e, s_=s_: e.memset(s_[:], 0.0), writes=[s_])
                k.op("act", lambda e, a=a, s_=s_: e.activation(out=junk[0:np_, :], in_=a[0:np_, :], func=AF.Square, accum_out=s_[0:np_, 0:1]), reads=[a, s_], writes=[junk, s_])
                k.op("act", lambda e, s_=s_: e.activation(out=s_[0:np_, 1:2], in_=s_[0:np_, 0:1], func=AF.Ln, scale=1.0 / D, bias=EPS), reads=[s_], writes=[s_])
                k.op("act", lambda e, s_=s_: e.activation(out=s_[0:np_, 1:2], in_=s_[0:np_, 1:2], func=AF.Exp, scale=-0.5), reads=[s_], writes=[s_])
                o_ = k.ring("p3o", ot)
                k.op("dve", lambda e, a=a, s_=s_, o_=o_: e.scalar_tensor_tensor(out=o_[0:np_, :], in0=a[0:np_, :], scalar=s_[0:np_, 1:2], in1=gpo[0:np_, :], op0=ALU.mult, op1=ALU.mult), reads=[a, s_, gpo], writes=[o_])
                k.op("pool", lambda e, o_=o_, x_=x_: e.tensor_tensor(out=o_[0:np_, :], in0=o_[0:np_, :], in1=x_[0:np_, :], op=ALU.add), reads=[o_, x_], writes=[o_])
                k.dma("pool", xdst[tok0 + s * np_: tok0 + (s + 1) * np_, :], o_[0:np_, :], reads=[o_])

    def p2_sb(self, stm, li, j, st):
        k = self.k
        I, O = self.I, self.O
        T = stm["T"]
        isp = stm["name"] == "p"
        FM, YT = stm["FM"], stm["YT"]
        N = min(512, T)
        sc = 1.0 / math.sqrt(128.0)
        PAST = self.PAST
        if isp:
            NB = T // 128
            koff = 0
        else:
            NB = (PAST + T + 127) // 128
            if NB % 2:
                NB += 1
            koff = NB * 128 - (PAST + T)
        nqs = T // N
        NSET = 2 if isp else 4
        early = not isp
        qT = [k.sb("qT%d" % i, [128, T], BF16, st) for i in range(NSET)]
        kT = [k.sb("kT%d" % i, [128, NB * 128], BF16, st) for i in range(NSET)]
        szT = [k.sb("szT%d" % i, [128, T], BF16, st) for i in range(NSET)]
        Vb = [k.sb("Vb%d" % i, [128, NB, 128], BF16, st) for i in range(NSET)]
        VST = 16
        vstg = [k.sb("vstg%d" % i, [128, VST, 128], F32, st) for i in range(2)] if isp else None
        vfull = [k.sb("vfull%d" % i, [128, NB, 128], F32, st) for i in range(2)] if not isp else None
        S = [k.ps("S%d" % i, [128, 2, N], F32, st) for i in range(2)]
        L = k.ps("L", [128, 2, N], F32, st)
        Racc = k.ps("Racc", [128, N], F32, st)
        oacc = k.ps("oacc", [128, N], F32, st)
        trs = k.ps("trs", [128, 512], F32, st) if not isp else None
        e_ = [k.sb("e%d" % i, [128, 2, N], BF16, st) for i in range(5)]
        c_ = [k.sb("c%d" % i, [128, 2, N], BF16, st) for i in range(3)]
        g_ = [k.sb("g%d" % i, [128, 2, N], BF16, st) for i in range(2)]
        a_ = [k.sb("a%d" % i, [128, 2, N], BF16, st) for i in range(3)]
        Lr = [k.sb("Lr%d" % i, [128, 2, N], F32, st) for i in range(2)]
        R = [k.sb("R%d" % i, [128, N], F32, st) for i in range(3)]
        yst = [k.sb("ysb%d" % i, [128, N], BF16, st) for i in range(2)]
        tri, ones = self.cst_b[:, 1, :], self.cst_b[:, 2, :]
        vout = (O["sbv_p"] if isp else O["sbv_s"])[j]
        kout = (O["sbk_p"] if isp else O["sbk_s"])[j]

        def load_head(h):
            si = h % NSET
            k.dma("sp", qT[si][:], FM[h], writes=[qT[si]])
            k.dma("sp", szT[si][:], FM[16 + h], writes=[szT[si]])
            if isp:
                k.dma("sp", kT[si][:], FM[8 + h], writes=[kT[si]])
                for b0 in range(0, NB, VST):
                    nb_ = min(VST, NB - b0)
                    vs = k.ring("vstg", vstg)
                    k.dma("sp", vs[:, 0:nb_, :], vout[b0 * 128:(b0 + nb_) * 128, h, :].rearrange("(b p) d -> p b d", p=128), writes=[vs])
                    k.op("pool", lambda e, vs=vs, b0=b0, nb_=nb_: e.tensor_copy(out=Vb[si][:, b0:b0 + nb_, :], in_=vs[:, 0:nb_, :]), reads=[vs], writes=[Vb[si]])
            else:
                for (src_c, src_n, isk) in ((I["csk"][j], kout, True), (I["csv"][j], vout, False)):
                    vs = k.ring("vfull", vfull)
                    k.op("pool", lambda e, vs=vs: e.memset(vs[:], 0.0), writes=[vs])
                    p0 = koff % 128
                    bq = koff // 128
                    t0 = (128 - p0) % 128
                    if t0:
                        k.dma("sp", vs[p0:128, bq, :], src_c[0:t0, h, :], writes=[vs])
                        bq += 1
                    nfull = (PAST - t0) // 128
                    if nfull:
                        k.dma("sp", vs[:, bq:bq + nfull, :], src_c[t0:t0 + nfull * 128, h, :].rearrange("(b p) d -> p b d", p=128), writes=[vs])
                    rem = PAST - t0 - nfull * 128
                    if rem:
                        k.dma("sp", vs[0:rem, bq + nfull, :], src_c[t0 + nfull * 128:PAST, h, :], writes=[vs])
                    k.dma("sp", vs[128 - T:128, NB - 1, :], src_n[0:T, h, :], writes=[vs])
                    if not isk:
                        k.op("pool", lambda e, vs=vs: e.tensor_copy(out=Vb[si][:], in_=vs[:]), reads=[vs], writes=[Vb[si]])
                    else:
                        for b0 in range(0, NB, 4):
                            nb_ = min(4, NB - b0)
                            for b in range(nb_):
                                k.tr(trs, trs[:, b * 128:(b + 1) * 128], vs[:, b0 + b, :], self.cst_f[:], reads=[vs, self.cst_f])
                            k.op("dve", lambda e, b0=b0, nb_=nb_: e.tensor_copy(out=kT[si][:, b0 * 128:(b0 + nb_) * 128], in_=trs[:, 0:nb_ * 128]), reads=[trs], writes=[kT[si]])

        G = []
        for h in range(8):
            for qs in range(nqs):
                q0 = qs * N
                grp = []
                if isp:
                    b = (q0 + N) // 128 - 1
                    while b >= 0:
                        m0 = (b - q0 // 128) if b >= q0 // 128 else None
                        m1 = ((b - 1) - q0 // 128) if (b - 1) >= q0 // 128 else None
                        grp.append((b, b - 1, m0, m1))
                        b -= 2
                else:
                    b = NB - 1
                    first_real = koff // 128
                    while b >= 0:
                        ms = []
                        for bb in (b, b - 1):
                            if bb == NB - 1:
                                ms.append(0)
                            elif bb == first_real and koff % 128:
                                ms.append(1)
                            elif bb < first_real:
                                ms.append(2)
                            else:
                                ms.append(None)
                        grp.append((b, b - 1, ms[0], ms[1]))
                        b -= 2
                ng = len(grp)
                for gi, (b0, b1, m0, m1) in enumerate(grp):
                    qlo = 0
                    if isp and gi == 0 and N == 512:
                        qlo = (b1 - q0 // 128) * 128
                    G.append(dict(h=h, si=h % NSET, q0=q0, b0=b0, b1=b1, m0=m0, m1=m1, first=(gi == 0), last=(gi == ng - 1),
                                  newhead=(qs == 0 and gi == 0), qlo=qlo))
        NG = len(G)
        mtile = self.msk if isp else self.smsk

        def st_S(n):
            d = G[n]
            if n == 0:
                for hh in range(min(NSET, 8)):
                    load_head(hh)
            si, q0 = d["si"], d["q0"]
            S_ = k.ring("S", S)
            for i, (b, m) in enumerate(((d["b0"], d["m0"]), (d["b1"], d["m1"]))):
                ql = d["qlo"]
                pairs = [(kT[si][:, b * 128:(b + 1) * 128], qT[si][:, q0 + ql:q0 + N])]
                rd = [kT[si], qT[si]]
                if m is not None:
                    pairs.append((self.cst_b[:, 0, :], mtile[:, m, ql:N]))
                    rd += [self.cst_b, mtile]
                k.mm(S_, S_[:, i, ql:N], pairs, reads=rd)
            d["S"] = S_

        def st_exp1(n):
            d = G[n]
            e = k.ring("e", e_)
            S_ = d["S"]
            ql = d["qlo"]
            k.op("act", lambda en: en.activation(out=e[:, :, ql:N], in_=S_[:, :, ql:N], func=AF.Exp, scale=sc), reads=[S_], writes=[e])
            d["e"] = e

        def st_ln(n):
            d = G[n]
            c = k.ring("c", c_)
            e = d["e"]
            ql = d["qlo"]
            k.op("act", lambda en: en.activation(out=c[:, :, ql:N], in_=e[:, :, ql:N], func=AF.Ln, bias=1.0), reads=[e], writes=[c])
            d["c"] = c

        def st_L(n):
            d = G[n]
            c = d["c"]
            ql = d["qlo"]
            k.mm(L, L[:, 0, ql:N], [(tri, c[:, 0, ql:N])], reads=[c, self.cst_b])
            k.mm(L, L[:, 1, ql:N], [(tri, c[:, 1, ql:N]), (ones, c[:, 0, ql:N])], reads=[c, self.cst_b])
            if not d["first"]:
                Rt = d["R"]
                lr = k.ring("Lr", Lr)
                for i in range(2):
                    k.op("dve", lambda en, i=i: en.tensor_tensor(out=lr[:, i, :], in0=L[:, i, 0:N], in1=Rt[:], op=ALU.add), reads=[L, Rt], writes=[lr])
                d["src"], d["srct"] = lr[:], lr
            else:
                lr = k.ring("Lr", Lr)
                k.op("dve", lambda en: en.tensor_copy(out=lr[:, :, ql:N], in_=L[:, :, ql:N]), reads=[L], writes=[lr])
                d["src"], d["srct"] = lr[:, :, ql:N], lr
            if not d["last"]:
                k.mm(Racc, Racc[:, ql:N], [(ones, c[:, 0, ql:N]), (ones, c[:, 1, ql:N])], reads=[c, self.cst_b])
                Rn = k.ring("R", R)
                if d["first"]:
                    if ql:
                        k.op("pool", lambda en: en.memset(Rn[:, 0:ql], 0.0), writes=[Rn])
                    k.op("dve", lambda en: en.tensor_copy(out=Rn[:, ql:N], in_=Racc[:, ql:N]), reads=[Racc], writes=[Rn])
                else:
                    Rt = d["R"]
                    k.op("dve", lambda en: en.tensor_tensor(out=Rn[:], in0=Racc[:, 0:N], in1=Rt[:], op=ALU.add), reads=[Racc, Rt], writes=[Rn])
                G[n + 1]["R"] = Rn

        def st_exp2(n):
            d = G[n]
            g = k.ring("g", g_)
            src, srct = d["src"], d["srct"]
            ql = d["qlo"]
            k.op("act", lambda en: en.activation(out=g[:, :, ql:N], in_=src, func=AF.Exp, scale=-1.0), reads=[srct], writes=[g])
            d["g"] = g

        def st_a(n):
            d = G[n]
            a = k.ring("a", a_)
            e, g = d["e"], d["g"]
            ql = d["qlo"]
            if ql:
                k.op("pool", lambda en: en.memset(a[:, :, 0:ql], 0.0), writes=[a])
            k.op("pool", lambda en: en.tensor_tensor(out=a[:, :, ql:N], in0=e[:, :, ql:N], in1=g[:, :, ql:N], op=ALU.mult), reads=[e, g], writes=[a])
            si, q0, h = d["si"], d["q0"], d["h"]
            k.mm(oacc, oacc[:, 0:N], [(Vb[si][:, d["b0"], :], a[:, 0, :]), (Vb[si][:, d["b1"], :], a[:, 1, :])], reads=[a, Vb[si]], start=d["first"], stop=d["last"])
            if d["last"]:
                y = k.ring("ysb", yst)
                k.op("dve", lambda en: en.tensor_tensor(out=y[:], in0=oacc[:, 0:N], in1=szT[si][:, q0:q0 + N], op=ALU.mult), reads=[oacc, szT[si]], writes=[y])
                k.dma("sp", YT[h, :, q0:q0 + N], y[:], reads=[y])
            for kk in ("S", "e", "c", "g", "src", "srct", "R"):
                d.pop(kk, None)
            if (n == NG - 1 or G[n + 1]["newhead"]) and d["h"] + NSET < 8:
                load_head(d["h"] + NSET)

        for t in range(-2, NG + 3):
            if 0 <= t + 2 < NG:
                st_S(t + 2)
            if 0 <= t + 1 < NG:
                st_exp1(t + 1)
            if 0 <= t < NG:
                st_ln(t)
            if 0 <= t - 1 < NG:
                st_L(t - 1)
            if 0 <= t - 2 < NG:
                st_exp2(t - 2)
            if 0 <= t - 3 < NG:
                st_a(t - 3)

    def p2_sb_sample(self, stm, li, j, st):
        k = self.k
        I, O = self.I, self.O
        T = stm["T"]
        FM, YT = stm["FM"], stm["YT"]
        N = T
        H = 8
        sc = 1.0 / math.sqrt(128.0)
        PAST = self.PAST
        NB = (PAST + T + 127) // 128
        if NB % 2:
            NB += 1
        koff = NB * 128 - (PAST + T)
        qT = k.sb("sqT", [128, H, N], BF16, st)
        szT = k.sb("sszT", [128, H, N], BF16, st)
        kT = [k.sb("skT%d" % h, [128, NB * 128], BF16, st) for h in range(H)]
        Vb = [k.sb("sVb%d" % h, [128, NB, 128], BF16, st) for h in range(H)]
        vfull = [k.sb("svfull%d" % i, [128, NB, 128], F32, st) for i in range(2)]
        S = [k.ps("sS%d" % i, [128, H, 2, N], F32, st) for i in range(2)]
        L = k.ps("sL", [128, H, 2, N], F32, st)
        Racc = k.ps("sRacc", [128, H, N], F32, st)
        oacc = k.ps("soacc", [128, H, N], F32, st)
        trs = k.ps("strs", [128, 512], F32, st)
        e_ = [k.sb("se%d" % i, [128, H, 2, N], BF16, st) for i in range(5)]
        c_ = [k.sb("sc%d" % i, [128, H, 2, N], BF16, st) for i in range(3)]
        g_ = [k.sb("sg%d" % i, [128, H, 2, N], BF16, st) for i in range(2)]
        a_ = [k.sb("sa%d" % i, [128, H, 2, N], BF16, st) for i in range(3)]
        Lr = [k.sb("sLr%d" % i, [128, H, 2, N], F32, st) for i in range(2)]
        R = [k.sb("sR%d" % i, [128, H, N], F32, st) for i in range(3)]
        yst = k.sb("sysb", [128, H, N], BF16, st)
        osb = k.sb("sosb", [128, H, N], F32, st)
        ident, tri, ones = self.cst_b[:, 0, :], self.cst_b[:, 1, :], self.cst_b[:, 2, :]
        vout = O["sbv_s"][j]
        kout = O["sbk_s"][j]
        k.dma("sp", qT[:], FM[0:8, :, :].rearrange("c p t -> p c t"), writes=[qT])
        k.dma("sp", szT[:], FM[16:24, :, :].rearrange("c p t -> p c t"), writes=[szT])
        for h in range(H):
            for (src_c, src_n, isk) in ((I["csk"][j], kout, True), (I["csv"][j], vout, False)):
                vs = k.ring("svfull", vfull)
                nz = koff // 128 + (1 if koff % 128 else 0)
                if nz:
                    k.op("pool", lambda e, vs=vs, nz=nz: e.memset(vs[:, 0:nz, :], 0.0), writes=[vs])
                p0 = koff % 128
                bq = koff // 128
                t0 = (128 - p0) % 128
                if t0:
                    k.dma("sp", vs[p0:128, bq, :], src_c[0:t0, h, :], writes=[vs])
                    bq += 1
                nfull = (PAST - t0) // 128
                if nfull:
                    k.dma("sp", vs[:, bq:bq + nfull, :], src_c[t0:t0 + nfull * 128, h, :].rearrange("(b p) d -> p b d", p=128), writes=[vs])
                rem = PAST - t0 - nfull * 128
                if rem:
                    k.dma("sp", vs[0:rem, bq + nfull, :], src_c[t0 + nfull * 128:PAST, h, :], writes=[vs])
                k.dma("sp", vs[128 - T:128, NB - 1, :], src_n[0:T, h, :], writes=[vs])
                if not isk:
                    if h % 2 == 0:
                        k.op("pool", lambda e, vs=vs, h=h: e.tensor_copy(out=Vb[h][:], in_=vs[:]), reads=[vs], writes=[Vb[h]])
                    else:
                        k.op("act", lambda e, vs=vs, h=h: e.activation(out=Vb[h][:], in_=vs[:], func=AF.Copy), reads=[vs], writes=[Vb[h]])
                else:
                    for b0 in range(0, NB, 4):
                        nb_ = min(4, NB - b0)
                        for b in range(nb_):
                            k.tr(trs, trs[:, b * 128:(b + 1) * 128], vs[:, b0 + b, :], self.cst_f[:], reads=[vs, self.cst_f])
                        eng = k.ring("sevac", ["dve", "act"])
                        if eng == "dve":
                            k.op("dve", lambda e, b0=b0, nb_=nb_, h=h: e.tensor_copy(out=kT[h][:, b0 * 128:(b0 + nb_) * 128], in_=trs[:, 0:nb_ * 128]), reads=[trs], writes=[kT[h]])
                        else:
                            k.op("act", lambda e, b0=b0, nb_=nb_, h=h: e.activation(out=kT[h][:, b0 * 128:(b0 + nb_) * 128], in_=trs[:, 0:nb_ * 128], func=AF.Copy), reads=[trs], writes=[kT[h]])
        G = []
        b = NB - 1
        first_real = koff // 128
        while b >= 0:
            ms = []
            for bb in (b, b - 1):
                if bb == NB - 1:
                    ms.append(0)
                elif bb == first_real and koff % 128:
                    ms.append(1)
                elif bb < first_real:
                    ms.append(2)
                else:
                    ms.append(None)
            G.append(dict(b0=b, b1=b - 1, m0=ms[0], m1=ms[1]))
            b -= 2
        NG = len(G)
        for n, d in enumerate(G):
            d["first"], d["last"] = (n == 0), (n == NG - 1)
        mtile = self.smsk

        def st_S(n):
            d = G[n]
            S_ = k.ring("sS", S)
            for h in range(H):
                for i, (b, m) in enumerate(((d["b0"], d["m0"]), (d["b1"], d["m1"]))):
                    pairs = [(kT[h][:, b * 128:(b + 1) * 128], qT[:, h, :])]
                    rd = [kT[h], qT]
                    if m is not None:
                        pairs.append((ident, mtile[:, m, 0:N]))
                        rd += [self.cst_b, mtile]
                    k.mm(S_, S_[:, h, i, :], pairs, reads=rd)
            d["S"] = S_

        def st_exp1(n):
            d = G[n]
            e = k.ring("se", e_)
            S_ = d["S"]
            k.op("act", lambda en: en.activation(out=e[:], in_=S_[:], func=AF.Exp, scale=sc), reads=[S_], writes=[e])
            d["e"] = e

        def st_ln(n):
            d = G[n]
            c = k.ring("sc", c_)
            e = d["e"]
            k.op("act", lambda en: en.activation(out=c[:], in_=e[:], func=AF.Ln, bias=1.0), reads=[e], writes=[c])
            d["c"] = c

        def st_L(n):
            d = G[n]
            c = d["c"]
            for h in range(H):
                k.mm(L, L[:, h, 0, :], [(tri, c[:, h, 0, :])], reads=[c, self.cst_b])
                k.mm(L, L[:, h, 1, :], [(tri, c[:, h, 1, :]), (ones, c[:, h, 0, :])], reads=[c, self.cst_b])
            lr = k.ring("sLr", Lr)
            if not d["first"]:
                Rt = d["R"]
                for i in range(2):
                    k.op("dve", lambda en, i=i: en.tensor_tensor(out=lr[:, :, i, :], in0=L[:, :, i, :], in1=Rt[:], op=ALU.add), reads=[L, Rt], writes=[lr])
            else:
                k.op("dve", lambda en: en.tensor_copy(out=lr[:], in_=L[:]), reads=[L], writes=[lr])
            d["lr"] = lr
            if not d["last"]:
                for h in range(H):
                    k.mm(Racc, Racc[:, h, :], [(ones, c[:, h, 0, :]), (ones, c[:, h, 1, :])], reads=[c, self.cst_b])
                Rn = k.ring("sR", R)
                if d["first"]:
                    k.op("dve", lambda en: en.tensor_copy(out=Rn[:], in_=Racc[:]), reads=[Racc], writes=[Rn])
                else:
                    Rt = d["R"]
                    k.op("dve", lambda en: en.tensor_tensor(out=Rn[:], in0=Racc[:], in1=Rt[:], op=ALU.add), reads=[Racc, Rt], writes=[Rn])
                G[n + 1]["R"] = Rn

        def st_exp2(n):
            d = G[n]
            g = k.ring("sg", g_)
            lr = d["lr"]
            k.op("act", lambda en: en.activation(out=g[:], in_=lr[:], func=AF.Exp, scale=-1.0), reads=[lr], writes=[g])
            d["g"] = g

        def st_a(n):
            d = G[n]
            a = k.ring("sa", a_)
            e, g = d["e"], d["g"]
            k.op("pool", lambda en: en.tensor_tensor(out=a[:], in0=e[:], in1=g[:], op=ALU.mult), reads=[e, g], writes=[a])
            for h in range(H):
                k.mm(oacc, oacc[:, h, :], [(Vb[h][:, d["b0"], :], a[:, h, 0, :]), (Vb[h][:, d["b1"], :], a[:, h, 1, :])], reads=[a, Vb[h]])
            if d["first"]:
                k.op("dve", lambda en: en.tensor_copy(out=osb[:], in_=oacc[:]), reads=[oacc], writes=[osb])
            else:
                k.op("dve", lambda en: en.tensor_tensor(out=osb[:], in0=oacc[:], in1=osb[:], op=ALU.add), reads=[oacc, osb], writes=[osb])
            if d["last"]:
                k.op("dve", lambda en: en.tensor_tensor(out=yst[:], in0=osb[:], in1=szT[:], op=ALU.mult), reads=[osb, szT], writes=[yst])
                k.dma("pool", YT[0:8, :, :].rearrange("c p t -> p c t"), yst[:], reads=[yst])

        for t in range(-2, NG + 3):
            if 0 <= t + 2 < NG:
                st_S(t + 2)
            if 0 <= t + 1 < NG:
                st_exp1(t + 1)
            if 0 <= t < NG:
                st_ln(t)
            if 0 <= t - 1 < NG:
                st_L(t - 1)
            if 0 <= t - 2 < NG:
                st_exp2(t - 2)
            if 0 <= t - 3 < NG:
                st_a(t - 3)

    def p2_ml(self, stm, li, j, st):
        k = self.k
        I, O = self.I, self.O
        T = stm["T"]
        isp = stm["name"] == "p"
        FM, YT, TMs, GT = stm["FM"], stm["YT"], stm["TMs"], stm["GT"]
        LC = min(128, T)
        nch = T // LC
        SEG = min(2048, T)
        ig = k.sb("ml_ig", [4, T], F32, st)
        fg = k.sb("ml_fg", [4, T], F32, st)
        Bt = k.sb("ml_B", [4, T], F32, st)
        ones4 = k.sb("ml_ones", [4, SEG], F32, st)
        gcol = k.sb("ml_gcol", [4, 2], F32, st)
        minit = k.sb("ml_minit", [4, 1], F32, st)
        sel = k.sb("ml_sel", [4, 4, 128], F32, st)
        negm = k.sb("ml_negm", [128, 128], F32, st)
        ghr = k.sb("ml_ghr", [128, D], F32, st)
        skp = k.sb("ml_skip", [128, 8], F32, st)
        C = k.sb("ml_C", [128, 4, 2, 257], F32, st)
        Cb = k.sb("ml_Cb", [128, 4, 2, 257], BF16, st)
        MendB = k.sb("ml_MendB", [128, nch + 1, 4], F32, st)
        nMendB = k.sb("ml_nMendB", [128, nch + 1, 4], F32, st)
        decB = k.sb("ml_decB", [128, nch, 4], F32, st)
        k.op("pool", lambda e: e.memset(ones4[:], 1.0), writes=[ones4])
        k.dma("sp", gcol[:], I["gcol"][j], writes=[gcol])
        k.dma("sp", sel[:].rearrange("r h m -> r (h m)"), I["sel4"][:, :], writes=[sel])
        k.dma("sp", negm[:], I["negmask"][:, :], writes=[negm])
        k.dma("sp", ghr[:], I["ghead_r"][j], writes=[ghr])
        k.dma("sp", skp[:], I["skipT"][j], writes=[skp])
        if isp:
            k.op("pool", lambda e: e.memset(minit[:], 0.0), writes=[minit])
            k.op("pool", lambda e: e.memset(C[:], 0.0), writes=[C])
        else:
            k.dma("sp", minit[:], I["smm"][j].rearrange("(h o) -> h o", o=1), writes=[minit])
            for h in range(4):
                k.dma("sp", C[:, h, :, 0:256], I["smc"][j, h].rearrange("(c p) e -> p c e", p=128), writes=[C])
                k.dma("sp", C[:, h, :, 256:257], I["smn"][j, h].rearrange("(c p o) -> p c o", p=128, o=1), writes=[C], allow_slow_non_contiguous=True)
        k.op("act", lambda e: e.activation(out=Cb[:], in_=C[:], func=AF.Copy), reads=[C], writes=[Cb])
        k.dma("sp", ig[:], GT[0:4, :], writes=[ig])
        k.dma("sp", fg[:], GT[4:8, :], writes=[fg])
        k.op("dve", lambda e: e.tensor_scalar(out=ig[:], in0=ig[:], scalar1=gcol[:, 0:1], scalar2=None, op0=ALU.add), reads=[ig, gcol], writes=[ig])
        k.op("dve", lambda e: e.tensor_scalar(out=fg[:], in0=fg[:], scalar1=gcol[:, 1:2], scalar2=None, op0=ALU.add), reads=[fg, gcol], writes=[fg])
        k.op("act", lambda e: e.activation(out=fg[:], in_=fg[:], func=AF.Exp, scale=-1.0), reads=[fg], writes=[fg])
        k.op("act", lambda e: e.activation(out=fg[:], in_=fg[:], func=AF.Ln, bias=1.0), reads=[fg], writes=[fg])
        k.op("dve", lambda e: e.tensor_scalar(out=fg[:], in0=fg[:], scalar1=-1.0, scalar2=None, op0=ALU.mult), reads=[fg], writes=[fg])
        for s0 in range(0, T, SEG):
            n = min(SEG, T - s0)
            init = 0.0 if s0 == 0 else Bt[:, s0 - 1:s0]
            k.op("dve", lambda e, s0=s0, n=n, init=init: e.tensor_tensor_scan(out=Bt[:, s0:s0 + n], data0=ones4[:, 0:n], data1=fg[:, s0:s0 + n], initial=init, op0=ALU.mult, op1=ALU.add), reads=[ones4, fg, Bt], writes=[Bt])
        k.op("dve", lambda e: e.tensor_tensor(out=ig[:], in0=ig[:], in1=Bt[:], op=ALU.subtract), reads=[ig, Bt], writes=[ig])
        for s0 in range(0, T, SEG):
            n = min(SEG, T - s0)
            init = minit[:, 0:1] if s0 == 0 else fg[:, s0 - 1:s0]
            k.op("dve", lambda e, s0=s0, n=n, init=init: e.tensor_tensor_scan(out=fg[:, s0:s0 + n], data0=ones4[:, 0:n], data1=ig[:, s0:s0 + n], initial=init, op0=ALU.mult, op1=ALU.max), reads=[ones4, ig, fg, minit], writes=[fg])
        k.op("dve", lambda e: e.tensor_tensor(out=Bt[:], in0=Bt[:], in1=fg[:], op=ALU.add), reads=[Bt, fg], writes=[Bt])
        with contextlib.ExitStack() as s2:
            mps = k.ps("ml_mps", [128, 4, nch + 1], F32, s2)
            for h in range(4):
                k.mm(mps, mps[:, h, 0:1], [(sel[:, h, :], minit[:, 0:1])], reads=[sel, minit])
                k.mm(mps, mps[:, h, 1:nch + 1], [(sel[:, h, :], fg[:, LC - 1:T:LC])], reads=[sel, fg])
            k.op("dve", lambda e: e.tensor_copy(out=MendB[:], in_=mps[:].rearrange("p h c -> p c h")), reads=[mps], writes=[MendB])
            k.op("dve", lambda e: e.tensor_scalar(out=nMendB[:], in0=MendB[:], scalar1=-1.0, scalar2=None, op0=ALU.mult), reads=[MendB], writes=[nMendB])
            k.op("dve", lambda e: e.tensor_tensor(out=decB[:], in0=MendB[:, 0:nch, :], in1=MendB[:, 1:nch + 1, :], op=ALU.subtract), reads=[MendB], writes=[decB])
            k.op("act", lambda e: e.activation(out=decB[:], in_=decB[:], func=AF.Exp), reads=[decB], writes=[decB])
            k.barrier()
        qk = [k.sb("ml_qk%d" % i, [128, 16, LC], BF16, st) for i in range(2)]
        ex = [k.sb("ml_ex%d" % i, [128, 24, LC], BF16, st) for i in range(2)]
        ktm = [k.sb("ml_ktm%d" % i, [128, D], BF16, st) for i in range(2)]
        vaug = [k.sb("ml_vaug%d" % i, [128, 4, 257], BF16, st) for i in range(2)]
        for v in vaug:
            k.op("pool", lambda e, v=v: e.memset(v[:], 1.0), writes=[v])
        cols = [k.sb("ml_cols%d" % i, [128, 12], F32, st) for i in range(2)]
        sm4 = [k.sb("ml_sm4%d" % i, [128, 16], F32, st) for i in range(2)]
        negm4 = k.sb("ml_negm4", [128, 4, 128], F32, st)
        for h in range(4):
            k.op("dve", lambda e, h=h: e.tensor_copy(out=negm4[:, h, :], in_=negm[:]), reads=[negm], writes=[negm4])
        w4 = [k.sb("ml_w4%d" % i, [128, 4, 128], F32, st) for i in range(2)]
        smb4 = [k.sb("ml_sm4b%d" % i, [128, 4, 128], BF16, st) for i in range(2)]
        nbs = [k.sb("ml_nbs%d" % i, [128, 257], F32, st) for i in range(2)]
        nd4 = [k.sb("ml_nd4%d" % i, [128, 4, 257], F32, st) for i in range(2)]
        sc1 = [k.sb("ml_sc%d" % i, [128, 20], F32, st) for i in range(2)]
        junk = k.sb("ml_junk", [128, 256], BF16, st)
        hn4 = [k.sb("ml_hn4%d" % i, [128, 4, 256], BF16, st) for i in range(2)]
        gk4 = [k.sb("ml_gk4%d" % i, [128, 4, 256], BF16, st) for i in range(2)]
        y8 = [k.sb("ml_y8%d" % i, [128, 8, LC], F32, st) for i in range(2)]
        yst = [k.sb("ml_yst%d" % i, [128, 8, LC], BF16, st) for i in range(2)]
        cps = k.ps("ml_cps", [128, 12], F32, st)
        mb4 = k.ps("ml_mb", [128, 4, 128], F32, st)
        sps4 = k.ps("ml_sps", [128, 4, 128], F32, st)
        tps4 = k.ps("ml_tps", [128, 8, 128], BF16, st)
        numA = k.ps("ml_numA", [128, 257], F32, st)
        numB = k.ps("ml_numB", [128, 257], F32, st)
        cups = [k.ps("ml_cups%d" % i, [128, 257], F32, st) for i in range(2)]
        identb = self.cst_b
        def ml_loads(c):
            t0, t1 = c * LC, (c + 1) * LC
            qk_, ex_, kt_, va_ = qk[c % 2], ex[c % 2], ktm[c % 2], vaug[c % 2]
            k.dma("sp", qk_[:], FM[0:16, :, t0:t1].rearrange("c p t -> p c t"), writes=[qk_])
            k.dma("sp", ex_[:], FM[16:40, :, t0:t1].rearrange("c p t -> p c t"), writes=[ex_])
            k.dma("sp", kt_[0:LC, :], TMs[0, t0:t1, :], writes=[kt_])
            k.dma("sp", va_[0:LC, :, 0:256], TMs[1, t0:t1, :].rearrange("t (h e) -> t h e", h=4), writes=[va_])

        ml_loads(0)
        for c in range(nch):
            t0, t1 = c * LC, (c + 1) * LC
            qk_ = qk[c % 2]
            ex_ = ex[c % 2]
            kt_ = ktm[c % 2]
            va_ = vaug[c % 2]
            if c + 1 < nch:
                ml_loads(c + 1)
            co = cols[c % 2]
            for qi, src in enumerate((ig, fg, Bt)):
                k.tr(cps, cps[0:LC, qi * 4:(qi + 1) * 4], src[0:4, t0:t1], self.cst_f[0:4, 0:4], reads=[src, self.cst_f])
            k.op("dve", lambda e: e.tensor_copy(out=co[0:LC, :], in_=cps[0:LC, :]), reads=[cps], writes=[co])
            s4 = sm4[c % 2]
            k.op("dve", lambda e: e.tensor_tensor(out=s4[0:LC, 0:4], in0=MendB[0:LC, c, :], in1=co[0:LC, 4:8], op=ALU.subtract), reads=[MendB, co], writes=[s4])
            k.op("dve", lambda e: e.tensor_tensor(out=s4[0:LC, 4:8], in0=co[0:LC, 0:4], in1=nMendB[0:LC, c + 1, :], op=ALU.add), reads=[nMendB, co], writes=[s4])
            k.op("dve", lambda e: e.tensor_scalar(out=s4[0:LC, 8:12], in0=co[0:LC, 8:12], scalar1=-1.0, scalar2=None, op0=ALU.mult), reads=[co], writes=[s4])
            k.op("act", lambda e: e.activation(out=s4[0:LC, 0:12], in_=s4[0:LC, 0:12], func=AF.Exp), reads=[s4], writes=[s4])
            ys = yst[c % 2]
            w_, sm_, nd_, sc, hn_, gk_, y_ = w4[c % 2], smb4[c % 2], nd4[c % 2], sc1[c % 2], hn4[c % 2], gk4[c % 2], y8[c % 2]
            for h in range(4):
                k.mm(mb4, mb4[:, h, 0:LC], [(sel[:, h, :], fg[0:4, t0:t1])], reads=[sel, fg])
            k.op("dve", lambda e: e.tensor_tensor(out=w_[0:LC, :, 0:LC], in0=negm4[0:LC, :, 0:LC], in1=mb4[0:LC, :, 0:LC], op=ALU.subtract), reads=[negm4, mb4], writes=[w_])
            for h in range(4):
                k.op("act", lambda e, h=h: e.activation(out=w_[0:LC, h, 0:LC], in_=w_[0:LC, h, 0:LC], func=AF.Exp, bias=co[0:LC, h:h + 1]), reads=[w_, co], writes=[w_])
            for h in range(4):
                k.mm(sps4, sps4[0:LC, h, 0:LC], [(qk_[:, 8 + 2 * h + ec, :], qk_[:, 2 * h + ec, :]) for ec in range(2)], reads=[qk_])
            k.op("dve", lambda e: e.tensor_tensor(out=sm_[0:LC, :, 0:LC], in0=sps4[0:LC, :, 0:LC], in1=w_[0:LC, :, 0:LC], op=ALU.mult), reads=[sps4, w_], writes=[sm_])
            for h in range(4):
                k.op("act", lambda e, h=h: e.activation(out=gk_[0:LC, h, :], in_=kt_[0:LC, h * 256:(h + 1) * 256], func=AF.Copy, scale=s4[0:LC, 4 + h:5 + h]), reads=[kt_, s4], writes=[gk_])
            for h in range(4):
                k.mm(numA, numA[0:LC, :], [(sm_[0:LC, h, 0:LC], va_[0:LC, h, :])], reads=[sm_, va_])
                k.mm(numB, numB[0:LC, :], [(qk_[:, 2 * h + dc, :], Cb[:, h, dc, :]) for dc in range(2)], reads=[qk_, Cb])
                nb_ = k.ring("ml_nbs", nbs)
                k.op("act", lambda e, nb_=nb_, h=h: e.activation(out=nb_[0:LC, :], in_=numB[0:LC, :], func=AF.Copy, scale=s4[0:LC, h:h + 1]), reads=[numB, s4], writes=[nb_])
                k.op("dve", lambda e, nb_=nb_, h=h: e.tensor_tensor(out=nd_[0:LC, h, :], in0=numA[0:LC, :], in1=nb_[0:LC, :], op=ALU.add), reads=[numA, nb_], writes=[nd_])
            k.op("pool", lambda e: e.memset(sc[:], 0.0), writes=[sc])
            k.op("act", lambda e: e.activation(out=sc[0:LC, 0:4], in_=nd_[0:LC, :, 256], func=AF.Abs), reads=[nd_, sc], writes=[sc])
            k.op("dve", lambda e: e.tensor_tensor(out=sc[0:LC, 0:4], in0=sc[0:LC, 0:4], in1=s4[0:LC, 8:12], op=ALU.max), reads=[sc, s4], writes=[sc])
            k.op("dve", lambda e: e.reciprocal(out=sc[0:LC, 4:8], in_=sc[0:LC, 0:4]), reads=[sc], writes=[sc])
            for h in range(4):
                k.op("act", lambda e, h=h: e.activation(out=junk[0:LC, :], in_=nd_[0:LC, h, 0:256], func=AF.Square, scale=sc[0:LC, 4 + h:5 + h], accum_out=sc[0:LC, 8 + h:9 + h]), reads=[nd_, sc], writes=[junk, sc])
            k.op("act", lambda e: e.activation(out=sc[0:LC, 12:16], in_=sc[0:LC, 8:12], func=AF.Ln, scale=1.0 / 256.0, bias=EPS), reads=[sc], writes=[sc])
            k.op("act", lambda e: e.activation(out=sc[0:LC, 12:16], in_=sc[0:LC, 12:16], func=AF.Exp, scale=-0.5), reads=[sc], writes=[sc])
            k.op("dve", lambda e: e.tensor_tensor(out=sc[0:LC, 16:20], in0=sc[0:LC, 12:16], in1=sc[0:LC, 4:8], op=ALU.mult), reads=[sc], writes=[sc])
            for h in range(4):
                k.op("dve", lambda e, h=h: e.scalar_tensor_tensor(out=hn_[0:LC, h, :], in0=nd_[0:LC, h, 0:256], scalar=sc[0:LC, 16 + h:17 + h], in1=ghr[0:LC, h * 256:(h + 1) * 256], op0=ALU.mult, op1=ALU.mult), reads=[nd_, sc, ghr], writes=[hn_])
            for h in range(4):
                for dc in range(2):
                    cu = cups[dc]
                    k.mm(cu, cu[:, :], [(gk_[0:LC, h, dc * 128:(dc + 1) * 128], va_[0:LC, h, :])], reads=[gk_, va_])
                    k.op("dve", lambda e, dc=dc, cu=cu, h=h: e.scalar_tensor_tensor(out=C[:, h, dc, :], in0=C[:, h, dc, :], scalar=decB[:, c, h:h + 1], in1=cu[:, :], op0=ALU.mult, op1=ALU.add), reads=[C, decB, cu], writes=[C])
            k.op("act", lambda e: e.activation(out=Cb[:], in_=C[:], func=AF.Copy), reads=[C], writes=[Cb])
            for h in range(4):
                for ec in range(2):
                    k.tr(tps4, tps4[:, 2 * h + ec, 0:LC], hn_[0:LC, h, ec * 128:(ec + 1) * 128], identb[0:LC, 0, 0:LC], reads=[hn_, identb])
            k.op("dve", lambda e: e.tensor_tensor(out=y_[:], in0=tps4[:, :, 0:LC], in1=ex_[:, 8:16, :], op=ALU.mult), reads=[tps4, ex_], writes=[y_])
            k.op("pool", lambda e: e.tensor_tensor(out=y_[:], in0=y_[:], in1=ex_[:, 0:8, :], op=ALU.add), reads=[y_, ex_], writes=[y_])
            k.op("pool", lambda e: e.tensor_tensor(out=ys[:], in0=y_[:], in1=ex_[:, 16:24, :], op=ALU.mult), reads=[y_, ex_], writes=[ys])
            k.dma("pool", YT[0:8, :, t0:t1].rearrange("c p t -> p c t"), ys[:], reads=[ys])
        co_, no_, mo_ = (O["mlc_p"], O["mln_p"], O["mlm_p"]) if isp else (O["mlc_s"], O["mln_s"], O["mlm_s"])
        for h in range(4):
            k.dma("pool", co_[j, h].rearrange("(c p) e -> p c e", p=128), C[:, h, :, 0:256], reads=[C])
            k.dma("pool", no_[j, h].rearrange("(c p o) -> p c o", p=128, o=1), C[:, h, :, 256:257], reads=[C], allow_slow_non_contiguous=True)
        k.dma("pool", mo_[j].rearrange("(h o) -> h o", o=1), Bt[:, T - 1:T], reads=[Bt])


def make_consts():
    c = np.zeros((128, 128 * 5 + 2048), np.float32)
    idx = np.arange(128)
    c[:, 0:128] = np.eye(128)
    c[:, 128:256] = (idx[:, None] >= idx[None, :])
    c[:, 256:384] = 1.0
    c[:, 384:512] = (idx[:, None] <= idx[None, :])
    q = np.arange(512)
    for jj in range(4):
        c[:, 640 + jj * 512: 640 + (jj + 1) * 512] = np.where((128 * jj + idx[:, None]) < q[None, :], 0.0, -30000.0)
    return c


def make_smask(PAST, TS, koff):
    m = np.zeros((128, 3, 32), np.float32)
    idx = np.arange(128)
    q = np.arange(32)
    nb = (koff + PAST + TS) // 128
    key = (nb - 1) * 128 + idx - koff
    m[:, 0, :TS] = np.where((key[:, None] < PAST) | ((key[:, None] - PAST) < q[None, :TS]), 0.0, -30000.0)
    fr = koff // 128
    key = fr * 128 + idx - koff
    m[:, 1, :TS] = np.where(key[:, None] >= 0, 0.0, -30000.0)
    m[:, 2, :] = -30000.0
    return m.reshape(128, 96)


_CACHE = {}


def _prep(inp):
    x_prompt = np.asarray(inp["x_prompt"], np.float32)
    x_sample = np.asarray(inp["x_sample"], np.float32)
    B, T, _ = x_prompt.shape
    SBN, TS, _ = x_sample.shape
    DEPTH = inp["g_pre"].shape[0]
    PAST = inp["cache_sb_k"].shape[2]
    key = (T, TS, PAST, DEPTH)
    if key not in _CACHE:
        p = Prog(T, TS, PAST, DEPTH)
        p.build()
        _CACHE[key] = p
    p = _CACHE[key]
    NSB, NML, NCV = p.NSB, p.NML, p.NCV
    f = lambda a: np.ascontiguousarray(np.asarray(a, np.float32))

    def rep(a):
        a = f(a)
        return np.ascontiguousarray(np.broadcast_to(a[:, None, :], (a.shape[0], 128, a.shape[1])))

    def colT(a):
        a = f(a)
        return np.ascontiguousarray(a.reshape(a.shape[0], 8, 128).transpose(0, 2, 1))

    def convT(a):
        a = f(a)
        return np.ascontiguousarray(a.reshape(a.shape[0], a.shape[1], 8, 128).transpose(0, 3, 2, 1))

    NBs = (PAST + TS + 127) // 128
    if NBs % 2:
        NBs += 1
    koff = NBs * 128 - (PAST + TS)
    consts = make_consts()
    smask = make_smask(PAST, TS, koff)
    nml = max(NML, 1)
    ncv = max(NCV, 1)

    def orz(a, shape):
        a = f(a)
        if a.shape[0] == 0:
            return np.zeros(shape, np.float32)
        return a

    convml = np.concatenate([f(inp["conv_ml_w"]), f(inp["conv_ml_b"])[:, None, :]], axis=1) if NML else np.zeros((1, 5, D), np.float32)
    gcol = np.ascontiguousarray(np.stack([f(inp["b_ig_ml"]), f(inp["b_fg_ml"])], axis=2)) if NML else np.zeros((1, 4, 2), np.float32)
    ii = np.arange(128)
    negmask = np.where(ii[:, None] <= ii[None, :], 0.0, -30000.0).astype(np.float32)
    sel4 = np.zeros((4, 4, 128), np.float32)
    for hh in range(4):
        sel4[hh, hh, :] = 1.0
    sel4 = sel4.reshape(4, 512)
    common = {
        "memp": None, "gpre_r": rep(inp["g_pre"]), "gpost_r": rep(inp["g_post"]), "gmem_r": rep(inp["g_mem"]),
        "wmemkv": f(inp["w_mem_kv"]), "wout": f(inp["w_out"]), "winsb": f(inp["w_in_sb"]),
        "winml": orz(inp["w_in_ml"], (1, D, 5128)), "wincv": orz(inp["w_in_cv"], (1, D, 5120)),
        "convmlT": convT(convml), "wq": orz(inp["wq_ml"], (1, 4, 256, 256)), "wk": orz(inp["wk_ml"], (1, 4, 256, 256)),
        "gcol": gcol, "ghead_r": rep(orz(inp["g_head_ml"], (1, D))), "negmask": negmask, "sel4": sel4, "skipT": colT(orz(inp["skip_ml"], (1, D))),
        "convcvT": convT(orz(inp["conv_cv_w"], (1, 3, D))), "consts": consts, "smask": smask,
    }
    in_maps = []
    ncores = 8
    for c in range(ncores):
        bp = c % B
        bs = c % SBN
        m = dict(common)
        m["xp"] = f(x_prompt[bp])
        m["xs"] = f(x_sample[bs])
        m["csk"] = f(inp["cache_sb_k"][:, bs])
        m["csv"] = f(inp["cache_sb_v"][:, bs])
        m["smc"] = orz(np.asarray(inp["state_ml_c"])[:, bs], (1, 4, 256, 256))
        m["smn"] = orz(np.asarray(inp["state_ml_n"])[:, bs], (1, 4, 256))
        m["smm"] = orz(np.asarray(inp["state_ml_m"])[:, bs], (1, 4))
        m["smconvT"] = convT(orz(np.asarray(inp["state_ml_conv"])[:, bs], (1, 3, D)))
        m["scvT"] = convT(orz(np.asarray(inp["state_cv_conv"])[:, bs], (1, 2, D)))
        m["cmk"] = f(inp["cache_mem_k"][:, bs])
        m["cmv"] = f(inp["cache_mem_v"][:, bs])
        m["memp"] = f(inp["mem_prompt"][bp])
        in_maps.append(m)
    return p, in_maps, B, SBN


def kernel(**inp):
    p, in_maps, B, SBN = _prep(inp)
    NML, NCV = p.NML, p.NCV
    res = run_bass_kernel_spmd(p.nc, in_maps, core_ids=list(range(len(in_maps))))
    R = res.results

    def gp(name, axis_b):
        return np.stack([np.asarray(R[c][name], np.float32) for c in range(B)], axis=axis_b)

    def gs(name, axis_b):
        return np.stack([np.asarray(R[c][name], np.float32) for c in range(SBN)], axis=axis_b)

    outs = (
        gp("yp", 0), gs("ys", 0),
        gp("sbk_p", 1), gp("sbv_p", 1),
        gp("mlc_p", 1)[:NML], gp("mln_p", 1)[:NML], gp("mlm_p", 1)[:NML], gp("mlconv_p", 1)[:NML],
        gp("cv_p", 1)[:NCV],
        gp("memk_p", 1), gp("memv_p", 1),
        gs("sbk_s", 1), gs("sbv_s", 1),
        gs("mlc_s", 1)[:NML], gs("mln_s", 1)[:NML], gs("mlm_s", 1)[:NML], gs("mlconv_s", 1)[:NML],
        gs("cv_s", 1)[:NCV],
    )
    return outs
```
